# Optimizing a Trainium2 kernel written in Bass

```python
import jax, jax.numpy as jnp
from jax import lax
import numpy as np

D_MODEL = 1024
BATCH = 16
SEQ = 256
DEPTH = 2
DEC_BATCH = 4
DEC_SEQ = 4096
PAST_LEN = 512

GRID_W = 64
SC_DIM = 256
SC_WIDTH = 3
N_Q_HEADS = 8
N_KV_HEADS = 2
N_GROUP = N_Q_HEADS // N_KV_HEADS
HEAD_DIM = 64
ATT_DIM = N_Q_HEADS * HEAD_DIM
KV_DIM = N_KV_HEADS * HEAD_DIM
WINDOW = 128
BLOCK = 128
N_DN_HEADS = 4
DN_HEAD_DIM = 64
DN_DIM = N_DN_HEADS * DN_HEAD_DIM
DN_CONV = 3
CHUNK = 64
MIX_DIM = SC_DIM + ATT_DIM + DN_DIM
IN_SIZES = (SC_DIM, SC_DIM, SC_DIM, ATT_DIM, KV_DIM, KV_DIM, 3 * DN_DIM, DN_DIM, 2 * N_DN_HEADS, 2 * N_DN_HEADS)
IN_DIM = sum(IN_SIZES)
D_FF = -(-8 * D_MODEL // (3 * 256)) * 256
ROPE_BASE = 10000.0
EPS = 1e-6
NEG = -1e30

kernel_name = 'hybrid_diffusion_trunk_step'

F32 = jnp.float32


def _rms(x, g):
    xf = x.astype(F32)
    y = xf * lax.rsqrt(jnp.mean(xf * xf, axis=-1, keepdims=True) + EPS)
    return (y * g.astype(F32)).astype(x.dtype)


def _l2n(x):
    xf = x.astype(F32)
    return xf * lax.rsqrt(jnp.sum(xf * xf, axis=-1, keepdims=True) + EPS)


def _dwconv(x, w):
    p = w.shape[0] // 2
    return lax.conv_general_dilated(x, w[:, None, :].astype(x.dtype), (1,), [(p, p)],
                                    dimension_numbers=('NWC', 'WIO', 'NWC'),
                                    feature_group_count=x.shape[-1])


def _rope_part(x, pos):
    h = x.shape[-1] // 2
    inv = 1.0 / (ROPE_BASE ** (jnp.arange(h, dtype=F32) / h))
    ang = pos.astype(F32)[:, None] * inv
    cos = jnp.cos(ang)[None, :, None, :]
    sin = jnp.sin(ang)[None, :, None, :]
    x1, x2 = x[..., :h], x[..., h:]
    return jnp.concatenate([x1 * cos - x2 * sin, x2 * cos + x1 * sin], axis=-1)


def _axial_rope(x):
    n = x.shape[1]
    n_rows = n // GRID_W
    row = jnp.repeat(jnp.arange(n_rows), GRID_W)
    col = jnp.tile(jnp.arange(GRID_W), n_rows)
    xf = x.astype(F32)
    half = HEAD_DIM // 2
    out = jnp.concatenate([_rope_part(xf[..., :half], row), _rope_part(xf[..., half:], col)], axis=-1)
    return out.astype(x.dtype)


def _sink_attend(qb, sink, kv_sets):
    scale = HEAD_DIM ** -0.5
    logits = []
    for k, v, m in kv_sets:
        s = jnp.einsum('bqkgd,bskd->bkgqs', qb, k).astype(F32) * scale
        if m is not None:
            s = jnp.where(m, s, NEG)
        logits.append(s)
    bsz, nq = qb.shape[0], qb.shape[1]
    sink_l = jnp.broadcast_to(sink.astype(F32).reshape(N_KV_HEADS, N_GROUP)[None, :, :, None, None],
                              (bsz, N_KV_HEADS, N_GROUP, nq, 1))
    p = jax.nn.softmax(jnp.concatenate([sink_l] + logits, axis=-1), axis=-1)
    out = None
    off = 1
    for (k, v, m), s in zip(kv_sets, logits):
        ns = s.shape[-1]
        o = jnp.einsum('bkgqs,bskd->bqkgd', p[..., off:off + ns].astype(v.dtype), v)
        out = o if out is None else out + o
        off += ns
    return out


def _context_attention(q, k, v, sink):
    bsz, n = q.shape[0], q.shape[1]
    def blk(i):
        qb = lax.dynamic_slice_in_dim(q, i * BLOCK, BLOCK, axis=1)
        return _sink_attend(qb, sink, [(k, v, None)])
    o = lax.map(blk, jnp.arange(n // BLOCK))
    return jnp.moveaxis(o, 0, 1).reshape(bsz, n, ATT_DIM)


def _latent_attention(q, k, v, k_ctx, v_ctx, sink):
    bsz, n = q.shape[0], q.shape[1]
    pad = ((0, 0), (BLOCK, BLOCK), (0, 0), (0, 0))
    kp = jnp.pad(k, pad)
    vp = jnp.pad(v, pad)
    qoff = jnp.arange(BLOCK)
    koff = jnp.arange(3 * BLOCK) - BLOCK
    def blk(i):
        start = i * BLOCK
        qb = lax.dynamic_slice_in_dim(q, start, BLOCK, axis=1)
        kb = lax.dynamic_slice_in_dim(kp, start, 3 * BLOCK, axis=1)
        vb = lax.dynamic_slice_in_dim(vp, start, 3 * BLOCK, axis=1)
        qpos = start + qoff
        kpos = start + koff
        m = (jnp.abs(qpos[:, None] - kpos[None, :]) <= WINDOW) & (kpos >= 0)[None, :] & (kpos < n)[None, :]
        return _sink_attend(qb, sink, [(k_ctx, v_ctx, None), (kb, vb, m)])
    o = lax.map(blk, jnp.arange(n // BLOCK))
    return jnp.moveaxis(o, 0, 1).reshape(bsz, n, ATT_DIM)


def _gated_delta_chunked(q, k, v, g, beta, s0):
    b, n, h, dk = q.shape
    dv = v.shape[-1]
    nc = n // CHUNK
    def chunks(t):
        return jnp.swapaxes(t, 1, 2).reshape((b, h, nc, CHUNK) + t.shape[3:])
    qc, kc, vc, gc, bc = (chunks(t) for t in (q, k, v, g, beta))
    gc = jnp.cumsum(gc, axis=-1)
    idx = jnp.arange(CHUNK)
    incl = idx[:, None] >= idx[None, :]
    strict = idx[:, None] > idx[None, :]
    dec_incl = jnp.exp(jnp.where(incl, gc[..., :, None] - gc[..., None, :], NEG))
    dec_strict = jnp.where(strict, dec_incl, 0.0)
    kb = kc * bc[..., None]
    lmat = jnp.einsum('bhnid,bhnjd->bhnij', kb, kc) * dec_strict
    eye = jnp.eye(CHUNK, dtype=F32)
    rhs = jnp.concatenate([vc * bc[..., None], kb * jnp.exp(gc)[..., None]], axis=-1)
    sol = lax.linalg.triangular_solve(lmat + eye, rhs, left_side=True, lower=True, unit_diagonal=True)
    u, w = sol[..., :dv], sol[..., dv:]
    a_in = jnp.einsum('bhnid,bhnjd->bhnij', qc, kc) * dec_incl
    def step(s, inp):
        q_t, k_t, u_t, w_t, g_t, a_t = inp
        v_new = u_t - jnp.einsum('bhck,bhkv->bhcv', w_t, s)
        o = (jnp.einsum('bhck,bhkv->bhcv', q_t * jnp.exp(g_t)[..., None], s)
             + jnp.einsum('bhij,bhjv->bhiv', a_t, v_new))
        g_last = g_t[..., -1]
        s = (s * jnp.exp(g_last)[..., None, None]
             + jnp.einsum('bhck,bhcv->bhkv', k_t * jnp.exp(g_last[..., None] - g_t)[..., None], v_new))
        return s, o
    xs = tuple(jnp.moveaxis(t, 2, 0) for t in (qc, kc, u, w, gc, a_in))
    s_fin, o = lax.scan(step, s0, xs)
    o = jnp.swapaxes(jnp.moveaxis(o, 0, 2).reshape(b, h, n, dv), 1, 2)
    return o, s_fin


def _delta_mixer(qkv_in, z, a, bg, conv_w, a_log, dt_bias, norm_g, s0):
    bsz, n, _ = qkv_in.shape
    qkv = jax.nn.silu(_dwconv(qkv_in, conv_w))
    q, k, v = jnp.split(qkv, 3, axis=-1)
    shp = (bsz, n, N_DN_HEADS, DN_HEAD_DIM)
    q = _l2n(q.reshape(shp)) * (DN_HEAD_DIM ** -0.5)
    k = _l2n(k.reshape(shp))
    v = v.reshape(shp).astype(F32)
    a = a.reshape(bsz, n, 2, N_DN_HEADS).astype(F32)
    bg = bg.reshape(bsz, n, 2, N_DN_HEADS).astype(F32)
    gdec = -jnp.exp(a_log.astype(F32)) * jax.nn.softplus(a + dt_bias.astype(F32))
    beta = jax.nn.sigmoid(bg)
    s0 = s0.astype(F32)
    o_f, s_f = _gated_delta_chunked(q, k, v, gdec[:, :, 0], beta[:, :, 0], s0[:, 0])
    fl = lambda t: jnp.flip(t, axis=1)
    o_b, s_b = _gated_delta_chunked(fl(q), fl(k), fl(v), fl(gdec[:, :, 1]), fl(beta[:, :, 1]), s0[:, 1])
    o = o_f + fl(o_b)
    o = _rms(o, norm_g) * jax.nn.silu(z.reshape(shp).astype(F32))
    return o.reshape(bsz, n, DN_DIM).astype(z.dtype), jnp.stack([s_f, s_b], axis=1)


def _layer(x, cvec, p, ctx):
    bsz, n, _ = x.shape
    mod = (jax.nn.silu(cvec) @ p['ada_w'] + p['ada_b'])[:, None, :]
    sh1, sc1, g1, sh2, sc2, g2 = jnp.split(mod, 6, axis=-1)
    h = _rms(x, p['norm1_g']) * (1 + sc1) + sh1
    splits = np.cumsum(IN_SIZES)[:-1].tolist()
    sc_b, sc_c, sc_h, q, k, v, dn_qkv, dn_z, dn_a, dn_b = jnp.split(h @ p['w_in'], splits, axis=-1)
    y_sc = sc_b * _dwconv(sc_c * sc_h, p['sc_conv_w'])
    q = _rms(q.reshape(bsz, n, N_Q_HEADS, HEAD_DIM), p['q_norm_g'])
    k = _rms(k.reshape(bsz, n, N_KV_HEADS, HEAD_DIM), p['k_norm_g'])
    v = v.reshape(bsz, n, N_KV_HEADS, HEAD_DIM)
    gshape = (bsz, n, N_KV_HEADS, N_GROUP, HEAD_DIM)
    if ctx is None:
        y_att = _context_attention(q.reshape(gshape), k, v, p['attn_sink'])
        s0 = jnp.zeros((bsz, 2, N_DN_HEADS, DN_HEAD_DIM, DN_HEAD_DIM), F32)
    else:
        k_ctx, v_ctx, s0 = ctx
        y_att = _latent_attention(_axial_rope(q).reshape(gshape), _axial_rope(k), v,
                                  k_ctx.astype(k.dtype), v_ctx.astype(v.dtype), p['attn_sink'])
    y_dn, s_out = _delta_mixer(dn_qkv, dn_z, dn_a, dn_b, p['dn_conv_w'], p['dn_A_log'],
                               p['dn_dt_bias'], p['dn_norm_g'], s0)
    y = jnp.concatenate([y_sc, y_att.astype(y_sc.dtype), y_dn], axis=-1) @ p['w_out']
    x = x + g1 * y
    h2 = _rms(x, p['norm2_g']) * (1 + sc2) + sh2
    x = x + g2 * ((jax.nn.silu(h2 @ p['w_gate']) * (h2 @ p['w_up'])) @ p['w_down'])
    if ctx is None:
        return x, (k, v, s_out.astype(x.dtype))
    return x, None


def setup_inputs(seed: int = 0) -> dict:
    key = jax.random.key(seed)
    ks = jax.random.split(key, 32)
    nrm = lambda i, shape, s: jax.random.normal(ks[i], shape, F32) * s
    a_raw = jax.random.uniform(ks[20], (DEPTH, 2, N_DN_HEADS), F32, 1.0, 16.0)
    dt = jnp.exp(jax.random.uniform(ks[21], (DEPTH, 2, N_DN_HEADS), F32, float(np.log(1e-3)), float(np.log(1e-1))))
    return {
        'x_prompt': nrm(0, (BATCH, SEQ, D_MODEL), 1.0),
        'x_sample': nrm(1, (DEC_BATCH, DEC_SEQ, D_MODEL), 1.0),
        'cache_k': nrm(2, (DEC_BATCH, DEPTH, PAST_LEN, N_KV_HEADS, HEAD_DIM), 1.0),
        'cache_v': nrm(3, (DEC_BATCH, DEPTH, PAST_LEN, N_KV_HEADS, HEAD_DIM), 1.0),
        'state_delta': nrm(4, (DEC_BATCH, DEPTH, 2, N_DN_HEADS, DN_HEAD_DIM, DN_HEAD_DIM), 0.1),
        'c': nrm(5, (DEC_BATCH, D_MODEL), 1.0),
        'c_ctx': nrm(6, (D_MODEL,), 1.0),
        'w_in': nrm(7, (DEPTH, D_MODEL, IN_DIM), D_MODEL ** -0.5),
        'w_out': nrm(8, (DEPTH, MIX_DIM, D_MODEL), MIX_DIM ** -0.5),
        'ada_w': nrm(9, (DEPTH, D_MODEL, 6 * D_MODEL), 0.5 * D_MODEL ** -0.5),
        'ada_b': nrm(10, (DEPTH, 6 * D_MODEL), 0.01),
        'norm1_g': 1.0 + nrm(11, (DEPTH, D_MODEL), 0.05),
        'norm2_g': 1.0 + nrm(12, (DEPTH, D_MODEL), 0.05),
        'sc_conv_w': nrm(13, (DEPTH, SC_WIDTH, SC_DIM), SC_WIDTH ** -0.5),
        'dn_conv_w': nrm(14, (DEPTH, DN_CONV, 3 * DN_DIM), DN_CONV ** -0.5),
        'q_norm_g': 1.0 + nrm(15, (DEPTH, HEAD_DIM), 0.05),
        'k_norm_g': 1.0 + nrm(16, (DEPTH, HEAD_DIM), 0.05),
        'attn_sink': nrm(17, (DEPTH, N_Q_HEADS), 0.5),
        'dn_A_log': jnp.log(a_raw),
        'dn_dt_bias': dt + jnp.log(-jnp.expm1(-dt)),
        'dn_norm_g': 1.0 + nrm(18, (DEPTH, DN_HEAD_DIM), 0.05),
        'w_gate': nrm(22, (DEPTH, D_MODEL, D_FF), D_MODEL ** -0.5),
        'w_up': nrm(23, (DEPTH, D_MODEL, D_FF), D_MODEL ** -0.5),
        'w_down': nrm(24, (DEPTH, D_FF, D_MODEL), D_FF ** -0.5),
    }


def reference(x_prompt, x_sample, cache_k, cache_v, state_delta, c, c_ctx, w_in, w_out, ada_w, ada_b,
              norm1_g, norm2_g, sc_conv_w, dn_conv_w, q_norm_g, k_norm_g, attn_sink, dn_A_log,
              dn_dt_bias, dn_norm_g, w_gate, w_up, w_down):
    xp, xs = x_prompt, x_sample
    new_k, new_v, new_s = [], [], []
    for l in range(DEPTH):
        p = {'w_in': w_in[l], 'w_out': w_out[l], 'ada_w': ada_w[l], 'ada_b': ada_b[l],
             'norm1_g': norm1_g[l], 'norm2_g': norm2_g[l], 'sc_conv_w': sc_conv_w[l],
             'dn_conv_w': dn_conv_w[l], 'q_norm_g': q_norm_g[l], 'k_norm_g': k_norm_g[l],
             'attn_sink': attn_sink[l], 'dn_A_log': dn_A_log[l], 'dn_dt_bias': dn_dt_bias[l],
             'dn_norm_g': dn_norm_g[l], 'w_gate': w_gate[l], 'w_up': w_up[l], 'w_down': w_down[l]}
        xp, (kl, vl, sl) = _layer(xp, c_ctx[None, :], p, None)
        new_k.append(kl)
        new_v.append(vl)
        new_s.append(sl)
        xs, _ = _layer(xs, c, p, (cache_k[:, l], cache_v[:, l], state_delta[:, l]))
    return (xp, xs, jnp.stack(new_k, axis=1), jnp.stack(new_v, axis=1), jnp.stack(new_s, axis=1))
```

```python
from contextlib import ExitStack
import numpy as np
import concourse.bass as bass
import concourse.mybir as mybir
from concourse.bass_utils import run_bass_kernel_spmd

F32 = mybir.dt.float32
BF16 = mybir.dt.bfloat16
AF = mybir.ActivationFunctionType
ALU = mybir.AluOpType
AX = mybir.AxisListType


class Dep:
    __slots__ = ("w", "r", "x")

    def __init__(self):
        self.w = None
        self.r = []
        self.x = False


class _Rec:
    def __init__(self):
        self.calls = []

    def __getattr__(self, name):
        def f(*a, **k):
            self.calls.append((name, a, k))
            return self
        return f


def _replay_calls(calls):
    def fn(e):
        ins = None
        for name, a, k in calls:
            ins = getattr(e, name)(*a, **k)
        return ins
    return fn


class Prog:
    CE = ("tensor", "vector", "scalar", "gpsimd")
    NDMA = {"sync": 12, "gpsimd": 6, "scalar": 4}

    def __init__(self, nc):
        self.nc = nc
        self.stack = ExitStack()
        self.ops = {e: [] for e in ("tensor", "vector", "scalar", "gpsimd", "sync")}
        self.sem = {}
        self.cnt = {}
        self.seen = {e: {} for e in self.ops}
        for e in self.CE:
            self.sem[e] = self.stack.enter_context(nc.semaphore("s_" + e))
            self.cnt[e] = 0
        self.dsem = {}
        self.dval = {}
        self.dnext = {}
        for q, n in self.NDMA.items():
            self.dsem[q] = [self.stack.enter_context(nc.semaphore("d_%s%d" % (q, i))) for i in range(n)]
            self.dnext[q] = 0
        for q in self.dsem:
            for s in self.dsem[q]:
                self.dval[id(s)] = 0
        self.n_alloc = 0

    def sbuf(self, name, shape, dtype):
        self.n_alloc += 1
        return self.stack.enter_context(self.nc.sbuf_tensor("%s_s%d" % (name, self.n_alloc), list(shape), dtype))

    def psum(self, name, shape, dtype):
        self.n_alloc += 1
        return self.stack.enter_context(self.nc.psum_tensor("%s_p%d" % (name, self.n_alloc), list(shape), dtype))

    def dep(self):
        return Dep()

    def deps(self, n):
        return [Dep() for _ in range(n)]

    def _collect(self, eng, reads, writes):
        toks = []
        for d in reads:
            if d.w is not None:
                toks.append(d.w)
        for d in writes:
            if d.w is not None:
                toks.append(d.w)
            toks.extend(d.r)
        seen = self.seen[eng]
        waits = {}
        for (s, v, src) in toks:
            if src == eng and eng == "tensor":
                continue
            k = id(s)
            if seen.get(k, 0) >= v:
                continue
            if k not in waits or waits[k][1] < v:
                waits[k] = (s, v)
        for k, (s, v) in waits.items():
            seen[k] = v
        return list(waits.values())

    def _commit(self, tok, reads, writes):
        for d in reads:
            d.r.append(tok)
        for d in writes:
            d.w = tok
            d.r = []

    max_ops = None
    n_ops = 0
    log = []

    def _skip(self, desc):
        Prog.n_ops += 1
        if Prog.max_ops is not None and Prog.n_ops > Prog.max_ops:
            return True
        Prog.log.append(desc)
        return False

    def op(self, eng, fn, reads=(), writes=()):
        if self._skip((eng,)):
            return None
        writes = list(writes) + [d for d in reads if d.x]
        reads = [d for d in reads if not d.x]
        waits = self._collect(eng, reads, writes)
        self.cnt[eng] += 1
        tok = (self.sem[eng], self.cnt[eng], eng)
        rec = _Rec()
        fn(rec)
        self.ops[eng].append((waits, _replay_calls(rec.calls), self.sem[eng], 1))
        self._commit(tok, reads, writes)
        return tok

    def dma(self, q, out, in_, reads=(), writes=(), **kw):
        if self._skip(("dma_" + q,)):
            return None
        pool = self.dsem[q]
        s = pool[self.dnext[q] % len(pool)]
        self.dnext[q] += 1
        waits = self._collect(q, reads, writes)
        prev = self.dval[id(s)]
        if prev > 0 and self.seen[q].get(id(s), 0) < prev:
            waits.append((s, prev))
            self.seen[q][id(s)] = prev
        self.dval[id(s)] = prev + 16
        tok = (s, prev + 16, "dma_" + q)
        self.ops[q].append((waits, lambda e: e.dma_start(out=out, in_=in_, **kw), s, 16))
        self._commit(tok, reads, writes)
        return tok

    def wait_on(self, eng, deps_):
        waits = self._collect(eng, deps_, ())
        self.ops[eng].append((waits, None, None, 0))

    def flush(self):
        for q in self.dsem:
            waits = []
            for s in self.dsem[q]:
                v = self.dval[id(s)]
                if v > 0 and self.seen["sync"].get(id(s), 0) < v:
                    waits.append((s, v))
                    self.seen["sync"][id(s)] = v
            if waits:
                self.ops["sync"].append((waits, None, None, 0))
        nc = self.nc

        def replay(name):
            def run(e):
                for waits, fn, s, inc in self.ops[name]:
                    for (ws, wv) in waits:
                        e.wait_ge(ws, wv)
                    if fn is not None:
                        ins = fn(e)
                        ins.then_inc(s, inc)
            return run

        with nc.Block() as block:
            block.tensor(replay("tensor"))
            block.vector(replay("vector"))
            block.scalar(replay("scalar"))
            block.gpsimd(replay("gpsimd"))
            block.sync(replay("sync"))
        for k in self.ops:
            self.ops[k] = []

    def phase(self):
        prog = self

        class _Ph:
            def __enter__(s):
                s.saved = prog.stack
                prog.stack = ExitStack()
                return prog

            def __exit__(s, *a):
                if a[0] is None:
                    prog.flush()
                prog.stack.close()
                prog.stack = s.saved
                return False
        return _Ph()

    def finish(self, out_deps=()):
        self.wait_on("sync", out_deps)
        self.flush()
        self.stack.close()


class Ring:
    def __init__(self, P, name, shape, dtype, n, space="sbuf"):
        mk = P.sbuf if space == "sbuf" else P.psum
        self.t = [mk("%s_%d" % (name, i), shape, dtype) for i in range(n)]
        self.d = [P.dep() for _ in range(n)]
        if space != "sbuf":
            for d in self.d:
                d.x = True
        self.i = 0

    def next(self):
        k = self.i % len(self.t)
        self.i += 1
        return self.t[k], self.d[k]

    @classmethod
    def of(cls, tiles, deps):
        r = cls.__new__(cls)
        r.t = list(tiles); r.d = list(deps); r.i = 0
        return r


def run_lockstep(gens):
    alive = list(gens)
    while alive:
        for g in list(alive):
            try:
                next(g)
            except StopIteration:
                alive.remove(g)


D = 1024
KC = 8
NCOLS = 2576
DFF = 2816
EPS = 1e-6
CW = 3200


def build(NS=4096, depth=2, dbg=False):
    NT = NS + 512
    nc = bass.Bass("TRN2", target_bir_lowering=False)

    def din(name, shape, dt=F32):
        return nc.dram_tensor(name, list(shape), dt, kind="ExternalInput").ap()

    def dout(name, shape, dt=F32):
        return nc.dram_tensor(name, list(shape), dt, kind="ExternalOutput").ap()

    def dint(name, shape, dt=F32):
        if dbg:
            return nc.dram_tensor(name, list(shape), dt, kind="ExternalOutput").ap()
        return nc.dram_tensor(name, list(shape), dt).ap()

    xs = din("xs", [NS, D]); xp = din("xp", [512, D])
    ck = din("ck", [depth, 512, 128]); cv = din("cv", [depth, 512, 128])
    s0in = din("s0", [depth, 8, 64, 64])
    cTin = din("cT", [128, 16])
    win = din("win", [depth, D, NCOLS]); wout = din("wout", [depth, D, D])
    adaw = din("adaw", [depth, D, 6 * D])
    wg = din("wg", [depth, D, DFF]); wu = din("wu", [depth, D, DFF]); wd = din("wd", [depth, DFF, D])
    pvin = din("pv", [depth, 128, 90]); bcin = din("bcp", [depth, 128, 96])
    cstin = din("cst", [128, CW]); ropec = din("ropec", [128, 4096]); ropes = din("ropes", [128, 4096])
    ys = dout("ys", [NS, D]); yp = dout("yp", [512, D])
    nk = dout("nk", [2, depth, 256, 128]); nv = dout("nv", [2, depth, 256, 128])
    nst = dout("nst", [2, depth, 8, 64, 64])

    XT = dint("XT", [KC, 128, NT]).rearrange("c p t -> p c t")
    PROJ = dint("PROJ", [20, 128, NT]).rearrange("c p t -> p c t")
    AB = dint("AB", [NT, 16])
    DNQ = dint("DNQ", [6, 128, NT])
    YMIX = dint("YMIX", [KC, 128, NT], BF16).rearrange("c p t -> p c t")
    H2 = dint("H2", [KC, 128, NT], BF16).rearrange("c p t -> p c t")
    OD = dint("OD", [2, NT, 256])
    dXT, dPROJ, dAB, dDNQ, dYMIX, dH2, dOD = [Dep() for _ in range(7)]
    dOUT = Dep()

    P = Prog(nc)
    seqs = [(0, NS, "s", 0), (NS, 256, "p", 0), (NS + 256, 256, "p", 1)]
    tiles = [(t0, 0) for t0 in range(0, NS, 512)] + [(NS, 1)]

    cst = P.sbuf("cst", [128, CW], F32); dcst = Dep()
    P.dma("sync", cst[:], cstin, writes=[dcst])
    identf = cst[:, 0:128]
    cb = P.sbuf("cb", [128, 6 * 128], BF16); dcb = Dep()
    P.op("vector", lambda e: e.tensor_copy(cb[:], cst[:, 0:768]), [dcst], [dcb])
    identb = cb[:, 0:128]; onesb = cb[:, 128:256]; bd64b = cb[:, 256:384]; rpermb = cb[:, 384:512]
    mwprev = cb[:, 512:640]; mwnext = cb[:, 640:768]
    MS8 = cst[:, 768:1280]; MI8 = cst[:, 1280:1792]; IDB8 = cst[:, 1792:2304]
    U8 = cst[:, 2432:2944].rearrange("p (k j) -> p k j", k=8)
    UCF2 = cst[:, 2944:3072]; UCB2 = cst[:, 3072:3200]; bd64f = cst[:, 256:384]
    UCF = cst[0:64, 2304:2368]; UCB = cst[0:64, 2368:2432]
    onesf64 = cst[0:64, 128:192]
    modv = P.sbuf("modv", [128, 48, 2], F32); dmod = Dep()
    A1 = P.sbuf("A1", [128, 8, 2], F32); A2 = P.sbuf("A2", [128, 8, 2], F32)
    pvt = P.sbuf("pvt", [128, 90], F32); bct = P.sbuf("bct", [128, 96], F32); dpv = Dep()
    sinkexp = P.sbuf("sinkexp", [128, 8], F32)
    negA = P.sbuf("negA", [64, 8], F32); negA2 = P.sbuf("negA2", [128, 8], F32)
    PVO = {"adab": 0, "n1": 48, "n2": 56, "scw": 64, "dnw": 70, "qg": 88, "kg": 89}
    BCO = {"alog": 0, "dtb": 8, "dng": 16, "sink": 88}
    P.flush()

    def mm_group(out, pairs, reads, writes):
        n = len(pairs)

        def fn(e):
            ins = None
            for i, (l_, r_) in enumerate(pairs):
                ins = e.matmul(out, l_, r_, start=(i == 0), stop=(i == n - 1))
            return ins
        P.op("tensor", fn, reads, writes)

    evac_i = [0]

    def evac(out, in_, reads, writes):
        evac_i[0] += 1
        if evac_i[0] % 2:
            P.op("vector", lambda e: e.tensor_copy(out, in_), reads, writes)
        else:
            P.op("scalar", lambda e: e.copy(out, in_), reads, writes)

    with P.phase():
        xin = Ring(P, "xin", [128, D], F32, 4)
        xo = Ring(P, "xo", [128, KC, 512], F32, 2)
        pst = Ring(P, "pst", [128, 512], F32, 8, "psum")
        for gi in range(NT // 512):
            o, do = xo.next()
            for sub in range(4):
                t0 = gi * 512 + sub * 128
                src = xs[t0:t0 + 128, :] if t0 < NS else xp[t0 - NS:t0 - NS + 128, :]
                a, da = xin.next()
                P.dma("sync", a[:], src, writes=[da])
                for hh in range(2):
                    ps, dps = pst.next()

                    def fn(e, ps=ps, a=a, hh=hh):
                        ins = None
                        for j in range(4):
                            c = hh * 4 + j
                            ins = e.transpose(ps[:, j * 128:(j + 1) * 128], a[:, c * 128:(c + 1) * 128], identf)
                        return ins
                    P.op("tensor", fn, [da, dcst], [dps])
                    evac(o[:, hh * 4:(hh + 1) * 4, sub * 128:(sub + 1) * 128], ps[:].rearrange("p (a b) -> p a b", a=4), [dps], [do])
            P.dma("sync", XT[:, :, gi * 512:(gi + 1) * 512], o[:], reads=[do], writes=[dXT])

    for l in range(depth):
        with P.phase():
            P.dma("sync", pvt[:], pvin[l], writes=[dpv])
            P.dma("sync", bct[:], bcin[l], writes=[dpv])
            ct = P.sbuf("ct", [128, 16], F32); dct = Dep()
            P.dma("sync", ct[:], cTin, writes=[dct])
            sil = P.sbuf("sil", [128, 16], BF16); dsil = Dep()
            P.op("scalar", lambda e: e.activation(sil[:], ct[:], AF.Silu), [dct], [dsil])
            war = Ring(P, "wa", [128, KC, 768], BF16, 3)
            psm = P.psum("psm", [128, 512], F32); dpsm = Dep()
            awv = adaw[l].rearrange("(kc p) n -> p kc n", p=128)
            for g in range(8):
                wa, dwa = war.next()
                for hh in range(2):
                    P.dma("gpsimd", wa[:, :, hh * 384:(hh + 1) * 384], awv[:, :, g * 768 + hh * 384:g * 768 + (hh + 1) * 384], writes=[dwa])
                for fcl in range(6):
                    fc = g * 6 + fcl
                    mm_group(psm[:, fc * 2:fc * 2 + 2],
                             [(wa[:, kc, fcl * 128:(fcl + 1) * 128], sil[:, kc * 2:kc * 2 + 2]) for kc in range(KC)],
                             [dwa, dsil], [dpsm])
            P.op("vector", lambda e: e.tensor_tensor(modv[:], psm[:, 0:96].rearrange("p (a b) -> p a b", b=2),
                                                      pvt[:, 0:48].unsqueeze(2).broadcast_to([128, 48, 2]), ALU.add), [dpsm, dpv], [dmod])
            P.op("vector", lambda e: e.scalar_tensor_tensor(A1[:], modv[:, 8:16, :], 1.0, pvt[:, 48:56].unsqueeze(2).broadcast_to([128, 8, 2]), ALU.add, ALU.mult), [dmod, dpv], [dmod])
            P.op("vector", lambda e: e.scalar_tensor_tensor(A2[:], modv[:, 32:40, :], 1.0, pvt[:, 56:64].unsqueeze(2).broadcast_to([128, 8, 2]), ALU.add, ALU.mult), [dmod, dpv], [dmod])
            P.op("scalar", lambda e: e.activation(sinkexp[:], bct[:, 88:96], AF.Exp), [dpv], [dmod])
            P.op("scalar", lambda e: e.activation(negA[:], bct[0:64, 0:8], AF.Exp), [dpv], [dmod])
            P.op("vector", lambda e: e.tensor_scalar(negA[:], negA[:], -1.0, None, ALU.mult), [dmod], [dmod])
            P.op("scalar", lambda e: e.activation(negA2[:], bct[:, 0:8], AF.Exp), [dpv], [dmod])
            P.op("vector", lambda e: e.tensor_scalar(negA2[:], negA2[:], -1.0, None, ALU.mult), [dmod], [dmod])

        def norm_to_h_gen(xt, dxt, hT, dh, seg, A_, B0, rings, T=512):
            sq, dsq = rings["sq"].next()
            P.op("scalar", lambda e: e.activation(sq[:, :, 0:T], xt[:, :, 0:T], AF.Square), [dxt], [dsq])
            yield
            psn, dpsn = rings["psn"].next()
            mm_group(psn[:, 0:T], [(onesb, sq[:, kc, 0:T]) for kc in range(KC)], [dsq, dcb], [dpsn])
            yield
            rs, drs = rings["rs"].next()
            P.op("scalar", lambda e: e.activation(rs[:, 0:T], psn[:, 0:T], AF.Ln, bias=EPS, scale=1.0 / D), [dpsn], [drs])
            yield
            P.op("scalar", lambda e: e.activation(rs[:, 0:T], rs[:, 0:T], AF.Exp, scale=-0.5), [drs], [drs])
            yield
            for kc in range(KC):
                tmp, dtmp = rings["tmp"].next()
                P.op("vector", lambda e, kc=kc, tmp=tmp: e.tensor_tensor(tmp[:, 0:T], xt[:, kc, 0:T], rs[:, 0:T], ALU.mult), [dxt, drs], [dtmp])
                yield
                P.op("scalar", lambda e, kc=kc, tmp=tmp: e.activation(hT[:, kc, 0:T], tmp[:, 0:T], AF.Identity,
                                                                      bias=modv[:, B0 + kc, seg:seg + 1], scale=A_[:, kc, seg:seg + 1]), [dtmp, dmod], [dh])
                yield


        def norm_to_h(*a, **k):
            for _ in norm_to_h_gen(*a, **k):
                pass

        def advance(gen, n):
            if gen is None:
                return
            for _ in range(n):
                try:
                    next(gen)
                except StopIteration:
                    return

        with P.phase():
            wsb = P.sbuf("wsb", [128, KC, NCOLS], BF16); dw = [Dep() for _ in range(6)]
            wv = win[l].rearrange("(kc p) n -> p kc n", p=128)
            for g in range(6):
                c0, c1 = g * 512, min((g + 1) * 512, NCOLS)
                P.dma("gpsimd", wsb[:, :, c0:c1], wv[:, :, c0:c1], writes=[dw[g]])
            rings = {"sq": Ring(P, "sq", [128, KC, 512], BF16, 2), "psn": Ring(P, "psn", [128, 512], F32, 1, "psum"),
                     "rs": Ring(P, "rs", [128, 512], F32, 2), "tmp": Ring(P, "tmp", [128, 512], F32, 3)}
            xr = Ring(P, "xt", [128, KC, 512], F32, 2)
            hr = Ring(P, "hT", [128, KC, 512], BF16, 2)
            pso = Ring(P, "pso", [128, 512], F32, 4, "psum")
            psab = Ring(P, "psab", [128, 512], F32, 1, "psum")
            stg = Ring(P, "stg", [128, 4, 512], F32, 2)
            abs_ = Ring(P, "abst", [128, 4, 16], F32, 2)
            def proj_pre(t0, seg):
                xt, dxt = xr.next()
                P.dma("sync", xt[:], XT[:, :, t0:t0 + 512], reads=[dXT], writes=[dxt])
                hT, dh = hr.next()
                return hT, dh, norm_to_h_gen(xt, dxt, hT, dh, seg, A1, 0, rings)
            cur = proj_pre(*tiles[0])
            advance(cur[2], 1000)
            for ti, (t0, seg) in enumerate(tiles):
                hT, dh, _ = cur
                nxt = proj_pre(*tiles[ti + 1]) if ti + 1 < len(tiles) else None
                for og in range(5):
                    st, dst = stg.next()
                    for j in range(4):
                        oc = og * 4 + j
                        ps, dps = pso.next()
                        mm_group(ps[:], [(wsb[:, kc, oc * 128:(oc + 1) * 128], hT[:, kc, :]) for kc in range(KC)], [dh] + dw, [dps])
                        evac(st[:, j, :], ps[:], [dps], [dst])
                        if nxt is not None:
                            advance(nxt[2], 1)
                    P.dma("sync", PROJ[:, og * 4:og * 4 + 4, t0:t0 + 512], st[:], reads=[dst], writes=[dPROJ])
                pa, dpa = psab.next()
                for sub in range(4):
                    mm_group(pa[:, sub * 16:(sub + 1) * 16], [(hT[:, kc, sub * 128:(sub + 1) * 128], wsb[:, kc, 2560:2576]) for kc in range(KC)], [dh] + dw, [dpa])
                ab, dab = abs_.next()
                evac(ab[:], pa[:, 0:64].rearrange("p (a b) -> p a b", b=16), [dpa], [dab])
                P.dma("sync", AB[t0:t0 + 512].rearrange("(s p) k -> p s k", p=128), ab[:], reads=[dab], writes=[dAB])
                if nxt is not None:
                    advance(nxt[2], 1000)
                cur = nxt
        if dbg == "proj":
            break

        with P.phase():
            inr = Ring(P, "cin", [128, 6, 514], F32, 2)
            ur = Ring(P, "cu", [128, 2, 514], F32, 2)
            accr = Ring(P, "cacc", [128, 512], F32, 9)
            yr = Ring(P, "cy", [128, 2, 512], BF16, 2)
            sr = Ring(P, "csil", [128, 512], F32, 5)
            sqr = Ring(P, "csq", [128, 512], BF16, 5)
            pcr = Ring(P, "pcr", [128, 512], F32, 6, "psum")
            rr = Ring(P, "crs", [128, 512], F32, 5)
            obr = Ring(P, "cob", [128, 6, 512], F32, 2)

            def load_halo(c0, s_, n_, t0, T):
                a, da = inr.next()
                lo = t0 - 1 if t0 > 0 else 0
                hi = t0 + T + 1 if t0 + T < n_ else n_
                if t0 == 0:
                    P.op("vector", lambda e, a=a: e.memset(a[:, :, 0:1], 0.0), [], [da])
                if t0 + T >= n_:
                    P.op("vector", lambda e, a=a: e.memset(a[:, :, T + 1:T + 2], 0.0), [], [da])
                P.dma("sync", a[:, :, (lo - (t0 - 1)):(hi - (t0 - 1))], PROJ[:, c0:c0 + 6, s_ + lo:s_ + hi], reads=[dPROJ], writes=[da])
                return a, da

            def conv3_ops(acc, dacc, u_ap_fn, wcol, rd):
                P.op("vector", lambda e: e.tensor_scalar(acc, u_ap_fn(1), pvt[:, wcol(1):wcol(1) + 1], None, ALU.mult), rd + [dpv], [dacc])
                yield 0
                P.op("vector", lambda e: e.scalar_tensor_tensor(acc, u_ap_fn(0), pvt[:, wcol(0):wcol(0) + 1], acc, ALU.mult, ALU.add), rd + [dpv], [dacc])
                yield 1
                P.op("vector", lambda e: e.scalar_tensor_tensor(acc, u_ap_fn(2), pvt[:, wcol(2):wcol(2) + 1], acc, ALU.mult, ALU.add), rd + [dpv], [dacc])
                yield 2

            for (s_, n_, kind, si) in seqs:
                T = min(512, n_)
                for t0 in range(0, n_, T):
                    a, da = load_halo(0, s_, n_, t0, T)
                    u, du = ur.next()
                    P.op("vector", lambda e, a=a, u=u: e.tensor_tensor(u[:, :, 0:T + 2], a[:, 2:4, 0:T + 2], a[:, 4:6, 0:T + 2], ALU.mult), [da], [du])
                    y, dy = yr.next()

                    def sc_gen(c, a=a, da=da, u=u, du=du, y=y, dy=dy):
                        acc, dacc = accr.next()
                        for op_ in conv3_ops(acc[:, 0:T], dacc, lambda k, u=u, c=c: u[:, c, k:k + T], lambda k, c=c: PVO["scw"] + k * 2 + c, [du]):
                            yield
                        P.op("vector", lambda e, a=a, y=y, c=c, acc=acc: e.tensor_tensor(y[:, c, 0:T], a[:, c, 1:T + 1], acc[:, 0:T], ALU.mult), [da, dacc], [dy])
                        yield
                    gens = [sc_gen(c) for c in range(2)]
                    a2, da2 = load_halo(12, s_, n_, t0, T)
                    ob, dob = obr.next()

                    def dn_gen(c, a=a2, da=da2, ob=ob, dob=dob):
                        acc, dacc = accr.next()
                        for op_ in conv3_ops(acc[:, 0:T], dacc, lambda k, a=a, c=c: a[:, c, k:k + T], lambda k, c=c: PVO["dnw"] + k * 6 + c, [da]):
                            yield
                        if c >= 4:
                            P.op("scalar", lambda e, ob=ob, c=c, acc=acc: e.activation(ob[:, c, 0:T], acc[:, 0:T], AF.Silu), [dacc], [dob])
                            yield
                            return
                        sl, dsl = sr.next()
                        P.op("scalar", lambda e, sl=sl, acc=acc: e.activation(sl[:, 0:T], acc[:, 0:T], AF.Silu), [dacc], [dsl])
                        yield
                        sq, dsq = sqr.next()
                        P.op("scalar", lambda e, sl=sl, sq=sq: e.activation(sq[:, 0:T], sl[:, 0:T], AF.Square), [dsl], [dsq])
                        yield
                        ps, dps = pcr.next()
                        mm_group(ps[:, 0:T], [(bd64b, sq[:, 0:T])], [dsq, dcb], [dps])
                        rs, drs = rr.next()
                        P.op("scalar", lambda e, rs=rs, ps=ps: e.activation(rs[:, 0:T], ps[:, 0:T], AF.Ln, bias=EPS, scale=1.0), [dps], [drs])
                        yield
                        P.op("scalar", lambda e, rs=rs: e.activation(rs[:, 0:T], rs[:, 0:T], AF.Exp, scale=-0.5), [drs], [drs])
                        yield
                        scl = 0.125 if c < 2 else 1.0
                        P.op("vector", lambda e, ob=ob, c=c, sl=sl, rs=rs, scl=scl: e.scalar_tensor_tensor(ob[:, c, 0:T], sl[:, 0:T], scl, rs[:, 0:T], ALU.mult, ALU.mult), [dsl, drs], [dob])
                        yield
                    gens += [dn_gen(c) for c in range(6)]
                    run_lockstep(gens)
                    P.dma("sync", YMIX[:, 0:2, s_ + t0:s_ + t0 + T], y[:, :, 0:T], reads=[dy], writes=[dYMIX])
                    P.dma("sync", DNQ.rearrange("c p t -> p c t")[:, :, s_ + t0:s_ + t0 + T], ob[:, :, 0:T], reads=[dob], writes=[dDNQ])
        if dbg in ("conv", "convA"):
            break

        for (s_, n_, kind, si) in seqs:
            if (dbg == "attP" and kind == "s") or (dbg in ("attS", "attSN") and kind == "p"):
                continue
            with P.phase():
                samp = kind == "s"
                T = min(512, n_)
                nblk = n_ // 128
                QN = P.sbuf("QN", [128, 4, n_], BF16); KN = P.sbuf("KN", [128, n_], BF16)
                VA = [P.sbuf("VA0", [128, nblk, 128], BF16), P.sbuf("VA1", [128, nblk, 128], BF16)]
                dQN, dKN, dVT = Dep(), Dep(), Dep()
                P.op("vector", lambda e: e.memset(VA[0][:, :, 64:128], 1.0), [], [dVT])
                P.op("vector", lambda e: e.memset(VA[1][:, :, 0:64], 1.0), [], [dVT])
                qkr = Ring(P, "aqk", [128, 5, 512], F32, 2)
                vr = Ring(P, "av", [128, 512], F32, 2)
                sqr = Ring(P, "asq", [128, 5, 512], BF16, 1)
                psA = Ring(P, "psA", [128, 512], F32, 8, "psum")
                rr = Ring(P, "ars", [128, 512], F32, 5)
                qnr = Ring(P, "aqn", [128, 512], F32, 5)
                qbr = Ring(P, "aqb", [128, 512], BF16, 5)
                t1r = Ring(P, "at1", [128, 512], F32, 5)
                t2r = Ring(P, "at2", [128, 512], F32, 5)
                cosr = Ring(P, "acos", [128, 512], F32, 2)
                sinr = Ring(P, "asin", [128, 512], F32, 2)
                stv = Ring(P, "astv", [128, 4, 128], F32, 2)
                for t0 in range(0, n_, T):
                    qk, dqk = qkr.next()
                    P.dma("sync", qk[:, :, 0:T], PROJ[:, 6:11, s_ + t0:s_ + t0 + T], reads=[dPROJ], writes=[dqk])
                    vv, dvv = vr.next()
                    P.dma("sync", vv[:, 0:T], PROJ[:, 11, s_ + t0:s_ + t0 + T], reads=[dPROJ], writes=[dvv])
                    if samp:
                        cs, dcs = cosr.next(); sn, dsn = sinr.next()
                        P.dma("sync", cs[:, 0:T], ropec[:, t0:t0 + T], writes=[dcs])
                        P.dma("sync", sn[:, 0:T], ropes[:, t0:t0 + T], writes=[dsn])
                    sq, dsq = sqr.next()
                    P.op("scalar", lambda e, sq=sq, qk=qk: e.activation(sq[:, :, 0:T], qk[:, :, 0:T], AF.Square), [dqk], [dsq])
                    def chunk_gen(c, sq=sq, dsq=dsq, qk=qk, dqk=dqk, t0=t0):
                            ps, dps = psA.next()
                            mm_group(ps[:, 0:T], [(bd64b, sq[:, c, 0:T])], [dsq, dcb], [dps])
                            yield
                            rs, drs = rr.next()
                            P.op("scalar", lambda e, rs=rs, ps=ps: e.activation(rs[:, 0:T], ps[:, 0:T], AF.Ln, bias=EPS, scale=1.0 / 64), [dps], [drs])
                            yield
                            P.op("scalar", lambda e, rs=rs: e.activation(rs[:, 0:T], rs[:, 0:T], AF.Exp, scale=-0.5), [drs], [drs])
                            yield
                            gcol = PVO["qg"] if c < 4 else PVO["kg"]
                            dst = QN[:, c, t0:t0 + T] if c < 4 else KN[:, t0:t0 + T]
                            ddst = dQN if c < 4 else dKN
                            if not samp and c < 4:
                                P.op("vector", lambda e, qk=qk, c=c, rs=rs, dst=dst, gcol=gcol: e.scalar_tensor_tensor(dst, qk[:, c, 0:T], pvt[:, gcol:gcol + 1], rs[:, 0:T], ALU.mult, ALU.mult), [dqk, drs, dpv], [ddst])
                                yield
                                return
                            qn, dqn = qnr.next()
                            P.op("vector", lambda e, qk=qk, c=c, rs=rs, qn=qn, gcol=gcol: e.scalar_tensor_tensor(qn[:, 0:T], qk[:, c, 0:T], pvt[:, gcol:gcol + 1], rs[:, 0:T], ALU.mult, ALU.mult), [dqk, drs, dpv], [dqn])
                            yield
                            if not samp:
                                P.op("scalar", lambda e, qn=qn, dst=dst: e.copy(dst, qn[:, 0:T]), [dqn], [ddst])
                                yield
                                ps2, dps2 = psA.next()

                                def fnk(e, ps2=ps2, qn=qn):
                                    ins = None
                                    for b in range(T // 128):
                                        ins = e.transpose(ps2[:, b * 128:(b + 1) * 128], qn[:, b * 128:(b + 1) * 128], identf)
                                    return ins
                                P.op("tensor", fnk, [dqn, dcst], [dps2])
                                yield
                                so, dso = stv.next()
                                evac(so[:, 0:T // 128, :], ps2[:, 0:T].rearrange("p (a b) -> p a b", b=128), [dps2], [dso])
                                yield
                                P.dma("sync", nk[si, l, t0:t0 + T, :].rearrange("(b p) f -> p b f", p=128), so[:, 0:T // 128, :], reads=[dso], writes=[dOUT])
                                yield
                                return
                            qb, dqb = qbr.next()
                            P.op("scalar", lambda e, qn=qn, qb=qb: e.copy(qb[:, 0:T], qn[:, 0:T]), [dqn], [dqb])
                            yield
                            ps2, dps2 = psA.next()
                            mm_group(ps2[:, 0:T], [(rpermb, qb[:, 0:T])], [dqb, dcb], [dps2])
                            yield
                            t1, dt1 = t1r.next(); t2, dt2 = t2r.next()
                            P.op("gpsimd", lambda e, t1=t1, qn=qn, cs=cs: e.tensor_tensor(t1[:, 0:T], qn[:, 0:T], cs[:, 0:T], ALU.mult), [dqn, dcs], [dt1])
                            yield
                            P.op("vector", lambda e, t2=t2, ps2=ps2, sn=sn: e.tensor_tensor(t2[:, 0:T], ps2[:, 0:T], sn[:, 0:T], ALU.mult), [dps2, dsn], [dt2])
                            yield
                            P.op("gpsimd", lambda e, t1=t1, t2=t2, dst=dst: e.tensor_tensor(dst, t1[:, 0:T], t2[:, 0:T], ALU.add), [dt1, dt2], [ddst])
                            yield

                    run_lockstep([chunk_gen(c) for c in range(5)])
                    ps3, dps3 = psA.next()

                    def fnv(e, ps3=ps3, vv=vv):
                        ins = None
                        for b in range(T // 128):
                            ins = e.transpose(ps3[:, b * 128:(b + 1) * 128], vv[:, b * 128:(b + 1) * 128], identf)
                        return ins
                    P.op("tensor", fnv, [dvv, dcst], [dps3])
                    b0 = t0 // 128
                    P.op("vector", lambda e, ps3=ps3, b0=b0: e.tensor_copy(VA[0][:, b0:b0 + T // 128, 0:64], ps3[:, 0:T].rearrange("p (a b) -> p a b", b=128)[:, :, 0:64]), [dps3], [dVT])
                    P.op("vector", lambda e, ps3=ps3, b0=b0: e.tensor_copy(VA[1][:, b0:b0 + T // 128, 64:128], ps3[:, 0:T].rearrange("p (a b) -> p a b", b=128)[:, :, 64:128]), [dps3], [dVT])
                    if not samp:
                        so, dso = stv.next()
                        P.op("scalar", lambda e, so=so, ps3=ps3: e.copy(so[:, 0:T // 128, :], ps3[:, 0:T].rearrange("p (a b) -> p a b", b=128)), [dps3], [dso])
                        P.dma("sync", nv[si, l, t0:t0 + T, :].rearrange("(b p) f -> p b f", p=128), so[:, 0:T // 128, :], reads=[dso], writes=[dOUT])
                if samp:
                    KCx = P.sbuf("KCx", [128, 512], BF16); dctx = Dep()
                    VCA = [P.sbuf("VCA0", [128, 4, 128], BF16), P.sbuf("VCA1", [128, 4, 128], BF16)]
                    P.op("vector", lambda e: e.memset(VCA[0][:, :, 64:128], 1.0), [], [dctx])
                    P.op("vector", lambda e: e.memset(VCA[1][:, :, 0:64], 1.0), [], [dctx])
                    ckt = P.sbuf("ckt", [128, 4, 128], F32); cvt = P.sbuf("cvt", [128, 4, 128], F32); dck = Dep()
                    P.dma("sync", ckt[:], ck[l].rearrange("(b p) f -> p b f", p=128), writes=[dck])
                    P.dma("sync", cvt[:], cv[l].rearrange("(b p) f -> p b f", p=128), writes=[dck])
                    ps4, dps4 = psA.next()

                    def fnc(e, ps4=ps4):
                        ins = None
                        for b in range(4):
                            ins = e.transpose(ps4[:, b * 128:(b + 1) * 128], ckt[:, b, :], identf)
                        return ins
                    P.op("tensor", fnc, [dck, dcst], [dps4])
                    P.op("vector", lambda e, ps4=ps4: e.tensor_copy(KCx[:], ps4[:]), [dps4], [dctx])
                    P.op("vector", lambda e: e.tensor_copy(VCA[0][:, :, 0:64], cvt[:, :, 0:64]), [dck], [dctx])
                    P.op("vector", lambda e: e.tensor_copy(VCA[1][:, :, 64:128], cvt[:, :, 64:128]), [dck], [dctx])
                pss = Ring.of(psA.t[0:4], psA.d[0:4])
                pso_ = Ring.of(psA.t[4:6], psA.d[4:6])
                psd_ = Ring.of(psA.t[6:8], psA.d[6:8])
                ptr = Ring(P, "apt", [128, 512], BF16, 6)
                denr = Ring(P, "aden", [128, 512], F32, 2); rdenr = Ring(P, "arden", [128, 512], F32, 2)
                ybr = Ring(P, "ayb", [128, 4, 128], BF16, 2)
                for i in range(nblk):
                    if dbg in ("attN", "attSN"):
                        break
                    pacc = [pso_.next(), psd_.next()]
                    for kvh in range(2):
                        rows = slice(kvh * 64, (kvh + 1) * 64)
                        po, dpo = pacc[kvh]
                        kt = []
                        if samp:
                            kt += [("c", j, None) for j in range(4)]
                            if i > 0:
                                kt.append(("l", i - 1, mwprev))
                            kt.append(("l", i, None))
                            if i < nblk - 1:
                                kt.append(("l", i + 1, mwnext))
                        else:
                            kt += [("l", j, None) for j in range(nblk)]
                        def issue_s(ki):
                            src, j, msk = kt[ki]
                            ks = KCx[rows, j * 128:(j + 1) * 128] if src == "c" else KN[rows, j * 128:(j + 1) * 128]
                            kd_ = [dctx] if src == "c" else [dKN, dVT]
                            ps, dps = pss.next()
                            mm_group(ps[:], [(ks, QN[rows, :, i * 128:(i + 1) * 128])], kd_ + [dQN], [dps])
                            return ps, dps
                        ahead = [issue_s(k_) for k_ in range(min(3, len(kt)))]
                        for ki, (src, j, msk) in enumerate(kt):
                            vsrc = VCA[kvh][:, j, :] if src == "c" else VA[kvh][:, j, :]
                            kd_ = [dctx] if src == "c" else [dKN, dVT]
                            ps, dps = ahead.pop(0)
                            if ki + 3 < len(kt):
                                ahead.append(issue_s(ki + 3))
                            pt, dpt = ptr.next()
                            P.op("scalar", lambda e, pt=pt, ps=ps: e.activation(pt[:], ps[:], AF.Exp, scale=0.125), [dps], [dpt])
                            if msk is not None:
                                P.op("vector", lambda e, pt=pt, msk=msk: e.tensor_tensor(pt[:].rearrange("p (a b) -> p a b", a=4), pt[:].rearrange("p (a b) -> p a b", a=4),
                                                                                       msk.unsqueeze(1).broadcast_to([128, 4, 128]), ALU.mult), [dpt, dcb], [dpt])
                            first, last = ki == 0, ki == len(kt) - 1

                            P.op("tensor", lambda e, po=po, vsrc=vsrc, pt=pt, first=first, last=last: e.matmul(po[:], vsrc, pt[:], start=first, stop=last), kd_ + [dpt, dcb], [dpo])
                    den, dden = denr.next()
                    for kvh in range(2):
                        po, dpo = pacc[kvh]
                        orow = slice(kvh * 64, (kvh + 1) * 64)
                        drow = slice((1 - kvh) * 64, (2 - kvh) * 64)
                        P.op("vector", lambda e, den=den, po=po, drow=drow: e.tensor_tensor(den[drow, :].rearrange("p (a b) -> p a b", a=4), po[drow, :].rearrange("p (a b) -> p a b", a=4),
                                                                                            sinkexp[drow, 4:8].unsqueeze(2).broadcast_to([64, 4, 128]), ALU.add), [dpo, dmod], [dden])
                    rden, drden = rdenr.next()
                    for kvh in range(2):
                        orow = slice(kvh * 64, (kvh + 1) * 64)
                        drow = slice((1 - kvh) * 64, (2 - kvh) * 64)
                        P.op("scalar", lambda e, rden=rden, den=den, orow=orow, drow=drow: e.activation(rden[orow, :], den[drow, :], AF.Ln), [dden], [drden])
                        P.op("scalar", lambda e, rden=rden, orow=orow: e.activation(rden[orow, :], rden[orow, :], AF.Exp, scale=-1.0), [drden], [drden])
                    yb, dyb = ybr.next()
                    for kvh in range(2):
                        po, dpo = pacc[kvh]
                        orow = slice(kvh * 64, (kvh + 1) * 64)
                        P.op("vector", lambda e, yb=yb, po=po, rden=rden, orow=orow: e.tensor_tensor(yb[orow, :, :].rearrange("p a b -> p (a b)"), po[orow, :], rden[orow, :], ALU.mult), [dpo, drden], [dyb])
                    P.dma("sync", YMIX[:, 2:6, s_ + i * 128:s_ + (i + 1) * 128], yb[:], reads=[dyb], writes=[dYMIX])
        if dbg and dbg.startswith("att"):
            break

        DNQh = DNQ.rearrange("c (h d) t -> d (c h) t", d=64)
        for (s_, n_, kind, si) in seqs:
            with P.phase():
                samp = kind == "s"
                NCH = n_ // 64
                V_ = lambda fn, r, w: P.op("vector", fn, r, w)
                A_ = lambda fn, r, w: P.op("scalar", fn, r, w)
                G_ = lambda fn, r, w: P.op("vector", fn, r, w)
                psr = Ring(P, "dps", [128, 512], F32, 8, "psum")

                def bcf(ap, n):
                    return ap.unsqueeze(2).broadcast_to([ap.shape[0], ap.shape[1], n])
                NP_ = NCH // 2
                HS = (slice(0, 64), slice(64, 128))
                abF = P.sbuf("abF", [128, NP_, 16], F32); abt = P.sbuf("abt", [128, NP_, 16], F32); dabt = Dep(); dabF = Dep()
                ABs = AB[s_:s_ + n_].rearrange("(g two t) k -> two t g k", two=2, t=64)
                for h in range(2):
                    P.dma("sync", abF[HS[h], :, :], ABs[h], reads=[dAB], writes=[dabF])
                for h in range(2):
                    P.op("vector", lambda e, h=h: e.tensor_copy(abt[HS[h], :, 0:4], abF[HS[h], :, 0:4]), [dabF], [dabt])
                    P.op("vector", lambda e, h=h: e.tensor_copy(abt[HS[h], :, 8:12], abF[HS[h], :, 8:12]), [dabF], [dabt])
                    for g in range(NP_):
                        P.op("scalar", lambda e, h=h, g=g: e.copy(abt[HS[h], g, 4:8], abF[HS[1 - h], NP_ - 1 - g, 4:8]), [dabF], [dabt])
                        P.op("scalar", lambda e, h=h, g=g: e.copy(abt[HS[h], g, 12:16], abF[HS[1 - h], NP_ - 1 - g, 12:16]), [dabF], [dabt])
                sc = {}
                for nm in ("g", "beta", "negg", "gc", "eg", "ed", "egt", "beg", "tmpa"):
                    sc[nm] = P.sbuf("dn_" + nm, [128, NP_, 8], F32)
                egtX = P.sbuf("dn_egtX", [64, NP_, 8], F32)
                dsc = Dep()
                g3 = sc["g"]
                V_(lambda e: e.tensor_tensor(sc["tmpa"][:], abt[:, :, 0:8], bct[:, 8:16].unsqueeze(1).broadcast_to([128, NP_, 8]), ALU.add), [dabt, dpv], [dsc])
                A_(lambda e: e.activation(sc["tmpa"][:], sc["tmpa"][:], AF.Exp), [dsc], [dsc])
                A_(lambda e: e.activation(sc["tmpa"][:], sc["tmpa"][:], AF.Ln, bias=1.0), [dsc], [dsc])
                V_(lambda e: e.tensor_tensor(g3[:], sc["tmpa"][:], negA2[:].unsqueeze(1).broadcast_to([128, NP_, 8]), ALU.mult), [dsc, dmod], [dsc])
                V_(lambda e: e.tensor_scalar(sc["negg"][:], g3[:], -1.0, None, ALU.mult), [dsc], [dsc])
                A_(lambda e: e.activation(sc["beta"][:], abt[:, :, 8:16], AF.Exp, scale=-1.0), [dabt], [dsc])
                V_(lambda e: e.tensor_scalar(sc["beta"][:], sc["beta"][:], 1.0, None, ALU.add), [dsc], [dsc])
                V_(lambda e: e.reciprocal(sc["beta"][:], sc["beta"][:]), [dsc], [dsc])
                pg, dpg = psr.next()
                pgF = pg[:, 0:NP_ * 4]; pgB = pg[:, NP_ * 4:NP_ * 8]

                def fng(e):
                    e.matmul(pgF, UCF2, g3[:, :, 0:4], start=True, stop=True)
                    return e.matmul(pgB, UCB2, g3[:, :, 4:8], start=True, stop=True)
                P.op("tensor", fng, [dsc, dcst], [dpg])
                for (pgX, lo) in ((pgF, 0), (pgB, 4)):
                    V_(lambda e, pgX=pgX, lo=lo: e.tensor_copy(sc["gc"][:, :, lo:lo + 4], pgX.rearrange("p (c k) -> p c k", k=4)), [dpg], [dsc])
                    A_(lambda e, pgX=pgX, lo=lo: e.activation(sc["eg"][:, :, lo:lo + 4], pgX.rearrange("p (c k) -> p c k", k=4), AF.Exp), [dpg], [dsc])
                pt_, dpt_ = psr.next()
                pt2 = pt_[:, 0:NP_ * 8]
                ptv = pt2.rearrange("p (c k) -> p c k", k=8)
                P.op("tensor", lambda e: e.matmul(pt2, bd64f, g3[:].rearrange("p c k -> p (c k)"), start=True, stop=True), [dsc, dcst], [dpt_])
                A_(lambda e: e.activation(sc["egt"][:], ptv, AF.Exp), [dpt_], [dsc])
                V_(lambda e: e.tensor_tensor(sc["ed"][:], ptv, sc["gc"][:], ALU.subtract), [dpt_, dsc], [dsc])
                A_(lambda e: e.activation(sc["ed"][:], sc["ed"][:], AF.Exp), [dsc], [dsc])
                V_(lambda e: e.tensor_tensor(sc["beg"][:], sc["beta"][:], sc["eg"][:], ALU.mult), [dsc], [dsc])
                A_(lambda e: e.copy(egtX[:], sc["egt"][64:128, :, :]), [dsc], [dsc])
                Sf = P.sbuf("Sf", [64, 8, 64], F32); Sb = P.sbuf("Sb", [128, 8, 64], BF16); dS = Dep(); dSb = Dep()
                if samp:
                    P.dma("sync", Sf[:], s0in[l].rearrange("k a b -> a k b"), writes=[dS])
                else:
                    V_(lambda e: e.memset(Sf[:], 0.0), [], [dS])
                A_(lambda e: e.copy(Sb[0:64], Sf[:]), [dS], [dSb])
                A_(lambda e: e.copy(Sb[64:128], Sf[:]), [dS], [dSb])

                def t64(name, dt, n=3, shape=(128, 8, 64)):
                    return Ring(P, name, list(shape), dt, n)
                qfr = t64("qf", F32, 3, (128, 2, 12, 64)); qbr_ = t64("qb", BF16, 3, (128, 2, 12, 64))
                NGr = t64("NG", F32, 2); Dr = t64("Dd", F32, 2); Er = t64("E", F32, 2); ERr = t64("ER", F32)
                Eir = t64("Ei", F32, 2); Esr = t64("Es", F32, 2)
                Xr = t64("X", F32, 3); Xbr = t64("Xb", BF16, 9); XTbr = t64("XTb", BF16, 9); TTbr = t64("TTb", BF16, 6); Tbr = t64("Tb", BF16); Rtr = t64("Rt", F32, 2); Rbr = t64("Rb", BF16); AINr = t64("AIN", BF16); AINTr = t64("AINT", BF16, 6)
                TTfr = t64("TTf", F32); TTb2r = t64("TTb2", BF16); TTber = t64("TTbe", BF16)
                KVr = t64("KV", BF16, 6, (128, 16, 64)); QEr = t64("QE", BF16, 6); Ur = t64("U", F32, 6); NWr = t64("NW", BF16, 6)
                VNfr = t64("VNf", F32, 2); VNr = t64("VN", BF16, 2); VNDr = t64("VND", BF16, 2); OBr = t64("OB", F32, 2); STr = t64("ST", F32, 2, (64, 8, 64))

                def per_k(out_fn, l_fn, r_fn, reads, writes, transpose=False, halves_=(0, 1)):
                    def fn(e):
                        ins = None
                        for h in halves_:
                            for k in range(8):
                                if transpose:
                                    ins = e.transpose(out_fn(h, k), l_fn(h, k), identb[HS[h], HS[h]])
                                else:
                                    ins = e.matmul(out_fn(h, k), l_fn(h, k), r_fn(h, k), start=True, stop=True)
                        return ins
                    P.op("tensor", fn, reads, writes)

                def v3(t):
                    return t[:, :].rearrange("p (k j) -> p k j", k=8)

                def vb(t):
                    return t[:, :].bitcast(BF16).rearrange("p (k j) -> p k j", j=64)
                hk = lambda t: (lambda h, k, t=t: t[HS[h], k, :])
                W_ = 3 if NP_ >= 3 else NP_

                def pre_gen(g, cx):
                        scb = lambda nm: bcf(sc[nm][:, g, :], 64)
                        chunk = lambda h, dr: (2 * g + h) if dr == 0 else (NCH - 1 - 2 * g - h)
                        qf, dqf = qfr.next()
                        for h in range(2):
                            for dr in range(2):
                                c = chunk(h, dr)
                                P.dma("sync", qf[HS[h], dr, :, :], DNQh[:, :, s_ + c * 64:s_ + c * 64 + 64], reads=[dDNQ], writes=[dqf])
                        qb, dqb = qbr_.next()
                        A_(lambda e, qb=qb, qf=qf: e.copy(qb[:], qf[:]), [dqf], [dqb])
                        Qk = lambda h, k, qb=qb: qb[HS[h], k // 4, k % 4, :]
                        Kk = lambda h, k, qb=qb: qb[HS[h], k // 4, 4 + k % 4, :]
                        Kf = lambda h, k, qf=qf: qf[HS[h], k // 4, 4 + k % 4, :]
                        pa, dpa = psr.next(); pqk, dpqk = psr.next(); pgr, dpgr = psr.next()
                        per_k(hk(v3(pa)), Kf, Kf, [dqf], [dpa])
                        per_k(hk(v3(pqk)), Qk, Kk, [dqb], [dpqk])
                        NG, dNG = NGr.next()
                        G_(lambda e, NG=NG: e.tensor_tensor(NG[:], U8, scb("negg"), ALU.mult), [dsc, dcst], [dNG])
                        P.op("tensor", lambda e, pgr=pgr, NG=NG: e.matmul(pgr[:, :], bd64f, NG[:].rearrange("p k j -> p (k j)"), start=True, stop=True), [dNG, dcst], [dpgr])
                        Dd, dD = Dr.next()
                        V_(lambda e, Dd=Dd, pgr=pgr: e.tensor_tensor(Dd[:], v3(pgr), scb("gc"), ALU.add), [dpgr, dsc], [dD])
                        V_(lambda e, Dd=Dd: e.tensor_scalar(Dd[:], Dd[:], 0.0, None, ALU.min), [dD], [dD])
                        E, dE = Er.next()
                        A_(lambda e, E=E, Dd=Dd: e.activation(E[:], Dd[:], AF.Exp), [dD], [dE])
                        ER, dER = ERr.next()
                        A_(lambda e, ER=ER, pgr=pgr: e.activation(ER[:], v3(pgr), AF.Exp, scale=-1.0), [dpgr], [dER])
                        Ei, dEi = Eir.next(); Es, dEs = Esr.next()
                        V_(lambda e, Ei=Ei, E=E: e.tensor_tensor(Ei[:].rearrange("p k j -> p (k j)"), E[:].rearrange("p k j -> p (k j)"), MI8, ALU.mult), [dE, dcst], [dEi])
                        G_(lambda e, Es=Es, E=E: e.tensor_tensor(Es[:].rearrange("p k j -> p (k j)"), E[:].rearrange("p k j -> p (k j)"), MS8, ALU.mult), [dE, dcst], [dEs])
                        G_(lambda e, Es=Es: e.tensor_tensor(Es[:], Es[:], scb("beta"), ALU.mult), [dEs, dsc], [dEs])
                        X, dX = Xr.next()
                        V_(lambda e, X=X, pa=pa, Es=Es: e.scalar_tensor_tensor(X[:].rearrange("p k j -> p (k j)"), pa[:, :], -1.0, Es[:].rearrange("p k j -> p (k j)"), ALU.mult, ALU.mult), [dpa, dEs], [dX])
                        AIN, dAIN = AINr.next()
                        V_(lambda e, AIN=AIN, pqk=pqk, Ei=Ei: e.tensor_tensor(AIN[:].rearrange("p k j -> p (k j)"), pqk[:, :], Ei[:].rearrange("p k j -> p (k j)"), ALU.mult), [dpqk, dEi], [dAIN])
                        yield
                        Xb, dXb = Xbr.next()
                        A_(lambda e, Xb=Xb, X=X: e.copy(Xb[:], X[:]), [dX], [dXb])
                        px, dpx = psr.next()
                        pxb = vb(px)
                        per_k(lambda h, k: pxb[HS[h], k, :], hk(Xb), None, [dXb, dcb], [dpx], transpose=True)
                        per_k(lambda h, k: pxb[HS[h], 8 + k, :], hk(AIN), None, [dAIN, dcb], [dpx], transpose=True)
                        XTb, dXTb = XTbr.next(); AINT, dAINT = AINTr.next()
                        V_(lambda e, XTb=XTb, pxb=pxb: e.tensor_copy(XTb[:], pxb[:, 0:8, :]), [dpx], [dXTb])
                        A_(lambda e, AINT=AINT, pxb=pxb: e.copy(AINT[:], pxb[:, 8:16, :]), [dpx], [dAINT])
                        yield
                        TTf, dTTf = TTfr.next(); TTb, dTTb = TTbr.next()
                        V_(lambda e, TTf=TTf, XTb=XTb: e.tensor_tensor(TTf[:].rearrange("p k j -> p (k j)"), XTb[:].rearrange("p k j -> p (k j)"), IDB8, ALU.add), [dXTb, dcst], [dTTf])
                        A_(lambda e, TTb=TTb, TTf=TTf: e.copy(TTb[:], TTf[:]), [dTTf], [dTTb])
                        Xc, dXc, XTc, dXTc = Xb, dXb, XTb, dXTb
                        for lev in range(4):
                            last = lev == 3
                            p2, dp2 = psr.next()
                            per_k(hk(v3(p2)), hk(XTc), hk(Xc), [dXc, dXTc], [dp2])
                            Xn, dXn = Xbr.next()
                            A_(lambda e, Xn=Xn, p2=p2: e.copy(Xn[:], v3(p2)), [dp2], [dXn])
                            if not last:
                                p3, dp3 = psr.next()
                                per_k(hk(v3(p3)), hk(Xc), hk(XTc), [dXc, dXTc], [dp3])
                                XTn, dXTn = XTbr.next()
                                A_(lambda e, XTn=XTn, p3=p3: e.copy(XTn[:], v3(p3)), [dp3], [dXTn])
                            p4, dp4 = psr.next()
                            per_k(hk(v3(p4)), hk(Xn), hk(TTb), [dXn, dTTb], [dp4])
                            V_(lambda e, TTf=TTf, p4=p4: e.tensor_tensor(TTf[:], TTf[:], v3(p4), ALU.add), [dp4, dTTf], [dTTf])
                            TTb, dTTb = TTbr.next()
                            A_(lambda e, TTb=TTb, TTf=TTf: e.copy(TTb[:], TTf[:]), [dTTf], [dTTb])
                            yield
                            Xc, dXc = Xn, dXn
                            if not last:
                                XTc, dXTc = XTn, dXTn
                        ptt, dptt = psr.next()
                        pttb = vb(ptt)
                        per_k(lambda h, k: pttb[HS[h], k, :], hk(TTb), None, [dTTb, dcb], [dptt], transpose=True)
                        Tb, dTb = Tbr.next()
                        A_(lambda e, Tb=Tb, pttb=pttb: e.copy(Tb[:], pttb[:, 0:8, :]), [dptt], [dTb])
                        Rt, dRt = Rtr.next()
                        V_(lambda e, Rt=Rt, TTf=TTf: e.scalar_tensor_tensor(Rt[:].rearrange("p k j -> p (k j)"), TTf[:].rearrange("p k j -> p (k j)"), -1.0, IDB8, ALU.mult, ALU.add), [dTTf, dcst], [dRt])
                        pr_, dpr_ = psr.next()
                        per_k(hk(v3(pr_)), hk(X), hk(TTf), [dX, dTTf], [dpr_])
                        Rb, dRb = Rbr.next()
                        V_(lambda e, Rb=Rb, Rt=Rt, pr_=pr_: e.tensor_tensor(Rb[:], Rt[:], v3(pr_), ALU.add), [dRt, dpr_], [dRb])
                        yield
                        pc_, dpc_ = psr.next()
                        per_k(hk(v3(pc_)), hk(Tb), hk(Rb), [dTb, dRb], [dpc_])
                        V_(lambda e, TTf=TTf, pc_=pc_: e.tensor_tensor(TTf[:], TTf[:], v3(pc_), ALU.add), [dpc_, dTTf], [dTTf])
                        yield
                        TTb2, dTTb2 = TTb2r.next(); TTbe, dTTbe = TTber.next()
                        G_(lambda e, TTb2=TTb2, TTf=TTf: e.tensor_tensor(TTb2[:], TTf[:], scb("beta"), ALU.mult), [dTTf, dsc], [dTTb2])
                        G_(lambda e, TTbe=TTbe, TTf=TTf: e.tensor_tensor(TTbe[:], TTf[:], scb("beg"), ALU.mult), [dTTf, dsc], [dTTbe])
                        KV, dKV = KVr.next()
                        pkv, dpkv = psr.next()
                        pkvb = vb(pkv)
                        per_k(lambda h, k: pkvb[HS[h], k, :], Kk, None, [dqb, dcb], [dpkv], transpose=True)
                        per_k(lambda h, k: pkvb[HS[h], 8 + k, :], lambda h, k, qb=qb: qb[HS[h], k // 4, 8 + k % 4, :], None, [dqb, dcb], [dpkv], transpose=True)
                        A_(lambda e, KV=KV, pkvb=pkvb: e.copy(KV[:], pkvb), [dpkv], [dKV])
                        QE, dQE = QEr.next()
                        G_(lambda e, QE=QE, qf=qf, ER=ER: e.tensor_tensor(QE[:].rearrange("p (a b) j -> p a b j", a=2), qf[:, :, 0:4, :], ER[:].rearrange("p (a b) j -> p a b j", a=2), ALU.mult), [dqf, dER], [dQE])
                        pu, dpu = psr.next()
                        per_k(hk(v3(pu)), hk(TTb2), lambda h, k, KV=KV: KV[HS[h], 8 + k, :], [dTTb2, dKV], [dpu])
                        U, dU = Ur.next()
                        A_(lambda e, U=U, pu=pu: e.copy(U[:], v3(pu)), [dpu], [dU])
                        yield
                        pw, dpw = psr.next()
                        per_k(hk(v3(pw)), hk(KV), hk(TTbe), [dTTbe, dKV], [dpw])
                        NW, dNW = NWr.next()
                        V_(lambda e, NW=NW, pw=pw: e.tensor_scalar(NW[:], v3(pw), -1.0, None, ALU.mult), [dpw], [dNW])
                        cx.update(dict(NW=NW, dNW=dNW, U=U, dU=dU, QE=QE, dQE=dQE, AINT=AINT, dAINT=dAINT, KV=KV, dKV=dKV))
                        yield

                def scan_step(g, cx, h):
                    NW = cx['NW']; dNW = cx['dNW']; U = cx['U']; dU = cx['dU']; QE = cx['QE']; dQE = cx['dQE']
                    AINT = cx['AINT']; dAINT = cx['dAINT']; KV = cx['KV']; dKV = cx['dKV']
                    hs = HS[h]
                    one = (h,)
                    pws, dpws = psr.next()
                    per_k(hk(v3(pws)), hk(NW), hk(Sb), [dNW, dSb], [dpws], halves_=one)
                    VNf, dVNf = VNfr.next(); VN, dVN = VNr.next(); VND, dVND = VNDr.next()
                    V_(lambda e, VNf=VNf, U=U, pws=pws: e.tensor_tensor(VNf[hs], U[hs], v3(pws)[hs], ALU.add), [dU, dpws], [dVNf])
                    yield
                    A_(lambda e, VN=VN, VNf=VNf: e.copy(VN[hs], VNf[hs]), [dVNf], [dVN])
                    yield
                    G_(lambda e, VND=VND, VNf=VNf: e.tensor_tensor(VND[hs], VNf[hs], bcf(sc["ed"][hs, g, :], 64), ALU.mult), [dVNf, dsc], [dVND])
                    yield
                    po_, dpo_ = psr.next()

                    def fno(e, po_=po_, QE=QE, AINT=AINT, VN=VN):
                        ins = None
                        for k in range(8):
                            e.matmul(v3(po_)[hs, k, :], QE[hs, k, :], Sb[hs, k, :], start=True, stop=False)
                            ins = e.matmul(v3(po_)[hs, k, :], AINT[hs, k, :], VN[hs, k, :], start=False, stop=True)
                        return ins
                    P.op("tensor", fno, [dQE, dSb, dAINT, dVN], [dpo_])
                    OB, dOB = OBr.next()
                    A_(lambda e, OB=OB, po_=po_: e.copy(OB[hs], v3(po_)[hs]), [dpo_], [dOB])
                    yield
                    for dr in range(2):
                        c = (2 * g + h) if dr == 0 else (NCH - 1 - 2 * g - h)
                        t_ = s_ + c * 64
                        P.dma("sync", OD[dr, t_:t_ + 64, :].rearrange("t (h v) -> t h v", h=4), OB[hs, dr * 4:dr * 4 + 4, :], reads=[dOB], writes=[dOD])
                    ST, dST = STr.next()
                    egs = sc["egt"][0:64, g, :] if h == 0 else egtX[:, g, :]
                    G_(lambda e, ST=ST: e.tensor_tensor(ST[:], Sf[:], bcf(egs, 64), ALU.mult), [dS, dsc], [dST])
                    yield
                    pS, dpS = psr.next()

                    def fns(e, pS=pS, KV=KV, VND=VND):
                        ins = None
                        for k in range(8):
                            ins = e.matmul(pS[0:64, k * 64:(k + 1) * 64], KV[hs, k, :], VND[hs, k, :], start=True, stop=True)
                        return ins
                    P.op("tensor", fns, [dKV, dVND], [dpS])
                    V_(lambda e, ST=ST, pS=pS: e.tensor_tensor(Sf[:], ST[:], pS[0:64, :].rearrange("p (k j) -> p k j", k=8), ALU.add), [dST, dpS], [dS])
                    yield
                    A_(lambda e: e.copy(Sb[0:64], Sf[:]), [dS], [dSb])
                    A_(lambda e: e.copy(Sb[64:128], Sf[:]), [dS], [dSb])
                    yield

                def scan_group(grp_, cxs_):
                    for g2 in grp_:
                        for h in range(2):
                            yield from scan_step(g2, cxs_[g2], h)
                prev = None
                for g0 in range(0, NP_, W_):
                    grp = list(range(g0, min(g0 + W_, NP_)))
                    cxs = {g2: {} for g2 in grp}
                    gens = [pre_gen(g2, cxs[g2]) for g2 in grp]
                    sg = scan_group(*prev) if prev is not None else None
                    alive = list(gens)
                    while alive:
                        for g_ in list(alive):
                            try:
                                next(g_)
                            except StopIteration:
                                alive.remove(g_)
                            if sg is not None:
                                try:
                                    next(sg)
                                except StopIteration:
                                    sg = None
                    if sg is not None:
                        run_lockstep([sg])
                    prev = (grp, cxs)
                run_lockstep([scan_group(*prev)])
                if not samp:
                    P.dma("sync", nst[si, l].rearrange("k a b -> a k b"), Sf[:], reads=[dS], writes=[dOUT])
        if dbg == "dn":
            break

        with P.phase():
            ofr = Ring(P, "cof", [128, 4, 256], F32, 2); obr2 = Ring(P, "cob2", [128, 4, 256], F32, 2)
            osr = Ring(P, "cos_", [128, 4, 256], F32, 2); sqr2 = Ring(P, "csq2", [128, 4, 256], F32, 2)
            ssr = Ring(P, "css", [128, 16], F32, 2); zr = Ring(P, "cz", [128, 2, 512], F32, 2)
            pcr2 = Ring(P, "pc2", [128, 512], F32, 4, "psum"); ydr = Ring(P, "cyd", [128, 2, 512], BF16, 2)
            for ti in range(NT // 512):
                t0 = ti * 512
                of, dof = ofr.next(); ob2, dob2 = obr2.next()
                P.dma("sync", of[:], OD[0, t0:t0 + 512, :].rearrange("(s p) f -> p s f", p=128), reads=[dOD], writes=[dof])
                P.dma("sync", ob2[:], OD[1, t0:t0 + 512, :].rearrange("(s p) f -> p s f", p=128), reads=[dOD], writes=[dob2])
                zt, dzt = zr.next()
                P.dma("sync", zt[:], PROJ[:, 18:20, t0:t0 + 512], reads=[dPROJ], writes=[dzt])
                o, do_ = osr.next()
                P.op("gpsimd", lambda e, o=o, of=of, ob2=ob2: e.tensor_tensor(o[:], of[:], ob2[:], ALU.add), [dof, dob2], [do_])
                sq, dsq = sqr2.next()
                P.op("scalar", lambda e, sq=sq, o=o: e.activation(sq[:], o[:], AF.Square), [do_], [dsq])
                ss, dss = ssr.next()
                P.op("vector", lambda e, ss=ss, sq=sq: e.tensor_reduce(ss[:], sq[:].rearrange("p s (h v) -> p (s h) v", h=4), AX.X, ALU.add), [dsq], [dss])
                P.op("scalar", lambda e, ss=ss: e.activation(ss[:], ss[:], AF.Sqrt, bias=EPS, scale=1.0 / 64), [dss], [dss])
                P.op("vector", lambda e, ss=ss: e.reciprocal(ss[:], ss[:]), [dss], [dss])
                P.op("vector", lambda e, o=o, ss=ss: e.tensor_tensor(o[:].rearrange("p s (h v) -> p (s h) v", h=4), o[:].rearrange("p s (h v) -> p (s h) v", h=4),
                                                                    ss[:].unsqueeze(2).broadcast_to([128, 16, 64]), ALU.mult), [do_, dss], [do_])
                P.op("gpsimd", lambda e, o=o: e.tensor_tensor(o[:].rearrange("p s (h v) -> p (s h) v", h=4), o[:].rearrange("p s (h v) -> p (s h) v", h=4),
                                                              bct[:, 16:80].unsqueeze(1).broadcast_to([128, 16, 64]), ALU.mult), [do_, dpv], [do_])
                P.op("scalar", lambda e, zt=zt: e.activation(zt[:], zt[:], AF.Silu), [dzt], [dzt])
                yd, dyd = ydr.next()
                for c in range(2):
                    ps, dps = pcr2.next()

                    def fnt(e, ps=ps, o=o, c=c):
                        ins = None
                        for sb in range(4):
                            ins = e.transpose(ps[:, sb * 128:(sb + 1) * 128], o[:, sb, c * 128:(c + 1) * 128], identf)
                        return ins
                    P.op("tensor", fnt, [do_, dcst], [dps])
                    P.op("vector", lambda e, yd=yd, ps=ps, zt=zt, c=c: e.tensor_tensor(yd[:, c, :], ps[:], zt[:, c, :], ALU.mult), [dps, dzt], [dyd])
                P.dma("sync", YMIX[:, 6:8, t0:t0 + 512], yd[:], reads=[dyd], writes=[dYMIX])
        if dbg == "mix":
            break

        HF = DFF // 2
        NJ = HF // 128
        for hf in range(2):
            with P.phase():
                wgs = P.sbuf("wgs", [128, KC, HF], BF16); wus = P.sbuf("wus", [128, KC, HF], BF16)
                wds = P.sbuf("wds", [128, NJ, D], BF16); dwf = [Dep() for _ in range(4)]
                P.dma("gpsimd", wgs[:], wg[l].rearrange("(kc p) n -> p kc n", p=128)[:, :, hf * HF:(hf + 1) * HF], writes=[dwf[0]])
                P.dma("gpsimd", wus[:], wu[l].rearrange("(kc p) n -> p kc n", p=128)[:, :, hf * HF:(hf + 1) * HF], writes=[dwf[1]])
                P.dma("gpsimd", wds[:], wd[l, hf * HF:(hf + 1) * HF, :].rearrange("(j p) n -> p j n", p=128), writes=[dwf[2]])
                if hf == 0:
                    wos = P.sbuf("wos", [128, KC, D], BF16)
                    P.dma("gpsimd", wos[:], wout[l].rearrange("(kc p) n -> p kc n", p=128), writes=[dwf[3]])
                    ymr = Ring(P, "ym", [128, KC, 512], BF16, 2)
                    rings = {"sq": Ring(P, "sq", [128, KC, 512], BF16, 1), "psn": Ring(P, "psn", [128, 512], F32, 1, "psum"),
                             "rs": Ring(P, "rs", [128, 512], F32, 2), "tmp": Ring(P, "tmp", [128, 512], F32, 3)}
                    psw = Ring(P, "psw", [128, 512], F32, 2, "psum")
                xr = Ring(P, "xt", [128, KC, 512], F32, 2)
                hr = Ring(P, "h2", [128, KC, 512], BF16, 2)
                actr = Ring(P, "act", [128, NJ, 512], BF16, 1)
                sgr = Ring(P, "sg", [128, 512], F32, 2)
                psf = Ring(P, "psf", [128, 512], F32, 5, "psum")
                direct_out = (hf == 1 and l == depth - 1 and not dbg)
                if direct_out:
                    pso2 = Ring(P, "pso2", [128, 512], F32, 2, "psum")
                    yo_ = Ring(P, "yo_", [128, D], F32, 2)
                def ffn_pre(t0, seg, cx):
                    xt, dxt = xr.next()
                    P.dma("sync", xt[:], XT[:, :, t0:t0 + 512], reads=[dXT], writes=[dxt])
                    h2, dh2 = hr.next()
                    cx.update(dict(xt=xt, dxt=dxt, h2=h2, dh2=dh2))
                    if hf == 0:
                        ym, dym = ymr.next()
                        P.dma("sync", ym[:], YMIX[:, :, t0:t0 + 512], reads=[dYMIX], writes=[dym])
                        yield
                        for oc in range(KC):
                            ps, dps = psw.next()
                            mm_group(ps[:], [(wos[:, kc, oc * 128:(oc + 1) * 128], ym[:, kc, :]) for kc in range(KC)], [dym, dwf[3]], [dps])
                            P.op("vector", lambda e, xt=xt, ps=ps, oc=oc, seg=seg: e.scalar_tensor_tensor(xt[:, oc, :], ps[:], modv[:, 16 + oc, seg:seg + 1], xt[:, oc, :], ALU.mult, ALU.add), [dps, dmod, dxt], [dxt])
                            yield
                        yield from norm_to_h_gen(xt, dxt, h2, dh2, seg, A2, 24, rings)
                        P.dma("sync", H2[:, :, t0:t0 + 512], h2[:], reads=[dh2], writes=[dH2])
                    else:
                        P.dma("sync", h2[:], H2[:, :, t0:t0 + 512], reads=[dH2], writes=[dh2])
                    yield
                cxc = {}
                g_ = ffn_pre(tiles[0][0], tiles[0][1], cxc)
                advance(g_, 1000)
                for ti, (t0, seg) in enumerate(tiles):
                    xt, dxt, h2, dh2 = cxc["xt"], cxc["dxt"], cxc["h2"], cxc["dh2"]
                    cxn = {}
                    nxt = ffn_pre(tiles[ti + 1][0], tiles[ti + 1][1], cxn) if ti + 1 < len(tiles) else None
                    act, dact = actr.next()
                    for j in range(NJ):
                        pg_, dpg_ = psf.next(); pu_, dpu_ = psf.next()
                        mm_group(pg_[:], [(wgs[:, kc, j * 128:(j + 1) * 128], h2[:, kc, :]) for kc in range(KC)], [dh2, dwf[0]], [dpg_])
                        mm_group(pu_[:], [(wus[:, kc, j * 128:(j + 1) * 128], h2[:, kc, :]) for kc in range(KC)], [dh2, dwf[1]], [dpu_])
                        sg, dsg = sgr.next()
                        P.op("scalar", lambda e, sg=sg, pg_=pg_: e.activation(sg[:], pg_[:], AF.Silu), [dpg_], [dsg])
                        P.op("vector", lambda e, act=act, j=j, sg=sg, pu_=pu_: e.tensor_tensor(act[:, j, :], sg[:], pu_[:], ALU.mult), [dsg, dpu_], [dact])
                        advance(nxt, 2)
                    for oc in range(KC):
                        ps, dps = psf.next()
                        mm_group(ps[:], [(wds[:, j, oc * 128:(oc + 1) * 128], act[:, j, :]) for j in range(NJ)], [dact, dwf[2]], [dps])
                        P.op("vector", lambda e, xt=xt, ps=ps, oc=oc, seg=seg: e.scalar_tensor_tensor(xt[:, oc, :], ps[:], modv[:, 40 + oc, seg:seg + 1], xt[:, oc, :], ALU.mult, ALU.add), [dps, dmod, dxt], [dxt])
                        advance(nxt, 2)
                    if direct_out:
                        for sub in range(4):
                            tt0 = t0 + sub * 128
                            o, do = yo_.next()
                            for hh in range(2):
                                ps, dps = pso2.next()

                                def fnT(e, ps=ps, xt=xt, hh=hh, sub=sub):
                                    ins = None
                                    for j in range(4):
                                        ins = e.transpose(ps[:, j * 128:(j + 1) * 128], xt[:, hh * 4 + j, sub * 128:(sub + 1) * 128], identf)
                                    return ins
                                P.op("tensor", fnT, [dxt, dcst], [dps])
                                evac(o[:, hh * 512:(hh + 1) * 512], ps[:], [dps], [do])
                            dstap = ys[tt0:tt0 + 128, :] if tt0 < NS else yp[tt0 - NS:tt0 - NS + 128, :]
                            P.dma("sync", dstap, o[:], reads=[do], writes=[dOUT])
                    else:
                        P.dma("sync", XT[:, :, t0:t0 + 512], xt[:], reads=[dxt], writes=[dXT])
                    advance(nxt, 1000)
                    cxc = cxn

    if not dbg:
        P.finish([dOUT])
        return nc
    with P.phase():
        xr = Ring(P, "fx", [128, KC, 512], F32, 2)
        yo = Ring(P, "fy", [128, D], F32, 4)
        pst = Ring(P, "fps", [128, 512], F32, 8, "psum")
        for gi in range(NT // 512):
            a, da = xr.next()
            P.dma("sync", a[:], XT[:, :, gi * 512:(gi + 1) * 512], reads=[dXT], writes=[da])
            for sub in range(4):
                t0 = gi * 512 + sub * 128
                o, do = yo.next()
                for hh in range(2):
                    ps, dps = pst.next()

                    def fn(e, ps=ps, a=a, hh=hh, sub=sub):
                        ins = None
                        for j in range(4):
                            ins = e.transpose(ps[:, j * 128:(j + 1) * 128], a[:, hh * 4 + j, sub * 128:(sub + 1) * 128], identf)
                        return ins
                    P.op("tensor", fn, [da, dcst], [dps])
                    evac(o[:, hh * 512:(hh + 1) * 512], ps[:], [dps], [do])
                dstap = ys[t0:t0 + 128, :] if t0 < NS else yp[t0 - NS:t0 - NS + 128, :]
                P.dma("sync", dstap, o[:], reads=[do], writes=[dOUT])
    P.finish([dOUT])
    return nc


def _consts():
    c = np.zeros((128, CW), np.float32)
    c[:, 0:128] = np.eye(128)
    c[:, 128:256] = 1.0
    c[0:64, 256:320] = 1.0
    c[64:128, 320:384] = 1.0
    m = np.arange(128)
    partner = np.where((m % 32) < 16, m + 16, m - 16)
    c[partner, 384 + m] = 1.0
    b = np.arange(128)[:, None]; a = np.arange(128)[None, :]
    c[:, 512:640] = (b >= a)
    c[:, 640:768] = (b <= a)
    i = np.arange(64)[:, None]; j = np.arange(64)[None, :]
    for k in range(8):
        fwd = k < 4
        c[0:64, 768 + k * 64:768 + (k + 1) * 64] = (i > j) if fwd else (i < j)
        c[0:64, 1280 + k * 64:1280 + (k + 1) * 64] = (i >= j) if fwd else (i <= j)
        c[0:64, 1792 + k * 64:1792 + (k + 1) * 64] = np.eye(64)
    c[0:64, 2304:2368] = (i <= j)
    c[0:64, 2368:2432] = (i >= j)
    for k in range(8):
        c[0:64, 2432 + k * 64:2432 + (k + 1) * 64] = (i <= j) if k < 4 else (i >= j)
    c[64:128, 768:2304] = c[0:64, 768:2304]
    c[64:128, 2432:2944] = c[0:64, 2432:2944]
    c[0:64, 2944:3008] = (i <= j); c[64:128, 3008:3072] = (i <= j)
    c[0:64, 3072:3136] = (i >= j); c[64:128, 3136:3200] = (i >= j)
    t = np.arange(4096)
    p = np.arange(128)
    d = p % 64
    jj = (d % 16).astype(np.float32)
    inv = (1.0 / (np.float32(10000.0) ** (jj / np.float32(16.0)))).astype(np.float32)
    pos = np.where((d < 32)[:, None], (t // 64)[None, :], (t % 64)[None, :]).astype(np.float32)
    ang = (pos * inv[:, None]).astype(np.float32)
    cos = np.cos(ang).astype(np.float32)
    sgn = np.where((d % 32) < 16, -1.0, 1.0).astype(np.float32)
    sin = (np.sin(ang) * sgn[:, None]).astype(np.float32)
    return c, cos, sin


def _col_perm():
    perm = np.arange(NCOLS)
    for c in range(4):
        for half, h in ((0, c), (1, 4 + c)):
            perm[768 + c * 128 + half * 64:768 + c * 128 + half * 64 + 64] = 768 + h * 64 + np.arange(64)
    return perm


def _row_perm():
    perm = np.arange(D)
    for c in range(4):
        for half, h in ((0, c), (1, 4 + c)):
            perm[256 + c * 128 + half * 64:256 + c * 128 + half * 64 + 64] = 256 + h * 64 + np.arange(64)
    return perm


def host_prep(inp, depth):
    f = lambda a: np.ascontiguousarray(np.asarray(a, dtype=np.float32))
    cst, cos, sin = _consts()
    cp, rp = _col_perm(), _row_perm()
    shared = {
        "win": f(np.asarray(inp["w_in"])[:depth][:, :, cp]),
        "wout": f(np.asarray(inp["w_out"])[:depth][:, rp, :]),
        "adaw": f(np.asarray(inp["ada_w"])[:depth]),
        "wg": f(np.asarray(inp["w_gate"])[:depth]), "wu": f(np.asarray(inp["w_up"])[:depth]), "wd": f(np.asarray(inp["w_down"])[:depth]),
        "cst": cst, "ropec": cos, "ropes": sin,
    }
    pv = np.zeros((depth, 128, 90), np.float32); bc = np.zeros((depth, 128, 96), np.float32)
    for l in range(depth):
        pv[l, :, 0:48] = np.asarray(inp["ada_b"])[l].reshape(48, 128).T
        pv[l, :, 48:56] = np.asarray(inp["norm1_g"])[l].reshape(8, 128).T
        pv[l, :, 56:64] = np.asarray(inp["norm2_g"])[l].reshape(8, 128).T
        scw = np.asarray(inp["sc_conv_w"])[l]; dnw = np.asarray(inp["dn_conv_w"])[l]
        for k in range(3):
            pv[l, :, 64 + k * 2:64 + k * 2 + 2] = scw[k].reshape(2, 128).T
            pv[l, :, 70 + k * 6:70 + k * 6 + 6] = dnw[k].reshape(6, 128).T
        pv[l, :, 88] = np.tile(np.asarray(inp["q_norm_g"])[l], 2)
        pv[l, :, 89] = np.tile(np.asarray(inp["k_norm_g"])[l], 2)
        bc[l, :, 0:8] = np.asarray(inp["dn_A_log"])[l].reshape(8)[None, :]
        bc[l, :, 8:16] = np.asarray(inp["dn_dt_bias"])[l].reshape(8)[None, :]
        bc[l, :, 16:80] = np.asarray(inp["dn_norm_g"])[l][None, :]
        sk = np.asarray(inp["attn_sink"])[l]
        bc[l, 0:64, 88:92] = sk[0:4][None, :]
        bc[l, 64:128, 88:92] = sk[4:8][None, :]
        bc[l, 0:64, 92:96] = sk[4:8][None, :]
        bc[l, 64:128, 92:96] = sk[0:4][None, :]
    shared["pv"] = pv; shared["bcp"] = bc
    return shared


def core_inputs(inp, shared, core, NS, depth, nsamp):
    f = lambda a: np.ascontiguousarray(np.asarray(a, dtype=np.float32))
    b = core % nsamp
    m = dict(shared)
    m["xs"] = f(np.asarray(inp["x_sample"])[b, :NS])
    m["xp"] = f(np.asarray(inp["x_prompt"])[2 * core:2 * core + 2].reshape(512, D))
    m["ck"] = f(np.asarray(inp["cache_k"])[b, :depth].reshape(depth, 512, 128))
    m["cv"] = f(np.asarray(inp["cache_v"])[b, :depth].reshape(depth, 512, 128))
    m["s0"] = f(np.asarray(inp["state_delta"])[b, :depth].reshape(depth, 8, 64, 64))
    cT = np.zeros((128, 16), np.float32)
    cT[:, 0::2] = np.asarray(inp["c"])[b].reshape(8, 128).T
    cT[:, 1::2] = np.asarray(inp["c_ctx"]).reshape(8, 128).T
    m["cT"] = cT
    return m


_NC_CACHE = {}


def kernel(**inputs):
    NS, depth, ncores = 4096, 2, 8
    if "full" not in _NC_CACHE:
        _NC_CACHE["full"] = build(NS, depth)
    nc = _NC_CACHE["full"]
    shared = host_prep(inputs, depth)
    in_maps = [core_inputs(inputs, shared, c, NS, depth, 4) for c in range(ncores)]
    res = run_bass_kernel_spmd(nc, in_maps, core_ids=list(range(ncores)))
    R = res.results
    y_p = np.concatenate([np.asarray(R[c]["yp"]).reshape(2, 256, D) for c in range(8)], 0)
    y_s = np.stack([np.asarray(R[c]["ys"]) for c in range(4)], 0)
    nk = np.concatenate([np.asarray(R[c]["nk"]).reshape(2, depth, 256, 2, 64) for c in range(8)], 0)
    nv = np.concatenate([np.asarray(R[c]["nv"]).reshape(2, depth, 256, 2, 64) for c in range(8)], 0)
    ns = np.concatenate([np.asarray(R[c]["nst"]).reshape(2, depth, 2, 4, 64, 64) for c in range(8)], 0)
    return (y_p.astype(np.float32), y_s.astype(np.float32), nk.astype(np.float32), nv.astype(np.float32), ns.astype(np.float32))
```

```python
from contextlib import ExitStack
import numpy as np
import concourse.bass as bass
import concourse.mybir as mybir
from concourse.bass_utils import run_bass_kernel_spmd

F32 = mybir.dt.float32
BF16 = mybir.dt.bfloat16
AF = mybir.ActivationFunctionType
ALU = mybir.AluOpType
AX = mybir.AxisListType


class Dep:
    __slots__ = ("w", "r", "x")

    def __init__(self):
        self.w = None
        self.r = []
        self.x = False


class _Rec:
    def __init__(self):
        self.calls = []

    def __getattr__(self, name):
        def f(*a, **k):
            self.calls.append((name, a, k))
            return self
        return f


def _replay_calls(calls):
    def fn(e):
        ins = None
        for name, a, k in calls:
            ins = getattr(e, name)(*a, **k)
        return ins
    return fn


class Prog:
    CE = ("tensor", "vector", "scalar", "gpsimd")
    NDMA = {"sync": 12, "gpsimd": 6, "scalar": 4}

    def __init__(self, nc):
        self.nc = nc
        self.stack = ExitStack()
        self.ops = {e: [] for e in ("tensor", "vector", "scalar", "gpsimd", "sync")}
        self.sem = {}
        self.cnt = {}
        self.seen = {e: {} for e in self.ops}
        for e in self.CE:
            self.sem[e] = self.stack.enter_context(nc.semaphore("s_" + e))
            self.cnt[e] = 0
        self.dsem = {}
        self.dval = {}
        self.dnext = {}
        for q, n in self.NDMA.items():
            self.dsem[q] = [self.stack.enter_context(nc.semaphore("d_%s%d" % (q, i))) for i in range(n)]
            self.dnext[q] = 0
        for q in self.dsem:
            for s in self.dsem[q]:
                self.dval[id(s)] = 0
        self.n_alloc = 0

    def sbuf(self, name, shape, dtype):
        self.n_alloc += 1
        return self.stack.enter_context(self.nc.sbuf_tensor("%s_s%d" % (name, self.n_alloc), list(shape), dtype))

    def psum(self, name, shape, dtype):
        self.n_alloc += 1
        return self.stack.enter_context(self.nc.psum_tensor("%s_p%d" % (name, self.n_alloc), list(shape), dtype))

    def dep(self):
        return Dep()

    def deps(self, n):
        return [Dep() for _ in range(n)]

    def _collect(self, eng, reads, writes):
        toks = []
        for d in reads:
            if d.w is not None:
                toks.append(d.w)
        for d in writes:
            if d.w is not None:
                toks.append(d.w)
            toks.extend(d.r)
        seen = self.seen[eng]
        waits = {}
        for (s, v, src) in toks:
            if src == eng and eng == "tensor":
                continue
            k = id(s)
            if seen.get(k, 0) >= v:
                continue
            if k not in waits or waits[k][1] < v:
                waits[k] = (s, v)
        for k, (s, v) in waits.items():
            seen[k] = v
        return list(waits.values())

    def _commit(self, tok, reads, writes):
        for d in reads:
            d.r.append(tok)
        for d in writes:
            d.w = tok
            d.r = []

    max_ops = None
    n_ops = 0
    log = []

    def _skip(self, desc):
        Prog.n_ops += 1
        if Prog.max_ops is not None and Prog.n_ops > Prog.max_ops:
            return True
        Prog.log.append(desc)
        return False

    def op(self, eng, fn, reads=(), writes=()):
        if self._skip((eng,)):
            return None
        writes = list(writes) + [d for d in reads if d.x]
        reads = [d for d in reads if not d.x]
        waits = self._collect(eng, reads, writes)
        self.cnt[eng] += 1
        tok = (self.sem[eng], self.cnt[eng], eng)
        rec = _Rec()
        fn(rec)
        self.ops[eng].append((waits, _replay_calls(rec.calls), self.sem[eng], 1))
        self._commit(tok, reads, writes)
        return tok

    def dma(self, q, out, in_, reads=(), writes=(), **kw):
        if self._skip(("dma_" + q,)):
            return None
        pool = self.dsem[q]
        s = pool[self.dnext[q] % len(pool)]
        self.dnext[q] += 1
        waits = self._collect(q, reads, writes)
        prev = self.dval[id(s)]
        if prev > 0 and self.seen[q].get(id(s), 0) < prev:
            waits.append((s, prev))
            self.seen[q][id(s)] = prev
        self.dval[id(s)] = prev + 16
        tok = (s, prev + 16, "dma_" + q)
        self.ops[q].append((waits, lambda e: e.dma_start(out=out, in_=in_, **kw), s, 16))
        self._commit(tok, reads, writes)
        return tok

    def wait_on(self, eng, deps_):
        waits = self._collect(eng, deps_, ())
        self.ops[eng].append((waits, None, None, 0))

    def flush(self):
        for q in self.dsem:
            waits = []
            for s in self.dsem[q]:
                v = self.dval[id(s)]
                if v > 0 and self.seen["sync"].get(id(s), 0) < v:
                    waits.append((s, v))
                    self.seen["sync"][id(s)] = v
            if waits:
                self.ops["sync"].append((waits, None, None, 0))
        nc = self.nc

        def replay(name):
            def run(e):
                for waits, fn, s, inc in self.ops[name]:
                    for (ws, wv) in waits:
                        e.wait_ge(ws, wv)
                    if fn is not None:
                        ins = fn(e)
                        ins.then_inc(s, inc)
            return run

        with nc.Block() as block:
            block.tensor(replay("tensor"))
            block.vector(replay("vector"))
            block.scalar(replay("scalar"))
            block.gpsimd(replay("gpsimd"))
            block.sync(replay("sync"))
        for k in self.ops:
            self.ops[k] = []

    def phase(self):
        prog = self

        class _Ph:
            def __enter__(s):
                s.saved = prog.stack
                prog.stack = ExitStack()
                return prog

            def __exit__(s, *a):
                if a[0] is None:
                    prog.flush()
                prog.stack.close()
                prog.stack = s.saved
                return False
        return _Ph()

    def finish(self, out_deps=()):
        self.wait_on("sync", out_deps)
        self.flush()
        self.stack.close()


class Ring:
    def __init__(self, P, name, shape, dtype, n, space="sbuf"):
        mk = P.sbuf if space == "sbuf" else P.psum
        self.t = [mk("%s_%d" % (name, i), shape, dtype) for i in range(n)]
        self.d = [P.dep() for _ in range(n)]
        if space != "sbuf":
            for d in self.d:
                d.x = True
        self.i = 0

    def next(self):
        k = self.i % len(self.t)
        self.i += 1
        return self.t[k], self.d[k]

    @classmethod
    def of(cls, tiles, deps):
        r = cls.__new__(cls)
        r.t = list(tiles); r.d = list(deps); r.i = 0
        return r


def run_lockstep(gens):
    alive = list(gens)
    while alive:
        for g in list(alive):
            try:
                next(g)
            except StopIteration:
                alive.remove(g)


D = 1024
KC = 8
NCOLS = 2576
DFF = 2816
EPS = 1e-6
CW = 3200


def build(NS=4096, depth=2, dbg=False):
    NT = NS + 512
    nc = bass.Bass("TRN2", target_bir_lowering=False)

    def din(name, shape, dt=F32):
        return nc.dram_tensor(name, list(shape), dt, kind="ExternalInput").ap()

    def dout(name, shape, dt=F32):
        return nc.dram_tensor(name, list(shape), dt, kind="ExternalOutput").ap()

    def dint(name, shape, dt=F32):
        if dbg:
            return nc.dram_tensor(name, list(shape), dt, kind="ExternalOutput").ap()
        return nc.dram_tensor(name, list(shape), dt).ap()

    xs = din("xs", [NS, D]); xp = din("xp", [512, D])
    ck = din("ck", [depth, 512, 128]); cv = din("cv", [depth, 512, 128])
    s0in = din("s0", [depth, 8, 64, 64])
    cTin = din("cT", [128, 16])
    win = din("win", [depth, D, NCOLS]); wout = din("wout", [depth, D, D])
    adaw = din("adaw", [depth, D, 6 * D])
    wg = din("wg", [depth, D, DFF]); wu = din("wu", [depth, D, DFF]); wd = din("wd", [depth, DFF, D])
    pvin = din("pv", [depth, 128, 90]); bcin = din("bcp", [depth, 128, 96])
    cstin = din("cst", [128, CW]); ropec = din("ropec", [128, 4096]); ropes = din("ropes", [128, 4096])
    ys = dout("ys", [NS, D]); yp = dout("yp", [512, D])
    nk = dout("nk", [2, depth, 256, 128]); nv = dout("nv", [2, depth, 256, 128])
    nst = dout("nst", [2, depth, 8, 64, 64])

    XT = dint("XT", [KC, 128, NT]).rearrange("c p t -> p c t")
    PROJ = dint("PROJ", [20, 128, NT]).rearrange("c p t -> p c t")
    AB = dint("AB", [NT, 16])
    DNQ = dint("DNQ", [6, 128, NT])
    YMIX = dint("YMIX", [KC, 128, NT], BF16).rearrange("c p t -> p c t")
    H2 = dint("H2", [KC, 128, NT], BF16).rearrange("c p t -> p c t")
    OD = dint("OD", [2, NT, 256])
    dXT, dPROJ, dAB, dDNQ, dYMIX, dH2, dOD = [Dep() for _ in range(7)]
    dOUT = Dep()

    P = Prog(nc)
    seqs = [(0, NS, "s", 0), (NS, 256, "p", 0), (NS + 256, 256, "p", 1)]
    tiles = [(t0, 0) for t0 in range(0, NS, 512)] + [(NS, 1)]

    cst = P.sbuf("cst", [128, CW], F32); dcst = Dep()
    P.dma("sync", cst[:], cstin, writes=[dcst])
    identf = cst[:, 0:128]
    cb = P.sbuf("cb", [128, 6 * 128], BF16); dcb = Dep()
    P.op("vector", lambda e: e.tensor_copy(cb[:], cst[:, 0:768]), [dcst], [dcb])
    identb = cb[:, 0:128]; onesb = cb[:, 128:256]; bd64b = cb[:, 256:384]; rpermb = cb[:, 384:512]
    mwprev = cb[:, 512:640]; mwnext = cb[:, 640:768]
    MS8 = cst[:, 768:1280]; MI8 = cst[:, 1280:1792]; IDB8 = cst[:, 1792:2304]
    U8 = cst[:, 2432:2944].rearrange("p (k j) -> p k j", k=8)
    UCF2 = cst[:, 2944:3072]; UCB2 = cst[:, 3072:3200]; bd64f = cst[:, 256:384]
    UCF = cst[0:64, 2304:2368]; UCB = cst[0:64, 2368:2432]
    onesf64 = cst[0:64, 128:192]
    modv = P.sbuf("modv", [128, 48, 2], F32); dmod = Dep()
    A1 = P.sbuf("A1", [128, 8, 2], F32); A2 = P.sbuf("A2", [128, 8, 2], F32)
    pvt = P.sbuf("pvt", [128, 90], F32); bct = P.sbuf("bct", [128, 96], F32); dpv = Dep()
    sinkexp = P.sbuf("sinkexp", [128, 8], F32)
    negA = P.sbuf("negA", [64, 8], F32); negA2 = P.sbuf("negA2", [128, 8], F32)
    PVO = {"adab": 0, "n1": 48, "n2": 56, "scw": 64, "dnw": 70, "qg": 88, "kg": 89}
    BCO = {"alog": 0, "dtb": 8, "dng": 16, "sink": 88}
    P.flush()

    def mm_group(out, pairs, reads, writes):
        n = len(pairs)

        def fn(e):
            ins = None
            for i, (l_, r_) in enumerate(pairs):
                ins = e.matmul(out, l_, r_, start=(i == 0), stop=(i == n - 1))
            return ins
        P.op("tensor", fn, reads, writes)

    evac_i = [0]

    def evac(out, in_, reads, writes):
        evac_i[0] += 1
        if evac_i[0] % 2:
            P.op("vector", lambda e: e.tensor_copy(out, in_), reads, writes)
        else:
            P.op("scalar", lambda e: e.copy(out, in_), reads, writes)

    with P.phase():
        xin = Ring(P, "xin", [128, D], F32, 4)
        xo = Ring(P, "xo", [128, KC, 512], F32, 2)
        pst = Ring(P, "pst", [128, 512], F32, 8, "psum")
        for gi in range(NT // 512):
            o, do = xo.next()
            for sub in range(4):
                t0 = gi * 512 + sub * 128
                src = xs[t0:t0 + 128, :] if t0 < NS else xp[t0 - NS:t0 - NS + 128, :]
                a, da = xin.next()
                P.dma("sync", a[:], src, writes=[da])
                for hh in range(2):
                    ps, dps = pst.next()

                    def fn(e, ps=ps, a=a, hh=hh):
                        ins = None
                        for j in range(4):
                            c = hh * 4 + j
                            ins = e.transpose(ps[:, j * 128:(j + 1) * 128], a[:, c * 128:(c + 1) * 128], identf)
                        return ins
                    P.op("tensor", fn, [da, dcst], [dps])
                    evac(o[:, hh * 4:(hh + 1) * 4, sub * 128:(sub + 1) * 128], ps[:].rearrange("p (a b) -> p a b", a=4), [dps], [do])
            P.dma("sync", XT[:, :, gi * 512:(gi + 1) * 512], o[:], reads=[do], writes=[dXT])

    for l in range(depth):
        with P.phase():
            P.dma("sync", pvt[:], pvin[l], writes=[dpv])
            P.dma("sync", bct[:], bcin[l], writes=[dpv])
            ct = P.sbuf("ct", [128, 16], F32); dct = Dep()
            P.dma("sync", ct[:], cTin, writes=[dct])
            sil = P.sbuf("sil", [128, 16], BF16); dsil = Dep()
            P.op("scalar", lambda e: e.activation(sil[:], ct[:], AF.Silu), [dct], [dsil])
            war = Ring(P, "wa", [128, KC, 768], BF16, 3)
            psm = P.psum("psm", [128, 512], F32); dpsm = Dep()
            awv = adaw[l].rearrange("(kc p) n -> p kc n", p=128)
            for g in range(8):
                wa, dwa = war.next()
                for hh in range(2):
                    P.dma("gpsimd", wa[:, :, hh * 384:(hh + 1) * 384], awv[:, :, g * 768 + hh * 384:g * 768 + (hh + 1) * 384], writes=[dwa])
                for fcl in range(6):
                    fc = g * 6 + fcl
                    mm_group(psm[:, fc * 2:fc * 2 + 2],
                             [(wa[:, kc, fcl * 128:(fcl + 1) * 128], sil[:, kc * 2:kc * 2 + 2]) for kc in range(KC)],
                             [dwa, dsil], [dpsm])
            P.op("vector", lambda e: e.tensor_tensor(modv[:], psm[:, 0:96].rearrange("p (a b) -> p a b", b=2),
                                                      pvt[:, 0:48].unsqueeze(2).broadcast_to([128, 48, 2]), ALU.add), [dpsm, dpv], [dmod])
            P.op("vector", lambda e: e.scalar_tensor_tensor(A1[:], modv[:, 8:16, :], 1.0, pvt[:, 48:56].unsqueeze(2).broadcast_to([128, 8, 2]), ALU.add, ALU.mult), [dmod, dpv], [dmod])
            P.op("vector", lambda e: e.scalar_tensor_tensor(A2[:], modv[:, 32:40, :], 1.0, pvt[:, 56:64].unsqueeze(2).broadcast_to([128, 8, 2]), ALU.add, ALU.mult), [dmod, dpv], [dmod])
            P.op("scalar", lambda e: e.activation(sinkexp[:], bct[:, 88:96], AF.Exp), [dpv], [dmod])
            P.op("scalar", lambda e: e.activation(negA[:], bct[0:64, 0:8], AF.Exp), [dpv], [dmod])
            P.op("vector", lambda e: e.tensor_scalar(negA[:], negA[:], -1.0, None, ALU.mult), [dmod], [dmod])
            P.op("scalar", lambda e: e.activation(negA2[:], bct[:, 0:8], AF.Exp), [dpv], [dmod])
            P.op("vector", lambda e: e.tensor_scalar(negA2[:], negA2[:], -1.0, None, ALU.mult), [dmod], [dmod])

        def norm_to_h_gen(xt, dxt, hT, dh, seg, A_, B0, rings, T=512):
            sq, dsq = rings["sq"].next()
            P.op("scalar", lambda e: e.activation(sq[:, :, 0:T], xt[:, :, 0:T], AF.Square), [dxt], [dsq])
            yield
            psn, dpsn = rings["psn"].next()
            mm_group(psn[:, 0:T], [(onesb, sq[:, kc, 0:T]) for kc in range(KC)], [dsq, dcb], [dpsn])
            yield
            rs, drs = rings["rs"].next()
            P.op("scalar", lambda e: e.activation(rs[:, 0:T], psn[:, 0:T], AF.Ln, bias=EPS, scale=1.0 / D), [dpsn], [drs])
            yield
            P.op("scalar", lambda e: e.activation(rs[:, 0:T], rs[:, 0:T], AF.Exp, scale=-0.5), [drs], [drs])
            yield
            for kc in range(KC):
                tmp, dtmp = rings["tmp"].next()
                P.op("vector", lambda e, kc=kc, tmp=tmp: e.tensor_tensor(tmp[:, 0:T], xt[:, kc, 0:T], rs[:, 0:T], ALU.mult), [dxt, drs], [dtmp])
                yield
                P.op("scalar", lambda e, kc=kc, tmp=tmp: e.activation(hT[:, kc, 0:T], tmp[:, 0:T], AF.Identity,
                                                                      bias=modv[:, B0 + kc, seg:seg + 1], scale=A_[:, kc, seg:seg + 1]), [dtmp, dmod], [dh])
                yield


        def norm_to_h(*a, **k):
            for _ in norm_to_h_gen(*a, **k):
                pass

        def advance(gen, n):
            if gen is None:
                return
            for _ in range(n):
                try:
                    next(gen)
                except StopIteration:
                    return

        with P.phase():
            wsb = P.sbuf("wsb", [128, KC, NCOLS], BF16); dw = [Dep() for _ in range(6)]
            wv = win[l].rearrange("(kc p) n -> p kc n", p=128)
            for g in range(6):
                c0, c1 = g * 512, min((g + 1) * 512, NCOLS)
                P.dma("gpsimd", wsb[:, :, c0:c1], wv[:, :, c0:c1], writes=[dw[g]])
            rings = {"sq": Ring(P, "sq", [128, KC, 512], BF16, 2), "psn": Ring(P, "psn", [128, 512], F32, 1, "psum"),
                     "rs": Ring(P, "rs", [128, 512], F32, 2), "tmp": Ring(P, "tmp", [128, 512], F32, 3)}
            xr = Ring(P, "xt", [128, KC, 512], F32, 2)
            hr = Ring(P, "hT", [128, KC, 512], BF16, 2)
            pso = Ring(P, "pso", [128, 512], F32, 4, "psum")
            psab = Ring(P, "psab", [128, 512], F32, 1, "psum")
            stg = Ring(P, "stg", [128, 4, 512], F32, 2)
            abs_ = Ring(P, "abst", [128, 4, 16], F32, 2)
            def proj_pre(t0, seg):
                xt, dxt = xr.next()
                P.dma("sync", xt[:], XT[:, :, t0:t0 + 512], reads=[dXT], writes=[dxt])
                hT, dh = hr.next()
                return hT, dh, norm_to_h_gen(xt, dxt, hT, dh, seg, A1, 0, rings)
            cur = proj_pre(*tiles[0])
            advance(cur[2], 1000)
            for ti, (t0, seg) in enumerate(tiles):
                hT, dh, _ = cur
                nxt = proj_pre(*tiles[ti + 1]) if ti + 1 < len(tiles) else None
                for og in range(5):
                    st, dst = stg.next()
                    for j in range(4):
                        oc = og * 4 + j
                        ps, dps = pso.next()
                        mm_group(ps[:], [(wsb[:, kc, oc * 128:(oc + 1) * 128], hT[:, kc, :]) for kc in range(KC)], [dh] + dw, [dps])
                        evac(st[:, j, :], ps[:], [dps], [dst])
                        if nxt is not None:
                            advance(nxt[2], 1)
                    P.dma("sync", PROJ[:, og * 4:og * 4 + 4, t0:t0 + 512], st[:], reads=[dst], writes=[dPROJ])
                pa, dpa = psab.next()
                for sub in range(4):
                    mm_group(pa[:, sub * 16:(sub + 1) * 16], [(hT[:, kc, sub * 128:(sub + 1) * 128], wsb[:, kc, 2560:2576]) for kc in range(KC)], [dh] + dw, [dpa])
                ab, dab = abs_.next()
                evac(ab[:], pa[:, 0:64].rearrange("p (a b) -> p a b", b=16), [dpa], [dab])
                P.dma("sync", AB[t0:t0 + 512].rearrange("(s p) k -> p s k", p=128), ab[:], reads=[dab], writes=[dAB])
                if nxt is not None:
                    advance(nxt[2], 1000)
                cur = nxt
        if dbg == "proj":
            break

        with P.phase():
            inr = Ring(P, "cin", [128, 6, 514], F32, 2)
            ur = Ring(P, "cu", [128, 2, 514], F32, 2)
            accr = Ring(P, "cacc", [128, 512], F32, 9)
            yr = Ring(P, "cy", [128, 2, 512], BF16, 2)
            sr = Ring(P, "csil", [128, 512], F32, 5)
            sqr = Ring(P, "csq", [128, 512], BF16, 5)
            pcr = Ring(P, "pcr", [128, 512], F32, 6, "psum")
            rr = Ring(P, "crs", [128, 512], F32, 5)
            obr = Ring(P, "cob", [128, 6, 512], F32, 2)

            def load_halo(c0, s_, n_, t0, T):
                a, da = inr.next()
                lo = t0 - 1 if t0 > 0 else 0
                hi = t0 + T + 1 if t0 + T < n_ else n_
                if t0 == 0:
                    P.op("vector", lambda e, a=a: e.memset(a[:, :, 0:1], 0.0), [], [da])
                if t0 + T >= n_:
                    P.op("vector", lambda e, a=a: e.memset(a[:, :, T + 1:T + 2], 0.0), [], [da])
                P.dma("sync", a[:, :, (lo - (t0 - 1)):(hi - (t0 - 1))], PROJ[:, c0:c0 + 6, s_ + lo:s_ + hi], reads=[dPROJ], writes=[da])
                return a, da

            def conv3_ops(acc, dacc, u_ap_fn, wcol, rd):
                P.op("vector", lambda e: e.tensor_scalar(acc, u_ap_fn(1), pvt[:, wcol(1):wcol(1) + 1], None, ALU.mult), rd + [dpv], [dacc])
                yield 0
                P.op("vector", lambda e: e.scalar_tensor_tensor(acc, u_ap_fn(0), pvt[:, wcol(0):wcol(0) + 1], acc, ALU.mult, ALU.add), rd + [dpv], [dacc])
                yield 1
                P.op("vector", lambda e: e.scalar_tensor_tensor(acc, u_ap_fn(2), pvt[:, wcol(2):wcol(2) + 1], acc, ALU.mult, ALU.add), rd + [dpv], [dacc])
                yield 2

            for (s_, n_, kind, si) in seqs:
                T = min(512, n_)
                for t0 in range(0, n_, T):
                    a, da = load_halo(0, s_, n_, t0, T)
                    u, du = ur.next()
                    P.op("vector", lambda e, a=a, u=u: e.tensor_tensor(u[:, :, 0:T + 2], a[:, 2:4, 0:T + 2], a[:, 4:6, 0:T + 2], ALU.mult), [da], [du])
                    y, dy = yr.next()

                    def sc_gen(c, a=a, da=da, u=u, du=du, y=y, dy=dy):
                        acc, dacc = accr.next()
                        for op_ in conv3_ops(acc[:, 0:T], dacc, lambda k, u=u, c=c: u[:, c, k:k + T], lambda k, c=c: PVO["scw"] + k * 2 + c, [du]):
                            yield
                        P.op("vector", lambda e, a=a, y=y, c=c, acc=acc: e.tensor_tensor(y[:, c, 0:T], a[:, c, 1:T + 1], acc[:, 0:T], ALU.mult), [da, dacc], [dy])
                        yield
                    gens = [sc_gen(c) for c in range(2)]
                    a2, da2 = load_halo(12, s_, n_, t0, T)
                    ob, dob = obr.next()

                    def dn_gen(c, a=a2, da=da2, ob=ob, dob=dob):
                        acc, dacc = accr.next()
                        for op_ in conv3_ops(acc[:, 0:T], dacc, lambda k, a=a, c=c: a[:, c, k:k + T], lambda k, c=c: PVO["dnw"] + k * 6 + c, [da]):
                            yield
                        if c >= 4:
                            P.op("scalar", lambda e, ob=ob, c=c, acc=acc: e.activation(ob[:, c, 0:T], acc[:, 0:T], AF.Silu), [dacc], [dob])
                            yield
                            return
                        sl, dsl = sr.next()
                        P.op("scalar", lambda e, sl=sl, acc=acc: e.activation(sl[:, 0:T], acc[:, 0:T], AF.Silu), [dacc], [dsl])
                        yield
                        sq, dsq = sqr.next()
                        P.op("scalar", lambda e, sl=sl, sq=sq: e.activation(sq[:, 0:T], sl[:, 0:T], AF.Square), [dsl], [dsq])
                        yield
                        ps, dps = pcr.next()
                        mm_group(ps[:, 0:T], [(bd64b, sq[:, 0:T])], [dsq, dcb], [dps])
                        rs, drs = rr.next()
                        P.op("scalar", lambda e, rs=rs, ps=ps: e.activation(rs[:, 0:T], ps[:, 0:T], AF.Ln, bias=EPS, scale=1.0), [dps], [drs])
                        yield
                        P.op("scalar", lambda e, rs=rs: e.activation(rs[:, 0:T], rs[:, 0:T], AF.Exp, scale=-0.5), [drs], [drs])
                        yield
                        scl = 0.125 if c < 2 else 1.0
                        P.op("vector", lambda e, ob=ob, c=c, sl=sl, rs=rs, scl=scl: e.scalar_tensor_tensor(ob[:, c, 0:T], sl[:, 0:T], scl, rs[:, 0:T], ALU.mult, ALU.mult), [dsl, drs], [dob])
                        yield
                    gens += [dn_gen(c) for c in range(6)]
                    run_lockstep(gens)
                    P.dma("sync", YMIX[:, 0:2, s_ + t0:s_ + t0 + T], y[:, :, 0:T], reads=[dy], writes=[dYMIX])
                    P.dma("sync", DNQ.rearrange("c p t -> p c t")[:, :, s_ + t0:s_ + t0 + T], ob[:, :, 0:T], reads=[dob], writes=[dDNQ])
        if dbg in ("conv", "convA"):
            break

        for (s_, n_, kind, si) in seqs:
            if (dbg == "attP" and kind == "s") or (dbg in ("attS", "attSN") and kind == "p"):
                continue
            with P.phase():
                samp = kind == "s"
                T = min(512, n_)
                nblk = n_ // 128
                QN = P.sbuf("QN", [128, 4, n_], BF16); KN = P.sbuf("KN", [128, n_], BF16)
                VA = [P.sbuf("VA0", [128, nblk, 128], BF16), P.sbuf("VA1", [128, nblk, 128], BF16)]
                dQN, dKN, dVT = Dep(), Dep(), Dep()
                P.op("vector", lambda e: e.memset(VA[0][:, :, 64:128], 1.0), [], [dVT])
                P.op("vector", lambda e: e.memset(VA[1][:, :, 0:64], 1.0), [], [dVT])
                qkr = Ring(P, "aqk", [128, 5, 512], F32, 2)
                vr = Ring(P, "av", [128, 512], F32, 2)
                sqr = Ring(P, "asq", [128, 5, 512], BF16, 1)
                psA = Ring(P, "psA", [128, 512], F32, 8, "psum")
                rr = Ring(P, "ars", [128, 512], F32, 5)
                qnr = Ring(P, "aqn", [128, 512], F32, 5)
                qbr = Ring(P, "aqb", [128, 512], BF16, 5)
                t1r = Ring(P, "at1", [128, 512], F32, 5)
                t2r = Ring(P, "at2", [128, 512], F32, 5)
                cosr = Ring(P, "acos", [128, 512], F32, 2)
                sinr = Ring(P, "asin", [128, 512], F32, 2)
                stv = Ring(P, "astv", [128, 4, 128], F32, 2)
                for t0 in range(0, n_, T):
                    qk, dqk = qkr.next()
                    P.dma("sync", qk[:, :, 0:T], PROJ[:, 6:11, s_ + t0:s_ + t0 + T], reads=[dPROJ], writes=[dqk])
                    vv, dvv = vr.next()
                    P.dma("sync", vv[:, 0:T], PROJ[:, 11, s_ + t0:s_ + t0 + T], reads=[dPROJ], writes=[dvv])
                    if samp:
                        cs, dcs = cosr.next(); sn, dsn = sinr.next()
                        P.dma("sync", cs[:, 0:T], ropec[:, t0:t0 + T], writes=[dcs])
                        P.dma("sync", sn[:, 0:T], ropes[:, t0:t0 + T], writes=[dsn])
                    sq, dsq = sqr.next()
                    P.op("scalar", lambda e, sq=sq, qk=qk: e.activation(sq[:, :, 0:T], qk[:, :, 0:T], AF.Square), [dqk], [dsq])
                    def chunk_gen(c, sq=sq, dsq=dsq, qk=qk, dqk=dqk, t0=t0):
                            ps, dps = psA.next()
                            mm_group(ps[:, 0:T], [(bd64b, sq[:, c, 0:T])], [dsq, dcb], [dps])
                            yield
                            rs, drs = rr.next()
                            P.op("scalar", lambda e, rs=rs, ps=ps: e.activation(rs[:, 0:T], ps[:, 0:T], AF.Ln, bias=EPS, scale=1.0 / 64), [dps], [drs])
                            yield
                            P.op("scalar", lambda e, rs=rs: e.activation(rs[:, 0:T], rs[:, 0:T], AF.Exp, scale=-0.5), [drs], [drs])
                            yield
                            gcol = PVO["qg"] if c < 4 else PVO["kg"]
                            dst = QN[:, c, t0:t0 + T] if c < 4 else KN[:, t0:t0 + T]
                            ddst = dQN if c < 4 else dKN
                            if not samp and c < 4:
                                P.op("vector", lambda e, qk=qk, c=c, rs=rs, dst=dst, gcol=gcol: e.scalar_tensor_tensor(dst, qk[:, c, 0:T], pvt[:, gcol:gcol + 1], rs[:, 0:T], ALU.mult, ALU.mult), [dqk, drs, dpv], [ddst])
                                yield
                                return
                            qn, dqn = qnr.next()
                            P.op("vector", lambda e, qk=qk, c=c, rs=rs, qn=qn, gcol=gcol: e.scalar_tensor_tensor(qn[:, 0:T], qk[:, c, 0:T], pvt[:, gcol:gcol + 1], rs[:, 0:T], ALU.mult, ALU.mult), [dqk, drs, dpv], [dqn])
                            yield
                            if not samp:
                                P.op("scalar", lambda e, qn=qn, dst=dst: e.copy(dst, qn[:, 0:T]), [dqn], [ddst])
                                yield
                                ps2, dps2 = psA.next()

                                def fnk(e, ps2=ps2, qn=qn):
                                    ins = None
                                    for b in range(T // 128):
                                        ins = e.transpose(ps2[:, b * 128:(b + 1) * 128], qn[:, b * 128:(b + 1) * 128], identf)
                                    return ins
                                P.op("tensor", fnk, [dqn, dcst], [dps2])
                                yield
                                so, dso = stv.next()
                                evac(so[:, 0:T // 128, :], ps2[:, 0:T].rearrange("p (a b) -> p a b", b=128), [dps2], [dso])
                                yield
                                P.dma("sync", nk[si, l, t0:t0 + T, :].rearrange("(b p) f -> p b f", p=128), so[:, 0:T // 128, :], reads=[dso], writes=[dOUT])
                                yield
                                return
                            qb, dqb = qbr.next()
                            P.op("scalar", lambda e, qn=qn, qb=qb: e.copy(qb[:, 0:T], qn[:, 0:T]), [dqn], [dqb])
                            yield
                            ps2, dps2 = psA.next()
                            mm_group(ps2[:, 0:T], [(rpermb, qb[:, 0:T])], [dqb, dcb], [dps2])
                            yield
                            t1, dt1 = t1r.next(); t2, dt2 = t2r.next()
                            P.op("gpsimd", lambda e, t1=t1, qn=qn, cs=cs: e.tensor_tensor(t1[:, 0:T], qn[:, 0:T], cs[:, 0:T], ALU.mult), [dqn, dcs], [dt1])
                            yield
                            P.op("vector", lambda e, t2=t2, ps2=ps2, sn=sn: e.tensor_tensor(t2[:, 0:T], ps2[:, 0:T], sn[:, 0:T], ALU.mult), [dps2, dsn], [dt2])
                            yield
                            P.op("gpsimd", lambda e, t1=t1, t2=t2, dst=dst: e.tensor_tensor(dst, t1[:, 0:T], t2[:, 0:T], ALU.add), [dt1, dt2], [ddst])
                            yield

                    run_lockstep([chunk_gen(c) for c in range(5)])
                    ps3, dps3 = psA.next()

                    def fnv(e, ps3=ps3, vv=vv):
                        ins = None
                        for b in range(T // 128):
                            ins = e.transpose(ps3[:, b * 128:(b + 1) * 128], vv[:, b * 128:(b + 1) * 128], identf)
                        return ins
                    P.op("tensor", fnv, [dvv, dcst], [dps3])
                    b0 = t0 // 128
                    P.op("vector", lambda e, ps3=ps3, b0=b0: e.tensor_copy(VA[0][:, b0:b0 + T // 128, 0:64], ps3[:, 0:T].rearrange("p (a b) -> p a b", b=128)[:, :, 0:64]), [dps3], [dVT])
                    P.op("vector", lambda e, ps3=ps3, b0=b0: e.tensor_copy(VA[1][:, b0:b0 + T // 128, 64:128], ps3[:, 0:T].rearrange("p (a b) -> p a b", b=128)[:, :, 64:128]), [dps3], [dVT])
                    if not samp:
                        so, dso = stv.next()
                        P.op("scalar", lambda e, so=so, ps3=ps3: e.copy(so[:, 0:T // 128, :], ps3[:, 0:T].rearrange("p (a b) -> p a b", b=128)), [dps3], [dso])
                        P.dma("sync", nv[si, l, t0:t0 + T, :].rearrange("(b p) f -> p b f", p=128), so[:, 0:T // 128, :], reads=[dso], writes=[dOUT])
                if samp:
                    KCx = P.sbuf("KCx", [128, 512], BF16); dctx = Dep()
                    VCA = [P.sbuf("VCA0", [128, 4, 128], BF16), P.sbuf("VCA1", [128, 4, 128], BF16)]
                    P.op("vector", lambda e: e.memset(VCA[0][:, :, 64:128], 1.0), [], [dctx])
                    P.op("vector", lambda e: e.memset(VCA[1][:, :, 0:64], 1.0), [], [dctx])
                    ckt = P.sbuf("ckt", [128, 4, 128], F32); cvt = P.sbuf("cvt", [128, 4, 128], F32); dck = Dep()
                    P.dma("sync", ckt[:], ck[l].rearrange("(b p) f -> p b f", p=128), writes=[dck])
                    P.dma("sync", cvt[:], cv[l].rearrange("(b p) f -> p b f", p=128), writes=[dck])
                    ps4, dps4 = psA.next()

                    def fnc(e, ps4=ps4):
                        ins = None
                        for b in range(4):
                            ins = e.transpose(ps4[:, b * 128:(b + 1) * 128], ckt[:, b, :], identf)
                        return ins
                    P.op("tensor", fnc, [dck, dcst], [dps4])
                    P.op("vector", lambda e, ps4=ps4: e.tensor_copy(KCx[:], ps4[:]), [dps4], [dctx])
                    P.op("vector", lambda e: e.tensor_copy(VCA[0][:, :, 0:64], cvt[:, :, 0:64]), [dck], [dctx])
                    P.op("vector", lambda e: e.tensor_copy(VCA[1][:, :, 64:128], cvt[:, :, 64:128]), [dck], [dctx])
                pss = Ring.of(psA.t[0:4], psA.d[0:4])
                pso_ = Ring.of(psA.t[4:6], psA.d[4:6])
                psd_ = Ring.of(psA.t[6:8], psA.d[6:8])
                ptr = Ring(P, "apt", [128, 512], BF16, 6)
                denr = Ring(P, "aden", [128, 512], F32, 2); rdenr = Ring(P, "arden", [128, 512], F32, 2)
                ybr = Ring(P, "ayb", [128, 4, 128], BF16, 2)
                for i in range(nblk):
                    if dbg in ("attN", "attSN"):
                        break
                    pacc = [pso_.next(), psd_.next()]
                    for kvh in range(2):
                        rows = slice(kvh * 64, (kvh + 1) * 64)
                        po, dpo = pacc[kvh]
                        kt = []
                        if samp:
                            kt += [("c", j, None) for j in range(4)]
                            if i > 0:
                                kt.append(("l", i - 1, mwprev))
                            kt.append(("l", i, None))
                            if i < nblk - 1:
                                kt.append(("l", i + 1, mwnext))
                        else:
                            kt += [("l", j, None) for j in range(nblk)]
                        def issue_s(ki):
                            src, j, msk = kt[ki]
                            ks = KCx[rows, j * 128:(j + 1) * 128] if src == "c" else KN[rows, j * 128:(j + 1) * 128]
                            kd_ = [dctx] if src == "c" else [dKN, dVT]
                            ps, dps = pss.next()
                            mm_group(ps[:], [(ks, QN[rows, :, i * 128:(i + 1) * 128])], kd_ + [dQN], [dps])
                            return ps, dps
                        ahead = [issue_s(k_) for k_ in range(min(3, len(kt)))]
                        for ki, (src, j, msk) in enumerate(kt):
                            vsrc = VCA[kvh][:, j, :] if src == "c" else VA[kvh][:, j, :]
                            kd_ = [dctx] if src == "c" else [dKN, dVT]
                            ps, dps = ahead.pop(0)
                            if ki + 3 < len(kt):
                                ahead.append(issue_s(ki + 3))
                            pt, dpt = ptr.next()
                            P.op("scalar", lambda e, pt=pt, ps=ps: e.activation(pt[:], ps[:], AF.Exp, scale=0.125), [dps], [dpt])
                            if msk is not None:
                                P.op("vector", lambda e, pt=pt, msk=msk: e.tensor_tensor(pt[:].rearrange("p (a b) -> p a b", a=4), pt[:].rearrange("p (a b) -> p a b", a=4),
                                                                                       msk.unsqueeze(1).broadcast_to([128, 4, 128]), ALU.mult), [dpt, dcb], [dpt])
                            first, last = ki == 0, ki == len(kt) - 1

                            P.op("tensor", lambda e, po=po, vsrc=vsrc, pt=pt, first=first, last=last: e.matmul(po[:], vsrc, pt[:], start=first, stop=last), kd_ + [dpt, dcb], [dpo])
                    den, dden = denr.next()
                    for kvh in range(2):
                        po, dpo = pacc[kvh]
                        orow = slice(kvh * 64, (kvh + 1) * 64)
                        drow = slice((1 - kvh) * 64, (2 - kvh) * 64)
                        P.op("vector", lambda e, den=den, po=po, drow=drow: e.tensor_tensor(den[drow, :].rearrange("p (a b) -> p a b", a=4), po[drow, :].rearrange("p (a b) -> p a b", a=4),
                                                                                            sinkexp[drow, 4:8].unsqueeze(2).broadcast_to([64, 4, 128]), ALU.add), [dpo, dmod], [dden])
                    rden, drden = rdenr.next()
                    for kvh in range(2):
                        orow = slice(kvh * 64, (kvh + 1) * 64)
                        drow = slice((1 - kvh) * 64, (2 - kvh) * 64)
                        P.op("scalar", lambda e, rden=rden, den=den, orow=orow, drow=drow: e.activation(rden[orow, :], den[drow, :], AF.Ln), [dden], [drden])
                        P.op("scalar", lambda e, rden=rden, orow=orow: e.activation(rden[orow, :], rden[orow, :], AF.Exp, scale=-1.0), [drden], [drden])
                    yb, dyb = ybr.next()
                    for kvh in range(2):
                        po, dpo = pacc[kvh]
                        orow = slice(kvh * 64, (kvh + 1) * 64)
                        P.op("vector", lambda e, yb=yb, po=po, rden=rden, orow=orow: e.tensor_tensor(yb[orow, :, :].rearrange("p a b -> p (a b)"), po[orow, :], rden[orow, :], ALU.mult), [dpo, drden], [dyb])
                    P.dma("sync", YMIX[:, 2:6, s_ + i * 128:s_ + (i + 1) * 128], yb[:], reads=[dyb], writes=[dYMIX])
        if dbg and dbg.startswith("att"):
            break

        DNQh = DNQ.rearrange("c (h d) t -> d (c h) t", d=64)
        for (s_, n_, kind, si) in seqs:
            with P.phase():
                samp = kind == "s"
                NCH = n_ // 64
                V_ = lambda fn, r, w: P.op("vector", fn, r, w)
                A_ = lambda fn, r, w: P.op("scalar", fn, r, w)
                G_ = lambda fn, r, w: P.op("vector", fn, r, w)
                psr = Ring(P, "dps", [128, 512], F32, 8, "psum")

                def bcf(ap, n):
                    return ap.unsqueeze(2).broadcast_to([ap.shape[0], ap.shape[1], n])
                NP_ = NCH // 2
                HS = (slice(0, 64), slice(64, 128))
                abF = P.sbuf("abF", [128, NP_, 16], F32); abt = P.sbuf("abt", [128, NP_, 16], F32); dabt = Dep(); dabF = Dep()
                ABs = AB[s_:s_ + n_].rearrange("(g two t) k -> two t g k", two=2, t=64)
                for h in range(2):
                    P.dma("sync", abF[HS[h], :, :], ABs[h], reads=[dAB], writes=[dabF])
                for h in range(2):
                    P.op("vector", lambda e, h=h: e.tensor_copy(abt[HS[h], :, 0:4], abF[HS[h], :, 0:4]), [dabF], [dabt])
                    P.op("vector", lambda e, h=h: e.tensor_copy(abt[HS[h], :, 8:12], abF[HS[h], :, 8:12]), [dabF], [dabt])
                    for g in range(NP_):
                        P.op("scalar", lambda e, h=h, g=g: e.copy(abt[HS[h], g, 4:8], abF[HS[1 - h], NP_ - 1 - g, 4:8]), [dabF], [dabt])
                        P.op("scalar", lambda e, h=h, g=g: e.copy(abt[HS[h], g, 12:16], abF[HS[1 - h], NP_ - 1 - g, 12:16]), [dabF], [dabt])
                sc = {}
                for nm in ("g", "beta", "negg", "gc", "eg", "ed", "egt", "beg", "tmpa"):
                    sc[nm] = P.sbuf("dn_" + nm, [128, NP_, 8], F32)
                egtX = P.sbuf("dn_egtX", [64, NP_, 8], F32)
                dsc = Dep()
                g3 = sc["g"]
                V_(lambda e: e.tensor_tensor(sc["tmpa"][:], abt[:, :, 0:8], bct[:, 8:16].unsqueeze(1).broadcast_to([128, NP_, 8]), ALU.add), [dabt, dpv], [dsc])
                A_(lambda e: e.activation(sc["tmpa"][:], sc["tmpa"][:], AF.Exp), [dsc], [dsc])
                A_(lambda e: e.activation(sc["tmpa"][:], sc["tmpa"][:], AF.Ln, bias=1.0), [dsc], [dsc])
                V_(lambda e: e.tensor_tensor(g3[:], sc["tmpa"][:], negA2[:].unsqueeze(1).broadcast_to([128, NP_, 8]), ALU.mult), [dsc, dmod], [dsc])
                V_(lambda e: e.tensor_scalar(sc["negg"][:], g3[:], -1.0, None, ALU.mult), [dsc], [dsc])
                A_(lambda e: e.activation(sc["beta"][:], abt[:, :, 8:16], AF.Exp, scale=-1.0), [dabt], [dsc])
                V_(lambda e: e.tensor_scalar(sc["beta"][:], sc["beta"][:], 1.0, None, ALU.add), [dsc], [dsc])
                V_(lambda e: e.reciprocal(sc["beta"][:], sc["beta"][:]), [dsc], [dsc])
                pg, dpg = psr.next()
                pgF = pg[:, 0:NP_ * 4]; pgB = pg[:, NP_ * 4:NP_ * 8]

                def fng(e):
                    e.matmul(pgF, UCF2, g3[:, :, 0:4], start=True, stop=True)
                    return e.matmul(pgB, UCB2, g3[:, :, 4:8], start=True, stop=True)
                P.op("tensor", fng, [dsc, dcst], [dpg])
                for (pgX, lo) in ((pgF, 0), (pgB, 4)):
                    V_(lambda e, pgX=pgX, lo=lo: e.tensor_copy(sc["gc"][:, :, lo:lo + 4], pgX.rearrange("p (c k) -> p c k", k=4)), [dpg], [dsc])
                    A_(lambda e, pgX=pgX, lo=lo: e.activation(sc["eg"][:, :, lo:lo + 4], pgX.rearrange("p (c k) -> p c k", k=4), AF.Exp), [dpg], [dsc])
                pt_, dpt_ = psr.next()
                pt2 = pt_[:, 0:NP_ * 8]
                ptv = pt2.rearrange("p (c k) -> p c k", k=8)
                P.op("tensor", lambda e: e.matmul(pt2, bd64f, g3[:].rearrange("p c k -> p (c k)"), start=True, stop=True), [dsc, dcst], [dpt_])
                A_(lambda e: e.activation(sc["egt"][:], ptv, AF.Exp), [dpt_], [dsc])
                V_(lambda e: e.tensor_tensor(sc["ed"][:], ptv, sc["gc"][:], ALU.subtract), [dpt_, dsc], [dsc])
                A_(lambda e: e.activation(sc["ed"][:], sc["ed"][:], AF.Exp), [dsc], [dsc])
                V_(lambda e: e.tensor_tensor(sc["beg"][:], sc["beta"][:], sc["eg"][:], ALU.mult), [dsc], [dsc])
                A_(lambda e: e.copy(egtX[:], sc["egt"][64:128, :, :]), [dsc], [dsc])
                Sf = P.sbuf("Sf", [64, 8, 64], F32); Sb = P.sbuf("Sb", [128, 8, 64], BF16); dS = Dep(); dSb = Dep()
                if samp:
                    P.dma("sync", Sf[:], s0in[l].rearrange("k a b -> a k b"), writes=[dS])
                else:
                    V_(lambda e: e.memset(Sf[:], 0.0), [], [dS])
                A_(lambda e: e.copy(Sb[0:64], Sf[:]), [dS], [dSb])
                A_(lambda e: e.copy(Sb[64:128], Sf[:]), [dS], [dSb])

                def t64(name, dt, n=3, shape=(128, 8, 64)):
                    return Ring(P, name, list(shape), dt, n)
                qfr = t64("qf", F32, 3, (128, 2, 12, 64)); qbr_ = t64("qb", BF16, 3, (128, 2, 12, 64))
                NGr = t64("NG", F32, 2); Dr = t64("Dd", F32, 2); Er = t64("E", F32, 2); ERr = t64("ER", F32)
                Eir = t64("Ei", F32, 2); Esr = t64("Es", F32, 2)
                Xr = t64("X", F32, 3); Xbr = t64("Xb", BF16, 9); XTbr = t64("XTb", BF16, 9); TTbr = t64("TTb", BF16, 6); Tbr = t64("Tb", BF16); Rtr = t64("Rt", F32, 2); Rbr = t64("Rb", BF16); AINr = t64("AIN", BF16); AINTr = t64("AINT", BF16, 6)
                TTfr = t64("TTf", F32); TTb2r = t64("TTb2", BF16); TTber = t64("TTbe", BF16)
                KVr = t64("KV", BF16, 6, (128, 16, 64)); QEr = t64("QE", BF16, 6); Ur = t64("U", F32, 6); NWr = t64("NW", BF16, 6)
                VNfr = t64("VNf", F32, 2); VNr = t64("VN", BF16, 2); VNDr = t64("VND", BF16, 2); OBr = t64("OB", F32, 2); STr = t64("ST", F32, 2, (64, 8, 64))

                def per_k(out_fn, l_fn, r_fn, reads, writes, transpose=False, halves_=(0, 1)):
                    def fn(e):
                        ins = None
                        for h in halves_:
                            for k in range(8):
                                if transpose:
                                    ins = e.transpose(out_fn(h, k), l_fn(h, k), identb[HS[h], HS[h]])
                                else:
                                    ins = e.matmul(out_fn(h, k), l_fn(h, k), r_fn(h, k), start=True, stop=True)
                        return ins
                    P.op("tensor", fn, reads, writes)

                def v3(t):
                    return t[:, :].rearrange("p (k j) -> p k j", k=8)

                def vb(t):
                    return t[:, :].bitcast(BF16).rearrange("p (k j) -> p k j", j=64)
                hk = lambda t: (lambda h, k, t=t: t[HS[h], k, :])
                W_ = 3 if NP_ >= 3 else NP_

                def pre_gen(g, cx):
                        scb = lambda nm: bcf(sc[nm][:, g, :], 64)
                        chunk = lambda h, dr: (2 * g + h) if dr == 0 else (NCH - 1 - 2 * g - h)
                        qf, dqf = qfr.next()
                        for h in range(2):
                            for dr in range(2):
                                c = chunk(h, dr)
                                P.dma("sync", qf[HS[h], dr, :, :], DNQh[:, :, s_ + c * 64:s_ + c * 64 + 64], reads=[dDNQ], writes=[dqf])
                        qb, dqb = qbr_.next()
                        A_(lambda e, qb=qb, qf=qf: e.copy(qb[:], qf[:]), [dqf], [dqb])
                        Qk = lambda h, k, qb=qb: qb[HS[h], k // 4, k % 4, :]
                        Kk = lambda h, k, qb=qb: qb[HS[h], k // 4, 4 + k % 4, :]
                        Kf = lambda h, k, qf=qf: qf[HS[h], k // 4, 4 + k % 4, :]
                        pa, dpa = psr.next(); pqk, dpqk = psr.next(); pgr, dpgr = psr.next()
                        per_k(hk(v3(pa)), Kk, Kk, [dqb], [dpa])
                        per_k(hk(v3(pqk)), Qk, Kk, [dqb], [dpqk])
                        NG, dNG = NGr.next()
                        G_(lambda e, NG=NG: e.tensor_tensor(NG[:], U8, scb("negg"), ALU.mult), [dsc, dcst], [dNG])
                        P.op("tensor", lambda e, pgr=pgr, NG=NG: e.matmul(pgr[:, :], bd64f, NG[:].rearrange("p k j -> p (k j)"), start=True, stop=True), [dNG, dcst], [dpgr])
                        Dd, dD = Dr.next()
                        V_(lambda e, Dd=Dd, pgr=pgr: e.tensor_tensor(Dd[:], v3(pgr), scb("gc"), ALU.add), [dpgr, dsc], [dD])
                        V_(lambda e, Dd=Dd: e.tensor_scalar(Dd[:], Dd[:], 0.0, None, ALU.min), [dD], [dD])
                        E, dE = Er.next()
                        A_(lambda e, E=E, Dd=Dd: e.activation(E[:], Dd[:], AF.Exp), [dD], [dE])
                        ER, dER = ERr.next()
                        A_(lambda e, ER=ER, pgr=pgr: e.activation(ER[:], v3(pgr), AF.Exp, scale=-1.0), [dpgr], [dER])
                        Ei, dEi = Eir.next(); Es, dEs = Esr.next()
                        V_(lambda e, Ei=Ei, E=E: e.tensor_tensor(Ei[:].rearrange("p k j -> p (k j)"), E[:].rearrange("p k j -> p (k j)"), MI8, ALU.mult), [dE, dcst], [dEi])
                        G_(lambda e, Es=Es, E=E: e.tensor_tensor(Es[:].rearrange("p k j -> p (k j)"), E[:].rearrange("p k j -> p (k j)"), MS8, ALU.mult), [dE, dcst], [dEs])
                        G_(lambda e, Es=Es: e.tensor_tensor(Es[:], Es[:], scb("beta"), ALU.mult), [dEs, dsc], [dEs])
                        X, dX = Xr.next()
                        V_(lambda e, X=X, pa=pa, Es=Es: e.scalar_tensor_tensor(X[:].rearrange("p k j -> p (k j)"), pa[:, :], -1.0, Es[:].rearrange("p k j -> p (k j)"), ALU.mult, ALU.mult), [dpa, dEs], [dX])
                        AIN, dAIN = AINr.next()
                        V_(lambda e, AIN=AIN, pqk=pqk, Ei=Ei: e.tensor_tensor(AIN[:].rearrange("p k j -> p (k j)"), pqk[:, :], Ei[:].rearrange("p k j -> p (k j)"), ALU.mult), [dpqk, dEi], [dAIN])
                        yield
                        Xb, dXb = Xbr.next()
                        A_(lambda e, Xb=Xb, X=X: e.copy(Xb[:], X[:]), [dX], [dXb])
                        px, dpx = psr.next()
                        pxb = vb(px)
                        per_k(lambda h, k: pxb[HS[h], k, :], hk(Xb), None, [dXb, dcb], [dpx], transpose=True)
                        per_k(lambda h, k: pxb[HS[h], 8 + k, :], hk(AIN), None, [dAIN, dcb], [dpx], transpose=True)
                        XTb, dXTb = XTbr.next(); AINT, dAINT = AINTr.next()
                        V_(lambda e, XTb=XTb, pxb=pxb: e.tensor_copy(XTb[:], pxb[:, 0:8, :]), [dpx], [dXTb])
                        A_(lambda e, AINT=AINT, pxb=pxb: e.copy(AINT[:], pxb[:, 8:16, :]), [dpx], [dAINT])
                        yield
                        TTf, dTTf = TTfr.next(); TTb, dTTb = TTbr.next()
                        V_(lambda e, TTf=TTf, XTb=XTb: e.tensor_tensor(TTf[:].rearrange("p k j -> p (k j)"), XTb[:].rearrange("p k j -> p (k j)"), IDB8, ALU.add), [dXTb, dcst], [dTTf])
                        A_(lambda e, TTb=TTb, TTf=TTf: e.copy(TTb[:], TTf[:]), [dTTf], [dTTb])
                        Xc, dXc, XTc, dXTc = Xb, dXb, XTb, dXTb
                        for lev in range(4):
                            last = lev == 3
                            p2, dp2 = psr.next()
                            per_k(hk(v3(p2)), hk(XTc), hk(Xc), [dXc, dXTc], [dp2])
                            Xn, dXn = Xbr.next()
                            A_(lambda e, Xn=Xn, p2=p2: e.copy(Xn[:], v3(p2)), [dp2], [dXn])
                            if not last:
                                p3, dp3 = psr.next()
                                per_k(hk(v3(p3)), hk(Xc), hk(XTc), [dXc, dXTc], [dp3])
                                XTn, dXTn = XTbr.next()
                                A_(lambda e, XTn=XTn, p3=p3: e.copy(XTn[:], v3(p3)), [dp3], [dXTn])
                            p4, dp4 = psr.next()
                            per_k(hk(v3(p4)), hk(Xn), hk(TTb), [dXn, dTTb], [dp4])
                            V_(lambda e, TTf=TTf, p4=p4: e.tensor_tensor(TTf[:], TTf[:], v3(p4), ALU.add), [dp4, dTTf], [dTTf])
                            TTb, dTTb = TTbr.next()
                            A_(lambda e, TTb=TTb, TTf=TTf: e.copy(TTb[:], TTf[:]), [dTTf], [dTTb])
                            yield
                            Xc, dXc = Xn, dXn
                            if not last:
                                XTc, dXTc = XTn, dXTn
                        ptt, dptt = psr.next()
                        pttb = vb(ptt)
                        per_k(lambda h, k: pttb[HS[h], k, :], hk(TTb), None, [dTTb, dcb], [dptt], transpose=True)
                        Tb, dTb = Tbr.next()
                        A_(lambda e, Tb=Tb, pttb=pttb: e.copy(Tb[:], pttb[:, 0:8, :]), [dptt], [dTb])
                        Rt, dRt = Rtr.next()
                        V_(lambda e, Rt=Rt, TTf=TTf: e.scalar_tensor_tensor(Rt[:].rearrange("p k j -> p (k j)"), TTf[:].rearrange("p k j -> p (k j)"), -1.0, IDB8, ALU.mult, ALU.add), [dTTf, dcst], [dRt])
                        pr_, dpr_ = psr.next()
                        per_k(hk(v3(pr_)), hk(X), hk(TTf), [dX, dTTf], [dpr_])
                        Rb, dRb = Rbr.next()
                        V_(lambda e, Rb=Rb, Rt=Rt, pr_=pr_: e.tensor_tensor(Rb[:], Rt[:], v3(pr_), ALU.add), [dRt, dpr_], [dRb])
                        yield
                        pc_, dpc_ = psr.next()
                        per_k(hk(v3(pc_)), hk(Tb), hk(Rb), [dTb, dRb], [dpc_])
                        V_(lambda e, TTf=TTf, pc_=pc_: e.tensor_tensor(TTf[:], TTf[:], v3(pc_), ALU.add), [dpc_, dTTf], [dTTf])
                        yield
                        TTb2, dTTb2 = TTb2r.next(); TTbe, dTTbe = TTber.next()
                        G_(lambda e, TTb2=TTb2, TTf=TTf: e.tensor_tensor(TTb2[:], TTf[:], scb("beta"), ALU.mult), [dTTf, dsc], [dTTb2])
                        G_(lambda e, TTbe=TTbe, TTf=TTf: e.tensor_tensor(TTbe[:], TTf[:], scb("beg"), ALU.mult), [dTTf, dsc], [dTTbe])
                        KV, dKV = KVr.next()
                        pkv, dpkv = psr.next()
                        pkvb = vb(pkv)
                        per_k(lambda h, k: pkvb[HS[h], k, :], Kk, None, [dqb, dcb], [dpkv], transpose=True)
                        per_k(lambda h, k: pkvb[HS[h], 8 + k, :], lambda h, k, qb=qb: qb[HS[h], k // 4, 8 + k % 4, :], None, [dqb, dcb], [dpkv], transpose=True)
                        A_(lambda e, KV=KV, pkvb=pkvb: e.copy(KV[:], pkvb), [dpkv], [dKV])
                        QE, dQE = QEr.next()
                        G_(lambda e, QE=QE, qf=qf, ER=ER: e.tensor_tensor(QE[:].rearrange("p (a b) j -> p a b j", a=2), qf[:, :, 0:4, :], ER[:].rearrange("p (a b) j -> p a b j", a=2), ALU.mult), [dqf, dER], [dQE])
                        pu, dpu = psr.next()
                        per_k(hk(v3(pu)), hk(TTb2), lambda h, k, KV=KV: KV[HS[h], 8 + k, :], [dTTb2, dKV], [dpu])
                        U, dU = Ur.next()
                        A_(lambda e, U=U, pu=pu: e.copy(U[:], v3(pu)), [dpu], [dU])
                        yield
                        pw, dpw = psr.next()
                        per_k(hk(v3(pw)), hk(KV), hk(TTbe), [dTTbe, dKV], [dpw])
                        NW, dNW = NWr.next()
                        V_(lambda e, NW=NW, pw=pw: e.tensor_scalar(NW[:], v3(pw), -1.0, None, ALU.mult), [dpw], [dNW])
                        cx.update(dict(NW=NW, dNW=dNW, U=U, dU=dU, QE=QE, dQE=dQE, AINT=AINT, dAINT=dAINT, KV=KV, dKV=dKV))
                        yield

                def scan_step(g, cx, h):
                    NW = cx['NW']; dNW = cx['dNW']; U = cx['U']; dU = cx['dU']; QE = cx['QE']; dQE = cx['dQE']
                    AINT = cx['AINT']; dAINT = cx['dAINT']; KV = cx['KV']; dKV = cx['dKV']
                    hs = HS[h]
                    one = (h,)
                    pws, dpws = psr.next()
                    per_k(hk(v3(pws)), hk(NW), hk(Sb), [dNW, dSb], [dpws], halves_=one)
                    VNf, dVNf = VNfr.next(); VN, dVN = VNr.next(); VND, dVND = VNDr.next()
                    V_(lambda e, VNf=VNf, U=U, pws=pws: e.tensor_tensor(VNf[hs], U[hs], v3(pws)[hs], ALU.add), [dU, dpws], [dVNf])
                    yield
                    A_(lambda e, VN=VN, VNf=VNf: e.copy(VN[hs], VNf[hs]), [dVNf], [dVN])
                    yield
                    G_(lambda e, VND=VND, VNf=VNf: e.tensor_tensor(VND[hs], VNf[hs], bcf(sc["ed"][hs, g, :], 64), ALU.mult), [dVNf, dsc], [dVND])
                    yield
                    po_, dpo_ = psr.next()

                    def fno(e, po_=po_, QE=QE, AINT=AINT, VN=VN):
                        ins = None
                        for k in range(8):
                            e.matmul(v3(po_)[hs, k, :], QE[hs, k, :], Sb[hs, k, :], start=True, stop=False)
                            ins = e.matmul(v3(po_)[hs, k, :], AINT[hs, k, :], VN[hs, k, :], start=False, stop=True)
                        return ins
                    P.op("tensor", fno, [dQE, dSb, dAINT, dVN], [dpo_])
                    OB, dOB = OBr.next()
                    A_(lambda e, OB=OB, po_=po_: e.copy(OB[hs], v3(po_)[hs]), [dpo_], [dOB])
                    yield
                    for dr in range(2):
                        c = (2 * g + h) if dr == 0 else (NCH - 1 - 2 * g - h)
                        t_ = s_ + c * 64
                        P.dma("sync", OD[dr, t_:t_ + 64, :].rearrange("t (h v) -> t h v", h=4), OB[hs, dr * 4:dr * 4 + 4, :], reads=[dOB], writes=[dOD])
                    ST, dST = STr.next()
                    egs = sc["egt"][0:64, g, :] if h == 0 else egtX[:, g, :]
                    G_(lambda e, ST=ST: e.tensor_tensor(ST[:], Sf[:], bcf(egs, 64), ALU.mult), [dS, dsc], [dST])
                    yield
                    pS, dpS = psr.next()

                    def fns(e, pS=pS, KV=KV, VND=VND):
                        ins = None
                        for k in range(8):
                            ins = e.matmul(pS[0:64, k * 64:(k + 1) * 64], KV[hs, k, :], VND[hs, k, :], start=True, stop=True)
                        return ins
                    P.op("tensor", fns, [dKV, dVND], [dpS])
                    V_(lambda e, ST=ST, pS=pS: e.tensor_tensor(Sf[:], ST[:], pS[0:64, :].rearrange("p (k j) -> p k j", k=8), ALU.add), [dST, dpS], [dS])
                    yield
                    A_(lambda e: e.copy(Sb[0:64], Sf[:]), [dS], [dSb])
                    A_(lambda e: e.copy(Sb[64:128], Sf[:]), [dS], [dSb])
                    yield

                def scan_group(grp_, cxs_):
                    for g2 in grp_:
                        for h in range(2):
                            yield from scan_step(g2, cxs_[g2], h)
                prev = None
                for g0 in range(0, NP_, W_):
                    grp = list(range(g0, min(g0 + W_, NP_)))
                    cxs = {g2: {} for g2 in grp}
                    gens = [pre_gen(g2, cxs[g2]) for g2 in grp]
                    sg = scan_group(*prev) if prev is not None else None
                    alive = list(gens)
                    while alive:
                        for g_ in list(alive):
                            try:
                                next(g_)
                            except StopIteration:
                                alive.remove(g_)
                            if sg is not None:
                                try:
                                    next(sg)
                                except StopIteration:
                                    sg = None
                    if sg is not None:
                        run_lockstep([sg])
                    prev = (grp, cxs)
                run_lockstep([scan_group(*prev)])
                if not samp:
                    P.dma("sync", nst[si, l].rearrange("k a b -> a k b"), Sf[:], reads=[dS], writes=[dOUT])
        if dbg == "dn":
            break

        with P.phase():
            ofr = Ring(P, "cof", [128, 4, 256], F32, 2); obr2 = Ring(P, "cob2", [128, 4, 256], F32, 2)
            osr = Ring(P, "cos_", [128, 4, 256], F32, 2); sqr2 = Ring(P, "csq2", [128, 4, 256], F32, 2)
            ssr = Ring(P, "css", [128, 16], F32, 2); zr = Ring(P, "cz", [128, 2, 512], F32, 2)
            pcr2 = Ring(P, "pc2", [128, 512], F32, 4, "psum"); ydr = Ring(P, "cyd", [128, 2, 512], BF16, 2)
            for ti in range(NT // 512):
                t0 = ti * 512
                of, dof = ofr.next(); ob2, dob2 = obr2.next()
                P.dma("sync", of[:], OD[0, t0:t0 + 512, :].rearrange("(s p) f -> p s f", p=128), reads=[dOD], writes=[dof])
                P.dma("sync", ob2[:], OD[1, t0:t0 + 512, :].rearrange("(s p) f -> p s f", p=128), reads=[dOD], writes=[dob2])
                zt, dzt = zr.next()
                P.dma("sync", zt[:], PROJ[:, 18:20, t0:t0 + 512], reads=[dPROJ], writes=[dzt])
                o, do_ = osr.next()
                P.op("gpsimd", lambda e, o=o, of=of, ob2=ob2: e.tensor_tensor(o[:], of[:], ob2[:], ALU.add), [dof, dob2], [do_])
                sq, dsq = sqr2.next()
                P.op("scalar", lambda e, sq=sq, o=o: e.activation(sq[:], o[:], AF.Square), [do_], [dsq])
                ss, dss = ssr.next()
                P.op("vector", lambda e, ss=ss, sq=sq: e.tensor_reduce(ss[:], sq[:].rearrange("p s (h v) -> p (s h) v", h=4), AX.X, ALU.add), [dsq], [dss])
                P.op("scalar", lambda e, ss=ss: e.activation(ss[:], ss[:], AF.Sqrt, bias=EPS, scale=1.0 / 64), [dss], [dss])
                P.op("vector", lambda e, ss=ss: e.reciprocal(ss[:], ss[:]), [dss], [dss])
                P.op("vector", lambda e, o=o, ss=ss: e.tensor_tensor(o[:].rearrange("p s (h v) -> p (s h) v", h=4), o[:].rearrange("p s (h v) -> p (s h) v", h=4),
                                                                    ss[:].unsqueeze(2).broadcast_to([128, 16, 64]), ALU.mult), [do_, dss], [do_])
                P.op("gpsimd", lambda e, o=o: e.tensor_tensor(o[:].rearrange("p s (h v) -> p (s h) v", h=4), o[:].rearrange("p s (h v) -> p (s h) v", h=4),
                                                              bct[:, 16:80].unsqueeze(1).broadcast_to([128, 16, 64]), ALU.mult), [do_, dpv], [do_])
                P.op("scalar", lambda e, zt=zt: e.activation(zt[:], zt[:], AF.Silu), [dzt], [dzt])
                yd, dyd = ydr.next()
                for c in range(2):
                    ps, dps = pcr2.next()

                    def fnt(e, ps=ps, o=o, c=c):
                        ins = None
                        for sb in range(4):
                            ins = e.transpose(ps[:, sb * 128:(sb + 1) * 128], o[:, sb, c * 128:(c + 1) * 128], identf)
                        return ins
                    P.op("tensor", fnt, [do_, dcst], [dps])
                    P.op("vector", lambda e, yd=yd, ps=ps, zt=zt, c=c: e.tensor_tensor(yd[:, c, :], ps[:], zt[:, c, :], ALU.mult), [dps, dzt], [dyd])
                P.dma("sync", YMIX[:, 6:8, t0:t0 + 512], yd[:], reads=[dyd], writes=[dYMIX])
        if dbg == "mix":
            break

        HF = DFF // 2
        NJ = HF // 128
        for hf in range(2):
            with P.phase():
                wgs = P.sbuf("wgs", [128, KC, HF], BF16); wus = P.sbuf("wus", [128, KC, HF], BF16)
                wds = P.sbuf("wds", [128, NJ, D], BF16); dwf = [Dep() for _ in range(4)]
                P.dma("gpsimd", wgs[:], wg[l].rearrange("(kc p) n -> p kc n", p=128)[:, :, hf * HF:(hf + 1) * HF], writes=[dwf[0]])
                P.dma("gpsimd", wus[:], wu[l].rearrange("(kc p) n -> p kc n", p=128)[:, :, hf * HF:(hf + 1) * HF], writes=[dwf[1]])
                P.dma("gpsimd", wds[:], wd[l, hf * HF:(hf + 1) * HF, :].rearrange("(j p) n -> p j n", p=128), writes=[dwf[2]])
                if hf == 0:
                    wos = P.sbuf("wos", [128, KC, D], BF16)
                    P.dma("gpsimd", wos[:], wout[l].rearrange("(kc p) n -> p kc n", p=128), writes=[dwf[3]])
                    ymr = Ring(P, "ym", [128, KC, 512], BF16, 2)
                    rings = {"sq": Ring(P, "sq", [128, KC, 512], BF16, 1), "psn": Ring(P, "psn", [128, 512], F32, 1, "psum"),
                             "rs": Ring(P, "rs", [128, 512], F32, 2), "tmp": Ring(P, "tmp", [128, 512], F32, 3)}
                    psw = Ring(P, "psw", [128, 512], F32, 2, "psum")
                xr = Ring(P, "xt", [128, KC, 512], F32, 2)
                hr = Ring(P, "h2", [128, KC, 512], BF16, 2)
                actr = Ring(P, "act", [128, NJ, 512], BF16, 1)
                sgr = Ring(P, "sg", [128, 512], F32, 2)
                psf = Ring(P, "psf", [128, 512], F32, 5, "psum")
                direct_out = (hf == 1 and l == depth - 1 and not dbg)
                if direct_out:
                    pso2 = Ring(P, "pso2", [128, 512], F32, 2, "psum")
                    yo_ = Ring(P, "yo_", [128, D], F32, 2)
                def ffn_pre(t0, seg, cx):
                    xt, dxt = xr.next()
                    P.dma("sync", xt[:], XT[:, :, t0:t0 + 512], reads=[dXT], writes=[dxt])
                    h2, dh2 = hr.next()
                    cx.update(dict(xt=xt, dxt=dxt, h2=h2, dh2=dh2))
                    if hf == 0:
                        ym, dym = ymr.next()
                        P.dma("sync", ym[:], YMIX[:, :, t0:t0 + 512], reads=[dYMIX], writes=[dym])
                        yield
                        for oc in range(KC):
                            ps, dps = psw.next()
                            mm_group(ps[:], [(wos[:, kc, oc * 128:(oc + 1) * 128], ym[:, kc, :]) for kc in range(KC)], [dym, dwf[3]], [dps])
                            P.op("vector", lambda e, xt=xt, ps=ps, oc=oc, seg=seg: e.scalar_tensor_tensor(xt[:, oc, :], ps[:], modv[:, 16 + oc, seg:seg + 1], xt[:, oc, :], ALU.mult, ALU.add), [dps, dmod, dxt], [dxt])
                            yield
                        yield from norm_to_h_gen(xt, dxt, h2, dh2, seg, A2, 24, rings)
                        P.dma("sync", H2[:, :, t0:t0 + 512], h2[:], reads=[dh2], writes=[dH2])
                    else:
                        P.dma("sync", h2[:], H2[:, :, t0:t0 + 512], reads=[dH2], writes=[dh2])
                    yield
                cxc = {}
                g_ = ffn_pre(tiles[0][0], tiles[0][1], cxc)
                advance(g_, 1000)
                for ti, (t0, seg) in enumerate(tiles):
                    xt, dxt, h2, dh2 = cxc["xt"], cxc["dxt"], cxc["h2"], cxc["dh2"]
                    cxn = {}
                    nxt = ffn_pre(tiles[ti + 1][0], tiles[ti + 1][1], cxn) if ti + 1 < len(tiles) else None
                    act, dact = actr.next()
                    for j in range(NJ):
                        pg_, dpg_ = psf.next(); pu_, dpu_ = psf.next()
                        mm_group(pg_[:], [(wgs[:, kc, j * 128:(j + 1) * 128], h2[:, kc, :]) for kc in range(KC)], [dh2, dwf[0]], [dpg_])
                        mm_group(pu_[:], [(wus[:, kc, j * 128:(j + 1) * 128], h2[:, kc, :]) for kc in range(KC)], [dh2, dwf[1]], [dpu_])
                        sg, dsg = sgr.next()
                        P.op("scalar", lambda e, sg=sg, pg_=pg_: e.activation(sg[:], pg_[:], AF.Silu), [dpg_], [dsg])
                        P.op("vector", lambda e, act=act, j=j, sg=sg, pu_=pu_: e.tensor_tensor(act[:, j, :], sg[:], pu_[:], ALU.mult), [dsg, dpu_], [dact])
                        advance(nxt, 2)
                    for oc in range(KC):
                        ps, dps = psf.next()
                        mm_group(ps[:], [(wds[:, j, oc * 128:(oc + 1) * 128], act[:, j, :]) for j in range(NJ)], [dact, dwf[2]], [dps])
                        P.op("vector", lambda e, xt=xt, ps=ps, oc=oc, seg=seg: e.scalar_tensor_tensor(xt[:, oc, :], ps[:], modv[:, 40 + oc, seg:seg + 1], xt[:, oc, :], ALU.mult, ALU.add), [dps, dmod, dxt], [dxt])
                        advance(nxt, 2)
                    if direct_out:
                        for sub in range(4):
                            tt0 = t0 + sub * 128
                            o, do = yo_.next()
                            for hh in range(2):
                                ps, dps = pso2.next()

                                def fnT(e, ps=ps, xt=xt, hh=hh, sub=sub):
                                    ins = None
                                    for j in range(4):
                                        ins = e.transpose(ps[:, j * 128:(j + 1) * 128], xt[:, hh * 4 + j, sub * 128:(sub + 1) * 128], identf)
                                    return ins
                                P.op("tensor", fnT, [dxt, dcst], [dps])
                                evac(o[:, hh * 512:(hh + 1) * 512], ps[:], [dps], [do])
                            dstap = ys[tt0:tt0 + 128, :] if tt0 < NS else yp[tt0 - NS:tt0 - NS + 128, :]
                            P.dma("sync", dstap, o[:], reads=[do], writes=[dOUT])
                    else:
                        P.dma("sync", XT[:, :, t0:t0 + 512], xt[:], reads=[dxt], writes=[dXT])
                    advance(nxt, 1000)
                    cxc = cxn

    if not dbg:
        P.finish([dOUT])
        return nc
    with P.phase():
        xr = Ring(P, "fx", [128, KC, 512], F32, 2)
        yo = Ring(P, "fy", [128, D], F32, 4)
        pst = Ring(P, "fps", [128, 512], F32, 8, "psum")
        for gi in range(NT // 512):
            a, da = xr.next()
            P.dma("sync", a[:], XT[:, :, gi * 512:(gi + 1) * 512], reads=[dXT], writes=[da])
            for sub in range(4):
                t0 = gi * 512 + sub * 128
                o, do = yo.next()
                for hh in range(2):
                    ps, dps = pst.next()

                    def fn(e, ps=ps, a=a, hh=hh, sub=sub):
                        ins = None
                        for j in range(4):
                            ins = e.transpose(ps[:, j * 128:(j + 1) * 128], a[:, hh * 4 + j, sub * 128:(sub + 1) * 128], identf)
                        return ins
                    P.op("tensor", fn, [da, dcst], [dps])
                    evac(o[:, hh * 512:(hh + 1) * 512], ps[:], [dps], [do])
                dstap = ys[t0:t0 + 128, :] if t0 < NS else yp[t0 - NS:t0 - NS + 128, :]
                P.dma("sync", dstap, o[:], reads=[do], writes=[dOUT])
    P.finish([dOUT])
    return nc


def _consts():
    c = np.zeros((128, CW), np.float32)
    c[:, 0:128] = np.eye(128)
    c[:, 128:256] = 1.0
    c[0:64, 256:320] = 1.0
    c[64:128, 320:384] = 1.0
    m = np.arange(128)
    partner = np.where((m % 32) < 16, m + 16, m - 16)
    c[partner, 384 + m] = 1.0
    b = np.arange(128)[:, None]; a = np.arange(128)[None, :]
    c[:, 512:640] = (b >= a)
    c[:, 640:768] = (b <= a)
    i = np.arange(64)[:, None]; j = np.arange(64)[None, :]
    for k in range(8):
        fwd = k < 4
        c[0:64, 768 + k * 64:768 + (k + 1) * 64] = (i > j) if fwd else (i < j)
        c[0:64, 1280 + k * 64:1280 + (k + 1) * 64] = (i >= j) if fwd else (i <= j)
        c[0:64, 1792 + k * 64:1792 + (k + 1) * 64] = np.eye(64)
    c[0:64, 2304:2368] = (i <= j)
    c[0:64, 2368:2432] = (i >= j)
    for k in range(8):
        c[0:64, 2432 + k * 64:2432 + (k + 1) * 64] = (i <= j) if k < 4 else (i >= j)
    c[64:128, 768:2304] = c[0:64, 768:2304]
    c[64:128, 2432:2944] = c[0:64, 2432:2944]
    c[0:64, 2944:3008] = (i <= j); c[64:128, 3008:3072] = (i <= j)
    c[0:64, 3072:3136] = (i >= j); c[64:128, 3136:3200] = (i >= j)
    t = np.arange(4096)
    p = np.arange(128)
    d = p % 64
    jj = (d % 16).astype(np.float32)
    inv = (1.0 / (np.float32(10000.0) ** (jj / np.float32(16.0)))).astype(np.float32)
    pos = np.where((d < 32)[:, None], (t // 64)[None, :], (t % 64)[None, :]).astype(np.float32)
    ang = (pos * inv[:, None]).astype(np.float32)
    cos = np.cos(ang).astype(np.float32)
    sgn = np.where((d % 32) < 16, -1.0, 1.0).astype(np.float32)
    sin = (np.sin(ang) * sgn[:, None]).astype(np.float32)
    return c, cos, sin


def _col_perm():
    perm = np.arange(NCOLS)
    for c in range(4):
        for half, h in ((0, c), (1, 4 + c)):
            perm[768 + c * 128 + half * 64:768 + c * 128 + half * 64 + 64] = 768 + h * 64 + np.arange(64)
    return perm


def _row_perm():
    perm = np.arange(D)
    for c in range(4):
        for half, h in ((0, c), (1, 4 + c)):
            perm[256 + c * 128 + half * 64:256 + c * 128 + half * 64 + 64] = 256 + h * 64 + np.arange(64)
    return perm


def host_prep(inp, depth):
    f = lambda a: np.ascontiguousarray(np.asarray(a, dtype=np.float32))
    cst, cos, sin = _consts()
    cp, rp = _col_perm(), _row_perm()
    shared = {
        "win": f(np.asarray(inp["w_in"])[:depth][:, :, cp]),
        "wout": f(np.asarray(inp["w_out"])[:depth][:, rp, :]),
        "adaw": f(np.asarray(inp["ada_w"])[:depth]),
        "wg": f(np.asarray(inp["w_gate"])[:depth]), "wu": f(np.asarray(inp["w_up"])[:depth]), "wd": f(np.asarray(inp["w_down"])[:depth]),
        "cst": cst, "ropec": cos, "ropes": sin,
    }
    pv = np.zeros((depth, 128, 90), np.float32); bc = np.zeros((depth, 128, 96), np.float32)
    for l in range(depth):
        pv[l, :, 0:48] = np.asarray(inp["ada_b"])[l].reshape(48, 128).T
        pv[l, :, 48:56] = np.asarray(inp["norm1_g"])[l].reshape(8, 128).T
        pv[l, :, 56:64] = np.asarray(inp["norm2_g"])[l].reshape(8, 128).T
        scw = np.asarray(inp["sc_conv_w"])[l]; dnw = np.asarray(inp["dn_conv_w"])[l]
        for k in range(3):
            pv[l, :, 64 + k * 2:64 + k * 2 + 2] = scw[k].reshape(2, 128).T
            pv[l, :, 70 + k * 6:70 + k * 6 + 6] = dnw[k].reshape(6, 128).T
        pv[l, :, 88] = np.tile(np.asarray(inp["q_norm_g"])[l], 2)
        pv[l, :, 89] = np.tile(np.asarray(inp["k_norm_g"])[l], 2)
        bc[l, :, 0:8] = np.asarray(inp["dn_A_log"])[l].reshape(8)[None, :]
        bc[l, :, 8:16] = np.asarray(inp["dn_dt_bias"])[l].reshape(8)[None, :]
        bc[l, :, 16:80] = np.asarray(inp["dn_norm_g"])[l][None, :]
        sk = np.asarray(inp["attn_sink"])[l]
        bc[l, 0:64, 88:92] = sk[0:4][None, :]
        bc[l, 64:128, 88:92] = sk[4:8][None, :]
        bc[l, 0:64, 92:96] = sk[4:8][None, :]
        bc[l, 64:128, 92:96] = sk[0:4][None, :]
    shared["pv"] = pv; shared["bcp"] = bc
    return shared


def core_inputs(inp, shared, core, NS, depth, nsamp):
    f = lambda a: np.ascontiguousarray(np.asarray(a, dtype=np.float32))
    b = core % nsamp
    m = dict(shared)
    m["xs"] = f(np.asarray(inp["x_sample"])[b, :NS])
    m["xp"] = f(np.asarray(inp["x_prompt"])[2 * core:2 * core + 2].reshape(512, D))
    m["ck"] = f(np.asarray(inp["cache_k"])[b, :depth].reshape(depth, 512, 128))
    m["cv"] = f(np.asarray(inp["cache_v"])[b, :depth].reshape(depth, 512, 128))
    m["s0"] = f(np.asarray(inp["state_delta"])[b, :depth].reshape(depth, 8, 64, 64))
    cT = np.zeros((128, 16), np.float32)
    cT[:, 0::2] = np.asarray(inp["c"])[b].reshape(8, 128).T
    cT[:, 1::2] = np.asarray(inp["c_ctx"]).reshape(8, 128).T
    m["cT"] = cT
    return m


_NC_CACHE = {}


def kernel(**inputs):
    NS, depth, ncores = 4096, 2, 8
    if "full" not in _NC_CACHE:
        _NC_CACHE["full"] = build(NS, depth)
    nc = _NC_CACHE["full"]
    shared = host_prep(inputs, depth)
    in_maps = [core_inputs(inputs, shared, c, NS, depth, 4) for c in range(ncores)]
    res = run_bass_kernel_spmd(nc, in_maps, core_ids=list(range(ncores)))
    R = res.results
    y_p = np.concatenate([np.asarray(R[c]["yp"]).reshape(2, 256, D) for c in range(8)], 0)
    y_s = np.stack([np.asarray(R[c]["ys"]) for c in range(4)], 0)
    nk = np.concatenate([np.asarray(R[c]["nk"]).reshape(2, depth, 256, 2, 64) for c in range(8)], 0)
    nv = np.concatenate([np.asarray(R[c]["nv"]).reshape(2, depth, 256, 2, 64) for c in range(8)], 0)
    ns = np.concatenate([np.asarray(R[c]["nst"]).reshape(2, depth, 2, 4, 64, 64) for c in range(8)], 0)
    return (y_p.astype(np.float32), y_s.astype(np.float32), nk.astype(np.float32), nv.astype(np.float32), ns.astype(np.float32))
```

```python
from contextlib import ExitStack
import numpy as np
import concourse.bass as bass
import concourse.mybir as mybir
from concourse.bass_utils import run_bass_kernel_spmd

F32 = mybir.dt.float32
BF16 = mybir.dt.bfloat16
AF = mybir.ActivationFunctionType
ALU = mybir.AluOpType
AX = mybir.AxisListType


class Dep:
    __slots__ = ("w", "r", "x")

    def __init__(self):
        self.w = None
        self.r = []
        self.x = False


class _Rec:
    def __init__(self):
        self.calls = []

    def __getattr__(self, name):
        def f(*a, **k):
            self.calls.append((name, a, k))
            return self
        return f


def _replay_calls(calls):
    def fn(e):
        ins = None
        for name, a, k in calls:
            ins = getattr(e, name)(*a, **k)
        return ins
    return fn


class Prog:
    CE = ("tensor", "vector", "scalar", "gpsimd")
    NDMA = {"sync": 12, "gpsimd": 6, "scalar": 4}

    def __init__(self, nc):
        self.nc = nc
        self.stack = ExitStack()
        self.ops = {e: [] for e in ("tensor", "vector", "scalar", "gpsimd", "sync")}
        self.sem = {}
        self.cnt = {}
        self.seen = {e: {} for e in self.ops}
        for e in self.CE:
            self.sem[e] = self.stack.enter_context(nc.semaphore("s_" + e))
            self.cnt[e] = 0
        self.dsem = {}
        self.dval = {}
        self.dnext = {}
        for q, n in self.NDMA.items():
            self.dsem[q] = [self.stack.enter_context(nc.semaphore("d_%s%d" % (q, i))) for i in range(n)]
            self.dnext[q] = 0
        for q in self.dsem:
            for s in self.dsem[q]:
                self.dval[id(s)] = 0
        self.n_alloc = 0

    def sbuf(self, name, shape, dtype):
        self.n_alloc += 1
        return self.stack.enter_context(self.nc.sbuf_tensor("%s_s%d" % (name, self.n_alloc), list(shape), dtype))

    def psum(self, name, shape, dtype):
        self.n_alloc += 1
        return self.stack.enter_context(self.nc.psum_tensor("%s_p%d" % (name, self.n_alloc), list(shape), dtype))

    def dep(self):
        return Dep()

    def deps(self, n):
        return [Dep() for _ in range(n)]

    def _collect(self, eng, reads, writes):
        toks = []
        for d in reads:
            if d.w is not None:
                toks.append(d.w)
        for d in writes:
            if d.w is not None:
                toks.append(d.w)
            toks.extend(d.r)
        seen = self.seen[eng]
        waits = {}
        for (s, v, src) in toks:
            if src == eng and eng == "tensor":
                continue
            k = id(s)
            if seen.get(k, 0) >= v:
                continue
            if k not in waits or waits[k][1] < v:
                waits[k] = (s, v)
        for k, (s, v) in waits.items():
            seen[k] = v
        return list(waits.values())

    def _commit(self, tok, reads, writes):
        for d in reads:
            d.r.append(tok)
        for d in writes:
            d.w = tok
            d.r = []

    max_ops = None
    n_ops = 0
    log = []

    def _skip(self, desc):
        Prog.n_ops += 1
        if Prog.max_ops is not None and Prog.n_ops > Prog.max_ops:
            return True
        Prog.log.append(desc)
        return False

    def op(self, eng, fn, reads=(), writes=()):
        if self._skip((eng,)):
            return None
        writes = list(writes) + [d for d in reads if d.x]
        reads = [d for d in reads if not d.x]
        waits = self._collect(eng, reads, writes)
        self.cnt[eng] += 1
        tok = (self.sem[eng], self.cnt[eng], eng)
        rec = _Rec()
        fn(rec)
        self.ops[eng].append((waits, _replay_calls(rec.calls), self.sem[eng], 1))
        self._commit(tok, reads, writes)
        return tok

    def dma(self, q, out, in_, reads=(), writes=(), **kw):
        if self._skip(("dma_" + q,)):
            return None
        pool = self.dsem[q]
        s = pool[self.dnext[q] % len(pool)]
        self.dnext[q] += 1
        waits = self._collect(q, reads, writes)
        prev = self.dval[id(s)]
        if prev > 0 and self.seen[q].get(id(s), 0) < prev:
            waits.append((s, prev))
            self.seen[q][id(s)] = prev
        self.dval[id(s)] = prev + 16
        tok = (s, prev + 16, "dma_" + q)
        self.ops[q].append((waits, lambda e: e.dma_start(out=out, in_=in_, **kw), s, 16))
        self._commit(tok, reads, writes)
        return tok

    def wait_on(self, eng, deps_):
        waits = self._collect(eng, deps_, ())
        self.ops[eng].append((waits, None, None, 0))

    def flush(self):
        for q in self.dsem:
            waits = []
            for s in self.dsem[q]:
                v = self.dval[id(s)]
                if v > 0 and self.seen["sync"].get(id(s), 0) < v:
                    waits.append((s, v))
                    self.seen["sync"][id(s)] = v
            if waits:
                self.ops["sync"].append((waits, None, None, 0))
        nc = self.nc

        def replay(name):
            def run(e):
                for waits, fn, s, inc in self.ops[name]:
                    for (ws, wv) in waits:
                        e.wait_ge(ws, wv)
                    if fn is not None:
                        ins = fn(e)
                        ins.then_inc(s, inc)
            return run

        with nc.Block() as block:
            block.tensor(replay("tensor"))
            block.vector(replay("vector"))
            block.scalar(replay("scalar"))
            block.gpsimd(replay("gpsimd"))
            block.sync(replay("sync"))
        for k in self.ops:
            self.ops[k] = []

    def phase(self):
        prog = self

        class _Ph:
            def __enter__(s):
                s.saved = prog.stack
                prog.stack = ExitStack()
                return prog

            def __exit__(s, *a):
                if a[0] is None:
                    prog.flush()
                prog.stack.close()
                prog.stack = s.saved
                return False
        return _Ph()

    def finish(self, out_deps=()):
        self.wait_on("sync", out_deps)
        self.flush()
        self.stack.close()


class Ring:
    def __init__(self, P, name, shape, dtype, n, space="sbuf"):
        mk = P.sbuf if space == "sbuf" else P.psum
        self.t = [mk("%s_%d" % (name, i), shape, dtype) for i in range(n)]
        self.d = [P.dep() for _ in range(n)]
        if space != "sbuf":
            for d in self.d:
                d.x = True
        self.i = 0

    def next(self):
        k = self.i % len(self.t)
        self.i += 1
        return self.t[k], self.d[k]

    @classmethod
    def of(cls, tiles, deps):
        r = cls.__new__(cls)
        r.t = list(tiles); r.d = list(deps); r.i = 0
        return r


def run_lockstep(gens):
    alive = list(gens)
    while alive:
        for g in list(alive):
            try:
                next(g)
            except StopIteration:
                alive.remove(g)


D = 1024
KC = 8
NCOLS = 2576
DFF = 2816
EPS = 1e-6
CW = 3200


def build(NS=4096, depth=2, dbg=False):
    NT = NS + 512
    nc = bass.Bass("TRN2", target_bir_lowering=False)

    def din(name, shape, dt=F32):
        return nc.dram_tensor(name, list(shape), dt, kind="ExternalInput").ap()

    def dout(name, shape, dt=F32):
        return nc.dram_tensor(name, list(shape), dt, kind="ExternalOutput").ap()

    def dint(name, shape, dt=F32):
        if dbg:
            return nc.dram_tensor(name, list(shape), dt, kind="ExternalOutput").ap()
        return nc.dram_tensor(name, list(shape), dt).ap()

    xs = din("xs", [NS, D]); xp = din("xp", [512, D])
    ck = din("ck", [depth, 512, 128]); cv = din("cv", [depth, 512, 128])
    s0in = din("s0", [depth, 8, 64, 64])
    cTin = din("cT", [128, 16])
    win = din("win", [depth, D, NCOLS]); wout = din("wout", [depth, D, D])
    adaw = din("adaw", [depth, D, 6 * D])
    wg = din("wg", [depth, D, DFF]); wu = din("wu", [depth, D, DFF]); wd = din("wd", [depth, DFF, D])
    pvin = din("pv", [depth, 128, 90]); bcin = din("bcp", [depth, 128, 96])
    cstin = din("cst", [128, CW]); ropec = din("ropec", [128, 4096]); ropes = din("ropes", [128, 4096])
    ys = dout("ys", [NS, D]); yp = dout("yp", [512, D])
    nk = dout("nk", [2, depth, 256, 128]); nv = dout("nv", [2, depth, 256, 128])
    nst = dout("nst", [2, depth, 8, 64, 64])

    XT = dint("XT", [KC, 128, NT]).rearrange("c p t -> p c t")
    PROJ = dint("PROJ", [20, 128, NT]).rearrange("c p t -> p c t")
    AB = dint("AB", [NT, 16])
    DNQ = dint("DNQ", [6, 128, NT])
    YMIX = dint("YMIX", [KC, 128, NT], BF16).rearrange("c p t -> p c t")
    H2 = dint("H2", [KC, 128, NT], BF16).rearrange("c p t -> p c t")
    OD = dint("OD", [2, NT, 256])
    dXT, dPROJ, dAB, dDNQ, dYMIX, dH2, dOD = [Dep() for _ in range(7)]
    dOUT = Dep()

    P = Prog(nc)
    seqs = [(0, NS, "s", 0), (NS, 256, "p", 0), (NS + 256, 256, "p", 1)]
    tiles = [(t0, 0) for t0 in range(0, NS, 512)] + [(NS, 1)]

    cst = P.sbuf("cst", [128, CW], F32); dcst = Dep()
    P.dma("sync", cst[:], cstin, writes=[dcst])
    identf = cst[:, 0:128]
    cb = P.sbuf("cb", [128, 6 * 128], BF16); dcb = Dep()
    P.op("vector", lambda e: e.tensor_copy(cb[:], cst[:, 0:768]), [dcst], [dcb])
    identb = cb[:, 0:128]; onesb = cb[:, 128:256]; bd64b = cb[:, 256:384]; rpermb = cb[:, 384:512]
    mwprev = cb[:, 512:640]; mwnext = cb[:, 640:768]
    MS8 = cst[:, 768:1280]; MI8 = cst[:, 1280:1792]; IDB8 = cst[:, 1792:2304]
    U8 = cst[:, 2432:2944].rearrange("p (k j) -> p k j", k=8)
    UCF2 = cst[:, 2944:3072]; UCB2 = cst[:, 3072:3200]; bd64f = cst[:, 256:384]
    UCF = cst[0:64, 2304:2368]; UCB = cst[0:64, 2368:2432]
    onesf64 = cst[0:64, 128:192]
    modv = P.sbuf("modv", [128, 48, 2], F32); dmod = Dep()
    A1 = P.sbuf("A1", [128, 8, 2], F32); A2 = P.sbuf("A2", [128, 8, 2], F32)
    pvt = P.sbuf("pvt", [128, 90], F32); bct = P.sbuf("bct", [128, 96], F32); dpv = Dep()
    sinkexp = P.sbuf("sinkexp", [128, 8], F32)
    negA = P.sbuf("negA", [64, 8], F32); negA2 = P.sbuf("negA2", [128, 8], F32)
    PVO = {"adab": 0, "n1": 48, "n2": 56, "scw": 64, "dnw": 70, "qg": 88, "kg": 89}
    BCO = {"alog": 0, "dtb": 8, "dng": 16, "sink": 88}
    P.flush()

    def mm_group(out, pairs, reads, writes):
        n = len(pairs)

        def fn(e):
            ins = None
            for i, (l_, r_) in enumerate(pairs):
                ins = e.matmul(out, l_, r_, start=(i == 0), stop=(i == n - 1))
            return ins
        P.op("tensor", fn, reads, writes)

    evac_i = [0]

    def evac(out, in_, reads, writes):
        evac_i[0] += 1
        if evac_i[0] % 2:
            P.op("vector", lambda e: e.tensor_copy(out, in_), reads, writes)
        else:
            P.op("scalar", lambda e: e.copy(out, in_), reads, writes)

    with P.phase():
        xin = Ring(P, "xin", [128, D], F32, 4)
        xo = Ring(P, "xo", [128, KC, 512], F32, 2)
        pst = Ring(P, "pst", [128, 512], F32, 8, "psum")
        for gi in range(NT // 512):
            o, do = xo.next()
            for sub in range(4):
                t0 = gi * 512 + sub * 128
                src = xs[t0:t0 + 128, :] if t0 < NS else xp[t0 - NS:t0 - NS + 128, :]
                a, da = xin.next()
                P.dma("sync", a[:], src, writes=[da])
                for hh in range(2):
                    ps, dps = pst.next()

                    def fn(e, ps=ps, a=a, hh=hh):
                        ins = None
                        for j in range(4):
                            c = hh * 4 + j
                            ins = e.transpose(ps[:, j * 128:(j + 1) * 128], a[:, c * 128:(c + 1) * 128], identf)
                        return ins
                    P.op("tensor", fn, [da, dcst], [dps])
                    evac(o[:, hh * 4:(hh + 1) * 4, sub * 128:(sub + 1) * 128], ps[:].rearrange("p (a b) -> p a b", a=4), [dps], [do])
            P.dma("sync", XT[:, :, gi * 512:(gi + 1) * 512], o[:], reads=[do], writes=[dXT])

    for l in range(depth):
        with P.phase():
            P.dma("sync", pvt[:], pvin[l], writes=[dpv])
            P.dma("sync", bct[:], bcin[l], writes=[dpv])
            ct = P.sbuf("ct", [128, 16], F32); dct = Dep()
            P.dma("sync", ct[:], cTin, writes=[dct])
            sil = P.sbuf("sil", [128, 16], BF16); dsil = Dep()
            P.op("scalar", lambda e: e.activation(sil[:], ct[:], AF.Silu), [dct], [dsil])
            war = Ring(P, "wa", [128, KC, 768], BF16, 3)
            psm = P.psum("psm", [128, 512], F32); dpsm = Dep()
            awv = adaw[l].rearrange("(kc p) n -> p kc n", p=128)
            for g in range(8):
                wa, dwa = war.next()
                for hh in range(2):
                    P.dma("gpsimd", wa[:, :, hh * 384:(hh + 1) * 384], awv[:, :, g * 768 + hh * 384:g * 768 + (hh + 1) * 384], writes=[dwa])
                for fcl in range(6):
                    fc = g * 6 + fcl
                    mm_group(psm[:, fc * 2:fc * 2 + 2],
                             [(wa[:, kc, fcl * 128:(fcl + 1) * 128], sil[:, kc * 2:kc * 2 + 2]) for kc in range(KC)],
                             [dwa, dsil], [dpsm])
            P.op("vector", lambda e: e.tensor_tensor(modv[:], psm[:, 0:96].rearrange("p (a b) -> p a b", b=2),
                                                      pvt[:, 0:48].unsqueeze(2).broadcast_to([128, 48, 2]), ALU.add), [dpsm, dpv], [dmod])
            P.op("vector", lambda e: e.scalar_tensor_tensor(A1[:], modv[:, 8:16, :], 1.0, pvt[:, 48:56].unsqueeze(2).broadcast_to([128, 8, 2]), ALU.add, ALU.mult), [dmod, dpv], [dmod])
            P.op("vector", lambda e: e.scalar_tensor_tensor(A2[:], modv[:, 32:40, :], 1.0, pvt[:, 56:64].unsqueeze(2).broadcast_to([128, 8, 2]), ALU.add, ALU.mult), [dmod, dpv], [dmod])
            P.op("scalar", lambda e: e.activation(sinkexp[:], bct[:, 88:96], AF.Exp), [dpv], [dmod])
            P.op("scalar", lambda e: e.activation(negA[:], bct[0:64, 0:8], AF.Exp), [dpv], [dmod])
            P.op("vector", lambda e: e.tensor_scalar(negA[:], negA[:], -1.0, None, ALU.mult), [dmod], [dmod])
            P.op("scalar", lambda e: e.activation(negA2[:], bct[:, 0:8], AF.Exp), [dpv], [dmod])
            P.op("vector", lambda e: e.tensor_scalar(negA2[:], negA2[:], -1.0, None, ALU.mult), [dmod], [dmod])

        def norm_to_h_gen(xt, dxt, hT, dh, seg, A_, B0, rings, T=512):
            sq, dsq = rings["sq"].next()
            P.op("scalar", lambda e: e.activation(sq[:, :, 0:T], xt[:, :, 0:T], AF.Square), [dxt], [dsq])
            yield
            psn, dpsn = rings["psn"].next()
            mm_group(psn[:, 0:T], [(onesb, sq[:, kc, 0:T]) for kc in range(KC)], [dsq, dcb], [dpsn])
            yield
            rs, drs = rings["rs"].next()
            P.op("scalar", lambda e: e.activation(rs[:, 0:T], psn[:, 0:T], AF.Ln, bias=EPS, scale=1.0 / D), [dpsn], [drs])
            yield
            P.op("scalar", lambda e: e.activation(rs[:, 0:T], rs[:, 0:T], AF.Exp, scale=-0.5), [drs], [drs])
            yield
            for kc in range(KC):
                tmp, dtmp = rings["tmp"].next()
                P.op("vector", lambda e, kc=kc, tmp=tmp: e.tensor_tensor(tmp[:, 0:T], xt[:, kc, 0:T], rs[:, 0:T], ALU.mult), [dxt, drs], [dtmp])
                yield
                P.op("scalar", lambda e, kc=kc, tmp=tmp: e.activation(hT[:, kc, 0:T], tmp[:, 0:T], AF.Identity,
                                                                      bias=modv[:, B0 + kc, seg:seg + 1], scale=A_[:, kc, seg:seg + 1]), [dtmp, dmod], [dh])
                yield


        def norm_to_h(*a, **k):
            for _ in norm_to_h_gen(*a, **k):
                pass

        def advance(gen, n):
            if gen is None:
                return
            for _ in range(n):
                try:
                    next(gen)
                except StopIteration:
                    return

        with P.phase():
            wsb = P.sbuf("wsb", [128, KC, NCOLS], BF16); dw = [Dep() for _ in range(6)]
            wv = win[l].rearrange("(kc p) n -> p kc n", p=128)
            for g in range(6):
                c0, c1 = g * 512, min((g + 1) * 512, NCOLS)
                P.dma("gpsimd", wsb[:, :, c0:c1], wv[:, :, c0:c1], writes=[dw[g]])
            rings = {"sq": Ring(P, "sq", [128, KC, 512], BF16, 2), "psn": Ring(P, "psn", [128, 512], F32, 1, "psum"),
                     "rs": Ring(P, "rs", [128, 512], F32, 2), "tmp": Ring(P, "tmp", [128, 512], F32, 3)}
            xr = Ring(P, "xt", [128, KC, 512], F32, 2)
            hr = Ring(P, "hT", [128, KC, 512], BF16, 2)
            pso = Ring(P, "pso", [128, 512], F32, 4, "psum")
            psab = Ring(P, "psab", [128, 512], F32, 1, "psum")
            stg = Ring(P, "stg", [128, 4, 512], F32, 2)
            abs_ = Ring(P, "abst", [128, 4, 16], F32, 2)
            def proj_pre(t0, seg):
                xt, dxt = xr.next()
                P.dma("sync", xt[:], XT[:, :, t0:t0 + 512], reads=[dXT], writes=[dxt])
                hT, dh = hr.next()
                return hT, dh, norm_to_h_gen(xt, dxt, hT, dh, seg, A1, 0, rings)
            cur = proj_pre(*tiles[0])
            advance(cur[2], 1000)
            for ti, (t0, seg) in enumerate(tiles):
                hT, dh, _ = cur
                nxt = proj_pre(*tiles[ti + 1]) if ti + 1 < len(tiles) else None
                for og in range(5):
                    st, dst = stg.next()
                    for j in range(4):
                        oc = og * 4 + j
                        ps, dps = pso.next()
                        mm_group(ps[:], [(wsb[:, kc, oc * 128:(oc + 1) * 128], hT[:, kc, :]) for kc in range(KC)], [dh] + dw, [dps])
                        evac(st[:, j, :], ps[:], [dps], [dst])
                        if nxt is not None:
                            advance(nxt[2], 1)
                    P.dma("sync", PROJ[:, og * 4:og * 4 + 4, t0:t0 + 512], st[:], reads=[dst], writes=[dPROJ])
                pa, dpa = psab.next()
                for sub in range(4):
                    mm_group(pa[:, sub * 16:(sub + 1) * 16], [(hT[:, kc, sub * 128:(sub + 1) * 128], wsb[:, kc, 2560:2576]) for kc in range(KC)], [dh] + dw, [dpa])
                ab, dab = abs_.next()
                evac(ab[:], pa[:, 0:64].rearrange("p (a b) -> p a b", b=16), [dpa], [dab])
                P.dma("sync", AB[t0:t0 + 512].rearrange("(s p) k -> p s k", p=128), ab[:], reads=[dab], writes=[dAB])
                if nxt is not None:
                    advance(nxt[2], 1000)
                cur = nxt
        if dbg == "proj":
            break

        with P.phase():
            inr = Ring(P, "cin", [128, 6, 514], F32, 4)
            ur = Ring(P, "cu", [128, 2, 514], F32, 2)
            accr = Ring(P, "cacc", [128, 512], F32, 18)
            yr = Ring(P, "cy", [128, 2, 512], BF16, 2)
            sr = Ring(P, "csil", [128, 512], F32, 10)
            sqr = Ring(P, "csq", [128, 512], BF16, 10)
            pcr = Ring(P, "pcr", [128, 512], F32, 6, "psum")
            rr = Ring(P, "crs", [128, 512], F32, 10)
            obr = Ring(P, "cob", [128, 6, 512], F32, 2)

            def load_halo(c0, s_, n_, t0, T):
                a, da = inr.next()
                lo = t0 - 1 if t0 > 0 else 0
                hi = t0 + T + 1 if t0 + T < n_ else n_
                if t0 == 0:
                    P.op("vector", lambda e, a=a: e.memset(a[:, :, 0:1], 0.0), [], [da])
                if t0 + T >= n_:
                    P.op("vector", lambda e, a=a: e.memset(a[:, :, T + 1:T + 2], 0.0), [], [da])
                P.dma("sync", a[:, :, (lo - (t0 - 1)):(hi - (t0 - 1))], PROJ[:, c0:c0 + 6, s_ + lo:s_ + hi], reads=[dPROJ], writes=[da])
                return a, da

            def conv3_ops(acc, dacc, u_ap_fn, wcol, rd):
                P.op("vector", lambda e: e.tensor_scalar(acc, u_ap_fn(1), pvt[:, wcol(1):wcol(1) + 1], None, ALU.mult), rd + [dpv], [dacc])
                yield 0
                P.op("vector", lambda e: e.scalar_tensor_tensor(acc, u_ap_fn(0), pvt[:, wcol(0):wcol(0) + 1], acc, ALU.mult, ALU.add), rd + [dpv], [dacc])
                yield 1
                P.op("vector", lambda e: e.scalar_tensor_tensor(acc, u_ap_fn(2), pvt[:, wcol(2):wcol(2) + 1], acc, ALU.mult, ALU.add), rd + [dpv], [dacc])
                yield 2

            for (s_, n_, kind, si) in seqs:
                T = min(512, n_)

                def conv_tile(t0, s_=s_, n_=n_, T=T):
                    a, da = load_halo(0, s_, n_, t0, T)
                    u, du = ur.next()
                    P.op("vector", lambda e, a=a, u=u: e.tensor_tensor(u[:, :, 0:T + 2], a[:, 2:4, 0:T + 2], a[:, 4:6, 0:T + 2], ALU.mult), [da], [du])
                    y, dy = yr.next()

                    def sc_gen(c, a=a, da=da, u=u, du=du, y=y, dy=dy):
                        acc, dacc = accr.next()
                        for op_ in conv3_ops(acc[:, 0:T], dacc, lambda k, u=u, c=c: u[:, c, k:k + T], lambda k, c=c: PVO["scw"] + k * 2 + c, [du]):
                            yield
                        P.op("vector", lambda e, a=a, y=y, c=c, acc=acc: e.tensor_tensor(y[:, c, 0:T], a[:, c, 1:T + 1], acc[:, 0:T], ALU.mult), [da, dacc], [dy])
                        yield
                    gens = [sc_gen(c) for c in range(2)]
                    a2, da2 = load_halo(12, s_, n_, t0, T)
                    ob, dob = obr.next()

                    def dn_gen(c, a=a2, da=da2, ob=ob, dob=dob):
                        acc, dacc = accr.next()
                        for op_ in conv3_ops(acc[:, 0:T], dacc, lambda k, a=a, c=c: a[:, c, k:k + T], lambda k, c=c: PVO["dnw"] + k * 6 + c, [da]):
                            yield
                        if c >= 4:
                            P.op("scalar", lambda e, ob=ob, c=c, acc=acc: e.activation(ob[:, c, 0:T], acc[:, 0:T], AF.Silu), [dacc], [dob])
                            yield
                            return
                        sl, dsl = sr.next()
                        P.op("scalar", lambda e, sl=sl, acc=acc: e.activation(sl[:, 0:T], acc[:, 0:T], AF.Silu), [dacc], [dsl])
                        yield
                        sq, dsq = sqr.next()
                        P.op("scalar", lambda e, sl=sl, sq=sq: e.activation(sq[:, 0:T], sl[:, 0:T], AF.Square), [dsl], [dsq])
                        yield
                        ps, dps = pcr.next()
                        mm_group(ps[:, 0:T], [(bd64b, sq[:, 0:T])], [dsq, dcb], [dps])
                        rs, drs = rr.next()
                        P.op("scalar", lambda e, rs=rs, ps=ps: e.activation(rs[:, 0:T], ps[:, 0:T], AF.Ln, bias=EPS, scale=1.0), [dps], [drs])
                        yield
                        P.op("scalar", lambda e, rs=rs: e.activation(rs[:, 0:T], rs[:, 0:T], AF.Exp, scale=-0.5), [drs], [drs])
                        yield
                        scl = 0.125 if c < 2 else 1.0
                        P.op("vector", lambda e, ob=ob, c=c, sl=sl, rs=rs, scl=scl: e.scalar_tensor_tensor(ob[:, c, 0:T], sl[:, 0:T], scl, rs[:, 0:T], ALU.mult, ALU.mult), [dsl, drs], [dob])
                        yield
                    gens += [dn_gen(c) for c in range(6)]

                    def finish(y=y, dy=dy, ob=ob, dob=dob):
                        P.dma("sync", YMIX[:, 0:2, s_ + t0:s_ + t0 + T], y[:, :, 0:T], reads=[dy], writes=[dYMIX])
                        P.dma("sync", DNQ.rearrange("c p t -> p c t")[:, :, s_ + t0:s_ + t0 + T], ob[:, :, 0:T], reads=[dob], writes=[dDNQ])
                    return gens, finish
                tl = list(range(0, n_, T))
                for i0 in range(0, len(tl), 2):
                    parts = [conv_tile(t0) for t0 in tl[i0:i0 + 2]]
                    run_lockstep([g for (gs, _) in parts for g in gs])
                    for (_, fin) in parts:
                        fin()
        if dbg in ("conv", "convA"):
            break

        for (s_, n_, kind, si) in seqs:
            if (dbg == "attP" and kind == "s") or (dbg in ("attS", "attSN") and kind == "p"):
                continue
            with P.phase():
                samp = kind == "s"
                T = min(512, n_)
                nblk = n_ // 128
                QN = P.sbuf("QN", [128, 4, n_], BF16); KN = P.sbuf("KN", [128, n_], BF16)
                VA = [P.sbuf("VA0", [128, nblk, 128], BF16), P.sbuf("VA1", [128, nblk, 128], BF16)]
                dQN, dKN, dVT = Dep(), Dep(), Dep()
                P.op("vector", lambda e: e.memset(VA[0][:, :, 64:128], 1.0), [], [dVT])
                P.op("vector", lambda e: e.memset(VA[1][:, :, 0:64], 1.0), [], [dVT])
                qkr = Ring(P, "aqk", [128, 5, 512], F32, 2)
                vr = Ring(P, "av", [128, 512], F32, 2)
                sqr = Ring(P, "asq", [128, 5, 512], BF16, 1)
                psA = Ring(P, "psA", [128, 512], F32, 8, "psum")
                rr = Ring(P, "ars", [128, 512], F32, 5)
                qnr = Ring(P, "aqn", [128, 512], F32, 5)
                qbr = Ring(P, "aqb", [128, 512], BF16, 5)
                t1r = Ring(P, "at1", [128, 512], F32, 5)
                t2r = Ring(P, "at2", [128, 512], F32, 5)
                cosr = Ring(P, "acos", [128, 512], F32, 2)
                sinr = Ring(P, "asin", [128, 512], F32, 2)
                stv = Ring(P, "astv", [128, 4, 128], F32, 2)
                for t0 in range(0, n_, T):
                    qk, dqk = qkr.next()
                    P.dma("sync", qk[:, :, 0:T], PROJ[:, 6:11, s_ + t0:s_ + t0 + T], reads=[dPROJ], writes=[dqk])
                    vv, dvv = vr.next()
                    P.dma("sync", vv[:, 0:T], PROJ[:, 11, s_ + t0:s_ + t0 + T], reads=[dPROJ], writes=[dvv])
                    if samp:
                        cs, dcs = cosr.next(); sn, dsn = sinr.next()
                        P.dma("sync", cs[:, 0:T], ropec[:, t0:t0 + T], writes=[dcs])
                        P.dma("sync", sn[:, 0:T], ropes[:, t0:t0 + T], writes=[dsn])
                    sq, dsq = sqr.next()
                    P.op("scalar", lambda e, sq=sq, qk=qk: e.activation(sq[:, :, 0:T], qk[:, :, 0:T], AF.Square), [dqk], [dsq])
                    def chunk_gen(c, sq=sq, dsq=dsq, qk=qk, dqk=dqk, t0=t0):
                            ps, dps = psA.next()
                            mm_group(ps[:, 0:T], [(bd64b, sq[:, c, 0:T])], [dsq, dcb], [dps])
                            yield
                            rs, drs = rr.next()
                            P.op("scalar", lambda e, rs=rs, ps=ps: e.activation(rs[:, 0:T], ps[:, 0:T], AF.Ln, bias=EPS, scale=1.0 / 64), [dps], [drs])
                            yield
                            P.op("scalar", lambda e, rs=rs: e.activation(rs[:, 0:T], rs[:, 0:T], AF.Exp, scale=-0.5), [drs], [drs])
                            yield
                            gcol = PVO["qg"] if c < 4 else PVO["kg"]
                            dst = QN[:, c, t0:t0 + T] if c < 4 else KN[:, t0:t0 + T]
                            ddst = dQN if c < 4 else dKN
                            if not samp and c < 4:
                                P.op("vector", lambda e, qk=qk, c=c, rs=rs, dst=dst, gcol=gcol: e.scalar_tensor_tensor(dst, qk[:, c, 0:T], pvt[:, gcol:gcol + 1], rs[:, 0:T], ALU.mult, ALU.mult), [dqk, drs, dpv], [ddst])
                                yield
                                return
                            qn, dqn = qnr.next()
                            P.op("vector", lambda e, qk=qk, c=c, rs=rs, qn=qn, gcol=gcol: e.scalar_tensor_tensor(qn[:, 0:T], qk[:, c, 0:T], pvt[:, gcol:gcol + 1], rs[:, 0:T], ALU.mult, ALU.mult), [dqk, drs, dpv], [dqn])
                            yield
                            if not samp:
                                P.op("scalar", lambda e, qn=qn, dst=dst: e.copy(dst, qn[:, 0:T]), [dqn], [ddst])
                                yield
                                ps2, dps2 = psA.next()

                                def fnk(e, ps2=ps2, qn=qn):
                                    ins = None
                                    for b in range(T // 128):
                                        ins = e.transpose(ps2[:, b * 128:(b + 1) * 128], qn[:, b * 128:(b + 1) * 128], identf)
                                    return ins
                                P.op("tensor", fnk, [dqn, dcst], [dps2])
                                yield
                                so, dso = stv.next()
                                evac(so[:, 0:T // 128, :], ps2[:, 0:T].rearrange("p (a b) -> p a b", b=128), [dps2], [dso])
                                yield
                                P.dma("sync", nk[si, l, t0:t0 + T, :].rearrange("(b p) f -> p b f", p=128), so[:, 0:T // 128, :], reads=[dso], writes=[dOUT])
                                yield
                                return
                            qb, dqb = qbr.next()
                            P.op("scalar", lambda e, qn=qn, qb=qb: e.copy(qb[:, 0:T], qn[:, 0:T]), [dqn], [dqb])
                            yield
                            ps2, dps2 = psA.next()
                            mm_group(ps2[:, 0:T], [(rpermb, qb[:, 0:T])], [dqb, dcb], [dps2])
                            yield
                            t1, dt1 = t1r.next(); t2, dt2 = t2r.next()
                            P.op("gpsimd", lambda e, t1=t1, qn=qn, cs=cs: e.tensor_tensor(t1[:, 0:T], qn[:, 0:T], cs[:, 0:T], ALU.mult), [dqn, dcs], [dt1])
                            yield
                            P.op("vector", lambda e, t2=t2, ps2=ps2, sn=sn: e.tensor_tensor(t2[:, 0:T], ps2[:, 0:T], sn[:, 0:T], ALU.mult), [dps2, dsn], [dt2])
                            yield
                            P.op("gpsimd", lambda e, t1=t1, t2=t2, dst=dst: e.tensor_tensor(dst, t1[:, 0:T], t2[:, 0:T], ALU.add), [dt1, dt2], [ddst])
                            yield

                    run_lockstep([chunk_gen(c) for c in range(5)])
                    ps3, dps3 = psA.next()

                    def fnv(e, ps3=ps3, vv=vv):
                        ins = None
                        for b in range(T // 128):
                            ins = e.transpose(ps3[:, b * 128:(b + 1) * 128], vv[:, b * 128:(b + 1) * 128], identf)
                        return ins
                    P.op("tensor", fnv, [dvv, dcst], [dps3])
                    b0 = t0 // 128
                    P.op("vector", lambda e, ps3=ps3, b0=b0: e.tensor_copy(VA[0][:, b0:b0 + T // 128, 0:64], ps3[:, 0:T].rearrange("p (a b) -> p a b", b=128)[:, :, 0:64]), [dps3], [dVT])
                    P.op("vector", lambda e, ps3=ps3, b0=b0: e.tensor_copy(VA[1][:, b0:b0 + T // 128, 64:128], ps3[:, 0:T].rearrange("p (a b) -> p a b", b=128)[:, :, 64:128]), [dps3], [dVT])
                    if not samp:
                        so, dso = stv.next()
                        P.op("scalar", lambda e, so=so, ps3=ps3: e.copy(so[:, 0:T // 128, :], ps3[:, 0:T].rearrange("p (a b) -> p a b", b=128)), [dps3], [dso])
                        P.dma("sync", nv[si, l, t0:t0 + T, :].rearrange("(b p) f -> p b f", p=128), so[:, 0:T // 128, :], reads=[dso], writes=[dOUT])
                if samp:
                    KCx = P.sbuf("KCx", [128, 512], BF16); dctx = Dep()
                    VCA = [P.sbuf("VCA0", [128, 4, 128], BF16), P.sbuf("VCA1", [128, 4, 128], BF16)]
                    P.op("vector", lambda e: e.memset(VCA[0][:, :, 64:128], 1.0), [], [dctx])
                    P.op("vector", lambda e: e.memset(VCA[1][:, :, 0:64], 1.0), [], [dctx])
                    ckt = P.sbuf("ckt", [128, 4, 128], F32); cvt = P.sbuf("cvt", [128, 4, 128], F32); dck = Dep()
                    P.dma("sync", ckt[:], ck[l].rearrange("(b p) f -> p b f", p=128), writes=[dck])
                    P.dma("sync", cvt[:], cv[l].rearrange("(b p) f -> p b f", p=128), writes=[dck])
                    ps4, dps4 = psA.next()

                    def fnc(e, ps4=ps4):
                        ins = None
                        for b in range(4):
                            ins = e.transpose(ps4[:, b * 128:(b + 1) * 128], ckt[:, b, :], identf)
                        return ins
                    P.op("tensor", fnc, [dck, dcst], [dps4])
                    P.op("vector", lambda e, ps4=ps4: e.tensor_copy(KCx[:], ps4[:]), [dps4], [dctx])
                    P.op("vector", lambda e: e.tensor_copy(VCA[0][:, :, 0:64], cvt[:, :, 0:64]), [dck], [dctx])
                    P.op("vector", lambda e: e.tensor_copy(VCA[1][:, :, 64:128], cvt[:, :, 64:128]), [dck], [dctx])
                pss = Ring.of(psA.t[0:4], psA.d[0:4])
                pso_ = Ring.of(psA.t[4:6], psA.d[4:6])
                psd_ = Ring.of(psA.t[6:8], psA.d[6:8])
                ptr = Ring(P, "apt", [128, 512], BF16, 6)
                denr = Ring(P, "aden", [128, 512], F32, 2); rdenr = Ring(P, "arden", [128, 512], F32, 2)
                ybr = Ring(P, "ayb", [128, 4, 128], BF16, 2)
                for i in range(nblk):
                    if dbg in ("attN", "attSN"):
                        break
                    pacc = [pso_.next(), psd_.next()]
                    for kvh in range(2):
                        rows = slice(kvh * 64, (kvh + 1) * 64)
                        po, dpo = pacc[kvh]
                        kt = []
                        if samp:
                            kt += [("c", j, None) for j in range(4)]
                            if i > 0:
                                kt.append(("l", i - 1, mwprev))
                            kt.append(("l", i, None))
                            if i < nblk - 1:
                                kt.append(("l", i + 1, mwnext))
                        else:
                            kt += [("l", j, None) for j in range(nblk)]
                        def issue_s(ki):
                            src, j, msk = kt[ki]
                            ks = KCx[rows, j * 128:(j + 1) * 128] if src == "c" else KN[rows, j * 128:(j + 1) * 128]
                            kd_ = [dctx] if src == "c" else [dKN, dVT]
                            ps, dps = pss.next()
                            mm_group(ps[:], [(ks, QN[rows, :, i * 128:(i + 1) * 128])], kd_ + [dQN], [dps])
                            return ps, dps
                        ahead = [issue_s(k_) for k_ in range(min(3, len(kt)))]
                        for ki, (src, j, msk) in enumerate(kt):
                            vsrc = VCA[kvh][:, j, :] if src == "c" else VA[kvh][:, j, :]
                            kd_ = [dctx] if src == "c" else [dKN, dVT]
                            ps, dps = ahead.pop(0)
                            if ki + 3 < len(kt):
                                ahead.append(issue_s(ki + 3))
                            pt, dpt = ptr.next()
                            P.op("scalar", lambda e, pt=pt, ps=ps: e.activation(pt[:], ps[:], AF.Exp, scale=0.125), [dps], [dpt])
                            if msk is not None:
                                P.op("vector", lambda e, pt=pt, msk=msk: e.tensor_tensor(pt[:].rearrange("p (a b) -> p a b", a=4), pt[:].rearrange("p (a b) -> p a b", a=4),
                                                                                       msk.unsqueeze(1).broadcast_to([128, 4, 128]), ALU.mult), [dpt, dcb], [dpt])
                            first, last = ki == 0, ki == len(kt) - 1

                            P.op("tensor", lambda e, po=po, vsrc=vsrc, pt=pt, first=first, last=last: e.matmul(po[:], vsrc, pt[:], start=first, stop=last), kd_ + [dpt, dcb], [dpo])
                    den, dden = denr.next()
                    for kvh in range(2):
                        po, dpo = pacc[kvh]
                        orow = slice(kvh * 64, (kvh + 1) * 64)
                        drow = slice((1 - kvh) * 64, (2 - kvh) * 64)
                        P.op("vector", lambda e, den=den, po=po, drow=drow: e.tensor_tensor(den[drow, :].rearrange("p (a b) -> p a b", a=4), po[drow, :].rearrange("p (a b) -> p a b", a=4),
                                                                                            sinkexp[drow, 4:8].unsqueeze(2).broadcast_to([64, 4, 128]), ALU.add), [dpo, dmod], [dden])
                    rden, drden = rdenr.next()
                    for kvh in range(2):
                        orow = slice(kvh * 64, (kvh + 1) * 64)
                        drow = slice((1 - kvh) * 64, (2 - kvh) * 64)
                        P.op("scalar", lambda e, rden=rden, den=den, orow=orow, drow=drow: e.activation(rden[orow, :], den[drow, :], AF.Ln), [dden], [drden])
                        P.op("scalar", lambda e, rden=rden, orow=orow: e.activation(rden[orow, :], rden[orow, :], AF.Exp, scale=-1.0), [drden], [drden])
                    yb, dyb = ybr.next()
                    for kvh in range(2):
                        po, dpo = pacc[kvh]
                        orow = slice(kvh * 64, (kvh + 1) * 64)
                        P.op("vector", lambda e, yb=yb, po=po, rden=rden, orow=orow: e.tensor_tensor(yb[orow, :, :].rearrange("p a b -> p (a b)"), po[orow, :], rden[orow, :], ALU.mult), [dpo, drden], [dyb])
                    P.dma("sync", YMIX[:, 2:6, s_ + i * 128:s_ + (i + 1) * 128], yb[:], reads=[dyb], writes=[dYMIX])
        if dbg and dbg.startswith("att"):
            break

        DNQh = DNQ.rearrange("c (h d) t -> d (c h) t", d=64)
        for (s_, n_, kind, si) in seqs:
            with P.phase():
                samp = kind == "s"
                NCH = n_ // 64
                V_ = lambda fn, r, w: P.op("vector", fn, r, w)
                A_ = lambda fn, r, w: P.op("scalar", fn, r, w)
                G_ = lambda fn, r, w: P.op("vector", fn, r, w)
                psr = Ring(P, "dps", [128, 512], F32, 8, "psum")

                def bcf(ap, n):
                    return ap.unsqueeze(2).broadcast_to([ap.shape[0], ap.shape[1], n])
                NP_ = NCH // 2
                HS = (slice(0, 64), slice(64, 128))
                abF = P.sbuf("abF", [128, NP_, 16], F32); abt = P.sbuf("abt", [128, NP_, 16], F32); dabt = Dep(); dabF = Dep()
                ABs = AB[s_:s_ + n_].rearrange("(g two t) k -> two t g k", two=2, t=64)
                for h in range(2):
                    P.dma("sync", abF[HS[h], :, :], ABs[h], reads=[dAB], writes=[dabF])
                for h in range(2):
                    P.op("vector", lambda e, h=h: e.tensor_copy(abt[HS[h], :, 0:4], abF[HS[h], :, 0:4]), [dabF], [dabt])
                    P.op("vector", lambda e, h=h: e.tensor_copy(abt[HS[h], :, 8:12], abF[HS[h], :, 8:12]), [dabF], [dabt])
                    for g in range(NP_):
                        P.op("scalar", lambda e, h=h, g=g: e.copy(abt[HS[h], g, 4:8], abF[HS[1 - h], NP_ - 1 - g, 4:8]), [dabF], [dabt])
                        P.op("scalar", lambda e, h=h, g=g: e.copy(abt[HS[h], g, 12:16], abF[HS[1 - h], NP_ - 1 - g, 12:16]), [dabF], [dabt])
                sc = {}
                for nm in ("g", "beta", "negg", "gc", "eg", "ed", "egt", "beg", "tmpa"):
                    sc[nm] = P.sbuf("dn_" + nm, [128, NP_, 8], F32)
                egtX = P.sbuf("dn_egtX", [64, NP_, 8], F32)
                dsc = Dep()
                g3 = sc["g"]
                V_(lambda e: e.tensor_tensor(sc["tmpa"][:], abt[:, :, 0:8], bct[:, 8:16].unsqueeze(1).broadcast_to([128, NP_, 8]), ALU.add), [dabt, dpv], [dsc])
                A_(lambda e: e.activation(sc["tmpa"][:], sc["tmpa"][:], AF.Exp), [dsc], [dsc])
                A_(lambda e: e.activation(sc["tmpa"][:], sc["tmpa"][:], AF.Ln, bias=1.0), [dsc], [dsc])
                V_(lambda e: e.tensor_tensor(g3[:], sc["tmpa"][:], negA2[:].unsqueeze(1).broadcast_to([128, NP_, 8]), ALU.mult), [dsc, dmod], [dsc])
                V_(lambda e: e.tensor_scalar(sc["negg"][:], g3[:], -1.0, None, ALU.mult), [dsc], [dsc])
                A_(lambda e: e.activation(sc["beta"][:], abt[:, :, 8:16], AF.Exp, scale=-1.0), [dabt], [dsc])
                V_(lambda e: e.tensor_scalar(sc["beta"][:], sc["beta"][:], 1.0, None, ALU.add), [dsc], [dsc])
                V_(lambda e: e.reciprocal(sc["beta"][:], sc["beta"][:]), [dsc], [dsc])
                pg, dpg = psr.next()
                pgF = pg[:, 0:NP_ * 4]; pgB = pg[:, NP_ * 4:NP_ * 8]

                def fng(e):
                    e.matmul(pgF, UCF2, g3[:, :, 0:4], start=True, stop=True)
                    return e.matmul(pgB, UCB2, g3[:, :, 4:8], start=True, stop=True)
                P.op("tensor", fng, [dsc, dcst], [dpg])
                for (pgX, lo) in ((pgF, 0), (pgB, 4)):
                    V_(lambda e, pgX=pgX, lo=lo: e.tensor_copy(sc["gc"][:, :, lo:lo + 4], pgX.rearrange("p (c k) -> p c k", k=4)), [dpg], [dsc])
                    A_(lambda e, pgX=pgX, lo=lo: e.activation(sc["eg"][:, :, lo:lo + 4], pgX.rearrange("p (c k) -> p c k", k=4), AF.Exp), [dpg], [dsc])
                pt_, dpt_ = psr.next()
                pt2 = pt_[:, 0:NP_ * 8]
                ptv = pt2.rearrange("p (c k) -> p c k", k=8)
                P.op("tensor", lambda e: e.matmul(pt2, bd64f, g3[:].rearrange("p c k -> p (c k)"), start=True, stop=True), [dsc, dcst], [dpt_])
                A_(lambda e: e.activation(sc["egt"][:], ptv, AF.Exp), [dpt_], [dsc])
                V_(lambda e: e.tensor_tensor(sc["ed"][:], ptv, sc["gc"][:], ALU.subtract), [dpt_, dsc], [dsc])
                A_(lambda e: e.activation(sc["ed"][:], sc["ed"][:], AF.Exp), [dsc], [dsc])
                V_(lambda e: e.tensor_tensor(sc["beg"][:], sc["beta"][:], sc["eg"][:], ALU.mult), [dsc], [dsc])
                A_(lambda e: e.copy(egtX[:], sc["egt"][64:128, :, :]), [dsc], [dsc])
                Sf = P.sbuf("Sf", [64, 8, 64], F32); Sb = P.sbuf("Sb", [128, 8, 64], BF16); dS = Dep(); dSb = Dep()
                if samp:
                    P.dma("sync", Sf[:], s0in[l].rearrange("k a b -> a k b"), writes=[dS])
                else:
                    V_(lambda e: e.memset(Sf[:], 0.0), [], [dS])
                A_(lambda e: e.copy(Sb[0:64], Sf[:]), [dS], [dSb])
                A_(lambda e: e.copy(Sb[64:128], Sf[:]), [dS], [dSb])

                def t64(name, dt, n=3, shape=(128, 8, 64)):
                    return Ring(P, name, list(shape), dt, n)
                qfr = t64("qf", F32, 3, (128, 2, 12, 64)); qbr_ = t64("qb", BF16, 3, (128, 2, 12, 64))
                NGr = t64("NG", F32, 2); Dr = t64("Dd", F32, 2); Er = t64("E", F32, 2); ERr = t64("ER", F32)
                Eir = t64("Ei", F32, 2); Esr = t64("Es", F32, 2)
                Xr = t64("X", F32, 3); Xbr = t64("Xb", BF16, 9); XTbr = t64("XTb", BF16, 9); TTbr = t64("TTb", BF16, 6); Tbr = t64("Tb", BF16); Rtr = t64("Rt", F32, 2); Rbr = t64("Rb", BF16); AINr = t64("AIN", BF16); AINTr = t64("AINT", BF16, 6)
                TTfr = t64("TTf", F32); TTb2r = t64("TTb2", BF16); TTber = t64("TTbe", BF16)
                KVr = t64("KV", BF16, 6, (128, 16, 64)); QEr = t64("QE", BF16, 6); Ur = t64("U", F32, 6); NWr = t64("NW", BF16, 6)
                VNfr = t64("VNf", F32, 2); VNr = t64("VN", BF16, 2); VNDr = t64("VND", BF16, 2); OBr = t64("OB", F32, 2); STr = t64("ST", F32, 2, (64, 8, 64))

                def per_k(out_fn, l_fn, r_fn, reads, writes, transpose=False, halves_=(0, 1)):
                    def fn(e):
                        ins = None
                        for h in halves_:
                            for k in range(8):
                                if transpose:
                                    ins = e.transpose(out_fn(h, k), l_fn(h, k), identb[HS[h], HS[h]])
                                else:
                                    ins = e.matmul(out_fn(h, k), l_fn(h, k), r_fn(h, k), start=True, stop=True)
                        return ins
                    P.op("tensor", fn, reads, writes)

                def v3(t):
                    return t[:, :].rearrange("p (k j) -> p k j", k=8)

                def vb(t):
                    return t[:, :].bitcast(BF16).rearrange("p (k j) -> p k j", j=64)
                hk = lambda t: (lambda h, k, t=t: t[HS[h], k, :])
                W_ = 3 if NP_ >= 3 else NP_

                def pre_gen(g, cx):
                        scb = lambda nm: bcf(sc[nm][:, g, :], 64)
                        chunk = lambda h, dr: (2 * g + h) if dr == 0 else (NCH - 1 - 2 * g - h)
                        qf, dqf = qfr.next()
                        for h in range(2):
                            for dr in range(2):
                                c = chunk(h, dr)
                                P.dma("sync", qf[HS[h], dr, :, :], DNQh[:, :, s_ + c * 64:s_ + c * 64 + 64], reads=[dDNQ], writes=[dqf])
                        qb, dqb = qbr_.next()
                        A_(lambda e, qb=qb, qf=qf: e.copy(qb[:], qf[:]), [dqf], [dqb])
                        Qk = lambda h, k, qb=qb: qb[HS[h], k // 4, k % 4, :]
                        Kk = lambda h, k, qb=qb: qb[HS[h], k // 4, 4 + k % 4, :]
                        Kf = lambda h, k, qf=qf: qf[HS[h], k // 4, 4 + k % 4, :]
                        pa, dpa = psr.next(); pqk, dpqk = psr.next(); pgr, dpgr = psr.next()
                        per_k(hk(v3(pa)), Kf, Kf, [dqf], [dpa])
                        per_k(hk(v3(pqk)), Qk, Kk, [dqb], [dpqk])
                        NG, dNG = NGr.next()
                        G_(lambda e, NG=NG: e.tensor_tensor(NG[:], U8, scb("negg"), ALU.mult), [dsc, dcst], [dNG])
                        P.op("tensor", lambda e, pgr=pgr, NG=NG: e.matmul(pgr[:, :], bd64f, NG[:].rearrange("p k j -> p (k j)"), start=True, stop=True), [dNG, dcst], [dpgr])
                        Dd, dD = Dr.next()
                        V_(lambda e, Dd=Dd, pgr=pgr: e.tensor_tensor(Dd[:], v3(pgr), scb("gc"), ALU.add), [dpgr, dsc], [dD])
                        V_(lambda e, Dd=Dd: e.tensor_scalar(Dd[:], Dd[:], 0.0, None, ALU.min), [dD], [dD])
                        E, dE = Er.next()
                        A_(lambda e, E=E, Dd=Dd: e.activation(E[:], Dd[:], AF.Exp), [dD], [dE])
                        ER, dER = ERr.next()
                        A_(lambda e, ER=ER, pgr=pgr: e.activation(ER[:], v3(pgr), AF.Exp, scale=-1.0), [dpgr], [dER])
                        Ei, dEi = Eir.next(); Es, dEs = Esr.next()
                        V_(lambda e, Ei=Ei, E=E: e.tensor_tensor(Ei[:].rearrange("p k j -> p (k j)"), E[:].rearrange("p k j -> p (k j)"), MI8, ALU.mult), [dE, dcst], [dEi])
                        G_(lambda e, Es=Es, E=E: e.tensor_tensor(Es[:].rearrange("p k j -> p (k j)"), E[:].rearrange("p k j -> p (k j)"), MS8, ALU.mult), [dE, dcst], [dEs])
                        G_(lambda e, Es=Es: e.tensor_tensor(Es[:], Es[:], scb("beta"), ALU.mult), [dEs, dsc], [dEs])
                        X, dX = Xr.next()
                        V_(lambda e, X=X, pa=pa, Es=Es: e.scalar_tensor_tensor(X[:].rearrange("p k j -> p (k j)"), pa[:, :], -1.0, Es[:].rearrange("p k j -> p (k j)"), ALU.mult, ALU.mult), [dpa, dEs], [dX])
                        AIN, dAIN = AINr.next()
                        V_(lambda e, AIN=AIN, pqk=pqk, Ei=Ei: e.tensor_tensor(AIN[:].rearrange("p k j -> p (k j)"), pqk[:, :], Ei[:].rearrange("p k j -> p (k j)"), ALU.mult), [dpqk, dEi], [dAIN])
                        yield
                        Xb, dXb = Xbr.next()
                        A_(lambda e, Xb=Xb, X=X: e.copy(Xb[:], X[:]), [dX], [dXb])
                        px, dpx = psr.next()
                        pxb = vb(px)
                        per_k(lambda h, k: pxb[HS[h], k, :], hk(Xb), None, [dXb, dcb], [dpx], transpose=True)
                        per_k(lambda h, k: pxb[HS[h], 8 + k, :], hk(AIN), None, [dAIN, dcb], [dpx], transpose=True)
                        XTb, dXTb = XTbr.next(); AINT, dAINT = AINTr.next()
                        V_(lambda e, XTb=XTb, pxb=pxb: e.tensor_copy(XTb[:], pxb[:, 0:8, :]), [dpx], [dXTb])
                        A_(lambda e, AINT=AINT, pxb=pxb: e.copy(AINT[:], pxb[:, 8:16, :]), [dpx], [dAINT])
                        yield
                        TTf, dTTf = TTfr.next(); TTb, dTTb = TTbr.next()
                        V_(lambda e, TTf=TTf, XTb=XTb: e.tensor_tensor(TTf[:].rearrange("p k j -> p (k j)"), XTb[:].rearrange("p k j -> p (k j)"), IDB8, ALU.add), [dXTb, dcst], [dTTf])
                        A_(lambda e, TTb=TTb, TTf=TTf: e.copy(TTb[:], TTf[:]), [dTTf], [dTTb])
                        Xc, dXc, XTc, dXTc = Xb, dXb, XTb, dXTb
                        for lev in range(4):
                            last = lev == 3
                            p2, dp2 = psr.next()
                            per_k(hk(v3(p2)), hk(XTc), hk(Xc), [dXc, dXTc], [dp2])
                            Xn, dXn = Xbr.next()
                            A_(lambda e, Xn=Xn, p2=p2: e.copy(Xn[:], v3(p2)), [dp2], [dXn])
                            if not last:
                                p3, dp3 = psr.next()
                                per_k(hk(v3(p3)), hk(Xc), hk(XTc), [dXc, dXTc], [dp3])
                                XTn, dXTn = XTbr.next()
                                A_(lambda e, XTn=XTn, p3=p3: e.copy(XTn[:], v3(p3)), [dp3], [dXTn])
                            p4, dp4 = psr.next()
                            per_k(hk(v3(p4)), hk(Xn), hk(TTb), [dXn, dTTb], [dp4])
                            V_(lambda e, TTf=TTf, p4=p4: e.tensor_tensor(TTf[:], TTf[:], v3(p4), ALU.add), [dp4, dTTf], [dTTf])
                            TTb, dTTb = TTbr.next()
                            A_(lambda e, TTb=TTb, TTf=TTf: e.copy(TTb[:], TTf[:]), [dTTf], [dTTb])
                            yield
                            Xc, dXc = Xn, dXn
                            if not last:
                                XTc, dXTc = XTn, dXTn
                        ptt, dptt = psr.next()
                        pttb = vb(ptt)
                        per_k(lambda h, k: pttb[HS[h], k, :], hk(TTb), None, [dTTb, dcb], [dptt], transpose=True)
                        Tb, dTb = Tbr.next()
                        A_(lambda e, Tb=Tb, pttb=pttb: e.copy(Tb[:], pttb[:, 0:8, :]), [dptt], [dTb])
                        Rt, dRt = Rtr.next()
                        V_(lambda e, Rt=Rt, TTf=TTf: e.scalar_tensor_tensor(Rt[:].rearrange("p k j -> p (k j)"), TTf[:].rearrange("p k j -> p (k j)"), -1.0, IDB8, ALU.mult, ALU.add), [dTTf, dcst], [dRt])
                        pr_, dpr_ = psr.next()
                        per_k(hk(v3(pr_)), hk(X), hk(TTf), [dX, dTTf], [dpr_])
                        Rb, dRb = Rbr.next()
                        V_(lambda e, Rb=Rb, Rt=Rt, pr_=pr_: e.tensor_tensor(Rb[:], Rt[:], v3(pr_), ALU.add), [dRt, dpr_], [dRb])
                        yield
                        pc_, dpc_ = psr.next()
                        per_k(hk(v3(pc_)), hk(Tb), hk(Rb), [dTb, dRb], [dpc_])
                        V_(lambda e, TTf=TTf, pc_=pc_: e.tensor_tensor(TTf[:], TTf[:], v3(pc_), ALU.add), [dpc_, dTTf], [dTTf])
                        yield
                        TTb2, dTTb2 = TTb2r.next(); TTbe, dTTbe = TTber.next()
                        G_(lambda e, TTb2=TTb2, TTf=TTf: e.tensor_tensor(TTb2[:], TTf[:], scb("beta"), ALU.mult), [dTTf, dsc], [dTTb2])
                        G_(lambda e, TTbe=TTbe, TTf=TTf: e.tensor_tensor(TTbe[:], TTf[:], scb("beg"), ALU.mult), [dTTf, dsc], [dTTbe])
                        KV, dKV = KVr.next()
                        pkv, dpkv = psr.next()
                        pkvb = vb(pkv)
                        per_k(lambda h, k: pkvb[HS[h], k, :], Kk, None, [dqb, dcb], [dpkv], transpose=True)
                        per_k(lambda h, k: pkvb[HS[h], 8 + k, :], lambda h, k, qb=qb: qb[HS[h], k // 4, 8 + k % 4, :], None, [dqb, dcb], [dpkv], transpose=True)
                        A_(lambda e, KV=KV, pkvb=pkvb: e.copy(KV[:], pkvb), [dpkv], [dKV])
                        QE, dQE = QEr.next()
                        G_(lambda e, QE=QE, qf=qf, ER=ER: e.tensor_tensor(QE[:].rearrange("p (a b) j -> p a b j", a=2), qf[:, :, 0:4, :], ER[:].rearrange("p (a b) j -> p a b j", a=2), ALU.mult), [dqf, dER], [dQE])
                        pu, dpu = psr.next()
                        per_k(hk(v3(pu)), hk(TTb2), lambda h, k, KV=KV: KV[HS[h], 8 + k, :], [dTTb2, dKV], [dpu])
                        U, dU = Ur.next()
                        A_(lambda e, U=U, pu=pu: e.copy(U[:], v3(pu)), [dpu], [dU])
                        yield
                        pw, dpw = psr.next()
                        per_k(hk(v3(pw)), hk(KV), hk(TTbe), [dTTbe, dKV], [dpw])
                        NW, dNW = NWr.next()
                        V_(lambda e, NW=NW, pw=pw: e.tensor_scalar(NW[:], v3(pw), -1.0, None, ALU.mult), [dpw], [dNW])
                        cx.update(dict(NW=NW, dNW=dNW, U=U, dU=dU, QE=QE, dQE=dQE, AINT=AINT, dAINT=dAINT, KV=KV, dKV=dKV))
                        yield

                def scan_step(g, cx, h):
                    NW = cx['NW']; dNW = cx['dNW']; U = cx['U']; dU = cx['dU']; QE = cx['QE']; dQE = cx['dQE']
                    AINT = cx['AINT']; dAINT = cx['dAINT']; KV = cx['KV']; dKV = cx['dKV']
                    hs = HS[h]
                    one = (h,)
                    pws, dpws = psr.next()
                    per_k(hk(v3(pws)), hk(NW), hk(Sb), [dNW, dSb], [dpws], halves_=one)
                    VNf, dVNf = VNfr.next(); VN, dVN = VNr.next(); VND, dVND = VNDr.next()
                    V_(lambda e, VNf=VNf, U=U, pws=pws: e.tensor_tensor(VNf[hs], U[hs], v3(pws)[hs], ALU.add), [dU, dpws], [dVNf])
                    yield
                    A_(lambda e, VN=VN, VNf=VNf: e.copy(VN[hs], VNf[hs]), [dVNf], [dVN])
                    yield
                    G_(lambda e, VND=VND, VNf=VNf: e.tensor_tensor(VND[hs], VNf[hs], bcf(sc["ed"][hs, g, :], 64), ALU.mult), [dVNf, dsc], [dVND])
                    yield
                    po_, dpo_ = psr.next()

                    def fno(e, po_=po_, QE=QE, AINT=AINT, VN=VN):
                        ins = None
                        for k in range(8):
                            e.matmul(v3(po_)[hs, k, :], QE[hs, k, :], Sb[hs, k, :], start=True, stop=False)
                            ins = e.matmul(v3(po_)[hs, k, :], AINT[hs, k, :], VN[hs, k, :], start=False, stop=True)
                        return ins
                    P.op("tensor", fno, [dQE, dSb, dAINT, dVN], [dpo_])
                    OB, dOB = OBr.next()
                    A_(lambda e, OB=OB, po_=po_: e.copy(OB[hs], v3(po_)[hs]), [dpo_], [dOB])
                    yield
                    for dr in range(2):
                        c = (2 * g + h) if dr == 0 else (NCH - 1 - 2 * g - h)
                        t_ = s_ + c * 64
                        P.dma("sync", OD[dr, t_:t_ + 64, :].rearrange("t (h v) -> t h v", h=4), OB[hs, dr * 4:dr * 4 + 4, :], reads=[dOB], writes=[dOD])
                    ST, dST = STr.next()
                    egs = sc["egt"][0:64, g, :] if h == 0 else egtX[:, g, :]
                    G_(lambda e, ST=ST: e.tensor_tensor(ST[:], Sf[:], bcf(egs, 64), ALU.mult), [dS, dsc], [dST])
                    yield
                    pS, dpS = psr.next()

                    def fns(e, pS=pS, KV=KV, VND=VND):
                        ins = None
                        for k in range(8):
                            ins = e.matmul(pS[0:64, k * 64:(k + 1) * 64], KV[hs, k, :], VND[hs, k, :], start=True, stop=True)
                        return ins
                    P.op("tensor", fns, [dKV, dVND], [dpS])
                    V_(lambda e, ST=ST, pS=pS: e.tensor_tensor(Sf[:], ST[:], pS[0:64, :].rearrange("p (k j) -> p k j", k=8), ALU.add), [dST, dpS], [dS])
                    yield
                    A_(lambda e: e.copy(Sb[0:64], Sf[:]), [dS], [dSb])
                    A_(lambda e: e.copy(Sb[64:128], Sf[:]), [dS], [dSb])
                    yield

                def scan_group(grp_, cxs_):
                    for g2 in grp_:
                        for h in range(2):
                            yield from scan_step(g2, cxs_[g2], h)
                prev = None
                for g0 in range(0, NP_, W_):
                    grp = list(range(g0, min(g0 + W_, NP_)))
                    cxs = {g2: {} for g2 in grp}
                    gens = [pre_gen(g2, cxs[g2]) for g2 in grp]
                    sg = scan_group(*prev) if prev is not None else None
                    alive = list(gens)
                    while alive:
                        for g_ in list(alive):
                            try:
                                next(g_)
                            except StopIteration:
                                alive.remove(g_)
                            if sg is not None:
                                try:
                                    next(sg)
                                except StopIteration:
                                    sg = None
                    if sg is not None:
                        run_lockstep([sg])
                    prev = (grp, cxs)
                run_lockstep([scan_group(*prev)])
                if not samp:
                    P.dma("sync", nst[si, l].rearrange("k a b -> a k b"), Sf[:], reads=[dS], writes=[dOUT])
        if dbg == "dn":
            break

        with P.phase():
            ofr = Ring(P, "cof", [128, 4, 256], F32, 2); obr2 = Ring(P, "cob2", [128, 4, 256], F32, 2)
            osr = Ring(P, "cos_", [128, 4, 256], F32, 2); sqr2 = Ring(P, "csq2", [128, 4, 256], F32, 2)
            ssr = Ring(P, "css", [128, 16], F32, 2); zr = Ring(P, "cz", [128, 2, 512], F32, 2)
            pcr2 = Ring(P, "pc2", [128, 512], F32, 4, "psum"); ydr = Ring(P, "cyd", [128, 2, 512], BF16, 2)
            for ti in range(NT // 512):
                t0 = ti * 512
                of, dof = ofr.next(); ob2, dob2 = obr2.next()
                P.dma("sync", of[:], OD[0, t0:t0 + 512, :].rearrange("(s p) f -> p s f", p=128), reads=[dOD], writes=[dof])
                P.dma("sync", ob2[:], OD[1, t0:t0 + 512, :].rearrange("(s p) f -> p s f", p=128), reads=[dOD], writes=[dob2])
                zt, dzt = zr.next()
                P.dma("sync", zt[:], PROJ[:, 18:20, t0:t0 + 512], reads=[dPROJ], writes=[dzt])
                o, do_ = osr.next()
                P.op("gpsimd", lambda e, o=o, of=of, ob2=ob2: e.tensor_tensor(o[:], of[:], ob2[:], ALU.add), [dof, dob2], [do_])
                sq, dsq = sqr2.next()
                P.op("scalar", lambda e, sq=sq, o=o: e.activation(sq[:], o[:], AF.Square), [do_], [dsq])
                ss, dss = ssr.next()
                P.op("vector", lambda e, ss=ss, sq=sq: e.tensor_reduce(ss[:], sq[:].rearrange("p s (h v) -> p (s h) v", h=4), AX.X, ALU.add), [dsq], [dss])
                P.op("scalar", lambda e, ss=ss: e.activation(ss[:], ss[:], AF.Sqrt, bias=EPS, scale=1.0 / 64), [dss], [dss])
                P.op("vector", lambda e, ss=ss: e.reciprocal(ss[:], ss[:]), [dss], [dss])
                P.op("vector", lambda e, o=o, ss=ss: e.tensor_tensor(o[:].rearrange("p s (h v) -> p (s h) v", h=4), o[:].rearrange("p s (h v) -> p (s h) v", h=4),
                                                                    ss[:].unsqueeze(2).broadcast_to([128, 16, 64]), ALU.mult), [do_, dss], [do_])
                P.op("gpsimd", lambda e, o=o: e.tensor_tensor(o[:].rearrange("p s (h v) -> p (s h) v", h=4), o[:].rearrange("p s (h v) -> p (s h) v", h=4),
                                                              bct[:, 16:80].unsqueeze(1).broadcast_to([128, 16, 64]), ALU.mult), [do_, dpv], [do_])
                P.op("scalar", lambda e, zt=zt: e.activation(zt[:], zt[:], AF.Silu), [dzt], [dzt])
                yd, dyd = ydr.next()
                for c in range(2):
                    ps, dps = pcr2.next()

                    def fnt(e, ps=ps, o=o, c=c):
                        ins = None
                        for sb in range(4):
                            ins = e.transpose(ps[:, sb * 128:(sb + 1) * 128], o[:, sb, c * 128:(c + 1) * 128], identf)
                        return ins
                    P.op("tensor", fnt, [do_, dcst], [dps])
                    P.op("vector", lambda e, yd=yd, ps=ps, zt=zt, c=c: e.tensor_tensor(yd[:, c, :], ps[:], zt[:, c, :], ALU.mult), [dps, dzt], [dyd])
                P.dma("sync", YMIX[:, 6:8, t0:t0 + 512], yd[:], reads=[dyd], writes=[dYMIX])
        if dbg == "mix":
            break

        HF = DFF // 2
        NJ = HF // 128
        for hf in range(2):
            with P.phase():
                wgs = P.sbuf("wgs", [128, KC, HF], BF16); wus = P.sbuf("wus", [128, KC, HF], BF16)
                wds = P.sbuf("wds", [128, NJ, D], BF16); dwf = [Dep() for _ in range(4)]
                P.dma("gpsimd", wgs[:], wg[l].rearrange("(kc p) n -> p kc n", p=128)[:, :, hf * HF:(hf + 1) * HF], writes=[dwf[0]])
                P.dma("gpsimd", wus[:], wu[l].rearrange("(kc p) n -> p kc n", p=128)[:, :, hf * HF:(hf + 1) * HF], writes=[dwf[1]])
                P.dma("gpsimd", wds[:], wd[l, hf * HF:(hf + 1) * HF, :].rearrange("(j p) n -> p j n", p=128), writes=[dwf[2]])
                if hf == 0:
                    wos = P.sbuf("wos", [128, KC, D], BF16)
                    P.dma("gpsimd", wos[:], wout[l].rearrange("(kc p) n -> p kc n", p=128), writes=[dwf[3]])
                    ymr = Ring(P, "ym", [128, KC, 512], BF16, 2)
                    rings = {"sq": Ring(P, "sq", [128, KC, 512], BF16, 1), "psn": Ring(P, "psn", [128, 512], F32, 1, "psum"),
                             "rs": Ring(P, "rs", [128, 512], F32, 2), "tmp": Ring(P, "tmp", [128, 512], F32, 3)}
                    psw = Ring(P, "psw", [128, 512], F32, 2, "psum")
                xr = Ring(P, "xt", [128, KC, 512], F32, 2)
                hr = Ring(P, "h2", [128, KC, 512], BF16, 2)
                actr = Ring(P, "act", [128, NJ, 512], BF16, 1)
                sgr = Ring(P, "sg", [128, 512], F32, 2)
                psf = Ring(P, "psf", [128, 512], F32, 5, "psum")
                direct_out = (hf == 1 and l == depth - 1 and not dbg)
                if direct_out:
                    pso2 = Ring(P, "pso2", [128, 512], F32, 2, "psum")
                    yo_ = Ring(P, "yo_", [128, D], F32, 2)
                def ffn_pre(t0, seg, cx):
                    xt, dxt = xr.next()
                    P.dma("sync", xt[:], XT[:, :, t0:t0 + 512], reads=[dXT], writes=[dxt])
                    h2, dh2 = hr.next()
                    cx.update(dict(xt=xt, dxt=dxt, h2=h2, dh2=dh2))
                    if hf == 0:
                        ym, dym = ymr.next()
                        P.dma("sync", ym[:], YMIX[:, :, t0:t0 + 512], reads=[dYMIX], writes=[dym])
                        yield
                        for oc in range(KC):
                            ps, dps = psw.next()
                            mm_group(ps[:], [(wos[:, kc, oc * 128:(oc + 1) * 128], ym[:, kc, :]) for kc in range(KC)], [dym, dwf[3]], [dps])
                            P.op("vector", lambda e, xt=xt, ps=ps, oc=oc, seg=seg: e.scalar_tensor_tensor(xt[:, oc, :], ps[:], modv[:, 16 + oc, seg:seg + 1], xt[:, oc, :], ALU.mult, ALU.add), [dps, dmod, dxt], [dxt])
                            yield
                        yield from norm_to_h_gen(xt, dxt, h2, dh2, seg, A2, 24, rings)
                        P.dma("sync", H2[:, :, t0:t0 + 512], h2[:], reads=[dh2], writes=[dH2])
                    else:
                        P.dma("sync", h2[:], H2[:, :, t0:t0 + 512], reads=[dH2], writes=[dh2])
                    yield
                cxc = {}
                g_ = ffn_pre(tiles[0][0], tiles[0][1], cxc)
                advance(g_, 1000)
                for ti, (t0, seg) in enumerate(tiles):
                    xt, dxt, h2, dh2 = cxc["xt"], cxc["dxt"], cxc["h2"], cxc["dh2"]
                    cxn = {}
                    nxt = ffn_pre(tiles[ti + 1][0], tiles[ti + 1][1], cxn) if ti + 1 < len(tiles) else None
                    act, dact = actr.next()
                    for j in range(NJ):
                        pg_, dpg_ = psf.next(); pu_, dpu_ = psf.next()
                        mm_group(pg_[:], [(wgs[:, kc, j * 128:(j + 1) * 128], h2[:, kc, :]) for kc in range(KC)], [dh2, dwf[0]], [dpg_])
                        mm_group(pu_[:], [(wus[:, kc, j * 128:(j + 1) * 128], h2[:, kc, :]) for kc in range(KC)], [dh2, dwf[1]], [dpu_])
                        sg, dsg = sgr.next()
                        P.op("scalar", lambda e, sg=sg, pg_=pg_: e.activation(sg[:], pg_[:], AF.Silu), [dpg_], [dsg])
                        P.op("vector", lambda e, act=act, j=j, sg=sg, pu_=pu_: e.tensor_tensor(act[:, j, :], sg[:], pu_[:], ALU.mult), [dsg, dpu_], [dact])
                        advance(nxt, 2)
                    for oc in range(KC):
                        ps, dps = psf.next()
                        mm_group(ps[:], [(wds[:, j, oc * 128:(oc + 1) * 128], act[:, j, :]) for j in range(NJ)], [dact, dwf[2]], [dps])
                        P.op("vector", lambda e, xt=xt, ps=ps, oc=oc, seg=seg: e.scalar_tensor_tensor(xt[:, oc, :], ps[:], modv[:, 40 + oc, seg:seg + 1], xt[:, oc, :], ALU.mult, ALU.add), [dps, dmod, dxt], [dxt])
                        advance(nxt, 2)
                    if direct_out:
                        for sub in range(4):
                            tt0 = t0 + sub * 128
                            o, do = yo_.next()
                            for hh in range(2):
                                ps, dps = pso2.next()

                                def fnT(e, ps=ps, xt=xt, hh=hh, sub=sub):
                                    ins = None
                                    for j in range(4):
                                        ins = e.transpose(ps[:, j * 128:(j + 1) * 128], xt[:, hh * 4 + j, sub * 128:(sub + 1) * 128], identf)
                                    return ins
                                P.op("tensor", fnT, [dxt, dcst], [dps])
                                evac(o[:, hh * 512:(hh + 1) * 512], ps[:], [dps], [do])
                            dstap = ys[tt0:tt0 + 128, :] if tt0 < NS else yp[tt0 - NS:tt0 - NS + 128, :]
                            P.dma("sync", dstap, o[:], reads=[do], writes=[dOUT])
                    else:
                        P.dma("sync", XT[:, :, t0:t0 + 512], xt[:], reads=[dxt], writes=[dXT])
                    advance(nxt, 1000)
                    cxc = cxn

    if not dbg:
        P.finish([dOUT])
        return nc
    with P.phase():
        xr = Ring(P, "fx", [128, KC, 512], F32, 2)
        yo = Ring(P, "fy", [128, D], F32, 4)
        pst = Ring(P, "fps", [128, 512], F32, 8, "psum")
        for gi in range(NT // 512):
            a, da = xr.next()
            P.dma("sync", a[:], XT[:, :, gi * 512:(gi + 1) * 512], reads=[dXT], writes=[da])
            for sub in range(4):
                t0 = gi * 512 + sub * 128
                o, do = yo.next()
                for hh in range(2):
                    ps, dps = pst.next()

                    def fn(e, ps=ps, a=a, hh=hh, sub=sub):
                        ins = None
                        for j in range(4):
                            ins = e.transpose(ps[:, j * 128:(j + 1) * 128], a[:, hh * 4 + j, sub * 128:(sub + 1) * 128], identf)
                        return ins
                    P.op("tensor", fn, [da, dcst], [dps])
                    evac(o[:, hh * 512:(hh + 1) * 512], ps[:], [dps], [do])
                dstap = ys[t0:t0 + 128, :] if t0 < NS else yp[t0 - NS:t0 - NS + 128, :]
                P.dma("sync", dstap, o[:], reads=[do], writes=[dOUT])
    P.finish([dOUT])
    return nc


def _consts():
    c = np.zeros((128, CW), np.float32)
    c[:, 0:128] = np.eye(128)
    c[:, 128:256] = 1.0
    c[0:64, 256:320] = 1.0
    c[64:128, 320:384] = 1.0
    m = np.arange(128)
    partner = np.where((m % 32) < 16, m + 16, m - 16)
    c[partner, 384 + m] = 1.0
    b = np.arange(128)[:, None]; a = np.arange(128)[None, :]
    c[:, 512:640] = (b >= a)
    c[:, 640:768] = (b <= a)
    i = np.arange(64)[:, None]; j = np.arange(64)[None, :]
    for k in range(8):
        fwd = k < 4
        c[0:64, 768 + k * 64:768 + (k + 1) * 64] = (i > j) if fwd else (i < j)
        c[0:64, 1280 + k * 64:1280 + (k + 1) * 64] = (i >= j) if fwd else (i <= j)
        c[0:64, 1792 + k * 64:1792 + (k + 1) * 64] = np.eye(64)
    c[0:64, 2304:2368] = (i <= j)
    c[0:64, 2368:2432] = (i >= j)
    for k in range(8):
        c[0:64, 2432 + k * 64:2432 + (k + 1) * 64] = (i <= j) if k < 4 else (i >= j)
    c[64:128, 768:2304] = c[0:64, 768:2304]
    c[64:128, 2432:2944] = c[0:64, 2432:2944]
    c[0:64, 2944:3008] = (i <= j); c[64:128, 3008:3072] = (i <= j)
    c[0:64, 3072:3136] = (i >= j); c[64:128, 3136:3200] = (i >= j)
    t = np.arange(4096)
    p = np.arange(128)
    d = p % 64
    jj = (d % 16).astype(np.float32)
    inv = (1.0 / (np.float32(10000.0) ** (jj / np.float32(16.0)))).astype(np.float32)
    pos = np.where((d < 32)[:, None], (t // 64)[None, :], (t % 64)[None, :]).astype(np.float32)
    ang = (pos * inv[:, None]).astype(np.float32)
    cos = np.cos(ang).astype(np.float32)
    sgn = np.where((d % 32) < 16, -1.0, 1.0).astype(np.float32)
    sin = (np.sin(ang) * sgn[:, None]).astype(np.float32)
    return c, cos, sin


def _col_perm():
    perm = np.arange(NCOLS)
    for c in range(4):
        for half, h in ((0, c), (1, 4 + c)):
            perm[768 + c * 128 + half * 64:768 + c * 128 + half * 64 + 64] = 768 + h * 64 + np.arange(64)
    return perm


def _row_perm():
    perm = np.arange(D)
    for c in range(4):
        for half, h in ((0, c), (1, 4 + c)):
            perm[256 + c * 128 + half * 64:256 + c * 128 + half * 64 + 64] = 256 + h * 64 + np.arange(64)
    return perm


def host_prep(inp, depth):
    f = lambda a: np.ascontiguousarray(np.asarray(a, dtype=np.float32))
    cst, cos, sin = _consts()
    cp, rp = _col_perm(), _row_perm()
    shared = {
        "win": f(np.asarray(inp["w_in"])[:depth][:, :, cp]),
        "wout": f(np.asarray(inp["w_out"])[:depth][:, rp, :]),
        "adaw": f(np.asarray(inp["ada_w"])[:depth]),
        "wg": f(np.asarray(inp["w_gate"])[:depth]), "wu": f(np.asarray(inp["w_up"])[:depth]), "wd": f(np.asarray(inp["w_down"])[:depth]),
        "cst": cst, "ropec": cos, "ropes": sin,
    }
    pv = np.zeros((depth, 128, 90), np.float32); bc = np.zeros((depth, 128, 96), np.float32)
    for l in range(depth):
        pv[l, :, 0:48] = np.asarray(inp["ada_b"])[l].reshape(48, 128).T
        pv[l, :, 48:56] = np.asarray(inp["norm1_g"])[l].reshape(8, 128).T
        pv[l, :, 56:64] = np.asarray(inp["norm2_g"])[l].reshape(8, 128).T
        scw = np.asarray(inp["sc_conv_w"])[l]; dnw = np.asarray(inp["dn_conv_w"])[l]
        for k in range(3):
            pv[l, :, 64 + k * 2:64 + k * 2 + 2] = scw[k].reshape(2, 128).T
            pv[l, :, 70 + k * 6:70 + k * 6 + 6] = dnw[k].reshape(6, 128).T
        pv[l, :, 88] = np.tile(np.asarray(inp["q_norm_g"])[l], 2)
        pv[l, :, 89] = np.tile(np.asarray(inp["k_norm_g"])[l], 2)
        bc[l, :, 0:8] = np.asarray(inp["dn_A_log"])[l].reshape(8)[None, :]
        bc[l, :, 8:16] = np.asarray(inp["dn_dt_bias"])[l].reshape(8)[None, :]
        bc[l, :, 16:80] = np.asarray(inp["dn_norm_g"])[l][None, :]
        sk = np.asarray(inp["attn_sink"])[l]
        bc[l, 0:64, 88:92] = sk[0:4][None, :]
        bc[l, 64:128, 88:92] = sk[4:8][None, :]
        bc[l, 0:64, 92:96] = sk[4:8][None, :]
        bc[l, 64:128, 92:96] = sk[0:4][None, :]
    shared["pv"] = pv; shared["bcp"] = bc
    return shared


def core_inputs(inp, shared, core, NS, depth, nsamp):
    f = lambda a: np.ascontiguousarray(np.asarray(a, dtype=np.float32))
    b = core % nsamp
    m = dict(shared)
    m["xs"] = f(np.asarray(inp["x_sample"])[b, :NS])
    m["xp"] = f(np.asarray(inp["x_prompt"])[2 * core:2 * core + 2].reshape(512, D))
    m["ck"] = f(np.asarray(inp["cache_k"])[b, :depth].reshape(depth, 512, 128))
    m["cv"] = f(np.asarray(inp["cache_v"])[b, :depth].reshape(depth, 512, 128))
    m["s0"] = f(np.asarray(inp["state_delta"])[b, :depth].reshape(depth, 8, 64, 64))
    cT = np.zeros((128, 16), np.float32)
    cT[:, 0::2] = np.asarray(inp["c"])[b].reshape(8, 128).T
    cT[:, 1::2] = np.asarray(inp["c_ctx"]).reshape(8, 128).T
    m["cT"] = cT
    return m


_NC_CACHE = {}


def kernel(**inputs):
    NS, depth, ncores = 4096, 2, 8
    if "full" not in _NC_CACHE:
        _NC_CACHE["full"] = build(NS, depth)
    nc = _NC_CACHE["full"]
    shared = host_prep(inputs, depth)
    in_maps = [core_inputs(inputs, shared, c, NS, depth, 4) for c in range(ncores)]
    res = run_bass_kernel_spmd(nc, in_maps, core_ids=list(range(ncores)))
    R = res.results
    y_p = np.concatenate([np.asarray(R[c]["yp"]).reshape(2, 256, D) for c in range(8)], 0)
    y_s = np.stack([np.asarray(R[c]["ys"]) for c in range(4)], 0)
    nk = np.concatenate([np.asarray(R[c]["nk"]).reshape(2, depth, 256, 2, 64) for c in range(8)], 0)
    nv = np.concatenate([np.asarray(R[c]["nv"]).reshape(2, depth, 256, 2, 64) for c in range(8)], 0)
    ns = np.concatenate([np.asarray(R[c]["nst"]).reshape(2, depth, 2, 4, 64, 64) for c in range(8)], 0)
    return (y_p.astype(np.float32), y_s.astype(np.float32), nk.astype(np.float32), nv.astype(np.float32), ns.astype(np.float32))
```

```python
from contextlib import ExitStack
import numpy as np
import concourse.bass as bass
import concourse.mybir as mybir
from concourse.bass_utils import run_bass_kernel_spmd

F32 = mybir.dt.float32
BF16 = mybir.dt.bfloat16
AF = mybir.ActivationFunctionType
ALU = mybir.AluOpType
AX = mybir.AxisListType


class Dep:
    __slots__ = ("w", "r", "x")

    def __init__(self):
        self.w = None
        self.r = []
        self.x = False


class _Rec:
    def __init__(self):
        self.calls = []

    def __getattr__(self, name):
        def f(*a, **k):
            self.calls.append((name, a, k))
            return self
        return f


def _replay_calls(calls):
    def fn(e):
        ins = None
        for name, a, k in calls:
            ins = getattr(e, name)(*a, **k)
        return ins
    return fn


class Prog:
    CE = ("tensor", "vector", "scalar", "gpsimd")
    NDMA = {"sync": 12, "gpsimd": 6, "scalar": 4}

    def __init__(self, nc):
        self.nc = nc
        self.stack = ExitStack()
        self.ops = {e: [] for e in ("tensor", "vector", "scalar", "gpsimd", "sync")}
        self.sem = {}
        self.cnt = {}
        self.seen = {e: {} for e in self.ops}
        for e in self.CE:
            self.sem[e] = self.stack.enter_context(nc.semaphore("s_" + e))
            self.cnt[e] = 0
        self.dsem = {}
        self.dval = {}
        self.dnext = {}
        for q, n in self.NDMA.items():
            self.dsem[q] = [self.stack.enter_context(nc.semaphore("d_%s%d" % (q, i))) for i in range(n)]
            self.dnext[q] = 0
        for q in self.dsem:
            for s in self.dsem[q]:
                self.dval[id(s)] = 0
        self.n_alloc = 0

    def sbuf(self, name, shape, dtype):
        self.n_alloc += 1
        return self.stack.enter_context(self.nc.sbuf_tensor("%s_s%d" % (name, self.n_alloc), list(shape), dtype))

    def psum(self, name, shape, dtype):
        self.n_alloc += 1
        return self.stack.enter_context(self.nc.psum_tensor("%s_p%d" % (name, self.n_alloc), list(shape), dtype))

    def dep(self):
        return Dep()

    def deps(self, n):
        return [Dep() for _ in range(n)]

    def _collect(self, eng, reads, writes):
        toks = []
        for d in reads:
            if d.w is not None:
                toks.append(d.w)
        for d in writes:
            if d.w is not None:
                toks.append(d.w)
            toks.extend(d.r)
        seen = self.seen[eng]
        waits = {}
        for (s, v, src) in toks:
            if src == eng and eng == "tensor":
                continue
            k = id(s)
            if seen.get(k, 0) >= v:
                continue
            if k not in waits or waits[k][1] < v:
                waits[k] = (s, v)
        for k, (s, v) in waits.items():
            seen[k] = v
        return list(waits.values())

    def _commit(self, tok, reads, writes):
        for d in reads:
            d.r.append(tok)
        for d in writes:
            d.w = tok
            d.r = []

    max_ops = None
    n_ops = 0
    log = []

    def _skip(self, desc):
        Prog.n_ops += 1
        if Prog.max_ops is not None and Prog.n_ops > Prog.max_ops:
            return True
        Prog.log.append(desc)
        return False

    def op(self, eng, fn, reads=(), writes=()):
        if self._skip((eng,)):
            return None
        writes = list(writes) + [d for d in reads if d.x]
        reads = [d for d in reads if not d.x]
        waits = self._collect(eng, reads, writes)
        self.cnt[eng] += 1
        tok = (self.sem[eng], self.cnt[eng], eng)
        rec = _Rec()
        fn(rec)
        self.ops[eng].append((waits, _replay_calls(rec.calls), self.sem[eng], 1))
        self._commit(tok, reads, writes)
        return tok

    def dma(self, q, out, in_, reads=(), writes=(), **kw):
        if self._skip(("dma_" + q,)):
            return None
        pool = self.dsem[q]
        s = pool[self.dnext[q] % len(pool)]
        self.dnext[q] += 1
        waits = self._collect(q, reads, writes)
        prev = self.dval[id(s)]
        if prev > 0 and self.seen[q].get(id(s), 0) < prev:
            waits.append((s, prev))
            self.seen[q][id(s)] = prev
        self.dval[id(s)] = prev + 16
        tok = (s, prev + 16, "dma_" + q)
        self.ops[q].append((waits, lambda e: e.dma_start(out=out, in_=in_, **kw), s, 16))
        self._commit(tok, reads, writes)
        return tok

    def wait_on(self, eng, deps_):
        waits = self._collect(eng, deps_, ())
        self.ops[eng].append((waits, None, None, 0))

    def flush(self):
        for q in self.dsem:
            waits = []
            for s in self.dsem[q]:
                v = self.dval[id(s)]
                if v > 0 and self.seen["sync"].get(id(s), 0) < v:
                    waits.append((s, v))
                    self.seen["sync"][id(s)] = v
            if waits:
                self.ops["sync"].append((waits, None, None, 0))
        nc = self.nc

        def replay(name):
            def run(e):
                for waits, fn, s, inc in self.ops[name]:
                    for (ws, wv) in waits:
                        e.wait_ge(ws, wv)
                    if fn is not None:
                        ins = fn(e)
                        ins.then_inc(s, inc)
            return run

        with nc.Block() as block:
            block.tensor(replay("tensor"))
            block.vector(replay("vector"))
            block.scalar(replay("scalar"))
            block.gpsimd(replay("gpsimd"))
            block.sync(replay("sync"))
        for k in self.ops:
            self.ops[k] = []

    def phase(self):
        prog = self

        class _Ph:
            def __enter__(s):
                s.saved = prog.stack
                prog.stack = ExitStack()
                return prog

            def __exit__(s, *a):
                if a[0] is None:
                    prog.flush()
                prog.stack.close()
                prog.stack = s.saved
                return False
        return _Ph()

    def finish(self, out_deps=()):
        self.wait_on("sync", out_deps)
        self.flush()
        self.stack.close()


class Ring:
    def __init__(self, P, name, shape, dtype, n, space="sbuf"):
        mk = P.sbuf if space == "sbuf" else P.psum
        self.t = [mk("%s_%d" % (name, i), shape, dtype) for i in range(n)]
        self.d = [P.dep() for _ in range(n)]
        if space != "sbuf":
            for d in self.d:
                d.x = True
        self.i = 0

    def next(self):
        k = self.i % len(self.t)
        self.i += 1
        return self.t[k], self.d[k]

    @classmethod
    def of(cls, tiles, deps):
        r = cls.__new__(cls)
        r.t = list(tiles); r.d = list(deps); r.i = 0
        return r


def run_lockstep(gens):
    alive = list(gens)
    while alive:
        for g in list(alive):
            try:
                next(g)
            except StopIteration:
                alive.remove(g)


D = 1024
KC = 8
NCOLS = 2576
DFF = 2816
EPS = 1e-6
CW = 3200


def build(NS=4096, depth=2, dbg=False):
    NT = NS + 512
    nc = bass.Bass("TRN2", target_bir_lowering=False)

    def din(name, shape, dt=F32):
        return nc.dram_tensor(name, list(shape), dt, kind="ExternalInput").ap()

    def dout(name, shape, dt=F32):
        return nc.dram_tensor(name, list(shape), dt, kind="ExternalOutput").ap()

    def dint(name, shape, dt=F32):
        if dbg:
            return nc.dram_tensor(name, list(shape), dt, kind="ExternalOutput").ap()
        return nc.dram_tensor(name, list(shape), dt).ap()

    xs = din("xs", [NS, D]); xp = din("xp", [512, D])
    ck = din("ck", [depth, 512, 128]); cv = din("cv", [depth, 512, 128])
    s0in = din("s0", [depth, 8, 64, 64])
    cTin = din("cT", [128, 16])
    win = din("win", [depth, D, NCOLS]); wout = din("wout", [depth, D, D])
    adaw = din("adaw", [depth, D, 6 * D])
    wg = din("wg", [depth, D, DFF]); wu = din("wu", [depth, D, DFF]); wd = din("wd", [depth, DFF, D])
    pvin = din("pv", [depth, 128, 90]); bcin = din("bcp", [depth, 128, 96])
    cstin = din("cst", [128, CW]); ropec = din("ropec", [128, 4096]); ropes = din("ropes", [128, 4096])
    ys = dout("ys", [NS, D]); yp = dout("yp", [512, D])
    nk = dout("nk", [2, depth, 256, 128]); nv = dout("nv", [2, depth, 256, 128])
    nst = dout("nst", [2, depth, 8, 64, 64])

    XT = dint("XT", [KC, 128, NT]).rearrange("c p t -> p c t")
    PROJ = dint("PROJ", [20, 128, NT]).rearrange("c p t -> p c t")
    AB = dint("AB", [NT, 16])
    DNQ = dint("DNQ", [6, 128, NT])
    YMIX = dint("YMIX", [KC, 128, NT], BF16).rearrange("c p t -> p c t")
    H2 = dint("H2", [KC, 128, NT], BF16).rearrange("c p t -> p c t")
    OD = dint("OD", [2, NT, 256])
    dXT, dPROJ, dAB, dDNQ, dYMIX, dH2, dOD = [Dep() for _ in range(7)]
    dOUT = Dep()

    P = Prog(nc)
    seqs = [(0, NS, "s", 0), (NS, 256, "p", 0), (NS + 256, 256, "p", 1)]
    tiles = [(t0, 0) for t0 in range(0, NS, 512)] + [(NS, 1)]

    cst = P.sbuf("cst", [128, CW], F32); dcst = Dep()
    P.dma("sync", cst[:], cstin, writes=[dcst])
    identf = cst[:, 0:128]
    cb = P.sbuf("cb", [128, 6 * 128], BF16); dcb = Dep()
    P.op("vector", lambda e: e.tensor_copy(cb[:], cst[:, 0:768]), [dcst], [dcb])
    identb = cb[:, 0:128]; onesb = cb[:, 128:256]; bd64b = cb[:, 256:384]; rpermb = cb[:, 384:512]
    mwprev = cb[:, 512:640]; mwnext = cb[:, 640:768]
    MS8 = cst[:, 768:1280]; MI8 = cst[:, 1280:1792]; IDB8 = cst[:, 1792:2304]
    U8 = cst[:, 2432:2944].rearrange("p (k j) -> p k j", k=8)
    UCF2 = cst[:, 2944:3072]; UCB2 = cst[:, 3072:3200]; bd64f = cst[:, 256:384]
    UCF = cst[0:64, 2304:2368]; UCB = cst[0:64, 2368:2432]
    onesf64 = cst[0:64, 128:192]
    modv = P.sbuf("modv", [128, 48, 2], F32); dmod = Dep()
    A1 = P.sbuf("A1", [128, 8, 2], F32); A2 = P.sbuf("A2", [128, 8, 2], F32)
    pvt = P.sbuf("pvt", [128, 90], F32); bct = P.sbuf("bct", [128, 96], F32); dpv = Dep()
    sinkexp = P.sbuf("sinkexp", [128, 8], F32)
    negA = P.sbuf("negA", [64, 8], F32); negA2 = P.sbuf("negA2", [128, 8], F32)
    PVO = {"adab": 0, "n1": 48, "n2": 56, "scw": 64, "dnw": 70, "qg": 88, "kg": 89}
    BCO = {"alog": 0, "dtb": 8, "dng": 16, "sink": 88}
    P.flush()

    def mm_group(out, pairs, reads, writes):
        n = len(pairs)

        def fn(e):
            ins = None
            for i, (l_, r_) in enumerate(pairs):
                ins = e.matmul(out, l_, r_, start=(i == 0), stop=(i == n - 1))
            return ins
        P.op("tensor", fn, reads, writes)

    evac_i = [0]

    def evac(out, in_, reads, writes):
        evac_i[0] += 1
        if evac_i[0] % 2:
            P.op("vector", lambda e: e.tensor_copy(out, in_), reads, writes)
        else:
            P.op("scalar", lambda e: e.copy(out, in_), reads, writes)

    with P.phase():
        xin = Ring(P, "xin", [128, D], F32, 4)
        xo = Ring(P, "xo", [128, KC, 512], F32, 2)
        pst = Ring(P, "pst", [128, 512], F32, 8, "psum")
        for gi in range(NT // 512):
            o, do = xo.next()
            for sub in range(4):
                t0 = gi * 512 + sub * 128
                src = xs[t0:t0 + 128, :] if t0 < NS else xp[t0 - NS:t0 - NS + 128, :]
                a, da = xin.next()
                P.dma("sync", a[:], src, writes=[da])
                for hh in range(2):
                    ps, dps = pst.next()

                    def fn(e, ps=ps, a=a, hh=hh):
                        ins = None
                        for j in range(4):
                            c = hh * 4 + j
                            ins = e.transpose(ps[:, j * 128:(j + 1) * 128], a[:, c * 128:(c + 1) * 128], identf)
                        return ins
                    P.op("tensor", fn, [da, dcst], [dps])
                    evac(o[:, hh * 4:(hh + 1) * 4, sub * 128:(sub + 1) * 128], ps[:].rearrange("p (a b) -> p a b", a=4), [dps], [do])
            P.dma("sync", XT[:, :, gi * 512:(gi + 1) * 512], o[:], reads=[do], writes=[dXT])

    for l in range(depth):
        with P.phase():
            P.dma("sync", pvt[:], pvin[l], writes=[dpv])
            P.dma("sync", bct[:], bcin[l], writes=[dpv])
            ct = P.sbuf("ct", [128, 16], F32); dct = Dep()
            P.dma("sync", ct[:], cTin, writes=[dct])
            sil = P.sbuf("sil", [128, 16], BF16); dsil = Dep()
            P.op("scalar", lambda e: e.activation(sil[:], ct[:], AF.Silu), [dct], [dsil])
            war = Ring(P, "wa", [128, KC, 768], BF16, 3)
            psm = P.psum("psm", [128, 512], F32); dpsm = Dep()
            awv = adaw[l].rearrange("(kc p) n -> p kc n", p=128)
            for g in range(8):
                wa, dwa = war.next()
                for hh in range(2):
                    P.dma("gpsimd", wa[:, :, hh * 384:(hh + 1) * 384], awv[:, :, g * 768 + hh * 384:g * 768 + (hh + 1) * 384], writes=[dwa])
                for fcl in range(6):
                    fc = g * 6 + fcl
                    mm_group(psm[:, fc * 2:fc * 2 + 2],
                             [(wa[:, kc, fcl * 128:(fcl + 1) * 128], sil[:, kc * 2:kc * 2 + 2]) for kc in range(KC)],
                             [dwa, dsil], [dpsm])
            P.op("vector", lambda e: e.tensor_tensor(modv[:], psm[:, 0:96].rearrange("p (a b) -> p a b", b=2),
                                                      pvt[:, 0:48].unsqueeze(2).broadcast_to([128, 48, 2]), ALU.add), [dpsm, dpv], [dmod])
            P.op("vector", lambda e: e.scalar_tensor_tensor(A1[:], modv[:, 8:16, :], 1.0, pvt[:, 48:56].unsqueeze(2).broadcast_to([128, 8, 2]), ALU.add, ALU.mult), [dmod, dpv], [dmod])
            P.op("vector", lambda e: e.scalar_tensor_tensor(A2[:], modv[:, 32:40, :], 1.0, pvt[:, 56:64].unsqueeze(2).broadcast_to([128, 8, 2]), ALU.add, ALU.mult), [dmod, dpv], [dmod])
            P.op("scalar", lambda e: e.activation(sinkexp[:], bct[:, 88:96], AF.Exp), [dpv], [dmod])
            P.op("scalar", lambda e: e.activation(negA[:], bct[0:64, 0:8], AF.Exp), [dpv], [dmod])
            P.op("vector", lambda e: e.tensor_scalar(negA[:], negA[:], -1.0, None, ALU.mult), [dmod], [dmod])
            P.op("scalar", lambda e: e.activation(negA2[:], bct[:, 0:8], AF.Exp), [dpv], [dmod])
            P.op("vector", lambda e: e.tensor_scalar(negA2[:], negA2[:], -1.0, None, ALU.mult), [dmod], [dmod])

        def norm_to_h_gen(xt, dxt, hT, dh, seg, A_, B0, rings, T=512):
            sq, dsq = rings["sq"].next()
            P.op("scalar", lambda e: e.activation(sq[:, :, 0:T], xt[:, :, 0:T], AF.Square), [dxt], [dsq])
            yield
            psn, dpsn = rings["psn"].next()
            mm_group(psn[:, 0:T], [(onesb, sq[:, kc, 0:T]) for kc in range(KC)], [dsq, dcb], [dpsn])
            yield
            rs, drs = rings["rs"].next()
            P.op("scalar", lambda e: e.activation(rs[:, 0:T], psn[:, 0:T], AF.Ln, bias=EPS, scale=1.0 / D), [dpsn], [drs])
            yield
            P.op("scalar", lambda e: e.activation(rs[:, 0:T], rs[:, 0:T], AF.Exp, scale=-0.5), [drs], [drs])
            yield
            for kc in range(KC):
                tmp, dtmp = rings["tmp"].next()
                P.op("vector", lambda e, kc=kc, tmp=tmp: e.tensor_tensor(tmp[:, 0:T], xt[:, kc, 0:T], rs[:, 0:T], ALU.mult), [dxt, drs], [dtmp])
                yield
                P.op("scalar", lambda e, kc=kc, tmp=tmp: e.activation(hT[:, kc, 0:T], tmp[:, 0:T], AF.Identity,
                                                                      bias=modv[:, B0 + kc, seg:seg + 1], scale=A_[:, kc, seg:seg + 1]), [dtmp, dmod], [dh])
                yield


        def norm_to_h(*a, **k):
            for _ in norm_to_h_gen(*a, **k):
                pass

        def advance(gen, n):
            if gen is None:
                return
            for _ in range(n):
                try:
                    next(gen)
                except StopIteration:
                    return

        with P.phase():
            wsb = P.sbuf("wsb", [128, KC, NCOLS], BF16); dw = [Dep() for _ in range(6)]
            wv = win[l].rearrange("(kc p) n -> p kc n", p=128)
            for g in range(6):
                c0, c1 = g * 512, min((g + 1) * 512, NCOLS)
                P.dma("gpsimd", wsb[:, :, c0:c1], wv[:, :, c0:c1], writes=[dw[g]])
            rings = {"sq": Ring(P, "sq", [128, KC, 512], BF16, 2), "psn": Ring(P, "psn", [128, 512], F32, 1, "psum"),
                     "rs": Ring(P, "rs", [128, 512], F32, 2), "tmp": Ring(P, "tmp", [128, 512], F32, 3)}
            xr = Ring(P, "xt", [128, KC, 512], F32, 2)
            hr = Ring(P, "hT", [128, KC, 512], BF16, 2)
            pso = Ring(P, "pso", [128, 512], F32, 4, "psum")
            psab = Ring(P, "psab", [128, 512], F32, 1, "psum")
            stg = Ring(P, "stg", [128, 4, 512], F32, 2)
            abs_ = Ring(P, "abst", [128, 4, 16], F32, 2)
            def proj_pre(t0, seg):
                xt, dxt = xr.next()
                P.dma("sync", xt[:], XT[:, :, t0:t0 + 512], reads=[dXT], writes=[dxt])
                hT, dh = hr.next()
                return hT, dh, norm_to_h_gen(xt, dxt, hT, dh, seg, A1, 0, rings)
            cur = proj_pre(*tiles[0])
            advance(cur[2], 1000)
            for ti, (t0, seg) in enumerate(tiles):
                hT, dh, _ = cur
                nxt = proj_pre(*tiles[ti + 1]) if ti + 1 < len(tiles) else None
                for og in range(5):
                    st, dst = stg.next()
                    for j in range(4):
                        oc = og * 4 + j
                        ps, dps = pso.next()
                        mm_group(ps[:], [(wsb[:, kc, oc * 128:(oc + 1) * 128], hT[:, kc, :]) for kc in range(KC)], [dh] + dw, [dps])
                        evac(st[:, j, :], ps[:], [dps], [dst])
                        if nxt is not None:
                            advance(nxt[2], 1)
                    P.dma("sync", PROJ[:, og * 4:og * 4 + 4, t0:t0 + 512], st[:], reads=[dst], writes=[dPROJ])
                pa, dpa = psab.next()
                for sub in range(4):
                    mm_group(pa[:, sub * 16:(sub + 1) * 16], [(hT[:, kc, sub * 128:(sub + 1) * 128], wsb[:, kc, 2560:2576]) for kc in range(KC)], [dh] + dw, [dpa])
                ab, dab = abs_.next()
                evac(ab[:], pa[:, 0:64].rearrange("p (a b) -> p a b", b=16), [dpa], [dab])
                P.dma("sync", AB[t0:t0 + 512].rearrange("(s p) k -> p s k", p=128), ab[:], reads=[dab], writes=[dAB])
                if nxt is not None:
                    advance(nxt[2], 1000)
                cur = nxt
        if dbg == "proj":
            break

        with P.phase():
            inr = Ring(P, "cin", [128, 6, 514], F32, 4)
            ur = Ring(P, "cu", [128, 2, 514], F32, 2)
            accr = Ring(P, "cacc", [128, 512], F32, 18)
            yr = Ring(P, "cy", [128, 2, 512], BF16, 2)
            sr = Ring(P, "csil", [128, 512], F32, 10)
            sqr = Ring(P, "csq", [128, 512], BF16, 10)
            pcr = Ring(P, "pcr", [128, 512], F32, 6, "psum")
            rr = Ring(P, "crs", [128, 512], F32, 10)
            obr = Ring(P, "cob", [128, 6, 512], F32, 2)

            def load_halo(c0, s_, n_, t0, T):
                a, da = inr.next()
                lo = t0 - 1 if t0 > 0 else 0
                hi = t0 + T + 1 if t0 + T < n_ else n_
                if t0 == 0:
                    P.op("vector", lambda e, a=a: e.memset(a[:, :, 0:1], 0.0), [], [da])
                if t0 + T >= n_:
                    P.op("vector", lambda e, a=a: e.memset(a[:, :, T + 1:T + 2], 0.0), [], [da])
                P.dma("sync", a[:, :, (lo - (t0 - 1)):(hi - (t0 - 1))], PROJ[:, c0:c0 + 6, s_ + lo:s_ + hi], reads=[dPROJ], writes=[da])
                return a, da

            def conv3_ops(acc, dacc, u_ap_fn, wcol, rd):
                P.op("vector", lambda e: e.tensor_scalar(acc, u_ap_fn(1), pvt[:, wcol(1):wcol(1) + 1], None, ALU.mult), rd + [dpv], [dacc])
                yield 0
                P.op("vector", lambda e: e.scalar_tensor_tensor(acc, u_ap_fn(0), pvt[:, wcol(0):wcol(0) + 1], acc, ALU.mult, ALU.add), rd + [dpv], [dacc])
                yield 1
                P.op("vector", lambda e: e.scalar_tensor_tensor(acc, u_ap_fn(2), pvt[:, wcol(2):wcol(2) + 1], acc, ALU.mult, ALU.add), rd + [dpv], [dacc])
                yield 2

            for (s_, n_, kind, si) in seqs:
                T = min(512, n_)

                def conv_tile(t0, s_=s_, n_=n_, T=T):
                    a, da = load_halo(0, s_, n_, t0, T)
                    u, du = ur.next()
                    P.op("vector", lambda e, a=a, u=u: e.tensor_tensor(u[:, :, 0:T + 2], a[:, 2:4, 0:T + 2], a[:, 4:6, 0:T + 2], ALU.mult), [da], [du])
                    y, dy = yr.next()

                    def sc_gen(c, a=a, da=da, u=u, du=du, y=y, dy=dy):
                        acc, dacc = accr.next()
                        for op_ in conv3_ops(acc[:, 0:T], dacc, lambda k, u=u, c=c: u[:, c, k:k + T], lambda k, c=c: PVO["scw"] + k * 2 + c, [du]):
                            yield
                        P.op("vector", lambda e, a=a, y=y, c=c, acc=acc: e.tensor_tensor(y[:, c, 0:T], a[:, c, 1:T + 1], acc[:, 0:T], ALU.mult), [da, dacc], [dy])
                        yield
                    gens = [sc_gen(c) for c in range(2)]
                    a2, da2 = load_halo(12, s_, n_, t0, T)
                    ob, dob = obr.next()

                    def dn_gen(c, a=a2, da=da2, ob=ob, dob=dob):
                        acc, dacc = accr.next()
                        for op_ in conv3_ops(acc[:, 0:T], dacc, lambda k, a=a, c=c: a[:, c, k:k + T], lambda k, c=c: PVO["dnw"] + k * 6 + c, [da]):
                            yield
                        if c >= 4:
                            P.op("scalar", lambda e, ob=ob, c=c, acc=acc: e.activation(ob[:, c, 0:T], acc[:, 0:T], AF.Silu), [dacc], [dob])
                            yield
                            return
                        sl, dsl = sr.next()
                        P.op("scalar", lambda e, sl=sl, acc=acc: e.activation(sl[:, 0:T], acc[:, 0:T], AF.Silu), [dacc], [dsl])
                        yield
                        sq, dsq = sqr.next()
                        P.op("scalar", lambda e, sl=sl, sq=sq: e.activation(sq[:, 0:T], sl[:, 0:T], AF.Square), [dsl], [dsq])
                        yield
                        ps, dps = pcr.next()
                        mm_group(ps[:, 0:T], [(bd64b, sq[:, 0:T])], [dsq, dcb], [dps])
                        rs, drs = rr.next()
                        P.op("scalar", lambda e, rs=rs, ps=ps: e.activation(rs[:, 0:T], ps[:, 0:T], AF.Ln, bias=EPS, scale=1.0), [dps], [drs])
                        yield
                        P.op("scalar", lambda e, rs=rs: e.activation(rs[:, 0:T], rs[:, 0:T], AF.Exp, scale=-0.5), [drs], [drs])
                        yield
                        scl = 0.125 if c < 2 else 1.0
                        P.op("vector", lambda e, ob=ob, c=c, sl=sl, rs=rs, scl=scl: e.scalar_tensor_tensor(ob[:, c, 0:T], sl[:, 0:T], scl, rs[:, 0:T], ALU.mult, ALU.mult), [dsl, drs], [dob])
                        yield
                    gens += [dn_gen(c) for c in range(6)]

                    def finish(y=y, dy=dy, ob=ob, dob=dob):
                        P.dma("sync", YMIX[:, 0:2, s_ + t0:s_ + t0 + T], y[:, :, 0:T], reads=[dy], writes=[dYMIX])
                        P.dma("sync", DNQ.rearrange("c p t -> p c t")[:, :, s_ + t0:s_ + t0 + T], ob[:, :, 0:T], reads=[dob], writes=[dDNQ])
                    return gens, finish
                tl = list(range(0, n_, T))
                for i0 in range(0, len(tl), 2):
                    parts = [conv_tile(t0) for t0 in tl[i0:i0 + 2]]
                    run_lockstep([g for (gs, _) in parts for g in gs])
                    for (_, fin) in parts:
                        fin()
        if dbg in ("conv", "convA"):
            break

        for (s_, n_, kind, si) in seqs:
            if (dbg == "attP" and kind == "s") or (dbg in ("attS", "attSN") and kind == "p"):
                continue
            with P.phase():
                samp = kind == "s"
                T = min(512, n_)
                nblk = n_ // 128
                QN = P.sbuf("QN", [128, 4, n_], BF16); KN = P.sbuf("KN", [128, n_], BF16)
                VA = [P.sbuf("VA0", [128, nblk, 128], BF16), P.sbuf("VA1", [128, nblk, 128], BF16)]
                dQN, dKN, dVT = Dep(), Dep(), Dep()
                P.op("vector", lambda e: e.memset(VA[0][:, :, 64:128], 1.0), [], [dVT])
                P.op("vector", lambda e: e.memset(VA[1][:, :, 0:64], 1.0), [], [dVT])
                qkr = Ring(P, "aqk", [128, 5, 512], F32, 2)
                vr = Ring(P, "av", [128, 512], F32, 2)
                sqr = Ring(P, "asq", [128, 5, 512], BF16, 1)
                psA = Ring(P, "psA", [128, 512], F32, 8, "psum")
                rr = Ring(P, "ars", [128, 512], F32, 5)
                qnr = Ring(P, "aqn", [128, 512], F32, 5)
                qbr = Ring(P, "aqb", [128, 512], BF16, 5)
                t1r = Ring(P, "at1", [128, 512], F32, 5)
                t2r = Ring(P, "at2", [128, 512], F32, 5)
                cosr = Ring(P, "acos", [128, 512], F32, 2)
                sinr = Ring(P, "asin", [128, 512], F32, 2)
                stv = Ring(P, "astv", [128, 4, 128], F32, 2)
                for t0 in range(0, n_, T):
                    qk, dqk = qkr.next()
                    P.dma("sync", qk[:, :, 0:T], PROJ[:, 6:11, s_ + t0:s_ + t0 + T], reads=[dPROJ], writes=[dqk])
                    vv, dvv = vr.next()
                    P.dma("sync", vv[:, 0:T], PROJ[:, 11, s_ + t0:s_ + t0 + T], reads=[dPROJ], writes=[dvv])
                    if samp:
                        cs, dcs = cosr.next(); sn, dsn = sinr.next()
                        P.dma("sync", cs[:, 0:T], ropec[:, t0:t0 + T], writes=[dcs])
                        P.dma("sync", sn[:, 0:T], ropes[:, t0:t0 + T], writes=[dsn])
                    sq, dsq = sqr.next()
                    P.op("scalar", lambda e, sq=sq, qk=qk: e.activation(sq[:, :, 0:T], qk[:, :, 0:T], AF.Square), [dqk], [dsq])
                    def chunk_gen(c, sq=sq, dsq=dsq, qk=qk, dqk=dqk, t0=t0):
                            ps, dps = psA.next()
                            mm_group(ps[:, 0:T], [(bd64b, sq[:, c, 0:T])], [dsq, dcb], [dps])
                            yield
                            rs, drs = rr.next()
                            P.op("scalar", lambda e, rs=rs, ps=ps: e.activation(rs[:, 0:T], ps[:, 0:T], AF.Ln, bias=EPS, scale=1.0 / 64), [dps], [drs])
                            yield
                            P.op("scalar", lambda e, rs=rs: e.activation(rs[:, 0:T], rs[:, 0:T], AF.Exp, scale=-0.5), [drs], [drs])
                            yield
                            gcol = PVO["qg"] if c < 4 else PVO["kg"]
                            dst = QN[:, c, t0:t0 + T] if c < 4 else KN[:, t0:t0 + T]
                            ddst = dQN if c < 4 else dKN
                            if not samp and c < 4:
                                P.op("vector", lambda e, qk=qk, c=c, rs=rs, dst=dst, gcol=gcol: e.scalar_tensor_tensor(dst, qk[:, c, 0:T], pvt[:, gcol:gcol + 1], rs[:, 0:T], ALU.mult, ALU.mult), [dqk, drs, dpv], [ddst])
                                yield
                                return
                            qn, dqn = qnr.next()
                            P.op("vector", lambda e, qk=qk, c=c, rs=rs, qn=qn, gcol=gcol: e.scalar_tensor_tensor(qn[:, 0:T], qk[:, c, 0:T], pvt[:, gcol:gcol + 1], rs[:, 0:T], ALU.mult, ALU.mult), [dqk, drs, dpv], [dqn])
                            yield
                            if not samp:
                                P.op("scalar", lambda e, qn=qn, dst=dst: e.copy(dst, qn[:, 0:T]), [dqn], [ddst])
                                yield
                                ps2, dps2 = psA.next()

                                def fnk(e, ps2=ps2, qn=qn):
                                    ins = None
                                    for b in range(T // 128):
                                        ins = e.transpose(ps2[:, b * 128:(b + 1) * 128], qn[:, b * 128:(b + 1) * 128], identf)
                                    return ins
                                P.op("tensor", fnk, [dqn, dcst], [dps2])
                                yield
                                so, dso = stv.next()
                                evac(so[:, 0:T // 128, :], ps2[:, 0:T].rearrange("p (a b) -> p a b", b=128), [dps2], [dso])
                                yield
                                P.dma("sync", nk[si, l, t0:t0 + T, :].rearrange("(b p) f -> p b f", p=128), so[:, 0:T // 128, :], reads=[dso], writes=[dOUT])
                                yield
                                return
                            qb, dqb = qbr.next()
                            P.op("scalar", lambda e, qn=qn, qb=qb: e.copy(qb[:, 0:T], qn[:, 0:T]), [dqn], [dqb])
                            yield
                            ps2, dps2 = psA.next()
                            mm_group(ps2[:, 0:T], [(rpermb, qb[:, 0:T])], [dqb, dcb], [dps2])
                            yield
                            t1, dt1 = t1r.next(); t2, dt2 = t2r.next()
                            P.op("gpsimd", lambda e, t1=t1, qn=qn, cs=cs: e.tensor_tensor(t1[:, 0:T], qn[:, 0:T], cs[:, 0:T], ALU.mult), [dqn, dcs], [dt1])
                            yield
                            P.op("vector", lambda e, t2=t2, ps2=ps2, sn=sn: e.tensor_tensor(t2[:, 0:T], ps2[:, 0:T], sn[:, 0:T], ALU.mult), [dps2, dsn], [dt2])
                            yield
                            P.op("gpsimd", lambda e, t1=t1, t2=t2, dst=dst: e.tensor_tensor(dst, t1[:, 0:T], t2[:, 0:T], ALU.add), [dt1, dt2], [ddst])
                            yield

                    run_lockstep([chunk_gen(c) for c in range(5)])
                    ps3, dps3 = psA.next()

                    def fnv(e, ps3=ps3, vv=vv):
                        ins = None
                        for b in range(T // 128):
                            ins = e.transpose(ps3[:, b * 128:(b + 1) * 128], vv[:, b * 128:(b + 1) * 128], identf)
                        return ins
                    P.op("tensor", fnv, [dvv, dcst], [dps3])
                    b0 = t0 // 128
                    P.op("vector", lambda e, ps3=ps3, b0=b0: e.tensor_copy(VA[0][:, b0:b0 + T // 128, 0:64], ps3[:, 0:T].rearrange("p (a b) -> p a b", b=128)[:, :, 0:64]), [dps3], [dVT])
                    P.op("vector", lambda e, ps3=ps3, b0=b0: e.tensor_copy(VA[1][:, b0:b0 + T // 128, 64:128], ps3[:, 0:T].rearrange("p (a b) -> p a b", b=128)[:, :, 64:128]), [dps3], [dVT])
                    if not samp:
                        so, dso = stv.next()
                        P.op("scalar", lambda e, so=so, ps3=ps3: e.copy(so[:, 0:T // 128, :], ps3[:, 0:T].rearrange("p (a b) -> p a b", b=128)), [dps3], [dso])
                        P.dma("sync", nv[si, l, t0:t0 + T, :].rearrange("(b p) f -> p b f", p=128), so[:, 0:T // 128, :], reads=[dso], writes=[dOUT])
                if samp:
                    KCx = P.sbuf("KCx", [128, 512], BF16); dctx = Dep()
                    VCA = [P.sbuf("VCA0", [128, 4, 128], BF16), P.sbuf("VCA1", [128, 4, 128], BF16)]
                    P.op("vector", lambda e: e.memset(VCA[0][:, :, 64:128], 1.0), [], [dctx])
                    P.op("vector", lambda e: e.memset(VCA[1][:, :, 0:64], 1.0), [], [dctx])
                    ckt = P.sbuf("ckt", [128, 4, 128], F32); cvt = P.sbuf("cvt", [128, 4, 128], F32); dck = Dep()
                    P.dma("sync", ckt[:], ck[l].rearrange("(b p) f -> p b f", p=128), writes=[dck])
                    P.dma("sync", cvt[:], cv[l].rearrange("(b p) f -> p b f", p=128), writes=[dck])
                    ps4, dps4 = psA.next()

                    def fnc(e, ps4=ps4):
                        ins = None
                        for b in range(4):
                            ins = e.transpose(ps4[:, b * 128:(b + 1) * 128], ckt[:, b, :], identf)
                        return ins
                    P.op("tensor", fnc, [dck, dcst], [dps4])
                    P.op("vector", lambda e, ps4=ps4: e.tensor_copy(KCx[:], ps4[:]), [dps4], [dctx])
                    P.op("vector", lambda e: e.tensor_copy(VCA[0][:, :, 0:64], cvt[:, :, 0:64]), [dck], [dctx])
                    P.op("vector", lambda e: e.tensor_copy(VCA[1][:, :, 64:128], cvt[:, :, 64:128]), [dck], [dctx])
                pss = Ring.of(psA.t[0:4], psA.d[0:4])
                pso_ = Ring.of(psA.t[4:6], psA.d[4:6])
                psd_ = Ring.of(psA.t[6:8], psA.d[6:8])
                ptr = Ring(P, "apt", [128, 512], BF16, 6)
                denr = Ring(P, "aden", [128, 512], F32, 2); rdenr = Ring(P, "arden", [128, 512], F32, 2)
                ybr = Ring(P, "ayb", [128, 4, 128], BF16, 2)
                for i in range(nblk):
                    if dbg in ("attN", "attSN"):
                        break
                    pacc = [pso_.next(), psd_.next()]
                    for kvh in range(2):
                        rows = slice(kvh * 64, (kvh + 1) * 64)
                        po, dpo = pacc[kvh]
                        kt = []
                        if samp:
                            kt += [("c", j, None) for j in range(4)]
                            if i > 0:
                                kt.append(("l", i - 1, mwprev))
                            kt.append(("l", i, None))
                            if i < nblk - 1:
                                kt.append(("l", i + 1, mwnext))
                        else:
                            kt += [("l", j, None) for j in range(nblk)]
                        def issue_s(ki):
                            src, j, msk = kt[ki]
                            ks = KCx[rows, j * 128:(j + 1) * 128] if src == "c" else KN[rows, j * 128:(j + 1) * 128]
                            kd_ = [dctx] if src == "c" else [dKN, dVT]
                            ps, dps = pss.next()
                            mm_group(ps[:], [(ks, QN[rows, :, i * 128:(i + 1) * 128])], kd_ + [dQN], [dps])
                            return ps, dps
                        ahead = [issue_s(k_) for k_ in range(min(3, len(kt)))]
                        for ki, (src, j, msk) in enumerate(kt):
                            vsrc = VCA[kvh][:, j, :] if src == "c" else VA[kvh][:, j, :]
                            kd_ = [dctx] if src == "c" else [dKN, dVT]
                            ps, dps = ahead.pop(0)
                            if ki + 3 < len(kt):
                                ahead.append(issue_s(ki + 3))
                            pt, dpt = ptr.next()
                            P.op("scalar", lambda e, pt=pt, ps=ps: e.activation(pt[:], ps[:], AF.Exp, scale=0.125), [dps], [dpt])
                            if msk is not None:
                                P.op("vector", lambda e, pt=pt, msk=msk: e.tensor_tensor(pt[:].rearrange("p (a b) -> p a b", a=4), pt[:].rearrange("p (a b) -> p a b", a=4),
                                                                                       msk.unsqueeze(1).broadcast_to([128, 4, 128]), ALU.mult), [dpt, dcb], [dpt])
                            first, last = ki == 0, ki == len(kt) - 1

                            P.op("tensor", lambda e, po=po, vsrc=vsrc, pt=pt, first=first, last=last: e.matmul(po[:], vsrc, pt[:], start=first, stop=last), kd_ + [dpt, dcb], [dpo])
                    den, dden = denr.next()
                    for kvh in range(2):
                        po, dpo = pacc[kvh]
                        orow = slice(kvh * 64, (kvh + 1) * 64)
                        drow = slice((1 - kvh) * 64, (2 - kvh) * 64)
                        P.op("vector", lambda e, den=den, po=po, drow=drow: e.tensor_tensor(den[drow, :].rearrange("p (a b) -> p a b", a=4), po[drow, :].rearrange("p (a b) -> p a b", a=4),
                                                                                            sinkexp[drow, 4:8].unsqueeze(2).broadcast_to([64, 4, 128]), ALU.add), [dpo, dmod], [dden])
                    rden, drden = rdenr.next()
                    for kvh in range(2):
                        orow = slice(kvh * 64, (kvh + 1) * 64)
                        drow = slice((1 - kvh) * 64, (2 - kvh) * 64)
                        P.op("scalar", lambda e, rden=rden, den=den, orow=orow, drow=drow: e.activation(rden[orow, :], den[drow, :], AF.Ln), [dden], [drden])
                        P.op("scalar", lambda e, rden=rden, orow=orow: e.activation(rden[orow, :], rden[orow, :], AF.Exp, scale=-1.0), [drden], [drden])
                    yb, dyb = ybr.next()
                    for kvh in range(2):
                        po, dpo = pacc[kvh]
                        orow = slice(kvh * 64, (kvh + 1) * 64)
                        P.op("vector", lambda e, yb=yb, po=po, rden=rden, orow=orow: e.tensor_tensor(yb[orow, :, :].rearrange("p a b -> p (a b)"), po[orow, :], rden[orow, :], ALU.mult), [dpo, drden], [dyb])
                    P.dma("sync", YMIX[:, 2:6, s_ + i * 128:s_ + (i + 1) * 128], yb[:], reads=[dyb], writes=[dYMIX])
        if dbg and dbg.startswith("att"):
            break

        DNQh = DNQ.rearrange("c (h d) t -> d (c h) t", d=64)
        for (s_, n_, kind, si) in seqs:
            with P.phase():
                samp = kind == "s"
                NCH = n_ // 64
                V_ = lambda fn, r, w: P.op("vector", fn, r, w)
                A_ = lambda fn, r, w: P.op("scalar", fn, r, w)
                G_ = lambda fn, r, w: P.op("vector", fn, r, w)
                psr = Ring(P, "dps", [128, 512], F32, 8, "psum")

                def bcf(ap, n):
                    return ap.unsqueeze(2).broadcast_to([ap.shape[0], ap.shape[1], n])
                NP_ = NCH // 2
                HS = (slice(0, 64), slice(64, 128))
                abF = P.sbuf("abF", [128, NP_, 16], F32); abt = P.sbuf("abt", [128, NP_, 16], F32); dabt = Dep(); dabF = Dep()
                ABs = AB[s_:s_ + n_].rearrange("(g two t) k -> two t g k", two=2, t=64)
                for h in range(2):
                    P.dma("sync", abF[HS[h], :, :], ABs[h], reads=[dAB], writes=[dabF])
                for h in range(2):
                    P.op("vector", lambda e, h=h: e.tensor_copy(abt[HS[h], :, 0:4], abF[HS[h], :, 0:4]), [dabF], [dabt])
                    P.op("vector", lambda e, h=h: e.tensor_copy(abt[HS[h], :, 8:12], abF[HS[h], :, 8:12]), [dabF], [dabt])
                    for g in range(NP_):
                        P.op("scalar", lambda e, h=h, g=g: e.copy(abt[HS[h], g, 4:8], abF[HS[1 - h], NP_ - 1 - g, 4:8]), [dabF], [dabt])
                        P.op("scalar", lambda e, h=h, g=g: e.copy(abt[HS[h], g, 12:16], abF[HS[1 - h], NP_ - 1 - g, 12:16]), [dabF], [dabt])
                sc = {}
                for nm in ("g", "beta", "negg", "gc", "eg", "ed", "egt", "beg", "tmpa"):
                    sc[nm] = P.sbuf("dn_" + nm, [128, NP_, 8], F32)
                egtX = P.sbuf("dn_egtX", [64, NP_, 8], F32)
                dsc = Dep()
                g3 = sc["g"]
                V_(lambda e: e.tensor_tensor(sc["tmpa"][:], abt[:, :, 0:8], bct[:, 8:16].unsqueeze(1).broadcast_to([128, NP_, 8]), ALU.add), [dabt, dpv], [dsc])
                A_(lambda e: e.activation(sc["tmpa"][:], sc["tmpa"][:], AF.Exp), [dsc], [dsc])
                A_(lambda e: e.activation(sc["tmpa"][:], sc["tmpa"][:], AF.Ln, bias=1.0), [dsc], [dsc])
                V_(lambda e: e.tensor_tensor(g3[:], sc["tmpa"][:], negA2[:].unsqueeze(1).broadcast_to([128, NP_, 8]), ALU.mult), [dsc, dmod], [dsc])
                V_(lambda e: e.tensor_scalar(sc["negg"][:], g3[:], -1.0, None, ALU.mult), [dsc], [dsc])
                A_(lambda e: e.activation(sc["beta"][:], abt[:, :, 8:16], AF.Exp, scale=-1.0), [dabt], [dsc])
                V_(lambda e: e.tensor_scalar(sc["beta"][:], sc["beta"][:], 1.0, None, ALU.add), [dsc], [dsc])
                V_(lambda e: e.reciprocal(sc["beta"][:], sc["beta"][:]), [dsc], [dsc])
                pg, dpg = psr.next()
                pgF = pg[:, 0:NP_ * 4]; pgB = pg[:, NP_ * 4:NP_ * 8]

                def fng(e):
                    e.matmul(pgF, UCF2, g3[:, :, 0:4], start=True, stop=True)
                    return e.matmul(pgB, UCB2, g3[:, :, 4:8], start=True, stop=True)
                P.op("tensor", fng, [dsc, dcst], [dpg])
                for (pgX, lo) in ((pgF, 0), (pgB, 4)):
                    V_(lambda e, pgX=pgX, lo=lo: e.tensor_copy(sc["gc"][:, :, lo:lo + 4], pgX.rearrange("p (c k) -> p c k", k=4)), [dpg], [dsc])
                    A_(lambda e, pgX=pgX, lo=lo: e.activation(sc["eg"][:, :, lo:lo + 4], pgX.rearrange("p (c k) -> p c k", k=4), AF.Exp), [dpg], [dsc])
                pt_, dpt_ = psr.next()
                pt2 = pt_[:, 0:NP_ * 8]
                ptv = pt2.rearrange("p (c k) -> p c k", k=8)
                P.op("tensor", lambda e: e.matmul(pt2, bd64f, g3[:].rearrange("p c k -> p (c k)"), start=True, stop=True), [dsc, dcst], [dpt_])
                A_(lambda e: e.activation(sc["egt"][:], ptv, AF.Exp), [dpt_], [dsc])
                V_(lambda e: e.tensor_tensor(sc["ed"][:], ptv, sc["gc"][:], ALU.subtract), [dpt_, dsc], [dsc])
                A_(lambda e: e.activation(sc["ed"][:], sc["ed"][:], AF.Exp), [dsc], [dsc])
                V_(lambda e: e.tensor_tensor(sc["beg"][:], sc["beta"][:], sc["eg"][:], ALU.mult), [dsc], [dsc])
                A_(lambda e: e.copy(egtX[:], sc["egt"][64:128, :, :]), [dsc], [dsc])
                Sf = P.sbuf("Sf", [64, 8, 64], F32); Sb = P.sbuf("Sb", [128, 8, 64], BF16); dS = Dep(); dSb = Dep()
                if samp:
                    P.dma("sync", Sf[:], s0in[l].rearrange("k a b -> a k b"), writes=[dS])
                else:
                    V_(lambda e: e.memset(Sf[:], 0.0), [], [dS])
                A_(lambda e: e.copy(Sb[0:64], Sf[:]), [dS], [dSb])
                A_(lambda e: e.copy(Sb[64:128], Sf[:]), [dS], [dSb])

                def t64(name, dt, n=3, shape=(128, 8, 64)):
                    return Ring(P, name, list(shape), dt, n)
                qfr = t64("qf", F32, 3, (128, 2, 12, 64)); qbr_ = t64("qb", BF16, 3, (128, 2, 12, 64))
                NGr = t64("NG", F32, 2); Dr = t64("Dd", F32, 2); Er = t64("E", F32, 2); ERr = t64("ER", F32)
                Eir = t64("Ei", F32, 2); Esr = t64("Es", F32, 2)
                Xr = t64("X", F32, 3); Xbr = t64("Xb", BF16, 9); XTbr = t64("XTb", BF16, 9); TTbr = t64("TTb", BF16, 6); Tbr = t64("Tb", BF16); Rtr = t64("Rt", F32, 2); Rbr = t64("Rb", BF16); AINr = t64("AIN", BF16); AINTr = t64("AINT", BF16, 6)
                TTfr = t64("TTf", F32); TTb2r = t64("TTb2", BF16); TTber = t64("TTbe", BF16)
                KVr = t64("KV", BF16, 6, (128, 16, 64)); QEr = t64("QE", BF16, 6); Ur = t64("U", F32, 6); NWr = t64("NW", BF16, 6)
                VNfr = t64("VNf", F32, 2); VNr = t64("VN", BF16, 2); VNDr = t64("VND", BF16, 2); OBr = t64("OB", F32, 2); STr = t64("ST", F32, 2, (64, 8, 64))

                def per_k(out_fn, l_fn, r_fn, reads, writes, transpose=False, halves_=(0, 1)):
                    def fn(e):
                        ins = None
                        for h in halves_:
                            for k in range(8):
                                if transpose:
                                    ins = e.transpose(out_fn(h, k), l_fn(h, k), identb[HS[h], HS[h]])
                                else:
                                    ins = e.matmul(out_fn(h, k), l_fn(h, k), r_fn(h, k), start=True, stop=True)
                        return ins
                    P.op("tensor", fn, reads, writes)

                def v3(t):
                    return t[:, :].rearrange("p (k j) -> p k j", k=8)

                def vb(t):
                    return t[:, :].bitcast(BF16).rearrange("p (k j) -> p k j", j=64)
                hk = lambda t: (lambda h, k, t=t: t[HS[h], k, :])
                W_ = 3 if NP_ >= 3 else NP_

                def pre_gen(g, cx):
                        scb = lambda nm: bcf(sc[nm][:, g, :], 64)
                        chunk = lambda h, dr: (2 * g + h) if dr == 0 else (NCH - 1 - 2 * g - h)
                        qf, dqf = qfr.next()
                        for h in range(2):
                            for dr in range(2):
                                c = chunk(h, dr)
                                P.dma("sync", qf[HS[h], dr, :, :], DNQh[:, :, s_ + c * 64:s_ + c * 64 + 64], reads=[dDNQ], writes=[dqf])
                        qb, dqb = qbr_.next()
                        A_(lambda e, qb=qb, qf=qf: e.copy(qb[:], qf[:]), [dqf], [dqb])
                        Qk = lambda h, k, qb=qb: qb[HS[h], k // 4, k % 4, :]
                        Kk = lambda h, k, qb=qb: qb[HS[h], k // 4, 4 + k % 4, :]
                        Kf = lambda h, k, qf=qf: qf[HS[h], k // 4, 4 + k % 4, :]
                        pa, dpa = psr.next(); pqk, dpqk = psr.next(); pgr, dpgr = psr.next()
                        per_k(hk(v3(pa)), Kf, Kf, [dqf], [dpa])
                        per_k(hk(v3(pqk)), Qk, Kk, [dqb], [dpqk])
                        NG, dNG = NGr.next()
                        G_(lambda e, NG=NG: e.tensor_tensor(NG[:], U8, scb("negg"), ALU.mult), [dsc, dcst], [dNG])
                        P.op("tensor", lambda e, pgr=pgr, NG=NG: e.matmul(pgr[:, :], bd64f, NG[:].rearrange("p k j -> p (k j)"), start=True, stop=True), [dNG, dcst], [dpgr])
                        Dd, dD = Dr.next()
                        V_(lambda e, Dd=Dd, pgr=pgr: e.tensor_tensor(Dd[:], v3(pgr), scb("gc"), ALU.add), [dpgr, dsc], [dD])
                        V_(lambda e, Dd=Dd: e.tensor_scalar(Dd[:], Dd[:], 0.0, None, ALU.min), [dD], [dD])
                        E, dE = Er.next()
                        A_(lambda e, E=E, Dd=Dd: e.activation(E[:], Dd[:], AF.Exp), [dD], [dE])
                        ER, dER = ERr.next()
                        A_(lambda e, ER=ER, pgr=pgr: e.activation(ER[:], v3(pgr), AF.Exp, scale=-1.0), [dpgr], [dER])
                        Ei, dEi = Eir.next(); Es, dEs = Esr.next()
                        V_(lambda e, Ei=Ei, E=E: e.tensor_tensor(Ei[:].rearrange("p k j -> p (k j)"), E[:].rearrange("p k j -> p (k j)"), MI8, ALU.mult), [dE, dcst], [dEi])
                        G_(lambda e, Es=Es, E=E: e.tensor_tensor(Es[:].rearrange("p k j -> p (k j)"), E[:].rearrange("p k j -> p (k j)"), MS8, ALU.mult), [dE, dcst], [dEs])
                        G_(lambda e, Es=Es: e.tensor_tensor(Es[:], Es[:], scb("beta"), ALU.mult), [dEs, dsc], [dEs])
                        X, dX = Xr.next()
                        V_(lambda e, X=X, pa=pa, Es=Es: e.scalar_tensor_tensor(X[:].rearrange("p k j -> p (k j)"), pa[:, :], -1.0, Es[:].rearrange("p k j -> p (k j)"), ALU.mult, ALU.mult), [dpa, dEs], [dX])
                        AIN, dAIN = AINr.next()
                        V_(lambda e, AIN=AIN, pqk=pqk, Ei=Ei: e.tensor_tensor(AIN[:].rearrange("p k j -> p (k j)"), pqk[:, :], Ei[:].rearrange("p k j -> p (k j)"), ALU.mult), [dpqk, dEi], [dAIN])
                        yield
                        Xb, dXb = Xbr.next()
                        A_(lambda e, Xb=Xb, X=X: e.copy(Xb[:], X[:]), [dX], [dXb])
                        px, dpx = psr.next()
                        pxb = vb(px)
                        per_k(lambda h, k: pxb[HS[h], k, :], hk(Xb), None, [dXb, dcb], [dpx], transpose=True)
                        per_k(lambda h, k: pxb[HS[h], 8 + k, :], hk(AIN), None, [dAIN, dcb], [dpx], transpose=True)
                        XTb, dXTb = XTbr.next(); AINT, dAINT = AINTr.next()
                        V_(lambda e, XTb=XTb, pxb=pxb: e.tensor_copy(XTb[:], pxb[:, 0:8, :]), [dpx], [dXTb])
                        A_(lambda e, AINT=AINT, pxb=pxb: e.copy(AINT[:], pxb[:, 8:16, :]), [dpx], [dAINT])
                        yield
                        TTf, dTTf = TTfr.next(); TTb, dTTb = TTbr.next()
                        V_(lambda e, TTf=TTf, XTb=XTb: e.tensor_tensor(TTf[:].rearrange("p k j -> p (k j)"), XTb[:].rearrange("p k j -> p (k j)"), IDB8, ALU.add), [dXTb, dcst], [dTTf])
                        A_(lambda e, TTb=TTb, TTf=TTf: e.copy(TTb[:], TTf[:]), [dTTf], [dTTb])
                        Xc, dXc, XTc, dXTc = Xb, dXb, XTb, dXTb
                        for lev in range(4):
                            last = lev == 3
                            p2, dp2 = psr.next()
                            per_k(hk(v3(p2)), hk(XTc), hk(Xc), [dXc, dXTc], [dp2])
                            Xn, dXn = Xbr.next()
                            A_(lambda e, Xn=Xn, p2=p2: e.copy(Xn[:], v3(p2)), [dp2], [dXn])
                            if not last:
                                p3, dp3 = psr.next()
                                per_k(hk(v3(p3)), hk(Xc), hk(XTc), [dXc, dXTc], [dp3])
                                XTn, dXTn = XTbr.next()
                                A_(lambda e, XTn=XTn, p3=p3: e.copy(XTn[:], v3(p3)), [dp3], [dXTn])
                            p4, dp4 = psr.next()
                            per_k(hk(v3(p4)), hk(Xn), hk(TTb), [dXn, dTTb], [dp4])
                            V_(lambda e, TTf=TTf, p4=p4: e.tensor_tensor(TTf[:], TTf[:], v3(p4), ALU.add), [dp4, dTTf], [dTTf])
                            TTb, dTTb = TTbr.next()
                            A_(lambda e, TTb=TTb, TTf=TTf: e.copy(TTb[:], TTf[:]), [dTTf], [dTTb])
                            yield
                            Xc, dXc = Xn, dXn
                            if not last:
                                XTc, dXTc = XTn, dXTn
                        ptt, dptt = psr.next()
                        pttb = vb(ptt)
                        per_k(lambda h, k: pttb[HS[h], k, :], hk(TTb), None, [dTTb, dcb], [dptt], transpose=True)
                        Tb, dTb = Tbr.next()
                        A_(lambda e, Tb=Tb, pttb=pttb: e.copy(Tb[:], pttb[:, 0:8, :]), [dptt], [dTb])
                        Rt, dRt = Rtr.next()
                        V_(lambda e, Rt=Rt, TTf=TTf: e.scalar_tensor_tensor(Rt[:].rearrange("p k j -> p (k j)"), TTf[:].rearrange("p k j -> p (k j)"), -1.0, IDB8, ALU.mult, ALU.add), [dTTf, dcst], [dRt])
                        pr_, dpr_ = psr.next()
                        per_k(hk(v3(pr_)), hk(X), hk(TTf), [dX, dTTf], [dpr_])
                        Rb, dRb = Rbr.next()
                        V_(lambda e, Rb=Rb, Rt=Rt, pr_=pr_: e.tensor_tensor(Rb[:], Rt[:], v3(pr_), ALU.add), [dRt, dpr_], [dRb])
                        yield
                        pc_, dpc_ = psr.next()
                        per_k(hk(v3(pc_)), hk(Tb), hk(Rb), [dTb, dRb], [dpc_])
                        V_(lambda e, TTf=TTf, pc_=pc_: e.tensor_tensor(TTf[:], TTf[:], v3(pc_), ALU.add), [dpc_, dTTf], [dTTf])
                        yield
                        TTb2, dTTb2 = TTb2r.next(); TTbe, dTTbe = TTber.next()
                        G_(lambda e, TTb2=TTb2, TTf=TTf: e.tensor_tensor(TTb2[:], TTf[:], scb("beta"), ALU.mult), [dTTf, dsc], [dTTb2])
                        G_(lambda e, TTbe=TTbe, TTf=TTf: e.tensor_tensor(TTbe[:], TTf[:], scb("beg"), ALU.mult), [dTTf, dsc], [dTTbe])
                        KV, dKV = KVr.next()
                        pkv, dpkv = psr.next()
                        pkvb = vb(pkv)
                        per_k(lambda h, k: pkvb[HS[h], k, :], Kk, None, [dqb, dcb], [dpkv], transpose=True)
                        per_k(lambda h, k: pkvb[HS[h], 8 + k, :], lambda h, k, qb=qb: qb[HS[h], k // 4, 8 + k % 4, :], None, [dqb, dcb], [dpkv], transpose=True)
                        A_(lambda e, KV=KV, pkvb=pkvb: e.copy(KV[:], pkvb), [dpkv], [dKV])
                        QE, dQE = QEr.next()
                        G_(lambda e, QE=QE, qf=qf, ER=ER: e.tensor_tensor(QE[:].rearrange("p (a b) j -> p a b j", a=2), qf[:, :, 0:4, :], ER[:].rearrange("p (a b) j -> p a b j", a=2), ALU.mult), [dqf, dER], [dQE])
                        pu, dpu = psr.next()
                        per_k(hk(v3(pu)), hk(TTb2), lambda h, k, KV=KV: KV[HS[h], 8 + k, :], [dTTb2, dKV], [dpu])
                        U, dU = Ur.next()
                        A_(lambda e, U=U, pu=pu: e.copy(U[:], v3(pu)), [dpu], [dU])
                        yield
                        pw, dpw = psr.next()
                        per_k(hk(v3(pw)), hk(KV), hk(TTbe), [dTTbe, dKV], [dpw])
                        NW, dNW = NWr.next()
                        V_(lambda e, NW=NW, pw=pw: e.tensor_scalar(NW[:], v3(pw), -1.0, None, ALU.mult), [dpw], [dNW])
                        cx.update(dict(NW=NW, dNW=dNW, U=U, dU=dU, QE=QE, dQE=dQE, AINT=AINT, dAINT=dAINT, KV=KV, dKV=dKV))
                        yield

                def scan_step(g, cx, h):
                    NW = cx['NW']; dNW = cx['dNW']; U = cx['U']; dU = cx['dU']; QE = cx['QE']; dQE = cx['dQE']
                    AINT = cx['AINT']; dAINT = cx['dAINT']; KV = cx['KV']; dKV = cx['dKV']
                    hs = HS[h]
                    one = (h,)
                    pws, dpws = psr.next()
                    per_k(hk(v3(pws)), hk(NW), hk(Sb), [dNW, dSb], [dpws], halves_=one)
                    VNf, dVNf = VNfr.next(); VN, dVN = VNr.next(); VND, dVND = VNDr.next()
                    V_(lambda e, VNf=VNf, U=U, pws=pws: e.tensor_tensor(VNf[hs], U[hs], v3(pws)[hs], ALU.add), [dU, dpws], [dVNf])
                    yield
                    A_(lambda e, VN=VN, VNf=VNf: e.copy(VN[hs], VNf[hs]), [dVNf], [dVN])
                    yield
                    G_(lambda e, VND=VND, VNf=VNf: e.tensor_tensor(VND[hs], VNf[hs], bcf(sc["ed"][hs, g, :], 64), ALU.mult), [dVNf, dsc], [dVND])
                    yield
                    po_, dpo_ = psr.next()

                    def fno(e, po_=po_, QE=QE, AINT=AINT, VN=VN):
                        ins = None
                        for k in range(8):
                            e.matmul(v3(po_)[hs, k, :], QE[hs, k, :], Sb[hs, k, :], start=True, stop=False)
                            ins = e.matmul(v3(po_)[hs, k, :], AINT[hs, k, :], VN[hs, k, :], start=False, stop=True)
                        return ins
                    P.op("tensor", fno, [dQE, dSb, dAINT, dVN], [dpo_])
                    OB, dOB = OBr.next()
                    A_(lambda e, OB=OB, po_=po_: e.copy(OB[hs], v3(po_)[hs]), [dpo_], [dOB])
                    yield
                    for dr in range(2):
                        c = (2 * g + h) if dr == 0 else (NCH - 1 - 2 * g - h)
                        t_ = s_ + c * 64
                        P.dma("sync", OD[dr, t_:t_ + 64, :].rearrange("t (h v) -> t h v", h=4), OB[hs, dr * 4:dr * 4 + 4, :], reads=[dOB], writes=[dOD])
                    ST, dST = STr.next()
                    egs = sc["egt"][0:64, g, :] if h == 0 else egtX[:, g, :]
                    G_(lambda e, ST=ST: e.tensor_tensor(ST[:], Sf[:], bcf(egs, 64), ALU.mult), [dS, dsc], [dST])
                    yield
                    pS, dpS = psr.next()

                    def fns(e, pS=pS, KV=KV, VND=VND):
                        ins = None
                        for k in range(8):
                            ins = e.matmul(pS[0:64, k * 64:(k + 1) * 64], KV[hs, k, :], VND[hs, k, :], start=True, stop=True)
                        return ins
                    P.op("tensor", fns, [dKV, dVND], [dpS])
                    V_(lambda e, ST=ST, pS=pS: e.tensor_tensor(Sf[:], ST[:], pS[0:64, :].rearrange("p (k j) -> p k j", k=8), ALU.add), [dST, dpS], [dS])
                    yield
                    A_(lambda e: e.copy(Sb[0:64], Sf[:]), [dS], [dSb])
                    A_(lambda e: e.copy(Sb[64:128], Sf[:]), [dS], [dSb])
                    yield

                def scan_group(grp_, cxs_):
                    for g2 in grp_:
                        for h in range(2):
                            yield from scan_step(g2, cxs_[g2], h)
                prev = None
                for g0 in range(0, NP_, W_):
                    grp = list(range(g0, min(g0 + W_, NP_)))
                    cxs = {g2: {} for g2 in grp}
                    gens = [pre_gen(g2, cxs[g2]) for g2 in grp]
                    sg = scan_group(*prev) if prev is not None else None
                    alive = list(gens)
                    while alive:
                        for g_ in list(alive):
                            try:
                                next(g_)
                            except StopIteration:
                                alive.remove(g_)
                            if sg is not None:
                                try:
                                    next(sg)
                                except StopIteration:
                                    sg = None
                    if sg is not None:
                        run_lockstep([sg])
                    prev = (grp, cxs)
                run_lockstep([scan_group(*prev)])
                if not samp:
                    P.dma("sync", nst[si, l].rearrange("k a b -> a k b"), Sf[:], reads=[dS], writes=[dOUT])
        if dbg == "dn":
            break

        with P.phase():
            ofr = Ring(P, "cof", [128, 4, 256], F32, 3); obr2 = Ring(P, "cob2", [128, 4, 256], F32, 3)
            osr = Ring(P, "cos_", [128, 4, 256], F32, 3); sqr2 = Ring(P, "csq2", [128, 4, 256], F32, 3)
            ssr = Ring(P, "css", [128, 16], F32, 3); zr = Ring(P, "cz", [128, 2, 512], F32, 3)
            pcr2 = Ring(P, "pc2", [128, 512], F32, 4, "psum"); ydr = Ring(P, "cyd", [128, 2, 512], BF16, 3)
            def comb_gen(ti):
                t0 = ti * 512
                of, dof = ofr.next(); ob2, dob2 = obr2.next()
                P.dma("sync", of[:], OD[0, t0:t0 + 512, :].rearrange("(s p) f -> p s f", p=128), reads=[dOD], writes=[dof])
                P.dma("sync", ob2[:], OD[1, t0:t0 + 512, :].rearrange("(s p) f -> p s f", p=128), reads=[dOD], writes=[dob2])
                zt, dzt = zr.next()
                P.dma("sync", zt[:], PROJ[:, 18:20, t0:t0 + 512], reads=[dPROJ], writes=[dzt])
                o, do_ = osr.next()
                P.op("gpsimd", lambda e, o=o, of=of, ob2=ob2: e.tensor_tensor(o[:], of[:], ob2[:], ALU.add), [dof, dob2], [do_])
                yield
                sq, dsq = sqr2.next()
                P.op("scalar", lambda e, sq=sq, o=o: e.activation(sq[:], o[:], AF.Square), [do_], [dsq])
                yield
                ss, dss = ssr.next()
                P.op("vector", lambda e, ss=ss, sq=sq: e.tensor_reduce(ss[:], sq[:].rearrange("p s (h v) -> p (s h) v", h=4), AX.X, ALU.add), [dsq], [dss])
                yield
                P.op("scalar", lambda e, ss=ss: e.activation(ss[:], ss[:], AF.Sqrt, bias=EPS, scale=1.0 / 64), [dss], [dss])
                yield
                P.op("vector", lambda e, ss=ss: e.reciprocal(ss[:], ss[:]), [dss], [dss])
                yield
                P.op("vector", lambda e, o=o, ss=ss: e.tensor_tensor(o[:].rearrange("p s (h v) -> p (s h) v", h=4), o[:].rearrange("p s (h v) -> p (s h) v", h=4),
                                                                    ss[:].unsqueeze(2).broadcast_to([128, 16, 64]), ALU.mult), [do_, dss], [do_])
                yield
                P.op("gpsimd", lambda e, o=o: e.tensor_tensor(o[:].rearrange("p s (h v) -> p (s h) v", h=4), o[:].rearrange("p s (h v) -> p (s h) v", h=4),
                                                              bct[:, 16:80].unsqueeze(1).broadcast_to([128, 16, 64]), ALU.mult), [do_, dpv], [do_])
                yield
                P.op("scalar", lambda e, zt=zt: e.activation(zt[:], zt[:], AF.Silu), [dzt], [dzt])
                yield
                yd, dyd = ydr.next()
                for c in range(2):
                    ps, dps = pcr2.next()

                    def fnt(e, ps=ps, o=o, c=c):
                        ins = None
                        for sb in range(4):
                            ins = e.transpose(ps[:, sb * 128:(sb + 1) * 128], o[:, sb, c * 128:(c + 1) * 128], identf)
                        return ins
                    P.op("tensor", fnt, [do_, dcst], [dps])
                    P.op("vector", lambda e, yd=yd, ps=ps, zt=zt, c=c: e.tensor_tensor(yd[:, c, :], ps[:], zt[:, c, :], ALU.mult), [dps, dzt], [dyd])
                    yield
                P.dma("sync", YMIX[:, 6:8, t0:t0 + 512], yd[:], reads=[dyd], writes=[dYMIX])
            tis = list(range(NT // 512))
            for i0 in range(0, len(tis), 3):
                run_lockstep([comb_gen(ti) for ti in tis[i0:i0 + 3]])
        if dbg == "mix":
            break

        HF = DFF // 2
        NJ = HF // 128
        for hf in range(2):
            with P.phase():
                wgs = P.sbuf("wgs", [128, KC, HF], BF16); wus = P.sbuf("wus", [128, KC, HF], BF16)
                wds = P.sbuf("wds", [128, NJ, D], BF16); dwf = [Dep() for _ in range(4)]
                P.dma("gpsimd", wgs[:], wg[l].rearrange("(kc p) n -> p kc n", p=128)[:, :, hf * HF:(hf + 1) * HF], writes=[dwf[0]])
                P.dma("gpsimd", wus[:], wu[l].rearrange("(kc p) n -> p kc n", p=128)[:, :, hf * HF:(hf + 1) * HF], writes=[dwf[1]])
                P.dma("gpsimd", wds[:], wd[l, hf * HF:(hf + 1) * HF, :].rearrange("(j p) n -> p j n", p=128), writes=[dwf[2]])
                if hf == 0:
                    wos = P.sbuf("wos", [128, KC, D], BF16)
                    P.dma("gpsimd", wos[:], wout[l].rearrange("(kc p) n -> p kc n", p=128), writes=[dwf[3]])
                    ymr = Ring(P, "ym", [128, KC, 512], BF16, 2)
                    rings = {"sq": Ring(P, "sq", [128, KC, 512], BF16, 1), "psn": Ring(P, "psn", [128, 512], F32, 1, "psum"),
                             "rs": Ring(P, "rs", [128, 512], F32, 2), "tmp": Ring(P, "tmp", [128, 512], F32, 3)}
                    psw = Ring(P, "psw", [128, 512], F32, 2, "psum")
                xr = Ring(P, "xt", [128, KC, 512], F32, 2)
                hr = Ring(P, "h2", [128, KC, 512], BF16, 2)
                actr = Ring(P, "act", [128, NJ, 512], BF16, 1)
                sgr = Ring(P, "sg", [128, 512], F32, 2)
                psf = Ring(P, "psf", [128, 512], F32, 5, "psum")
                direct_out = (hf == 1 and l == depth - 1 and not dbg)
                if direct_out:
                    pso2 = Ring(P, "pso2", [128, 512], F32, 2, "psum")
                    yo_ = Ring(P, "yo_", [128, D], F32, 2)
                def ffn_pre(t0, seg, cx):
                    xt, dxt = xr.next()
                    P.dma("sync", xt[:], XT[:, :, t0:t0 + 512], reads=[dXT], writes=[dxt])
                    h2, dh2 = hr.next()
                    cx.update(dict(xt=xt, dxt=dxt, h2=h2, dh2=dh2))
                    if hf == 0:
                        ym, dym = ymr.next()
                        P.dma("sync", ym[:], YMIX[:, :, t0:t0 + 512], reads=[dYMIX], writes=[dym])
                        yield
                        for oc in range(KC):
                            ps, dps = psw.next()
                            mm_group(ps[:], [(wos[:, kc, oc * 128:(oc + 1) * 128], ym[:, kc, :]) for kc in range(KC)], [dym, dwf[3]], [dps])
                            P.op("vector", lambda e, xt=xt, ps=ps, oc=oc, seg=seg: e.scalar_tensor_tensor(xt[:, oc, :], ps[:], modv[:, 16 + oc, seg:seg + 1], xt[:, oc, :], ALU.mult, ALU.add), [dps, dmod, dxt], [dxt])
                            yield
                        yield from norm_to_h_gen(xt, dxt, h2, dh2, seg, A2, 24, rings)
                        P.dma("sync", H2[:, :, t0:t0 + 512], h2[:], reads=[dh2], writes=[dH2])
                    else:
                        P.dma("sync", h2[:], H2[:, :, t0:t0 + 512], reads=[dH2], writes=[dh2])
                    yield
                cxc = {}
                g_ = ffn_pre(tiles[0][0], tiles[0][1], cxc)
                advance(g_, 1000)
                for ti, (t0, seg) in enumerate(tiles):
                    xt, dxt, h2, dh2 = cxc["xt"], cxc["dxt"], cxc["h2"], cxc["dh2"]
                    cxn = {}
                    nxt = ffn_pre(tiles[ti + 1][0], tiles[ti + 1][1], cxn) if ti + 1 < len(tiles) else None
                    act, dact = actr.next()
                    for j in range(NJ):
                        pg_, dpg_ = psf.next(); pu_, dpu_ = psf.next()
                        mm_group(pg_[:], [(wgs[:, kc, j * 128:(j + 1) * 128], h2[:, kc, :]) for kc in range(KC)], [dh2, dwf[0]], [dpg_])
                        mm_group(pu_[:], [(wus[:, kc, j * 128:(j + 1) * 128], h2[:, kc, :]) for kc in range(KC)], [dh2, dwf[1]], [dpu_])
                        sg, dsg = sgr.next()
                        P.op("scalar", lambda e, sg=sg, pg_=pg_: e.activation(sg[:], pg_[:], AF.Silu), [dpg_], [dsg])
                        P.op("vector", lambda e, act=act, j=j, sg=sg, pu_=pu_: e.tensor_tensor(act[:, j, :], sg[:], pu_[:], ALU.mult), [dsg, dpu_], [dact])
                        advance(nxt, 2)
                    for oc in range(KC):
                        ps, dps = psf.next()
                        mm_group(ps[:], [(wds[:, j, oc * 128:(oc + 1) * 128], act[:, j, :]) for j in range(NJ)], [dact, dwf[2]], [dps])
                        P.op("vector", lambda e, xt=xt, ps=ps, oc=oc, seg=seg: e.scalar_tensor_tensor(xt[:, oc, :], ps[:], modv[:, 40 + oc, seg:seg + 1], xt[:, oc, :], ALU.mult, ALU.add), [dps, dmod, dxt], [dxt])
                        advance(nxt, 2)
                    if direct_out:
                        for sub in range(4):
                            tt0 = t0 + sub * 128
                            o, do = yo_.next()
                            for hh in range(2):
                                ps, dps = pso2.next()

                                def fnT(e, ps=ps, xt=xt, hh=hh, sub=sub):
                                    ins = None
                                    for j in range(4):
                                        ins = e.transpose(ps[:, j * 128:(j + 1) * 128], xt[:, hh * 4 + j, sub * 128:(sub + 1) * 128], identf)
                                    return ins
                                P.op("tensor", fnT, [dxt, dcst], [dps])
                                evac(o[:, hh * 512:(hh + 1) * 512], ps[:], [dps], [do])
                            dstap = ys[tt0:tt0 + 128, :] if tt0 < NS else yp[tt0 - NS:tt0 - NS + 128, :]
                            P.dma("sync", dstap, o[:], reads=[do], writes=[dOUT])
                    else:
                        P.dma("sync", XT[:, :, t0:t0 + 512], xt[:], reads=[dxt], writes=[dXT])
                    advance(nxt, 1000)
                    cxc = cxn

    if not dbg:
        P.finish([dOUT])
        return nc
    with P.phase():
        xr = Ring(P, "fx", [128, KC, 512], F32, 2)
        yo = Ring(P, "fy", [128, D], F32, 4)
        pst = Ring(P, "fps", [128, 512], F32, 8, "psum")
        for gi in range(NT // 512):
            a, da = xr.next()
            P.dma("sync", a[:], XT[:, :, gi * 512:(gi + 1) * 512], reads=[dXT], writes=[da])
            for sub in range(4):
                t0 = gi * 512 + sub * 128
                o, do = yo.next()
                for hh in range(2):
                    ps, dps = pst.next()

                    def fn(e, ps=ps, a=a, hh=hh, sub=sub):
                        ins = None
                        for j in range(4):
                            ins = e.transpose(ps[:, j * 128:(j + 1) * 128], a[:, hh * 4 + j, sub * 128:(sub + 1) * 128], identf)
                        return ins
                    P.op("tensor", fn, [da, dcst], [dps])
                    evac(o[:, hh * 512:(hh + 1) * 512], ps[:], [dps], [do])
                dstap = ys[t0:t0 + 128, :] if t0 < NS else yp[t0 - NS:t0 - NS + 128, :]
                P.dma("sync", dstap, o[:], reads=[do], writes=[dOUT])
    P.finish([dOUT])
    return nc


def _consts():
    c = np.zeros((128, CW), np.float32)
    c[:, 0:128] = np.eye(128)
    c[:, 128:256] = 1.0
    c[0:64, 256:320] = 1.0
    c[64:128, 320:384] = 1.0
    m = np.arange(128)
    partner = np.where((m % 32) < 16, m + 16, m - 16)
    c[partner, 384 + m] = 1.0
    b = np.arange(128)[:, None]; a = np.arange(128)[None, :]
    c[:, 512:640] = (b >= a)
    c[:, 640:768] = (b <= a)
    i = np.arange(64)[:, None]; j = np.arange(64)[None, :]
    for k in range(8):
        fwd = k < 4
        c[0:64, 768 + k * 64:768 + (k + 1) * 64] = (i > j) if fwd else (i < j)
        c[0:64, 1280 + k * 64:1280 + (k + 1) * 64] = (i >= j) if fwd else (i <= j)
        c[0:64, 1792 + k * 64:1792 + (k + 1) * 64] = np.eye(64)
    c[0:64, 2304:2368] = (i <= j)
    c[0:64, 2368:2432] = (i >= j)
    for k in range(8):
        c[0:64, 2432 + k * 64:2432 + (k + 1) * 64] = (i <= j) if k < 4 else (i >= j)
    c[64:128, 768:2304] = c[0:64, 768:2304]
    c[64:128, 2432:2944] = c[0:64, 2432:2944]
    c[0:64, 2944:3008] = (i <= j); c[64:128, 3008:3072] = (i <= j)
    c[0:64, 3072:3136] = (i >= j); c[64:128, 3136:3200] = (i >= j)
    t = np.arange(4096)
    p = np.arange(128)
    d = p % 64
    jj = (d % 16).astype(np.float32)
    inv = (1.0 / (np.float32(10000.0) ** (jj / np.float32(16.0)))).astype(np.float32)
    pos = np.where((d < 32)[:, None], (t // 64)[None, :], (t % 64)[None, :]).astype(np.float32)
    ang = (pos * inv[:, None]).astype(np.float32)
    cos = np.cos(ang).astype(np.float32)
    sgn = np.where((d % 32) < 16, -1.0, 1.0).astype(np.float32)
    sin = (np.sin(ang) * sgn[:, None]).astype(np.float32)
    return c, cos, sin


def _col_perm():
    perm = np.arange(NCOLS)
    for c in range(4):
        for half, h in ((0, c), (1, 4 + c)):
            perm[768 + c * 128 + half * 64:768 + c * 128 + half * 64 + 64] = 768 + h * 64 + np.arange(64)
    return perm


def _row_perm():
    perm = np.arange(D)
    for c in range(4):
        for half, h in ((0, c), (1, 4 + c)):
            perm[256 + c * 128 + half * 64:256 + c * 128 + half * 64 + 64] = 256 + h * 64 + np.arange(64)
    return perm


def host_prep(inp, depth):
    f = lambda a: np.ascontiguousarray(np.asarray(a, dtype=np.float32))
    cst, cos, sin = _consts()
    cp, rp = _col_perm(), _row_perm()
    shared = {
        "win": f(np.asarray(inp["w_in"])[:depth][:, :, cp]),
        "wout": f(np.asarray(inp["w_out"])[:depth][:, rp, :]),
        "adaw": f(np.asarray(inp["ada_w"])[:depth]),
        "wg": f(np.asarray(inp["w_gate"])[:depth]), "wu": f(np.asarray(inp["w_up"])[:depth]), "wd": f(np.asarray(inp["w_down"])[:depth]),
        "cst": cst, "ropec": cos, "ropes": sin,
    }
    pv = np.zeros((depth, 128, 90), np.float32); bc = np.zeros((depth, 128, 96), np.float32)
    for l in range(depth):
        pv[l, :, 0:48] = np.asarray(inp["ada_b"])[l].reshape(48, 128).T
        pv[l, :, 48:56] = np.asarray(inp["norm1_g"])[l].reshape(8, 128).T
        pv[l, :, 56:64] = np.asarray(inp["norm2_g"])[l].reshape(8, 128).T
        scw = np.asarray(inp["sc_conv_w"])[l]; dnw = np.asarray(inp["dn_conv_w"])[l]
        for k in range(3):
            pv[l, :, 64 + k * 2:64 + k * 2 + 2] = scw[k].reshape(2, 128).T
            pv[l, :, 70 + k * 6:70 + k * 6 + 6] = dnw[k].reshape(6, 128).T
        pv[l, :, 88] = np.tile(np.asarray(inp["q_norm_g"])[l], 2)
        pv[l, :, 89] = np.tile(np.asarray(inp["k_norm_g"])[l], 2)
        bc[l, :, 0:8] = np.asarray(inp["dn_A_log"])[l].reshape(8)[None, :]
        bc[l, :, 8:16] = np.asarray(inp["dn_dt_bias"])[l].reshape(8)[None, :]
        bc[l, :, 16:80] = np.asarray(inp["dn_norm_g"])[l][None, :]
        sk = np.asarray(inp["attn_sink"])[l]
        bc[l, 0:64, 88:92] = sk[0:4][None, :]
        bc[l, 64:128, 88:92] = sk[4:8][None, :]
        bc[l, 0:64, 92:96] = sk[4:8][None, :]
        bc[l, 64:128, 92:96] = sk[0:4][None, :]
    shared["pv"] = pv; shared["bcp"] = bc
    return shared


def core_inputs(inp, shared, core, NS, depth, nsamp):
    f = lambda a: np.ascontiguousarray(np.asarray(a, dtype=np.float32))
    b = core % nsamp
    m = dict(shared)
    m["xs"] = f(np.asarray(inp["x_sample"])[b, :NS])
    m["xp"] = f(np.asarray(inp["x_prompt"])[2 * core:2 * core + 2].reshape(512, D))
    m["ck"] = f(np.asarray(inp["cache_k"])[b, :depth].reshape(depth, 512, 128))
    m["cv"] = f(np.asarray(inp["cache_v"])[b, :depth].reshape(depth, 512, 128))
    m["s0"] = f(np.asarray(inp["state_delta"])[b, :depth].reshape(depth, 8, 64, 64))
    cT = np.zeros((128, 16), np.float32)
    cT[:, 0::2] = np.asarray(inp["c"])[b].reshape(8, 128).T
    cT[:, 1::2] = np.asarray(inp["c_ctx"]).reshape(8, 128).T
    m["cT"] = cT
    return m


_NC_CACHE = {}


def kernel(**inputs):
    NS, depth, ncores = 4096, 2, 8
    if "full" not in _NC_CACHE:
        _NC_CACHE["full"] = build(NS, depth)
    nc = _NC_CACHE["full"]
    shared = host_prep(inputs, depth)
    in_maps = [core_inputs(inputs, shared, c, NS, depth, 4) for c in range(ncores)]
    res = run_bass_kernel_spmd(nc, in_maps, core_ids=list(range(ncores)))
    R = res.results
    y_p = np.concatenate([np.asarray(R[c]["yp"]).reshape(2, 256, D) for c in range(8)], 0)
    y_s = np.stack([np.asarray(R[c]["ys"]) for c in range(4)], 0)
    nk = np.concatenate([np.asarray(R[c]["nk"]).reshape(2, depth, 256, 2, 64) for c in range(8)], 0)
    nv = np.concatenate([np.asarray(R[c]["nv"]).reshape(2, depth, 256, 2, 64) for c in range(8)], 0)
    ns = np.concatenate([np.asarray(R[c]["nst"]).reshape(2, depth, 2, 4, 64, 64) for c in range(8)], 0)
    return (y_p.astype(np.float32), y_s.astype(np.float32), nk.astype(np.float32), nv.astype(np.float32), ns.astype(np.float32))
```

```python
from contextlib import ExitStack
import numpy as np
import concourse.bass as bass
import concourse.mybir as mybir
from concourse.bass_utils import run_bass_kernel_spmd

F32 = mybir.dt.float32
BF16 = mybir.dt.bfloat16
AF = mybir.ActivationFunctionType
ALU = mybir.AluOpType
AX = mybir.AxisListType


class Dep:
    __slots__ = ("w", "r", "x")

    def __init__(self):
        self.w = None
        self.r = []
        self.x = False


class _Rec:
    def __init__(self):
        self.calls = []

    def __getattr__(self, name):
        def f(*a, **k):
            self.calls.append((name, a, k))
            return self
        return f


def _replay_calls(calls):
    def fn(e):
        ins = None
        for name, a, k in calls:
            ins = getattr(e, name)(*a, **k)
        return ins
    return fn


class Prog:
    CE = ("tensor", "vector", "scalar", "gpsimd")
    NDMA = {"sync": 12, "gpsimd": 6, "scalar": 4}

    def __init__(self, nc):
        self.nc = nc
        self.stack = ExitStack()
        self.ops = {e: [] for e in ("tensor", "vector", "scalar", "gpsimd", "sync")}
        self.sem = {}
        self.cnt = {}
        self.seen = {e: {} for e in self.ops}
        for e in self.CE:
            self.sem[e] = self.stack.enter_context(nc.semaphore("s_" + e))
            self.cnt[e] = 0
        self.dsem = {}
        self.dval = {}
        self.dnext = {}
        for q, n in self.NDMA.items():
            self.dsem[q] = [self.stack.enter_context(nc.semaphore("d_%s%d" % (q, i))) for i in range(n)]
            self.dnext[q] = 0
        for q in self.dsem:
            for s in self.dsem[q]:
                self.dval[id(s)] = 0
        self.n_alloc = 0

    def sbuf(self, name, shape, dtype):
        self.n_alloc += 1
        return self.stack.enter_context(self.nc.sbuf_tensor("%s_s%d" % (name, self.n_alloc), list(shape), dtype))

    def psum(self, name, shape, dtype):
        self.n_alloc += 1
        return self.stack.enter_context(self.nc.psum_tensor("%s_p%d" % (name, self.n_alloc), list(shape), dtype))

    def dep(self):
        return Dep()

    def deps(self, n):
        return [Dep() for _ in range(n)]

    def _collect(self, eng, reads, writes):
        toks = []
        for d in reads:
            if d.w is not None:
                toks.append(d.w)
        for d in writes:
            if d.w is not None:
                toks.append(d.w)
            toks.extend(d.r)
        seen = self.seen[eng]
        waits = {}
        for (s, v, src) in toks:
            if src == eng and eng == "tensor":
                continue
            k = id(s)
            if seen.get(k, 0) >= v:
                continue
            if k not in waits or waits[k][1] < v:
                waits[k] = (s, v)
        for k, (s, v) in waits.items():
            seen[k] = v
        return list(waits.values())

    def _commit(self, tok, reads, writes):
        for d in reads:
            d.r.append(tok)
        for d in writes:
            d.w = tok
            d.r = []

    max_ops = None
    n_ops = 0
    log = []

    def _skip(self, desc):
        Prog.n_ops += 1
        if Prog.max_ops is not None and Prog.n_ops > Prog.max_ops:
            return True
        Prog.log.append(desc)
        return False

    def op(self, eng, fn, reads=(), writes=()):
        if self._skip((eng,)):
            return None
        writes = list(writes) + [d for d in reads if d.x]
        reads = [d for d in reads if not d.x]
        waits = self._collect(eng, reads, writes)
        self.cnt[eng] += 1
        tok = (self.sem[eng], self.cnt[eng], eng)
        rec = _Rec()
        fn(rec)
        self.ops[eng].append((waits, _replay_calls(rec.calls), self.sem[eng], 1))
        self._commit(tok, reads, writes)
        return tok

    def dma(self, q, out, in_, reads=(), writes=(), **kw):
        if self._skip(("dma_" + q,)):
            return None
        pool = self.dsem[q]
        s = pool[self.dnext[q] % len(pool)]
        self.dnext[q] += 1
        waits = self._collect(q, reads, writes)
        prev = self.dval[id(s)]
        if prev > 0 and self.seen[q].get(id(s), 0) < prev:
            waits.append((s, prev))
            self.seen[q][id(s)] = prev
        self.dval[id(s)] = prev + 16
        tok = (s, prev + 16, "dma_" + q)
        self.ops[q].append((waits, lambda e: e.dma_start(out=out, in_=in_, **kw), s, 16))
        self._commit(tok, reads, writes)
        return tok

    def wait_on(self, eng, deps_):
        waits = self._collect(eng, deps_, ())
        self.ops[eng].append((waits, None, None, 0))

    def flush(self):
        for q in self.dsem:
            waits = []
            for s in self.dsem[q]:
                v = self.dval[id(s)]
                if v > 0 and self.seen["sync"].get(id(s), 0) < v:
                    waits.append((s, v))
                    self.seen["sync"][id(s)] = v
            if waits:
                self.ops["sync"].append((waits, None, None, 0))
        nc = self.nc

        def replay(name):
            def run(e):
                for waits, fn, s, inc in self.ops[name]:
                    for (ws, wv) in waits:
                        e.wait_ge(ws, wv)
                    if fn is not None:
                        ins = fn(e)
                        ins.then_inc(s, inc)
            return run

        with nc.Block() as block:
            block.tensor(replay("tensor"))
            block.vector(replay("vector"))
            block.scalar(replay("scalar"))
            block.gpsimd(replay("gpsimd"))
            block.sync(replay("sync"))
        for k in self.ops:
            self.ops[k] = []

    def phase(self):
        prog = self

        class _Ph:
            def __enter__(s):
                s.saved = prog.stack
                prog.stack = ExitStack()
                return prog

            def __exit__(s, *a):
                if a[0] is None:
                    prog.flush()
                prog.stack.close()
                prog.stack = s.saved
                return False
        return _Ph()

    def finish(self, out_deps=()):
        self.wait_on("sync", out_deps)
        self.flush()
        self.stack.close()


class Ring:
    def __init__(self, P, name, shape, dtype, n, space="sbuf"):
        mk = P.sbuf if space == "sbuf" else P.psum
        self.t = [mk("%s_%d" % (name, i), shape, dtype) for i in range(n)]
        self.d = [P.dep() for _ in range(n)]
        if space != "sbuf":
            for d in self.d:
                d.x = True
        self.i = 0

    def next(self):
        k = self.i % len(self.t)
        self.i += 1
        return self.t[k], self.d[k]

    @classmethod
    def of(cls, tiles, deps):
        r = cls.__new__(cls)
        r.t = list(tiles); r.d = list(deps); r.i = 0
        return r


def run_lockstep(gens):
    alive = list(gens)
    while alive:
        for g in list(alive):
            try:
                next(g)
            except StopIteration:
                alive.remove(g)


D = 1024
KC = 8
NCOLS = 2576
DFF = 2816
EPS = 1e-6
CW = 3200


def build(NS=4096, depth=2, dbg=False):
    NT = NS + 512
    nc = bass.Bass("TRN2", target_bir_lowering=False)

    def din(name, shape, dt=F32):
        return nc.dram_tensor(name, list(shape), dt, kind="ExternalInput").ap()

    def dout(name, shape, dt=F32):
        return nc.dram_tensor(name, list(shape), dt, kind="ExternalOutput").ap()

    def dint(name, shape, dt=F32):
        if dbg:
            return nc.dram_tensor(name, list(shape), dt, kind="ExternalOutput").ap()
        return nc.dram_tensor(name, list(shape), dt).ap()

    xs = din("xs", [NS, D]); xp = din("xp", [512, D])
    ck = din("ck", [depth, 512, 128]); cv = din("cv", [depth, 512, 128])
    s0in = din("s0", [depth, 8, 64, 64])
    cTin = din("cT", [128, 16])
    win = din("win", [depth, D, NCOLS]); wout = din("wout", [depth, D, D])
    adaw = din("adaw", [depth, D, 6 * D])
    wg = din("wg", [depth, D, DFF]); wu = din("wu", [depth, D, DFF]); wd = din("wd", [depth, DFF, D])
    pvin = din("pv", [depth, 128, 90]); bcin = din("bcp", [depth, 128, 96])
    cstin = din("cst", [128, CW]); ropec = din("ropec", [128, 4096]); ropes = din("ropes", [128, 4096])
    ys = dout("ys", [NS, D]); yp = dout("yp", [512, D])
    nk = dout("nk", [2, depth, 256, 128]); nv = dout("nv", [2, depth, 256, 128])
    nst = dout("nst", [2, depth, 8, 64, 64])

    XT = dint("XT", [KC, 128, NT]).rearrange("c p t -> p c t")
    PROJ = dint("PROJ", [20, 128, NT]).rearrange("c p t -> p c t")
    AB = dint("AB", [NT, 16])
    DNQ = dint("DNQ", [6, 128, NT])
    YMIX = dint("YMIX", [KC, 128, NT], BF16).rearrange("c p t -> p c t")
    H2 = dint("H2", [KC, 128, NT], BF16).rearrange("c p t -> p c t")
    OD = dint("OD", [2, NT, 256])
    dXT, dPROJ, dAB, dDNQ, dYMIX, dH2, dOD = [Dep() for _ in range(7)]
    dOUT = Dep()

    P = Prog(nc)
    seqs = [(0, NS, "s", 0), (NS, 256, "p", 0), (NS + 256, 256, "p", 1)]
    tiles = [(t0, 0) for t0 in range(0, NS, 512)] + [(NS, 1)]

    cst = P.sbuf("cst", [128, CW], F32); dcst = Dep()
    P.dma("sync", cst[:], cstin, writes=[dcst])
    identf = cst[:, 0:128]
    cb = P.sbuf("cb", [128, 6 * 128], BF16); dcb = Dep()
    P.op("vector", lambda e: e.tensor_copy(cb[:], cst[:, 0:768]), [dcst], [dcb])
    identb = cb[:, 0:128]; onesb = cb[:, 128:256]; bd64b = cb[:, 256:384]; rpermb = cb[:, 384:512]
    mwprev = cb[:, 512:640]; mwnext = cb[:, 640:768]
    MS8 = cst[:, 768:1280]; MI8 = cst[:, 1280:1792]; IDB8 = cst[:, 1792:2304]
    U8 = cst[:, 2432:2944].rearrange("p (k j) -> p k j", k=8)
    UCF2 = cst[:, 2944:3072]; UCB2 = cst[:, 3072:3200]; bd64f = cst[:, 256:384]
    UCF = cst[0:64, 2304:2368]; UCB = cst[0:64, 2368:2432]
    onesf64 = cst[0:64, 128:192]
    modv = P.sbuf("modv", [128, 48, 2], F32); dmod = Dep()
    A1 = P.sbuf("A1", [128, 8, 2], F32); A2 = P.sbuf("A2", [128, 8, 2], F32)
    pvt = P.sbuf("pvt", [128, 90], F32); bct = P.sbuf("bct", [128, 96], F32); dpv = Dep()
    sinkexp = P.sbuf("sinkexp", [128, 8], F32)
    negA = P.sbuf("negA", [64, 8], F32); negA2 = P.sbuf("negA2", [128, 8], F32)
    PVO = {"adab": 0, "n1": 48, "n2": 56, "scw": 64, "dnw": 70, "qg": 88, "kg": 89}
    BCO = {"alog": 0, "dtb": 8, "dng": 16, "sink": 88}
    P.flush()

    def mm_group(out, pairs, reads, writes):
        n = len(pairs)

        def fn(e):
            ins = None
            for i, (l_, r_) in enumerate(pairs):
                ins = e.matmul(out, l_, r_, start=(i == 0), stop=(i == n - 1))
            return ins
        P.op("tensor", fn, reads, writes)

    evac_i = [0]

    def evac(out, in_, reads, writes):
        evac_i[0] += 1
        if evac_i[0] % 2:
            P.op("vector", lambda e: e.tensor_copy(out, in_), reads, writes)
        else:
            P.op("scalar", lambda e: e.copy(out, in_), reads, writes)

    with P.phase():
        xin = Ring(P, "xin", [128, D], F32, 4)
        xo = Ring(P, "xo", [128, KC, 512], F32, 2)
        pst = Ring(P, "pst", [128, 512], F32, 8, "psum")
        for gi in range(NT // 512):
            o, do = xo.next()
            for sub in range(4):
                t0 = gi * 512 + sub * 128
                src = xs[t0:t0 + 128, :] if t0 < NS else xp[t0 - NS:t0 - NS + 128, :]
                a, da = xin.next()
                P.dma("sync", a[:], src, writes=[da])
                for hh in range(2):
                    ps, dps = pst.next()

                    def fn(e, ps=ps, a=a, hh=hh):
                        ins = None
                        for j in range(4):
                            c = hh * 4 + j
                            ins = e.transpose(ps[:, j * 128:(j + 1) * 128], a[:, c * 128:(c + 1) * 128], identf)
                        return ins
                    P.op("tensor", fn, [da, dcst], [dps])
                    evac(o[:, hh * 4:(hh + 1) * 4, sub * 128:(sub + 1) * 128], ps[:].rearrange("p (a b) -> p a b", a=4), [dps], [do])
            P.dma("sync", XT[:, :, gi * 512:(gi + 1) * 512], o[:], reads=[do], writes=[dXT])

    for l in range(depth):
        with P.phase():
            P.dma("sync", pvt[:], pvin[l], writes=[dpv])
            P.dma("sync", bct[:], bcin[l], writes=[dpv])
            ct = P.sbuf("ct", [128, 16], F32); dct = Dep()
            P.dma("sync", ct[:], cTin, writes=[dct])
            sil = P.sbuf("sil", [128, 16], BF16); dsil = Dep()
            P.op("scalar", lambda e: e.activation(sil[:], ct[:], AF.Silu), [dct], [dsil])
            war = Ring(P, "wa", [128, KC, 768], BF16, 3)
            psm = P.psum("psm", [128, 512], F32); dpsm = Dep()
            awv = adaw[l].rearrange("(kc p) n -> p kc n", p=128)
            for g in range(8):
                wa, dwa = war.next()
                for hh in range(2):
                    P.dma("gpsimd", wa[:, :, hh * 384:(hh + 1) * 384], awv[:, :, g * 768 + hh * 384:g * 768 + (hh + 1) * 384], writes=[dwa])
                for fcl in range(6):
                    fc = g * 6 + fcl
                    mm_group(psm[:, fc * 2:fc * 2 + 2],
                             [(wa[:, kc, fcl * 128:(fcl + 1) * 128], sil[:, kc * 2:kc * 2 + 2]) for kc in range(KC)],
                             [dwa, dsil], [dpsm])
            P.op("vector", lambda e: e.tensor_tensor(modv[:], psm[:, 0:96].rearrange("p (a b) -> p a b", b=2),
                                                      pvt[:, 0:48].unsqueeze(2).broadcast_to([128, 48, 2]), ALU.add), [dpsm, dpv], [dmod])
            P.op("vector", lambda e: e.scalar_tensor_tensor(A1[:], modv[:, 8:16, :], 1.0, pvt[:, 48:56].unsqueeze(2).broadcast_to([128, 8, 2]), ALU.add, ALU.mult), [dmod, dpv], [dmod])
            P.op("vector", lambda e: e.scalar_tensor_tensor(A2[:], modv[:, 32:40, :], 1.0, pvt[:, 56:64].unsqueeze(2).broadcast_to([128, 8, 2]), ALU.add, ALU.mult), [dmod, dpv], [dmod])
            P.op("scalar", lambda e: e.activation(sinkexp[:], bct[:, 88:96], AF.Exp), [dpv], [dmod])
            P.op("scalar", lambda e: e.activation(negA[:], bct[0:64, 0:8], AF.Exp), [dpv], [dmod])
            P.op("vector", lambda e: e.tensor_scalar(negA[:], negA[:], -1.0, None, ALU.mult), [dmod], [dmod])
            P.op("scalar", lambda e: e.activation(negA2[:], bct[:, 0:8], AF.Exp), [dpv], [dmod])
            P.op("vector", lambda e: e.tensor_scalar(negA2[:], negA2[:], -1.0, None, ALU.mult), [dmod], [dmod])

        def norm_to_h_gen(xt, dxt, hT, dh, seg, A_, B0, rings, T=512):
            sq, dsq = rings["sq"].next()
            P.op("scalar", lambda e: e.activation(sq[:, :, 0:T], xt[:, :, 0:T], AF.Square), [dxt], [dsq])
            yield
            psn, dpsn = rings["psn"].next()
            mm_group(psn[:, 0:T], [(onesb, sq[:, kc, 0:T]) for kc in range(KC)], [dsq, dcb], [dpsn])
            yield
            rs, drs = rings["rs"].next()
            P.op("scalar", lambda e: e.activation(rs[:, 0:T], psn[:, 0:T], AF.Ln, bias=EPS, scale=1.0 / D), [dpsn], [drs])
            yield
            P.op("scalar", lambda e: e.activation(rs[:, 0:T], rs[:, 0:T], AF.Exp, scale=-0.5), [drs], [drs])
            yield
            for kc in range(KC):
                tmp, dtmp = rings["tmp"].next()
                P.op("vector", lambda e, kc=kc, tmp=tmp: e.tensor_tensor(tmp[:, 0:T], xt[:, kc, 0:T], rs[:, 0:T], ALU.mult), [dxt, drs], [dtmp])
                yield
                P.op("scalar", lambda e, kc=kc, tmp=tmp: e.activation(hT[:, kc, 0:T], tmp[:, 0:T], AF.Identity,
                                                                      bias=modv[:, B0 + kc, seg:seg + 1], scale=A_[:, kc, seg:seg + 1]), [dtmp, dmod], [dh])
                yield


        def norm_to_h(*a, **k):
            for _ in norm_to_h_gen(*a, **k):
                pass

        def advance(gen, n):
            if gen is None:
                return
            for _ in range(n):
                try:
                    next(gen)
                except StopIteration:
                    return

        with P.phase():
            wsb = P.sbuf("wsb", [128, KC, NCOLS], BF16); dw = [Dep() for _ in range(6)]
            wv = win[l].rearrange("(kc p) n -> p kc n", p=128)
            for g in range(6):
                c0, c1 = g * 512, min((g + 1) * 512, NCOLS)
                P.dma("gpsimd", wsb[:, :, c0:c1], wv[:, :, c0:c1], writes=[dw[g]])
            rings = {"sq": Ring(P, "sq", [128, KC, 512], BF16, 2), "psn": Ring(P, "psn", [128, 512], F32, 1, "psum"),
                     "rs": Ring(P, "rs", [128, 512], F32, 2), "tmp": Ring(P, "tmp", [128, 512], F32, 3)}
            xr = Ring(P, "xt", [128, KC, 512], F32, 2)
            hr = Ring(P, "hT", [128, KC, 512], BF16, 2)
            pso = Ring(P, "pso", [128, 512], F32, 4, "psum")
            psab = Ring(P, "psab", [128, 512], F32, 1, "psum")
            stg = Ring(P, "stg", [128, 4, 512], F32, 2)
            abs_ = Ring(P, "abst", [128, 4, 16], F32, 2)
            def proj_pre(t0, seg):
                xt, dxt = xr.next()
                P.dma("sync", xt[:], XT[:, :, t0:t0 + 512], reads=[dXT], writes=[dxt])
                hT, dh = hr.next()
                return hT, dh, norm_to_h_gen(xt, dxt, hT, dh, seg, A1, 0, rings)
            cur = proj_pre(*tiles[0])
            advance(cur[2], 1000)
            for ti, (t0, seg) in enumerate(tiles):
                hT, dh, _ = cur
                nxt = proj_pre(*tiles[ti + 1]) if ti + 1 < len(tiles) else None
                for og in range(5):
                    st, dst = stg.next()
                    for j in range(4):
                        oc = og * 4 + j
                        ps, dps = pso.next()
                        mm_group(ps[:], [(wsb[:, kc, oc * 128:(oc + 1) * 128], hT[:, kc, :]) for kc in range(KC)], [dh] + dw, [dps])
                        evac(st[:, j, :], ps[:], [dps], [dst])
                        if nxt is not None:
                            advance(nxt[2], 1)
                    P.dma("sync", PROJ[:, og * 4:og * 4 + 4, t0:t0 + 512], st[:], reads=[dst], writes=[dPROJ])
                pa, dpa = psab.next()
                for sub in range(4):
                    mm_group(pa[:, sub * 16:(sub + 1) * 16], [(hT[:, kc, sub * 128:(sub + 1) * 128], wsb[:, kc, 2560:2576]) for kc in range(KC)], [dh] + dw, [dpa])
                ab, dab = abs_.next()
                evac(ab[:], pa[:, 0:64].rearrange("p (a b) -> p a b", b=16), [dpa], [dab])
                P.dma("sync", AB[t0:t0 + 512].rearrange("(s p) k -> p s k", p=128), ab[:], reads=[dab], writes=[dAB])
                if nxt is not None:
                    advance(nxt[2], 1000)
                cur = nxt
        if dbg == "proj":
            break

        with P.phase():
            inr = Ring(P, "cin", [128, 6, 514], F32, 4)
            ur = Ring(P, "cu", [128, 2, 514], F32, 2)
            accr = Ring(P, "cacc", [128, 512], F32, 18)
            yr = Ring(P, "cy", [128, 2, 512], BF16, 2)
            sr = Ring(P, "csil", [128, 512], F32, 10)
            sqr = Ring(P, "csq", [128, 512], BF16, 10)
            pcr = Ring(P, "pcr", [128, 512], F32, 6, "psum")
            rr = Ring(P, "crs", [128, 512], F32, 10)
            obr = Ring(P, "cob", [128, 6, 512], F32, 2)

            def load_halo(c0, s_, n_, t0, T):
                a, da = inr.next()
                lo = t0 - 1 if t0 > 0 else 0
                hi = t0 + T + 1 if t0 + T < n_ else n_
                if t0 == 0:
                    P.op("vector", lambda e, a=a: e.memset(a[:, :, 0:1], 0.0), [], [da])
                if t0 + T >= n_:
                    P.op("vector", lambda e, a=a: e.memset(a[:, :, T + 1:T + 2], 0.0), [], [da])
                P.dma("sync", a[:, :, (lo - (t0 - 1)):(hi - (t0 - 1))], PROJ[:, c0:c0 + 6, s_ + lo:s_ + hi], reads=[dPROJ], writes=[da])
                return a, da

            def conv3_ops(acc, dacc, u_ap_fn, wcol, rd):
                P.op("vector", lambda e: e.tensor_scalar(acc, u_ap_fn(1), pvt[:, wcol(1):wcol(1) + 1], None, ALU.mult), rd + [dpv], [dacc])
                yield 0
                P.op("vector", lambda e: e.scalar_tensor_tensor(acc, u_ap_fn(0), pvt[:, wcol(0):wcol(0) + 1], acc, ALU.mult, ALU.add), rd + [dpv], [dacc])
                yield 1
                P.op("vector", lambda e: e.scalar_tensor_tensor(acc, u_ap_fn(2), pvt[:, wcol(2):wcol(2) + 1], acc, ALU.mult, ALU.add), rd + [dpv], [dacc])
                yield 2

            for (s_, n_, kind, si) in seqs:
                T = min(512, n_)

                def conv_tile(t0, s_=s_, n_=n_, T=T):
                    a, da = load_halo(0, s_, n_, t0, T)
                    u, du = ur.next()
                    P.op("vector", lambda e, a=a, u=u: e.tensor_tensor(u[:, :, 0:T + 2], a[:, 2:4, 0:T + 2], a[:, 4:6, 0:T + 2], ALU.mult), [da], [du])
                    y, dy = yr.next()

                    def sc_gen(c, a=a, da=da, u=u, du=du, y=y, dy=dy):
                        acc, dacc = accr.next()
                        for op_ in conv3_ops(acc[:, 0:T], dacc, lambda k, u=u, c=c: u[:, c, k:k + T], lambda k, c=c: PVO["scw"] + k * 2 + c, [du]):
                            yield
                        P.op("vector", lambda e, a=a, y=y, c=c, acc=acc: e.tensor_tensor(y[:, c, 0:T], a[:, c, 1:T + 1], acc[:, 0:T], ALU.mult), [da, dacc], [dy])
                        yield
                    gens = [sc_gen(c) for c in range(2)]
                    a2, da2 = load_halo(12, s_, n_, t0, T)
                    ob, dob = obr.next()

                    def dn_gen(c, a=a2, da=da2, ob=ob, dob=dob):
                        acc, dacc = accr.next()
                        for op_ in conv3_ops(acc[:, 0:T], dacc, lambda k, a=a, c=c: a[:, c, k:k + T], lambda k, c=c: PVO["dnw"] + k * 6 + c, [da]):
                            yield
                        if c >= 4:
                            P.op("scalar", lambda e, ob=ob, c=c, acc=acc: e.activation(ob[:, c, 0:T], acc[:, 0:T], AF.Silu), [dacc], [dob])
                            yield
                            return
                        sl, dsl = sr.next()
                        P.op("scalar", lambda e, sl=sl, acc=acc: e.activation(sl[:, 0:T], acc[:, 0:T], AF.Silu), [dacc], [dsl])
                        yield
                        sq, dsq = sqr.next()
                        P.op("scalar", lambda e, sl=sl, sq=sq: e.activation(sq[:, 0:T], sl[:, 0:T], AF.Square), [dsl], [dsq])
                        yield
                        ps, dps = pcr.next()
                        mm_group(ps[:, 0:T], [(bd64b, sq[:, 0:T])], [dsq, dcb], [dps])
                        rs, drs = rr.next()
                        P.op("scalar", lambda e, rs=rs, ps=ps: e.activation(rs[:, 0:T], ps[:, 0:T], AF.Ln, bias=EPS, scale=1.0), [dps], [drs])
                        yield
                        P.op("scalar", lambda e, rs=rs: e.activation(rs[:, 0:T], rs[:, 0:T], AF.Exp, scale=-0.5), [drs], [drs])
                        yield
                        scl = 0.125 if c < 2 else 1.0
                        P.op("vector", lambda e, ob=ob, c=c, sl=sl, rs=rs, scl=scl: e.scalar_tensor_tensor(ob[:, c, 0:T], sl[:, 0:T], scl, rs[:, 0:T], ALU.mult, ALU.mult), [dsl, drs], [dob])
                        yield
                    gens += [dn_gen(c) for c in range(6)]

                    def finish(y=y, dy=dy, ob=ob, dob=dob):
                        P.dma("sync", YMIX[:, 0:2, s_ + t0:s_ + t0 + T], y[:, :, 0:T], reads=[dy], writes=[dYMIX])
                        P.dma("sync", DNQ.rearrange("c p t -> p c t")[:, :, s_ + t0:s_ + t0 + T], ob[:, :, 0:T], reads=[dob], writes=[dDNQ])
                    return gens, finish
                tl = list(range(0, n_, T))
                for i0 in range(0, len(tl), 2):
                    parts = [conv_tile(t0) for t0 in tl[i0:i0 + 2]]
                    run_lockstep([g for (gs, _) in parts for g in gs])
                    for (_, fin) in parts:
                        fin()
        if dbg in ("conv", "convA"):
            break

        for (s_, n_, kind, si) in seqs:
            if (dbg == "attP" and kind == "s") or (dbg in ("attS", "attSN") and kind == "p"):
                continue
            with P.phase():
                samp = kind == "s"
                T = min(512, n_)
                nblk = n_ // 128
                QN = P.sbuf("QN", [128, 4, n_], BF16); KN = P.sbuf("KN", [128, n_], BF16)
                VA = [P.sbuf("VA0", [128, nblk, 128], BF16), P.sbuf("VA1", [128, nblk, 128], BF16)]
                dQN, dKN, dVT = Dep(), Dep(), Dep()
                P.op("vector", lambda e: e.memset(VA[0][:, :, 64:128], 1.0), [], [dVT])
                P.op("vector", lambda e: e.memset(VA[1][:, :, 0:64], 1.0), [], [dVT])
                qkr = Ring(P, "aqk", [128, 5, 512], F32, 2)
                vr = Ring(P, "av", [128, 512], F32, 2)
                sqr = Ring(P, "asq", [128, 5, 512], BF16, 1)
                psA = Ring(P, "psA", [128, 512], F32, 8, "psum")
                rr = Ring(P, "ars", [128, 512], F32, 5)
                qnr = Ring(P, "aqn", [128, 512], F32, 5)
                qbr = Ring(P, "aqb", [128, 512], BF16, 5)
                t1r = Ring(P, "at1", [128, 512], F32, 5)
                t2r = Ring(P, "at2", [128, 512], F32, 5)
                cosr = Ring(P, "acos", [128, 512], F32, 2)
                sinr = Ring(P, "asin", [128, 512], F32, 2)
                stv = Ring(P, "astv", [128, 4, 128], F32, 2)
                for t0 in range(0, n_, T):
                    qk, dqk = qkr.next()
                    P.dma("sync", qk[:, :, 0:T], PROJ[:, 6:11, s_ + t0:s_ + t0 + T], reads=[dPROJ], writes=[dqk])
                    vv, dvv = vr.next()
                    P.dma("sync", vv[:, 0:T], PROJ[:, 11, s_ + t0:s_ + t0 + T], reads=[dPROJ], writes=[dvv])
                    if samp:
                        cs, dcs = cosr.next(); sn, dsn = sinr.next()
                        P.dma("sync", cs[:, 0:T], ropec[:, t0:t0 + T], writes=[dcs])
                        P.dma("sync", sn[:, 0:T], ropes[:, t0:t0 + T], writes=[dsn])
                    sq, dsq = sqr.next()
                    P.op("scalar", lambda e, sq=sq, qk=qk: e.activation(sq[:, :, 0:T], qk[:, :, 0:T], AF.Square), [dqk], [dsq])
                    def chunk_gen(c, sq=sq, dsq=dsq, qk=qk, dqk=dqk, t0=t0):
                            ps, dps = psA.next()
                            mm_group(ps[:, 0:T], [(bd64b, sq[:, c, 0:T])], [dsq, dcb], [dps])
                            yield
                            rs, drs = rr.next()
                            P.op("scalar", lambda e, rs=rs, ps=ps: e.activation(rs[:, 0:T], ps[:, 0:T], AF.Ln, bias=EPS, scale=1.0 / 64), [dps], [drs])
                            yield
                            P.op("scalar", lambda e, rs=rs: e.activation(rs[:, 0:T], rs[:, 0:T], AF.Exp, scale=-0.5), [drs], [drs])
                            yield
                            gcol = PVO["qg"] if c < 4 else PVO["kg"]
                            dst = QN[:, c, t0:t0 + T] if c < 4 else KN[:, t0:t0 + T]
                            ddst = dQN if c < 4 else dKN
                            if not samp and c < 4:
                                P.op("vector", lambda e, qk=qk, c=c, rs=rs, dst=dst, gcol=gcol: e.scalar_tensor_tensor(dst, qk[:, c, 0:T], pvt[:, gcol:gcol + 1], rs[:, 0:T], ALU.mult, ALU.mult), [dqk, drs, dpv], [ddst])
                                yield
                                return
                            qn, dqn = qnr.next()
                            P.op("vector", lambda e, qk=qk, c=c, rs=rs, qn=qn, gcol=gcol: e.scalar_tensor_tensor(qn[:, 0:T], qk[:, c, 0:T], pvt[:, gcol:gcol + 1], rs[:, 0:T], ALU.mult, ALU.mult), [dqk, drs, dpv], [dqn])
                            yield
                            if not samp:
                                P.op("scalar", lambda e, qn=qn, dst=dst: e.copy(dst, qn[:, 0:T]), [dqn], [ddst])
                                yield
                                ps2, dps2 = psA.next()

                                def fnk(e, ps2=ps2, qn=qn):
                                    ins = None
                                    for b in range(T // 128):
                                        ins = e.transpose(ps2[:, b * 128:(b + 1) * 128], qn[:, b * 128:(b + 1) * 128], identf)
                                    return ins
                                P.op("tensor", fnk, [dqn, dcst], [dps2])
                                yield
                                so, dso = stv.next()
                                evac(so[:, 0:T // 128, :], ps2[:, 0:T].rearrange("p (a b) -> p a b", b=128), [dps2], [dso])
                                yield
                                P.dma("sync", nk[si, l, t0:t0 + T, :].rearrange("(b p) f -> p b f", p=128), so[:, 0:T // 128, :], reads=[dso], writes=[dOUT])
                                yield
                                return
                            qb, dqb = qbr.next()
                            P.op("scalar", lambda e, qn=qn, qb=qb: e.copy(qb[:, 0:T], qn[:, 0:T]), [dqn], [dqb])
                            yield
                            ps2, dps2 = psA.next()
                            mm_group(ps2[:, 0:T], [(rpermb, qb[:, 0:T])], [dqb, dcb], [dps2])
                            yield
                            t1, dt1 = t1r.next(); t2, dt2 = t2r.next()
                            P.op("gpsimd", lambda e, t1=t1, qn=qn, cs=cs: e.tensor_tensor(t1[:, 0:T], qn[:, 0:T], cs[:, 0:T], ALU.mult), [dqn, dcs], [dt1])
                            yield
                            P.op("vector", lambda e, t2=t2, ps2=ps2, sn=sn: e.tensor_tensor(t2[:, 0:T], ps2[:, 0:T], sn[:, 0:T], ALU.mult), [dps2, dsn], [dt2])
                            yield
                            P.op("gpsimd", lambda e, t1=t1, t2=t2, dst=dst: e.tensor_tensor(dst, t1[:, 0:T], t2[:, 0:T], ALU.add), [dt1, dt2], [ddst])
                            yield

                    run_lockstep([chunk_gen(c) for c in range(5)])
                    ps3, dps3 = psA.next()

                    def fnv(e, ps3=ps3, vv=vv):
                        ins = None
                        for b in range(T // 128):
                            ins = e.transpose(ps3[:, b * 128:(b + 1) * 128], vv[:, b * 128:(b + 1) * 128], identf)
                        return ins
                    P.op("tensor", fnv, [dvv, dcst], [dps3])
                    b0 = t0 // 128
                    P.op("vector", lambda e, ps3=ps3, b0=b0: e.tensor_copy(VA[0][:, b0:b0 + T // 128, 0:64], ps3[:, 0:T].rearrange("p (a b) -> p a b", b=128)[:, :, 0:64]), [dps3], [dVT])
                    P.op("vector", lambda e, ps3=ps3, b0=b0: e.tensor_copy(VA[1][:, b0:b0 + T // 128, 64:128], ps3[:, 0:T].rearrange("p (a b) -> p a b", b=128)[:, :, 64:128]), [dps3], [dVT])
                    if not samp:
                        so, dso = stv.next()
                        P.op("scalar", lambda e, so=so, ps3=ps3: e.copy(so[:, 0:T // 128, :], ps3[:, 0:T].rearrange("p (a b) -> p a b", b=128)), [dps3], [dso])
                        P.dma("sync", nv[si, l, t0:t0 + T, :].rearrange("(b p) f -> p b f", p=128), so[:, 0:T // 128, :], reads=[dso], writes=[dOUT])
                if samp:
                    KCx = P.sbuf("KCx", [128, 512], BF16); dctx = Dep()
                    VCA = [P.sbuf("VCA0", [128, 4, 128], BF16), P.sbuf("VCA1", [128, 4, 128], BF16)]
                    P.op("vector", lambda e: e.memset(VCA[0][:, :, 64:128], 1.0), [], [dctx])
                    P.op("vector", lambda e: e.memset(VCA[1][:, :, 0:64], 1.0), [], [dctx])
                    ckt = P.sbuf("ckt", [128, 4, 128], F32); cvt = P.sbuf("cvt", [128, 4, 128], F32); dck = Dep()
                    P.dma("sync", ckt[:], ck[l].rearrange("(b p) f -> p b f", p=128), writes=[dck])
                    P.dma("sync", cvt[:], cv[l].rearrange("(b p) f -> p b f", p=128), writes=[dck])
                    ps4, dps4 = psA.next()

                    def fnc(e, ps4=ps4):
                        ins = None
                        for b in range(4):
                            ins = e.transpose(ps4[:, b * 128:(b + 1) * 128], ckt[:, b, :], identf)
                        return ins
                    P.op("tensor", fnc, [dck, dcst], [dps4])
                    P.op("vector", lambda e, ps4=ps4: e.tensor_copy(KCx[:], ps4[:]), [dps4], [dctx])
                    P.op("vector", lambda e: e.tensor_copy(VCA[0][:, :, 0:64], cvt[:, :, 0:64]), [dck], [dctx])
                    P.op("vector", lambda e: e.tensor_copy(VCA[1][:, :, 64:128], cvt[:, :, 64:128]), [dck], [dctx])
                pss = Ring.of(psA.t[0:4], psA.d[0:4])
                pso_ = Ring.of(psA.t[4:6], psA.d[4:6])
                psd_ = Ring.of(psA.t[6:8], psA.d[6:8])
                ptr = Ring(P, "apt", [128, 512], BF16, 6)
                denr = Ring(P, "aden", [128, 512], F32, 2); rdenr = Ring(P, "arden", [128, 512], F32, 2)
                ybr = Ring(P, "ayb", [128, 4, 128], BF16, 2)
                for i in range(nblk):
                    if dbg in ("attN", "attSN"):
                        break
                    pacc = [pso_.next(), psd_.next()]
                    for kvh in range(2):
                        rows = slice(kvh * 64, (kvh + 1) * 64)
                        po, dpo = pacc[kvh]
                        kt = []
                        if samp:
                            kt += [("c", j, None) for j in range(4)]
                            if i > 0:
                                kt.append(("l", i - 1, mwprev))
                            kt.append(("l", i, None))
                            if i < nblk - 1:
                                kt.append(("l", i + 1, mwnext))
                        else:
                            kt += [("l", j, None) for j in range(nblk)]
                        def issue_s(ki):
                            src, j, msk = kt[ki]
                            ks = KCx[rows, j * 128:(j + 1) * 128] if src == "c" else KN[rows, j * 128:(j + 1) * 128]
                            kd_ = [dctx] if src == "c" else [dKN, dVT]
                            ps, dps = pss.next()
                            mm_group(ps[:], [(ks, QN[rows, :, i * 128:(i + 1) * 128])], kd_ + [dQN], [dps])
                            return ps, dps
                        ahead = [issue_s(k_) for k_ in range(min(3, len(kt)))]
                        for ki, (src, j, msk) in enumerate(kt):
                            vsrc = VCA[kvh][:, j, :] if src == "c" else VA[kvh][:, j, :]
                            kd_ = [dctx] if src == "c" else [dKN, dVT]
                            ps, dps = ahead.pop(0)
                            if ki + 3 < len(kt):
                                ahead.append(issue_s(ki + 3))
                            pt, dpt = ptr.next()
                            P.op("scalar", lambda e, pt=pt, ps=ps: e.activation(pt[:], ps[:], AF.Exp, scale=0.125), [dps], [dpt])
                            if msk is not None:
                                P.op("vector", lambda e, pt=pt, msk=msk: e.tensor_tensor(pt[:].rearrange("p (a b) -> p a b", a=4), pt[:].rearrange("p (a b) -> p a b", a=4),
                                                                                       msk.unsqueeze(1).broadcast_to([128, 4, 128]), ALU.mult), [dpt, dcb], [dpt])
                            first, last = ki == 0, ki == len(kt) - 1

                            P.op("tensor", lambda e, po=po, vsrc=vsrc, pt=pt, first=first, last=last: e.matmul(po[:], vsrc, pt[:], start=first, stop=last), kd_ + [dpt, dcb], [dpo])
                    den, dden = denr.next()
                    for kvh in range(2):
                        po, dpo = pacc[kvh]
                        orow = slice(kvh * 64, (kvh + 1) * 64)
                        drow = slice((1 - kvh) * 64, (2 - kvh) * 64)
                        P.op("vector", lambda e, den=den, po=po, drow=drow: e.tensor_tensor(den[drow, :].rearrange("p (a b) -> p a b", a=4), po[drow, :].rearrange("p (a b) -> p a b", a=4),
                                                                                            sinkexp[drow, 4:8].unsqueeze(2).broadcast_to([64, 4, 128]), ALU.add), [dpo, dmod], [dden])
                    rden, drden = rdenr.next()
                    for kvh in range(2):
                        orow = slice(kvh * 64, (kvh + 1) * 64)
                        drow = slice((1 - kvh) * 64, (2 - kvh) * 64)
                        P.op("scalar", lambda e, rden=rden, den=den, orow=orow, drow=drow: e.activation(rden[orow, :], den[drow, :], AF.Ln), [dden], [drden])
                        P.op("scalar", lambda e, rden=rden, orow=orow: e.activation(rden[orow, :], rden[orow, :], AF.Exp, scale=-1.0), [drden], [drden])
                    yb, dyb = ybr.next()
                    for kvh in range(2):
                        po, dpo = pacc[kvh]
                        orow = slice(kvh * 64, (kvh + 1) * 64)
                        P.op("vector", lambda e, yb=yb, po=po, rden=rden, orow=orow: e.tensor_tensor(yb[orow, :, :].rearrange("p a b -> p (a b)"), po[orow, :], rden[orow, :], ALU.mult), [dpo, drden], [dyb])
                    P.dma("sync", YMIX[:, 2:6, s_ + i * 128:s_ + (i + 1) * 128], yb[:], reads=[dyb], writes=[dYMIX])
        if dbg and dbg.startswith("att"):
            break

        DNQh = DNQ.rearrange("c (h d) t -> d (c h) t", d=64)
        for (s_, n_, kind, si) in seqs:
            with P.phase():
                samp = kind == "s"
                NCH = n_ // 64
                V_ = lambda fn, r, w: P.op("vector", fn, r, w)
                A_ = lambda fn, r, w: P.op("scalar", fn, r, w)
                G_ = lambda fn, r, w: P.op("vector", fn, r, w)
                psr = Ring(P, "dps", [128, 512], F32, 8, "psum")

                def bcf(ap, n):
                    return ap.unsqueeze(2).broadcast_to([ap.shape[0], ap.shape[1], n])
                NP_ = NCH // 2
                HS = (slice(0, 64), slice(64, 128))
                abF = P.sbuf("abF", [128, NP_, 16], F32); abt = P.sbuf("abt", [128, NP_, 16], F32); dabt = Dep(); dabF = Dep()
                ABs = AB[s_:s_ + n_].rearrange("(g two t) k -> two t g k", two=2, t=64)
                for h in range(2):
                    P.dma("sync", abF[HS[h], :, :], ABs[h], reads=[dAB], writes=[dabF])
                for h in range(2):
                    P.op("vector", lambda e, h=h: e.tensor_copy(abt[HS[h], :, 0:4], abF[HS[h], :, 0:4]), [dabF], [dabt])
                    P.op("vector", lambda e, h=h: e.tensor_copy(abt[HS[h], :, 8:12], abF[HS[h], :, 8:12]), [dabF], [dabt])
                    for g in range(NP_):
                        P.op("scalar", lambda e, h=h, g=g: e.copy(abt[HS[h], g, 4:8], abF[HS[1 - h], NP_ - 1 - g, 4:8]), [dabF], [dabt])
                        P.op("scalar", lambda e, h=h, g=g: e.copy(abt[HS[h], g, 12:16], abF[HS[1 - h], NP_ - 1 - g, 12:16]), [dabF], [dabt])
                sc = {}
                for nm in ("g", "beta", "negg", "gc", "eg", "ed", "egt", "beg", "tmpa"):
                    sc[nm] = P.sbuf("dn_" + nm, [128, NP_, 8], F32)
                egtX = P.sbuf("dn_egtX", [64, NP_, 8], F32)
                dsc = Dep()
                g3 = sc["g"]
                V_(lambda e: e.tensor_tensor(sc["tmpa"][:], abt[:, :, 0:8], bct[:, 8:16].unsqueeze(1).broadcast_to([128, NP_, 8]), ALU.add), [dabt, dpv], [dsc])
                A_(lambda e: e.activation(sc["tmpa"][:], sc["tmpa"][:], AF.Exp), [dsc], [dsc])
                A_(lambda e: e.activation(sc["tmpa"][:], sc["tmpa"][:], AF.Ln, bias=1.0), [dsc], [dsc])
                V_(lambda e: e.tensor_tensor(g3[:], sc["tmpa"][:], negA2[:].unsqueeze(1).broadcast_to([128, NP_, 8]), ALU.mult), [dsc, dmod], [dsc])
                V_(lambda e: e.tensor_scalar(sc["negg"][:], g3[:], -1.0, None, ALU.mult), [dsc], [dsc])
                A_(lambda e: e.activation(sc["beta"][:], abt[:, :, 8:16], AF.Exp, scale=-1.0), [dabt], [dsc])
                V_(lambda e: e.tensor_scalar(sc["beta"][:], sc["beta"][:], 1.0, None, ALU.add), [dsc], [dsc])
                V_(lambda e: e.reciprocal(sc["beta"][:], sc["beta"][:]), [dsc], [dsc])
                pg, dpg = psr.next()
                pgF = pg[:, 0:NP_ * 4]; pgB = pg[:, NP_ * 4:NP_ * 8]

                def fng(e):
                    e.matmul(pgF, UCF2, g3[:, :, 0:4], start=True, stop=True)
                    return e.matmul(pgB, UCB2, g3[:, :, 4:8], start=True, stop=True)
                P.op("tensor", fng, [dsc, dcst], [dpg])
                for (pgX, lo) in ((pgF, 0), (pgB, 4)):
                    V_(lambda e, pgX=pgX, lo=lo: e.tensor_copy(sc["gc"][:, :, lo:lo + 4], pgX.rearrange("p (c k) -> p c k", k=4)), [dpg], [dsc])
                    A_(lambda e, pgX=pgX, lo=lo: e.activation(sc["eg"][:, :, lo:lo + 4], pgX.rearrange("p (c k) -> p c k", k=4), AF.Exp), [dpg], [dsc])
                pt_, dpt_ = psr.next()
                pt2 = pt_[:, 0:NP_ * 8]
                ptv = pt2.rearrange("p (c k) -> p c k", k=8)
                P.op("tensor", lambda e: e.matmul(pt2, bd64f, g3[:].rearrange("p c k -> p (c k)"), start=True, stop=True), [dsc, dcst], [dpt_])
                A_(lambda e: e.activation(sc["egt"][:], ptv, AF.Exp), [dpt_], [dsc])
                V_(lambda e: e.tensor_tensor(sc["ed"][:], ptv, sc["gc"][:], ALU.subtract), [dpt_, dsc], [dsc])
                A_(lambda e: e.activation(sc["ed"][:], sc["ed"][:], AF.Exp), [dsc], [dsc])
                V_(lambda e: e.tensor_tensor(sc["beg"][:], sc["beta"][:], sc["eg"][:], ALU.mult), [dsc], [dsc])
                A_(lambda e: e.copy(egtX[:], sc["egt"][64:128, :, :]), [dsc], [dsc])
                Sf = P.sbuf("Sf", [64, 8, 64], F32); Sb = P.sbuf("Sb", [128, 8, 64], BF16); dS = Dep(); dSb = Dep()
                if samp:
                    P.dma("sync", Sf[:], s0in[l].rearrange("k a b -> a k b"), writes=[dS])
                else:
                    V_(lambda e: e.memset(Sf[:], 0.0), [], [dS])
                A_(lambda e: e.copy(Sb[0:64], Sf[:]), [dS], [dSb])
                A_(lambda e: e.copy(Sb[64:128], Sf[:]), [dS], [dSb])

                def t64(name, dt, n=3, shape=(128, 8, 64)):
                    return Ring(P, name, list(shape), dt, n)
                qfr = t64("qf", F32, 3, (128, 2, 12, 64)); qbr_ = t64("qb", BF16, 3, (128, 2, 12, 64))
                NGr = t64("NG", F32, 2); Dr = t64("Dd", F32, 2); Er = t64("E", F32, 2); ERr = t64("ER", F32)
                Eir = t64("Ei", F32, 2); Esr = t64("Es", F32, 2)
                Xr = t64("X", F32, 3); Xbr = t64("Xb", BF16, 9); XTbr = t64("XTb", BF16, 9); TTbr = t64("TTb", BF16, 6); Tbr = t64("Tb", BF16); Rtr = t64("Rt", F32, 2); Rbr = t64("Rb", BF16); AINr = t64("AIN", BF16); AINTr = t64("AINT", BF16, 6)
                TTfr = t64("TTf", F32); TTb2r = t64("TTb2", BF16); TTber = t64("TTbe", BF16)
                KVr = t64("KV", BF16, 6, (128, 16, 64)); QEr = t64("QE", BF16, 6); Ur = t64("U", F32, 6); NWr = t64("NW", BF16, 6)
                VNfr = t64("VNf", F32, 2); VNr = t64("VN", BF16, 2); VNDr = t64("VND", BF16, 2); OBr = t64("OB", F32, 2); STr = t64("ST", F32, 2, (64, 8, 64))

                def per_k(out_fn, l_fn, r_fn, reads, writes, transpose=False, halves_=(0, 1)):
                    def fn(e):
                        ins = None
                        for h in halves_:
                            for k in range(8):
                                if transpose:
                                    ins = e.transpose(out_fn(h, k), l_fn(h, k), identb[HS[h], HS[h]])
                                else:
                                    ins = e.matmul(out_fn(h, k), l_fn(h, k), r_fn(h, k), start=True, stop=True)
                        return ins
                    P.op("tensor", fn, reads, writes)

                def v3(t):
                    return t[:, :].rearrange("p (k j) -> p k j", k=8)

                def vb(t):
                    return t[:, :].bitcast(BF16).rearrange("p (k j) -> p k j", j=64)
                hk = lambda t: (lambda h, k, t=t: t[HS[h], k, :])
                W_ = 3 if NP_ >= 3 else NP_

                def pre_gen(g, cx):
                        scb = lambda nm: bcf(sc[nm][:, g, :], 64)
                        chunk = lambda h, dr: (2 * g + h) if dr == 0 else (NCH - 1 - 2 * g - h)
                        qf, dqf = qfr.next()
                        for h in range(2):
                            for dr in range(2):
                                c = chunk(h, dr)
                                P.dma("sync", qf[HS[h], dr, :, :], DNQh[:, :, s_ + c * 64:s_ + c * 64 + 64], reads=[dDNQ], writes=[dqf])
                        qb, dqb = qbr_.next()
                        A_(lambda e, qb=qb, qf=qf: e.copy(qb[:], qf[:]), [dqf], [dqb])
                        Qk = lambda h, k, qb=qb: qb[HS[h], k // 4, k % 4, :]
                        Kk = lambda h, k, qb=qb: qb[HS[h], k // 4, 4 + k % 4, :]
                        Kf = lambda h, k, qf=qf: qf[HS[h], k // 4, 4 + k % 4, :]
                        pa, dpa = psr.next(); pqk, dpqk = psr.next(); pgr, dpgr = psr.next()
                        per_k(hk(v3(pa)), Kf, Kf, [dqf], [dpa])
                        per_k(hk(v3(pqk)), Qk, Kk, [dqb], [dpqk])
                        NG, dNG = NGr.next()
                        G_(lambda e, NG=NG: e.tensor_tensor(NG[:], U8, scb("negg"), ALU.mult), [dsc, dcst], [dNG])
                        P.op("tensor", lambda e, pgr=pgr, NG=NG: e.matmul(pgr[:, :], bd64f, NG[:].rearrange("p k j -> p (k j)"), start=True, stop=True), [dNG, dcst], [dpgr])
                        Dd, dD = Dr.next()
                        V_(lambda e, Dd=Dd, pgr=pgr: e.tensor_tensor(Dd[:], v3(pgr), scb("gc"), ALU.add), [dpgr, dsc], [dD])
                        V_(lambda e, Dd=Dd: e.tensor_scalar(Dd[:], Dd[:], 0.0, None, ALU.min), [dD], [dD])
                        E, dE = Er.next()
                        A_(lambda e, E=E, Dd=Dd: e.activation(E[:], Dd[:], AF.Exp), [dD], [dE])
                        ER, dER = ERr.next()
                        A_(lambda e, ER=ER, pgr=pgr: e.activation(ER[:], v3(pgr), AF.Exp, scale=-1.0), [dpgr], [dER])
                        Ei, dEi = Eir.next(); Es, dEs = Esr.next()
                        V_(lambda e, Ei=Ei, E=E: e.tensor_tensor(Ei[:].rearrange("p k j -> p (k j)"), E[:].rearrange("p k j -> p (k j)"), MI8, ALU.mult), [dE, dcst], [dEi])
                        G_(lambda e, Es=Es, E=E: e.tensor_tensor(Es[:].rearrange("p k j -> p (k j)"), E[:].rearrange("p k j -> p (k j)"), MS8, ALU.mult), [dE, dcst], [dEs])
                        G_(lambda e, Es=Es: e.tensor_tensor(Es[:], Es[:], scb("beta"), ALU.mult), [dEs, dsc], [dEs])
                        X, dX = Xr.next()
                        V_(lambda e, X=X, pa=pa, Es=Es: e.scalar_tensor_tensor(X[:].rearrange("p k j -> p (k j)"), pa[:, :], -1.0, Es[:].rearrange("p k j -> p (k j)"), ALU.mult, ALU.mult), [dpa, dEs], [dX])
                        AIN, dAIN = AINr.next()
                        V_(lambda e, AIN=AIN, pqk=pqk, Ei=Ei: e.tensor_tensor(AIN[:].rearrange("p k j -> p (k j)"), pqk[:, :], Ei[:].rearrange("p k j -> p (k j)"), ALU.mult), [dpqk, dEi], [dAIN])
                        yield
                        Xb, dXb = Xbr.next()
                        A_(lambda e, Xb=Xb, X=X: e.copy(Xb[:], X[:]), [dX], [dXb])
                        px, dpx = psr.next()
                        pxb = vb(px)
                        per_k(lambda h, k: pxb[HS[h], k, :], hk(Xb), None, [dXb, dcb], [dpx], transpose=True)
                        per_k(lambda h, k: pxb[HS[h], 8 + k, :], hk(AIN), None, [dAIN, dcb], [dpx], transpose=True)
                        XTb, dXTb = XTbr.next(); AINT, dAINT = AINTr.next()
                        V_(lambda e, XTb=XTb, pxb=pxb: e.tensor_copy(XTb[:], pxb[:, 0:8, :]), [dpx], [dXTb])
                        A_(lambda e, AINT=AINT, pxb=pxb: e.copy(AINT[:], pxb[:, 8:16, :]), [dpx], [dAINT])
                        yield
                        TTf, dTTf = TTfr.next(); TTb, dTTb = TTbr.next()
                        V_(lambda e, TTf=TTf, XTb=XTb: e.tensor_tensor(TTf[:].rearrange("p k j -> p (k j)"), XTb[:].rearrange("p k j -> p (k j)"), IDB8, ALU.add), [dXTb, dcst], [dTTf])
                        A_(lambda e, TTb=TTb, TTf=TTf: e.copy(TTb[:], TTf[:]), [dTTf], [dTTb])
                        Xc, dXc, XTc, dXTc = Xb, dXb, XTb, dXTb
                        for lev in range(4):
                            last = lev == 3
                            p2, dp2 = psr.next()
                            per_k(hk(v3(p2)), hk(XTc), hk(Xc), [dXc, dXTc], [dp2])
                            Xn, dXn = Xbr.next()
                            A_(lambda e, Xn=Xn, p2=p2: e.copy(Xn[:], v3(p2)), [dp2], [dXn])
                            if not last:
                                p3, dp3 = psr.next()
                                per_k(hk(v3(p3)), hk(Xc), hk(XTc), [dXc, dXTc], [dp3])
                                XTn, dXTn = XTbr.next()
                                A_(lambda e, XTn=XTn, p3=p3: e.copy(XTn[:], v3(p3)), [dp3], [dXTn])
                            p4, dp4 = psr.next()
                            per_k(hk(v3(p4)), hk(Xn), hk(TTb), [dXn, dTTb], [dp4])
                            V_(lambda e, TTf=TTf, p4=p4: e.tensor_tensor(TTf[:], TTf[:], v3(p4), ALU.add), [dp4, dTTf], [dTTf])
                            TTb, dTTb = TTbr.next()
                            A_(lambda e, TTb=TTb, TTf=TTf: e.copy(TTb[:], TTf[:]), [dTTf], [dTTb])
                            yield
                            Xc, dXc = Xn, dXn
                            if not last:
                                XTc, dXTc = XTn, dXTn
                        ptt, dptt = psr.next()
                        pttb = vb(ptt)
                        per_k(lambda h, k: pttb[HS[h], k, :], hk(TTb), None, [dTTb, dcb], [dptt], transpose=True)
                        Tb, dTb = Tbr.next()
                        A_(lambda e, Tb=Tb, pttb=pttb: e.copy(Tb[:], pttb[:, 0:8, :]), [dptt], [dTb])
                        Rt, dRt = Rtr.next()
                        V_(lambda e, Rt=Rt, TTf=TTf: e.scalar_tensor_tensor(Rt[:].rearrange("p k j -> p (k j)"), TTf[:].rearrange("p k j -> p (k j)"), -1.0, IDB8, ALU.mult, ALU.add), [dTTf, dcst], [dRt])
                        pr_, dpr_ = psr.next()
                        per_k(hk(v3(pr_)), hk(X), hk(TTf), [dX, dTTf], [dpr_])
                        Rb, dRb = Rbr.next()
                        V_(lambda e, Rb=Rb, Rt=Rt, pr_=pr_: e.tensor_tensor(Rb[:], Rt[:], v3(pr_), ALU.add), [dRt, dpr_], [dRb])
                        yield
                        pc_, dpc_ = psr.next()
                        per_k(hk(v3(pc_)), hk(Tb), hk(Rb), [dTb, dRb], [dpc_])
                        V_(lambda e, TTf=TTf, pc_=pc_: e.tensor_tensor(TTf[:], TTf[:], v3(pc_), ALU.add), [dpc_, dTTf], [dTTf])
                        yield
                        TTb2, dTTb2 = TTb2r.next(); TTbe, dTTbe = TTber.next()
                        G_(lambda e, TTb2=TTb2, TTf=TTf: e.tensor_tensor(TTb2[:], TTf[:], scb("beta"), ALU.mult), [dTTf, dsc], [dTTb2])
                        G_(lambda e, TTbe=TTbe, TTf=TTf: e.tensor_tensor(TTbe[:], TTf[:], scb("beg"), ALU.mult), [dTTf, dsc], [dTTbe])
                        KV, dKV = KVr.next()
                        pkv, dpkv = psr.next()
                        pkvb = vb(pkv)
                        per_k(lambda h, k: pkvb[HS[h], k, :], Kk, None, [dqb, dcb], [dpkv], transpose=True)
                        per_k(lambda h, k: pkvb[HS[h], 8 + k, :], lambda h, k, qb=qb: qb[HS[h], k // 4, 8 + k % 4, :], None, [dqb, dcb], [dpkv], transpose=True)
                        A_(lambda e, KV=KV, pkvb=pkvb: e.copy(KV[:], pkvb), [dpkv], [dKV])
                        QE, dQE = QEr.next()
                        G_(lambda e, QE=QE, qf=qf, ER=ER: e.tensor_tensor(QE[:].rearrange("p (a b) j -> p a b j", a=2), qf[:, :, 0:4, :], ER[:].rearrange("p (a b) j -> p a b j", a=2), ALU.mult), [dqf, dER], [dQE])
                        pu, dpu = psr.next()
                        per_k(hk(v3(pu)), hk(TTb2), lambda h, k, KV=KV: KV[HS[h], 8 + k, :], [dTTb2, dKV], [dpu])
                        U, dU = Ur.next()
                        A_(lambda e, U=U, pu=pu: e.copy(U[:], v3(pu)), [dpu], [dU])
                        yield
                        pw, dpw = psr.next()
                        per_k(hk(v3(pw)), hk(KV), hk(TTbe), [dTTbe, dKV], [dpw])
                        NW, dNW = NWr.next()
                        V_(lambda e, NW=NW, pw=pw: e.tensor_scalar(NW[:], v3(pw), -1.0, None, ALU.mult), [dpw], [dNW])
                        cx.update(dict(NW=NW, dNW=dNW, U=U, dU=dU, QE=QE, dQE=dQE, AINT=AINT, dAINT=dAINT, KV=KV, dKV=dKV))
                        yield

                def scan_step(g, cx, h):
                    NW = cx['NW']; dNW = cx['dNW']; U = cx['U']; dU = cx['dU']; QE = cx['QE']; dQE = cx['dQE']
                    AINT = cx['AINT']; dAINT = cx['dAINT']; KV = cx['KV']; dKV = cx['dKV']
                    hs = HS[h]
                    one = (h,)
                    pws, dpws = psr.next()
                    per_k(hk(v3(pws)), hk(NW), hk(Sb), [dNW, dSb], [dpws], halves_=one)
                    VNf, dVNf = VNfr.next(); VN, dVN = VNr.next(); VND, dVND = VNDr.next()
                    V_(lambda e, VNf=VNf, U=U, pws=pws: e.tensor_tensor(VNf[hs], U[hs], v3(pws)[hs], ALU.add), [dU, dpws], [dVNf])
                    yield
                    A_(lambda e, VN=VN, VNf=VNf: e.copy(VN[hs], VNf[hs]), [dVNf], [dVN])
                    yield
                    G_(lambda e, VND=VND, VNf=VNf: e.tensor_tensor(VND[hs], VNf[hs], bcf(sc["ed"][hs, g, :], 64), ALU.mult), [dVNf, dsc], [dVND])
                    yield
                    po_, dpo_ = psr.next()

                    def fno(e, po_=po_, QE=QE, AINT=AINT, VN=VN):
                        ins = None
                        for k in range(8):
                            e.matmul(v3(po_)[hs, k, :], QE[hs, k, :], Sb[hs, k, :], start=True, stop=False)
                            ins = e.matmul(v3(po_)[hs, k, :], AINT[hs, k, :], VN[hs, k, :], start=False, stop=True)
                        return ins
                    P.op("tensor", fno, [dQE, dSb, dAINT, dVN], [dpo_])
                    OB, dOB = OBr.next()
                    A_(lambda e, OB=OB, po_=po_: e.copy(OB[hs], v3(po_)[hs]), [dpo_], [dOB])
                    yield
                    for dr in range(2):
                        c = (2 * g + h) if dr == 0 else (NCH - 1 - 2 * g - h)
                        t_ = s_ + c * 64
                        P.dma("sync", OD[dr, t_:t_ + 64, :].rearrange("t (h v) -> t h v", h=4), OB[hs, dr * 4:dr * 4 + 4, :], reads=[dOB], writes=[dOD])
                    ST, dST = STr.next()
                    egs = sc["egt"][0:64, g, :] if h == 0 else egtX[:, g, :]
                    G_(lambda e, ST=ST: e.tensor_tensor(ST[:], Sf[:], bcf(egs, 64), ALU.mult), [dS, dsc], [dST])
                    yield
                    pS, dpS = psr.next()

                    def fns(e, pS=pS, KV=KV, VND=VND):
                        ins = None
                        for k in range(8):
                            ins = e.matmul(pS[0:64, k * 64:(k + 1) * 64], KV[hs, k, :], VND[hs, k, :], start=True, stop=True)
                        return ins
                    P.op("tensor", fns, [dKV, dVND], [dpS])
                    V_(lambda e, ST=ST, pS=pS: e.tensor_tensor(Sf[:], ST[:], pS[0:64, :].rearrange("p (k j) -> p k j", k=8), ALU.add), [dST, dpS], [dS])
                    yield
                    A_(lambda e: e.copy(Sb[0:64], Sf[:]), [dS], [dSb])
                    A_(lambda e: e.copy(Sb[64:128], Sf[:]), [dS], [dSb])
                    yield

                def scan_group(grp_, cxs_):
                    for g2 in grp_:
                        for h in range(2):
                            yield from scan_step(g2, cxs_[g2], h)
                prev = None
                for g0 in range(0, NP_, W_):
                    grp = list(range(g0, min(g0 + W_, NP_)))
                    cxs = {g2: {} for g2 in grp}
                    gens = [pre_gen(g2, cxs[g2]) for g2 in grp]
                    sg = scan_group(*prev) if prev is not None else None
                    alive = list(gens)
                    while alive:
                        for g_ in list(alive):
                            try:
                                next(g_)
                            except StopIteration:
                                alive.remove(g_)
                            for _rep in range(4):
                                if sg is not None:
                                    try:
                                        next(sg)
                                    except StopIteration:
                                        sg = None
                    if sg is not None:
                        run_lockstep([sg])
                    prev = (grp, cxs)
                run_lockstep([scan_group(*prev)])
                if not samp:
                    P.dma("sync", nst[si, l].rearrange("k a b -> a k b"), Sf[:], reads=[dS], writes=[dOUT])
        if dbg == "dn":
            break

        with P.phase():
            ofr = Ring(P, "cof", [128, 4, 256], F32, 2); obr2 = Ring(P, "cob2", [128, 4, 256], F32, 2)
            osr = Ring(P, "cos_", [128, 4, 256], F32, 2); sqr2 = Ring(P, "csq2", [128, 4, 256], F32, 2)
            ssr = Ring(P, "css", [128, 16], F32, 2); zr = Ring(P, "cz", [128, 2, 512], F32, 2)
            pcr2 = Ring(P, "pc2", [128, 512], F32, 4, "psum"); ydr = Ring(P, "cyd", [128, 2, 512], BF16, 2)
            for ti in range(NT // 512):
                t0 = ti * 512
                of, dof = ofr.next(); ob2, dob2 = obr2.next()
                P.dma("sync", of[:], OD[0, t0:t0 + 512, :].rearrange("(s p) f -> p s f", p=128), reads=[dOD], writes=[dof])
                P.dma("sync", ob2[:], OD[1, t0:t0 + 512, :].rearrange("(s p) f -> p s f", p=128), reads=[dOD], writes=[dob2])
                zt, dzt = zr.next()
                P.dma("sync", zt[:], PROJ[:, 18:20, t0:t0 + 512], reads=[dPROJ], writes=[dzt])
                o, do_ = osr.next()
                P.op("gpsimd", lambda e, o=o, of=of, ob2=ob2: e.tensor_tensor(o[:], of[:], ob2[:], ALU.add), [dof, dob2], [do_])
                sq, dsq = sqr2.next()
                P.op("scalar", lambda e, sq=sq, o=o: e.activation(sq[:], o[:], AF.Square), [do_], [dsq])
                ss, dss = ssr.next()
                P.op("vector", lambda e, ss=ss, sq=sq: e.tensor_reduce(ss[:], sq[:].rearrange("p s (h v) -> p (s h) v", h=4), AX.X, ALU.add), [dsq], [dss])
                P.op("scalar", lambda e, ss=ss: e.activation(ss[:], ss[:], AF.Sqrt, bias=EPS, scale=1.0 / 64), [dss], [dss])
                P.op("vector", lambda e, ss=ss: e.reciprocal(ss[:], ss[:]), [dss], [dss])
                P.op("vector", lambda e, o=o, ss=ss: e.tensor_tensor(o[:].rearrange("p s (h v) -> p (s h) v", h=4), o[:].rearrange("p s (h v) -> p (s h) v", h=4),
                                                                    ss[:].unsqueeze(2).broadcast_to([128, 16, 64]), ALU.mult), [do_, dss], [do_])
                P.op("gpsimd", lambda e, o=o: e.tensor_tensor(o[:].rearrange("p s (h v) -> p (s h) v", h=4), o[:].rearrange("p s (h v) -> p (s h) v", h=4),
                                                              bct[:, 16:80].unsqueeze(1).broadcast_to([128, 16, 64]), ALU.mult), [do_, dpv], [do_])
                P.op("scalar", lambda e, zt=zt: e.activation(zt[:], zt[:], AF.Silu), [dzt], [dzt])
                yd, dyd = ydr.next()
                for c in range(2):
                    ps, dps = pcr2.next()

                    def fnt(e, ps=ps, o=o, c=c):
                        ins = None
                        for sb in range(4):
                            ins = e.transpose(ps[:, sb * 128:(sb + 1) * 128], o[:, sb, c * 128:(c + 1) * 128], identf)
                        return ins
                    P.op("tensor", fnt, [do_, dcst], [dps])
                    P.op("vector", lambda e, yd=yd, ps=ps, zt=zt, c=c: e.tensor_tensor(yd[:, c, :], ps[:], zt[:, c, :], ALU.mult), [dps, dzt], [dyd])
                P.dma("sync", YMIX[:, 6:8, t0:t0 + 512], yd[:], reads=[dyd], writes=[dYMIX])
        if dbg == "mix":
            break

        HF = DFF // 2
        NJ = HF // 128
        for hf in range(2):
            with P.phase():
                wgs = P.sbuf("wgs", [128, KC, HF], BF16); wus = P.sbuf("wus", [128, KC, HF], BF16)
                wds = P.sbuf("wds", [128, NJ, D], BF16); dwf = [Dep() for _ in range(4)]
                P.dma("gpsimd", wgs[:], wg[l].rearrange("(kc p) n -> p kc n", p=128)[:, :, hf * HF:(hf + 1) * HF], writes=[dwf[0]])
                P.dma("gpsimd", wus[:], wu[l].rearrange("(kc p) n -> p kc n", p=128)[:, :, hf * HF:(hf + 1) * HF], writes=[dwf[1]])
                P.dma("gpsimd", wds[:], wd[l, hf * HF:(hf + 1) * HF, :].rearrange("(j p) n -> p j n", p=128), writes=[dwf[2]])
                if hf == 0:
                    wos = P.sbuf("wos", [128, KC, D], BF16)
                    P.dma("gpsimd", wos[:], wout[l].rearrange("(kc p) n -> p kc n", p=128), writes=[dwf[3]])
                    ymr = Ring(P, "ym", [128, KC, 512], BF16, 2)
                    rings = {"sq": Ring(P, "sq", [128, KC, 512], BF16, 1), "psn": Ring(P, "psn", [128, 512], F32, 1, "psum"),
                             "rs": Ring(P, "rs", [128, 512], F32, 2), "tmp": Ring(P, "tmp", [128, 512], F32, 3)}
                    psw = Ring(P, "psw", [128, 512], F32, 2, "psum")
                xr = Ring(P, "xt", [128, KC, 512], F32, 2)
                hr = Ring(P, "h2", [128, KC, 512], BF16, 2)
                actr = Ring(P, "act", [128, NJ, 512], BF16, 1)
                sgr = Ring(P, "sg", [128, 512], F32, 2)
                psf = Ring(P, "psf", [128, 512], F32, 5, "psum")
                direct_out = (hf == 1 and l == depth - 1 and not dbg)
                if direct_out:
                    pso2 = Ring(P, "pso2", [128, 512], F32, 2, "psum")
                    yo_ = Ring(P, "yo_", [128, D], F32, 2)
                def ffn_pre(t0, seg, cx):
                    xt, dxt = xr.next()
                    P.dma("sync", xt[:], XT[:, :, t0:t0 + 512], reads=[dXT], writes=[dxt])
                    h2, dh2 = hr.next()
                    cx.update(dict(xt=xt, dxt=dxt, h2=h2, dh2=dh2))
                    if hf == 0:
                        ym, dym = ymr.next()
                        P.dma("sync", ym[:], YMIX[:, :, t0:t0 + 512], reads=[dYMIX], writes=[dym])
                        yield
                        for oc in range(KC):
                            ps, dps = psw.next()
                            mm_group(ps[:], [(wos[:, kc, oc * 128:(oc + 1) * 128], ym[:, kc, :]) for kc in range(KC)], [dym, dwf[3]], [dps])
                            P.op("vector", lambda e, xt=xt, ps=ps, oc=oc, seg=seg: e.scalar_tensor_tensor(xt[:, oc, :], ps[:], modv[:, 16 + oc, seg:seg + 1], xt[:, oc, :], ALU.mult, ALU.add), [dps, dmod, dxt], [dxt])
                            yield
                        yield from norm_to_h_gen(xt, dxt, h2, dh2, seg, A2, 24, rings)
                        P.dma("sync", H2[:, :, t0:t0 + 512], h2[:], reads=[dh2], writes=[dH2])
                    else:
                        P.dma("sync", h2[:], H2[:, :, t0:t0 + 512], reads=[dH2], writes=[dh2])
                    yield
                cxc = {}
                g_ = ffn_pre(tiles[0][0], tiles[0][1], cxc)
                advance(g_, 1000)
                for ti, (t0, seg) in enumerate(tiles):
                    xt, dxt, h2, dh2 = cxc["xt"], cxc["dxt"], cxc["h2"], cxc["dh2"]
                    cxn = {}
                    nxt = ffn_pre(tiles[ti + 1][0], tiles[ti + 1][1], cxn) if ti + 1 < len(tiles) else None
                    act, dact = actr.next()
                    for j in range(NJ):
                        pg_, dpg_ = psf.next(); pu_, dpu_ = psf.next()
                        mm_group(pg_[:], [(wgs[:, kc, j * 128:(j + 1) * 128], h2[:, kc, :]) for kc in range(KC)], [dh2, dwf[0]], [dpg_])
                        mm_group(pu_[:], [(wus[:, kc, j * 128:(j + 1) * 128], h2[:, kc, :]) for kc in range(KC)], [dh2, dwf[1]], [dpu_])
                        sg, dsg = sgr.next()
                        P.op("scalar", lambda e, sg=sg, pg_=pg_: e.activation(sg[:], pg_[:], AF.Silu), [dpg_], [dsg])
                        P.op("vector", lambda e, act=act, j=j, sg=sg, pu_=pu_: e.tensor_tensor(act[:, j, :], sg[:], pu_[:], ALU.mult), [dsg, dpu_], [dact])
                        advance(nxt, 2)
                    for oc in range(KC):
                        ps, dps = psf.next()
                        mm_group(ps[:], [(wds[:, j, oc * 128:(oc + 1) * 128], act[:, j, :]) for j in range(NJ)], [dact, dwf[2]], [dps])
                        P.op("vector", lambda e, xt=xt, ps=ps, oc=oc, seg=seg: e.scalar_tensor_tensor(xt[:, oc, :], ps[:], modv[:, 40 + oc, seg:seg + 1], xt[:, oc, :], ALU.mult, ALU.add), [dps, dmod, dxt], [dxt])
                        advance(nxt, 2)
                    if direct_out:
                        for sub in range(4):
                            tt0 = t0 + sub * 128
                            o, do = yo_.next()
                            for hh in range(2):
                                ps, dps = pso2.next()

                                def fnT(e, ps=ps, xt=xt, hh=hh, sub=sub):
                                    ins = None
                                    for j in range(4):
                                        ins = e.transpose(ps[:, j * 128:(j + 1) * 128], xt[:, hh * 4 + j, sub * 128:(sub + 1) * 128], identf)
                                    return ins
                                P.op("tensor", fnT, [dxt, dcst], [dps])
                                evac(o[:, hh * 512:(hh + 1) * 512], ps[:], [dps], [do])
                            dstap = ys[tt0:tt0 + 128, :] if tt0 < NS else yp[tt0 - NS:tt0 - NS + 128, :]
                            P.dma("sync", dstap, o[:], reads=[do], writes=[dOUT])
                    else:
                        P.dma("sync", XT[:, :, t0:t0 + 512], xt[:], reads=[dxt], writes=[dXT])
                    advance(nxt, 1000)
                    cxc = cxn

    if not dbg:
        P.finish([dOUT])
        return nc
    with P.phase():
        xr = Ring(P, "fx", [128, KC, 512], F32, 2)
        yo = Ring(P, "fy", [128, D], F32, 4)
        pst = Ring(P, "fps", [128, 512], F32, 8, "psum")
        for gi in range(NT // 512):
            a, da = xr.next()
            P.dma("sync", a[:], XT[:, :, gi * 512:(gi + 1) * 512], reads=[dXT], writes=[da])
            for sub in range(4):
                t0 = gi * 512 + sub * 128
                o, do = yo.next()
                for hh in range(2):
                    ps, dps = pst.next()

                    def fn(e, ps=ps, a=a, hh=hh, sub=sub):
                        ins = None
                        for j in range(4):
                            ins = e.transpose(ps[:, j * 128:(j + 1) * 128], a[:, hh * 4 + j, sub * 128:(sub + 1) * 128], identf)
                        return ins
                    P.op("tensor", fn, [da, dcst], [dps])
                    evac(o[:, hh * 512:(hh + 1) * 512], ps[:], [dps], [do])
                dstap = ys[t0:t0 + 128, :] if t0 < NS else yp[t0 - NS:t0 - NS + 128, :]
                P.dma("sync", dstap, o[:], reads=[do], writes=[dOUT])
    P.finish([dOUT])
    return nc


def _consts():
    c = np.zeros((128, CW), np.float32)
    c[:, 0:128] = np.eye(128)
    c[:, 128:256] = 1.0
    c[0:64, 256:320] = 1.0
    c[64:128, 320:384] = 1.0
    m = np.arange(128)
    partner = np.where((m % 32) < 16, m + 16, m - 16)
    c[partner, 384 + m] = 1.0
    b = np.arange(128)[:, None]; a = np.arange(128)[None, :]
    c[:, 512:640] = (b >= a)
    c[:, 640:768] = (b <= a)
    i = np.arange(64)[:, None]; j = np.arange(64)[None, :]
    for k in range(8):
        fwd = k < 4
        c[0:64, 768 + k * 64:768 + (k + 1) * 64] = (i > j) if fwd else (i < j)
        c[0:64, 1280 + k * 64:1280 + (k + 1) * 64] = (i >= j) if fwd else (i <= j)
        c[0:64, 1792 + k * 64:1792 + (k + 1) * 64] = np.eye(64)
    c[0:64, 2304:2368] = (i <= j)
    c[0:64, 2368:2432] = (i >= j)
    for k in range(8):
        c[0:64, 2432 + k * 64:2432 + (k + 1) * 64] = (i <= j) if k < 4 else (i >= j)
    c[64:128, 768:2304] = c[0:64, 768:2304]
    c[64:128, 2432:2944] = c[0:64, 2432:2944]
    c[0:64, 2944:3008] = (i <= j); c[64:128, 3008:3072] = (i <= j)
    c[0:64, 3072:3136] = (i >= j); c[64:128, 3136:3200] = (i >= j)
    t = np.arange(4096)
    p = np.arange(128)
    d = p % 64
    jj = (d % 16).astype(np.float32)
    inv = (1.0 / (np.float32(10000.0) ** (jj / np.float32(16.0)))).astype(np.float32)
    pos = np.where((d < 32)[:, None], (t // 64)[None, :], (t % 64)[None, :]).astype(np.float32)
    ang = (pos * inv[:, None]).astype(np.float32)
    cos = np.cos(ang).astype(np.float32)
    sgn = np.where((d % 32) < 16, -1.0, 1.0).astype(np.float32)
    sin = (np.sin(ang) * sgn[:, None]).astype(np.float32)
    return c, cos, sin


def _col_perm():
    perm = np.arange(NCOLS)
    for c in range(4):
        for half, h in ((0, c), (1, 4 + c)):
            perm[768 + c * 128 + half * 64:768 + c * 128 + half * 64 + 64] = 768 + h * 64 + np.arange(64)
    return perm


def _row_perm():
    perm = np.arange(D)
    for c in range(4):
        for half, h in ((0, c), (1, 4 + c)):
            perm[256 + c * 128 + half * 64:256 + c * 128 + half * 64 + 64] = 256 + h * 64 + np.arange(64)
    return perm


def host_prep(inp, depth):
    f = lambda a: np.ascontiguousarray(np.asarray(a, dtype=np.float32))
    cst, cos, sin = _consts()
    cp, rp = _col_perm(), _row_perm()
    shared = {
        "win": f(np.asarray(inp["w_in"])[:depth][:, :, cp]),
        "wout": f(np.asarray(inp["w_out"])[:depth][:, rp, :]),
        "adaw": f(np.asarray(inp["ada_w"])[:depth]),
        "wg": f(np.asarray(inp["w_gate"])[:depth]), "wu": f(np.asarray(inp["w_up"])[:depth]), "wd": f(np.asarray(inp["w_down"])[:depth]),
        "cst": cst, "ropec": cos, "ropes": sin,
    }
    pv = np.zeros((depth, 128, 90), np.float32); bc = np.zeros((depth, 128, 96), np.float32)
    for l in range(depth):
        pv[l, :, 0:48] = np.asarray(inp["ada_b"])[l].reshape(48, 128).T
        pv[l, :, 48:56] = np.asarray(inp["norm1_g"])[l].reshape(8, 128).T
        pv[l, :, 56:64] = np.asarray(inp["norm2_g"])[l].reshape(8, 128).T
        scw = np.asarray(inp["sc_conv_w"])[l]; dnw = np.asarray(inp["dn_conv_w"])[l]
        for k in range(3):
            pv[l, :, 64 + k * 2:64 + k * 2 + 2] = scw[k].reshape(2, 128).T
            pv[l, :, 70 + k * 6:70 + k * 6 + 6] = dnw[k].reshape(6, 128).T
        pv[l, :, 88] = np.tile(np.asarray(inp["q_norm_g"])[l], 2)
        pv[l, :, 89] = np.tile(np.asarray(inp["k_norm_g"])[l], 2)
        bc[l, :, 0:8] = np.asarray(inp["dn_A_log"])[l].reshape(8)[None, :]
        bc[l, :, 8:16] = np.asarray(inp["dn_dt_bias"])[l].reshape(8)[None, :]
        bc[l, :, 16:80] = np.asarray(inp["dn_norm_g"])[l][None, :]
        sk = np.asarray(inp["attn_sink"])[l]
        bc[l, 0:64, 88:92] = sk[0:4][None, :]
        bc[l, 64:128, 88:92] = sk[4:8][None, :]
        bc[l, 0:64, 92:96] = sk[4:8][None, :]
        bc[l, 64:128, 92:96] = sk[0:4][None, :]
    shared["pv"] = pv; shared["bcp"] = bc
    return shared


def core_inputs(inp, shared, core, NS, depth, nsamp):
    f = lambda a: np.ascontiguousarray(np.asarray(a, dtype=np.float32))
    b = core % nsamp
    m = dict(shared)
    m["xs"] = f(np.asarray(inp["x_sample"])[b, :NS])
    m["xp"] = f(np.asarray(inp["x_prompt"])[2 * core:2 * core + 2].reshape(512, D))
    m["ck"] = f(np.asarray(inp["cache_k"])[b, :depth].reshape(depth, 512, 128))
    m["cv"] = f(np.asarray(inp["cache_v"])[b, :depth].reshape(depth, 512, 128))
    m["s0"] = f(np.asarray(inp["state_delta"])[b, :depth].reshape(depth, 8, 64, 64))
    cT = np.zeros((128, 16), np.float32)
    cT[:, 0::2] = np.asarray(inp["c"])[b].reshape(8, 128).T
    cT[:, 1::2] = np.asarray(inp["c_ctx"]).reshape(8, 128).T
    m["cT"] = cT
    return m


_NC_CACHE = {}


def kernel(**inputs):
    NS, depth, ncores = 4096, 2, 8
    if "full" not in _NC_CACHE:
        _NC_CACHE["full"] = build(NS, depth)
    nc = _NC_CACHE["full"]
    shared = host_prep(inputs, depth)
    in_maps = [core_inputs(inputs, shared, c, NS, depth, 4) for c in range(ncores)]
    res = run_bass_kernel_spmd(nc, in_maps, core_ids=list(range(ncores)))
    R = res.results
    y_p = np.concatenate([np.asarray(R[c]["yp"]).reshape(2, 256, D) for c in range(8)], 0)
    y_s = np.stack([np.asarray(R[c]["ys"]) for c in range(4)], 0)
    nk = np.concatenate([np.asarray(R[c]["nk"]).reshape(2, depth, 256, 2, 64) for c in range(8)], 0)
    nv = np.concatenate([np.asarray(R[c]["nv"]).reshape(2, depth, 256, 2, 64) for c in range(8)], 0)
    ns = np.concatenate([np.asarray(R[c]["nst"]).reshape(2, depth, 2, 4, 64, 64) for c in range(8)], 0)
    return (y_p.astype(np.float32), y_s.astype(np.float32), nk.astype(np.float32), nv.astype(np.float32), ns.astype(np.float32))
```
